# Optimizing a Trainium2 kernel written in Bass

```python
import jax, jax.numpy as jnp
from jax import lax
import numpy as np

D_MODEL = 1024
BATCH = 8
SEQ = 2048
DEPTH = 2
DEC_BATCH = 8
DEC_SEQ = 16
PAST_LEN = 4096

CHUNK = 64
Q_BLOCK = 128
EPS = 1e-6
N_AB = (DEPTH + 1) // 2
N_C = DEPTH // 2

D_A = D_MODEL
N_BLK_A = 8
BLK_A = D_A // N_BLK_A
CONV_W = 4
LRU_C = 8.0

H_B = 4
DK_B = D_MODEL // 8
DV_B = D_MODEL // 4
GATE_RANK = 16
GATE_TAU = 16.0

AB_SPLITS = (D_A, D_A, H_B * DK_B, H_B * DK_B, H_B * DV_B, H_B * DV_B, GATE_RANK)
IN_AB = 2 * D_A + 2 * H_B * DK_B + 2 * H_B * DV_B + GATE_RANK
MIX_AB = D_A + H_B * DV_B

H_C = 16
Q_LORA = 384
KV_LORA = 256
NOPE = 64
ROPE = 32
V_HEAD = 64
ROPE_BASE = 10000.0
IN_C = Q_LORA + KV_LORA + ROPE
SM_SCALE = (NOPE + ROPE) ** -0.5

D_FF = -(-8 * D_MODEL // (3 * 256)) * 256

kernel_name = 'hybrid_lru_gla_mla_streaming_step'

F32 = jnp.float32


def rms_norm(x, g):
    xf = x.astype(F32)
    y = xf * lax.rsqrt(jnp.mean(xf * xf, axis=-1, keepdims=True) + EPS)
    return (y * g.astype(F32)).astype(x.dtype)


def rope(x, pos):
    half = ROPE // 2
    inv = ROPE_BASE ** (-jnp.arange(half, dtype=F32) / half)
    ang = pos.astype(F32)[:, None] * inv[None, :]
    shape = (1, ang.shape[0]) + (1,) * (x.ndim - 3) + (half,)
    cos = jnp.cos(ang).reshape(shape)
    sin = jnp.sin(ang).reshape(shape)
    xf = x.astype(F32)
    x1, x2 = xf[..., :half], xf[..., half:]
    return jnp.concatenate([x1 * cos - x2 * sin, x1 * sin + x2 * cos], axis=-1).astype(x.dtype)


def causal_dwconv(x, buf, w, b):
    L = x.shape[1]
    xp = jnp.concatenate([buf.astype(x.dtype), x], axis=1)
    y = b + sum(xp[:, j:j + L] * w[j] for j in range(CONV_W))
    return y, xp[:, L:]


def _lin_combine(left, right):
    a_l, b_l = left
    a_r, b_r = right
    return a_l * a_r, a_r * b_l + b_r


def rg_lru(x, h0, w_a, b_a, w_x, b_x, lam):
    Bn, L, _ = x.shape
    xf = x.astype(F32)
    xb = xf.reshape(Bn, L, N_BLK_A, BLK_A)
    r = jax.nn.sigmoid(jnp.einsum('blnc,ncd->blnd', xb, w_a.astype(F32)).reshape(Bn, L, D_A) + b_a.astype(F32))
    i = jax.nn.sigmoid(jnp.einsum('blnc,ncd->blnd', xb, w_x.astype(F32)).reshape(Bn, L, D_A) + b_x.astype(F32))
    log_a = -LRU_C * r * jax.nn.softplus(-lam.astype(F32))
    a = jnp.exp(log_a)
    u = jnp.sqrt(-jnp.expm1(2.0 * log_a)) * (i * xf)
    u = u.at[:, 0].add(a[:, 0] * h0.astype(F32))
    _, h = lax.associative_scan(_lin_combine, (a, u), axis=1)
    return h, h[:, -1]


def gla(q, k, v, log_g, S0):
    Bn, H, L, _ = q.shape
    blk = CHUNK if L % CHUNK == 0 else L
    n = L // blk

    def split(t):
        return jnp.moveaxis(t.astype(F32).reshape(Bn, H, n, blk, t.shape[-1]), 2, 0)

    causal = jnp.tril(jnp.ones((blk, blk), dtype=bool))

    def step(S, inp):
        qc, kc, vc, gc = inp
        b = jnp.cumsum(gc, axis=2)
        o_inter = jnp.einsum('bhtk,bhkv->bhtv', qc * jnp.exp(b), S)
        diff = b[:, :, :, None, :] - b[:, :, None, :, :]
        decay = jnp.exp(jnp.where(causal[:, :, None], diff, -jnp.inf))
        A = jnp.einsum('bhtk,bhsk,bhtsk->bhts', qc, kc, decay)
        o = o_inter + jnp.einsum('bhts,bhsv->bhtv', A, vc)
        b_last = b[:, :, -1:, :]
        S_new = jnp.exp(b_last[:, :, 0, :])[..., None] * S + jnp.einsum('bhsk,bhsv->bhkv', kc * jnp.exp(b_last - b), vc)
        return S_new, o

    S_fin, o = lax.scan(step, S0.astype(F32), (split(q), split(k), split(v), split(log_g)))
    o = jnp.moveaxis(o, 0, 2).reshape(Bn, H, L, -1)
    return o, S_fin


def mixer_ab(u, conv_buf, lru_h0, gla_S0, w_in, conv_w, conv_b, w_a, b_a, w_x, b_x, lam,
             w_gate, b_gate, g_norm, w_out):
    Bn, L, _ = u.shape
    cuts = [int(c) for c in np.cumsum(AB_SPLITS)[:-1]]
    xa, ga, qb, kb, vb, gb, zr = jnp.split(u @ w_in, cuts, axis=-1)
    xa, new_buf = causal_dwconv(xa, conv_buf, conv_w, conv_b)
    ha, new_h = rg_lru(xa, lru_h0, w_a, b_a, w_x, b_x, lam)
    ya = ha.astype(u.dtype) * jax.nn.gelu(ga)
    def heads(t, d):
        return t.reshape(Bn, L, H_B, d).transpose(0, 2, 1, 3)
    log_g = jax.nn.log_sigmoid((zr @ w_gate + b_gate).astype(F32)) / GATE_TAU
    o, new_S = gla(heads(qb, DK_B) * (DK_B ** -0.5), heads(kb, DK_B), heads(vb, DV_B), heads(log_g, DK_B), gla_S0)
    o = rms_norm(o, g_norm).transpose(0, 2, 1, 3).reshape(Bn, L, H_B * DV_B).astype(u.dtype)
    yb = o * jax.nn.silu(gb)
    y = jnp.concatenate([ya, yb], axis=-1) @ w_out
    return y, new_buf, new_h.astype(lru_h0.dtype), new_S.astype(gla_S0.dtype)


def mla_project(u, pos, w_in, q_norm, w_uq, kv_norm):
    Bn, L, _ = u.shape
    cq, ckv, kpe = jnp.split(u @ w_in, [Q_LORA, Q_LORA + KV_LORA], axis=-1)
    q = (rms_norm(cq, q_norm) @ w_uq).reshape(Bn, L, H_C, NOPE + ROPE)
    q_nope = q[..., :NOPE]
    q_pe = rope(q[..., NOPE:], pos)
    return q_nope, q_pe, rms_norm(ckv, kv_norm), rope(kpe, pos)


def mla_prompt(u, w_in, q_norm, w_uq, kv_norm, w_uk, w_uv, w_out):
    Bn, L, _ = u.shape
    pos = jnp.arange(L)
    q_nope, q_pe, ckv, kpe = mla_project(u, pos, w_in, q_norm, w_uq, kv_norm)
    k_nope = jnp.einsum('bsc,chd->bhsd', ckv, w_uk)
    v = jnp.einsum('bsc,chd->bhsd', ckv, w_uv)
    key_chunk = pos // CHUNK

    def block(i):
        qs = i * Q_BLOCK
        qn = lax.dynamic_slice_in_dim(q_nope, qs, Q_BLOCK, axis=1)
        qp = lax.dynamic_slice_in_dim(q_pe, qs, Q_BLOCK, axis=1)
        s = (jnp.einsum('bqhd,bhkd->bhqk', qn, k_nope) + jnp.einsum('bqhr,bkr->bhqk', qp, kpe)).astype(F32) * SM_SCALE
        q_chunk = (qs + jnp.arange(Q_BLOCK)) // CHUNK
        s = jnp.where(key_chunk[None, :] <= q_chunk[:, None], s, -jnp.inf)
        p = jax.nn.softmax(s, axis=-1).astype(v.dtype)
        return jnp.einsum('bhqk,bhkd->bqhd', p, v)

    o = lax.map(block, jnp.arange(L // Q_BLOCK))
    o = jnp.moveaxis(o, 0, 1).reshape(Bn, L, H_C * V_HEAD)
    return o @ w_out, ckv, kpe


def mla_sample(u, cache_ckv, cache_kpe, w_in, q_norm, w_uq, kv_norm, w_uk, w_uv, w_out):
    Bn, L, _ = u.shape
    P = cache_ckv.shape[1]
    pos = P + jnp.arange(L)
    q_nope, q_pe, ckv, kpe = mla_project(u, pos, w_in, q_norm, w_uq, kv_norm)
    q_lat = jnp.einsum('blhd,chd->blhc', q_nope, w_uk)
    c_all = jnp.concatenate([cache_ckv.astype(ckv.dtype), ckv], axis=1)
    k_all = jnp.concatenate([cache_kpe.astype(kpe.dtype), kpe], axis=1)
    s = (jnp.einsum('blhc,bkc->bhlk', q_lat, c_all) + jnp.einsum('blhr,bkr->bhlk', q_pe, k_all)).astype(F32) * SM_SCALE
    p = jax.nn.softmax(s, axis=-1).astype(c_all.dtype)
    o_lat = jnp.einsum('bhlk,bkc->blhc', p, c_all)
    o = jnp.einsum('blhc,chd->blhd', o_lat, w_uv).reshape(Bn, L, H_C * V_HEAD)
    return o @ w_out, ckv, kpe


def swiglu(h, w_g, w_u, w_d):
    return (jax.nn.silu(h @ w_g) * (h @ w_u)) @ w_d


def setup_inputs(seed: int = 0) -> dict:
    key = jax.random.key(seed)
    ks = iter(jax.random.split(key, 48))

    def nrm(shape, scale=1.0):
        return jax.random.normal(next(ks), shape, F32) * scale

    def gain(shape):
        return 1.0 + nrm(shape, 0.02)

    a0 = jax.random.uniform(next(ks), (N_AB, D_A), F32, 0.9, 0.999) ** (1.0 / LRU_C)
    lam = jnp.log(a0) - jnp.log1p(-a0)
    return {
        'x_prompt': nrm((BATCH, SEQ, D_MODEL)),
        'x_sample': nrm((DEC_BATCH, DEC_SEQ, D_MODEL)),
        'state_conv_a': nrm((N_AB, DEC_BATCH, CONV_W - 1, D_A)),
        'state_lru_h': nrm((N_AB, DEC_BATCH, D_A), 0.5),
        'state_gla_S': nrm((N_AB, DEC_BATCH, H_B, DK_B, DV_B)),
        'cache_mla_ckv': nrm((N_C, DEC_BATCH, PAST_LEN, KV_LORA)),
        'cache_mla_kpe': nrm((N_C, DEC_BATCH, PAST_LEN, ROPE)),
        'norm_mix_pre': gain((DEPTH, D_MODEL)),
        'norm_mix_post': gain((DEPTH, D_MODEL)),
        'norm_ffn_pre': gain((DEPTH, D_MODEL)),
        'norm_ffn_post': gain((DEPTH, D_MODEL)),
        'w_in_ab': nrm((N_AB, D_MODEL, IN_AB), D_MODEL ** -0.5),
        'conv_w_a': nrm((N_AB, CONV_W, D_A), 0.5),
        'conv_b_a': nrm((N_AB, D_A), 0.01),
        'lru_w_a': nrm((N_AB, N_BLK_A, BLK_A, BLK_A), BLK_A ** -0.5),
        'lru_b_a': nrm((N_AB, D_A), 0.01),
        'lru_w_x': nrm((N_AB, N_BLK_A, BLK_A, BLK_A), BLK_A ** -0.5),
        'lru_b_x': nrm((N_AB, D_A), 0.01),
        'lru_lambda': lam,
        'gla_w_gate': nrm((N_AB, GATE_RANK, H_B * DK_B), GATE_RANK ** -0.5),
        'gla_b_gate': nrm((N_AB, H_B * DK_B), 0.01),
        'gla_norm': gain((N_AB, DV_B)),
        'w_out_ab': nrm((N_AB, MIX_AB, D_MODEL), MIX_AB ** -0.5),
        'w_in_c': nrm((N_C, D_MODEL, IN_C), D_MODEL ** -0.5),
        'mla_q_norm': gain((N_C, Q_LORA)),
        'w_uq': nrm((N_C, Q_LORA, H_C * (NOPE + ROPE)), Q_LORA ** -0.5),
        'mla_kv_norm': gain((N_C, KV_LORA)),
        'w_uk': nrm((N_C, KV_LORA, H_C, NOPE), KV_LORA ** -0.5),
        'w_uv': nrm((N_C, KV_LORA, H_C, V_HEAD), KV_LORA ** -0.5),
        'w_out_c': nrm((N_C, H_C * V_HEAD, D_MODEL), (H_C * V_HEAD) ** -0.5),
        'ffn_w_gate': nrm((DEPTH, D_MODEL, D_FF), D_MODEL ** -0.5),
        'ffn_w_up': nrm((DEPTH, D_MODEL, D_FF), D_MODEL ** -0.5),
        'ffn_w_down': nrm((DEPTH, D_FF, D_MODEL), D_FF ** -0.5),
    }


def reference(x_prompt, x_sample, state_conv_a, state_lru_h, state_gla_S, cache_mla_ckv, cache_mla_kpe,
              norm_mix_pre, norm_mix_post, norm_ffn_pre, norm_ffn_post,
              w_in_ab, conv_w_a, conv_b_a, lru_w_a, lru_b_a, lru_w_x, lru_b_x, lru_lambda,
              gla_w_gate, gla_b_gate, gla_norm, w_out_ab,
              w_in_c, mla_q_norm, w_uq, mla_kv_norm, w_uk, w_uv, w_out_c,
              ffn_w_gate, ffn_w_up, ffn_w_down):

    def run(x, conv_state, lru_state, gla_state, ckv_cache, kpe_cache):
        h = x
        n_conv, n_h, n_S, n_ckv, n_kpe = [], [], [], [], []
        for layer in range(DEPTH):
            j = layer // 2
            u = rms_norm(h, norm_mix_pre[layer])
            if layer % 2 == 0:
                y, cb, hl, S = mixer_ab(u, conv_state[j], lru_state[j], gla_state[j], w_in_ab[j],
                                        conv_w_a[j], conv_b_a[j], lru_w_a[j], lru_b_a[j], lru_w_x[j],
                                        lru_b_x[j], lru_lambda[j], gla_w_gate[j], gla_b_gate[j],
                                        gla_norm[j], w_out_ab[j])
                n_conv.append(cb)
                n_h.append(hl)
                n_S.append(S)
            else:
                if ckv_cache is None:
                    y, c, kp = mla_prompt(u, w_in_c[j], mla_q_norm[j], w_uq[j], mla_kv_norm[j],
                                          w_uk[j], w_uv[j], w_out_c[j])
                else:
                    y, c, kp = mla_sample(u, ckv_cache[j], kpe_cache[j], w_in_c[j], mla_q_norm[j], w_uq[j],
                                          mla_kv_norm[j], w_uk[j], w_uv[j], w_out_c[j])
                n_ckv.append(c)
                n_kpe.append(kp)
            h = h + rms_norm(y, norm_mix_post[layer])
            f = swiglu(rms_norm(h, norm_ffn_pre[layer]), ffn_w_gate[layer], ffn_w_up[layer], ffn_w_down[layer])
            h = h + rms_norm(f, norm_ffn_post[layer])
        return h, jnp.stack(n_conv), jnp.stack(n_h), jnp.stack(n_S), jnp.stack(n_ckv), jnp.stack(n_kpe)

    Bp = x_prompt.shape[0]
    zero_conv = jnp.zeros((N_AB, Bp, CONV_W - 1, D_A), state_conv_a.dtype)
    zero_h = jnp.zeros((N_AB, Bp, D_A), state_lru_h.dtype)
    zero_S = jnp.zeros((N_AB, Bp, H_B, DK_B, DV_B), state_gla_S.dtype)
    y_prompt, p_conv, p_h, p_S, p_ckv, p_kpe = run(x_prompt, zero_conv, zero_h, zero_S, None, None)
    y_sample, s_conv, s_h, s_S, s_ckv, s_kpe = run(x_sample, state_conv_a, state_lru_h, state_gla_S,
                                                   cache_mla_ckv, cache_mla_kpe)
    return (y_prompt, y_sample, p_conv, p_h, p_S, p_ckv, p_kpe, s_conv, s_h, s_S, s_ckv, s_kpe)
```

```python
import os
import numpy as np
import concourse.bass as bass
import concourse.mybir as mybir
from concourse.bass_utils import run_bass_kernel_spmd
F32 = mybir.dt.float32
BF16 = mybir.dt.bfloat16
AF = mybir.ActivationFunctionType
OP = mybir.AluOpType
D = 1024
T = 512
NG = 4
TS = 16
DFF = 2816
NFF = 22
EPS = 1e-6
SM_SCALE = 96 ** -0.5
GELU_K = 1.5957691216057308
_VNAMES = []
for _l in range(2):
    _VNAMES += [("mix_pre%d" % _l, 8), ("mix_post%d" % _l, 8), ("ffn_pre%d" % _l, 8), ("ffn_post%d" % _l, 8)]
_VNAMES += [("conv_w", 32), ("conv_b", 8), ("lru_ba", 8), ("lru_bx", 8), ("lam", 8), ("b_gate", 4),
            ("gla_norm", 2), ("q_norm", 3), ("kv_norm", 2), ("h0", 8), ("convs", 24)]
VOFF = {}
_c = 0
for _n, _k in _VNAMES:
    VOFF[_n] = _c
    _c += _k
NV = _c
class Buf:
    def __init__(self, t, name):
        self.t = t
        self.name = name
        self.st = {}
        self.excl = False
    def __getitem__(self, idx):
        return self.t[idx]
class Prog:
    ENGS = ("pe", "act", "dve", "pool", "sp")
    def __init__(self, nc):
        self.nc = nc
        self.ops = {e: [] for e in self.ENGS}
        self.known = {e: {} for e in self.ENGS}
        self.sems = {}
        self.cnt = {}
        self._stack = []
        self.epoch = {e: 0 for e in self.ENGS}
        self.dma_rr = {}
        for e in self.ENGS:
            self.new_sem("E_%s_0" % e)
    def enter(self, cm):
        r = cm.__enter__()
        self._stack.append(cm)
        return r
    def close(self):
        while self._stack:
            self._stack.pop().__exit__(None, None, None)
    def new_sem(self, sid):
        if sid not in self.sems:
            self.sems[sid] = self.enter(self.nc.semaphore(sid))
            self.cnt[sid] = 0
        return sid
    def sbuf(self, name, shape, dt):
        return Buf(self.enter(self.nc.sbuf_tensor("sb_" + name, list(shape), dt)), name)
    def psum(self, name, shape, dt=F32):
        b = Buf(self.enter(self.nc.psum_tensor("ps_" + name, list(shape), dt)), name)
        b.excl = True
        return b
    @staticmethod
    def _norm(lst):
        out = []
        for x in lst or []:
            out.append((x, None) if isinstance(x, Buf) else x)
        return out
    def _deps(self, reads, writes):
        w = {}
        def add(s, v):
            if w.get(s, 0) < v:
                w[s] = v
        for b, k in reads:
            keys = [k, None] if k is not None else list(b.st.keys())
            for kk in keys:
                st = b.st.get(kk)
                if st and st[0] is not None:
                    add(*st[0])
        for b, k in writes:
            keys = [k, None] if k is not None else list(b.st.keys())
            for kk in keys:
                st = b.st.get(kk)
                if st:
                    if st[0] is not None:
                        add(*st[0])
                    for s, v in st[1].items():
                        add(s, v)
        return w
    def _record(self, reads, writes, tok):
        for b, k in writes:
            if k is None:
                b.st = {None: [tok, {}]}
            else:
                b.st[k] = [tok, {}]
        for b, k in reads:
            if k is None:
                b.st.setdefault(None, [None, {}])
                for st in b.st.values():
                    if st[1].get(tok[0], 0) < tok[1]:
                        st[1][tok[0]] = tok[1]
            else:
                st = b.st.setdefault(k, [None, {}])
                if st[1].get(tok[0], 0) < tok[1]:
                    st[1][tok[0]] = tok[1]
    def op(self, eng, fn, reads=None, writes=None, dma_sem=None):
        reads = self._norm(reads)
        writes = self._norm(writes)
        xr = [r for r in reads if r[0].excl]
        if xr:
            reads = [r for r in reads if not r[0].excl]
            writes = writes + [r for r in xr if r not in writes]
        w = self._deps(reads, writes)
        if self.cnt["E_%s_%d" % (eng, self.epoch[eng])] >= 30000:
            self.epoch[eng] += 1
            self.new_sem("E_%s_%d" % (eng, self.epoch[eng]))
        own = "E_%s_%d" % (eng, self.epoch[eng])
        kn = self.known[eng]
        waits = []
        for s, v in w.items():
            if eng == "pe" and s.startswith("E_pe_"):
                continue
            if kn.get(s, 0) >= v:
                continue
            kn[s] = v
            waits.append((s, v))
        if dma_sem is not None:
            npool = 8
            i = self.dma_rr.get(eng, 0)
            self.dma_rr[eng] = i + 1
            dma_sem = "D_%s_%d" % (eng, i % npool)
            self.new_sem(dma_sem)
            if self.cnt[dma_sem] > 0 and kn.get(dma_sem, 0) < self.cnt[dma_sem]:
                kn[dma_sem] = self.cnt[dma_sem]
                waits.append((dma_sem, self.cnt[dma_sem]))
            self.cnt[dma_sem] += 16
            tok = (dma_sem, self.cnt[dma_sem])
            self.ops[eng].append((waits, fn, (dma_sem, 16)))
        else:
            self.cnt[own] += 1
            tok = (own, self.cnt[own])
            self.ops[eng].append((waits, fn, (own, 1)))
        self._record(reads, writes, tok)
        return tok
    def emit(self):
        nc = self.nc
        with nc.Block() as block:
            def body(ename):
                def f(e):
                    for waits, fn, (s, inc) in self.ops[ename]:
                        for ws, wv in waits:
                            e.wait_ge(self.sems[ws], wv)
                        ins = fn(e)
                        ins.then_inc(self.sems[s], inc)
                    if ename == "sp":
                        for s, c in self.cnt.items():
                            if c > 0:
                                e.wait_ge(self.sems[s], c)
                return f
            block.tensor(body("pe"))
            block.scalar(body("act"))
            block.vector(body("dve"))
            block.gpsimd(body("pool"))
            block.sync(body("sp"))
class Ctx:
    pass
def build(stage=99, ngroups=NG):
    nc = bass.Bass("TRN2", target_bir_lowering=False)
    P = Prog(nc)
    def di(n, s):
        return nc.dram_tensor(n, list(s), F32, kind="ExternalInput").ap()
    def do(n, s):
        return nc.dram_tensor(n, list(s), F32, kind="ExternalOutput").ap()
    xp = di("xp", [2048, D])
    xs = di("xs", [TS, D])
    gla_s = di("gla_s", [4, 128, 256])
    cckv = di("cckv", [4096, 256])
    ckpe = di("ckpe", [4096, 32])
    vecs_d = di("vecs", [128, NV])
    w_in_ab = di("w_in_ab", [D, 5136])
    lru_wa = di("lru_wa", [1024, 128])
    lru_wx = di("lru_wx", [1024, 128])
    w_gate = di("w_gate", [16, 512])
    w_out_ab = di("w_out_ab", [2048, D])
    w_in_c = di("w_in_c", [D, 768])
    w_uq = di("w_uq", [384, 1536])
    w_uq_sw = di("w_uq_sw", [384, 1536])
    w_ukv = di("w_ukv", [256, 2048])
    w_ukT = di("w_ukT", [64, 4096])
    w_out_c = di("w_out_c", [1024, D])
    ffn_wg = di("ffn_wg", [2 * D, DFF])
    ffn_wu = di("ffn_wu", [2 * D, DFF])
    ffn_wd = di("ffn_wd", [2 * DFF, D])
    ident_d = di("ident", [128, 128])
    tri4_d = di("tri4", [128, 512])
    tri4s_d = di("tri4s", [128, 64])
    rmask_d = di("rmask", [128, 512])
    ropeP_d = di("ropeP", [32, 2 * 2048])
    ropeS_d = di("ropeS", [32, 2 * TS])
    y_p = do("y_p", [2048, D])
    y_s = do("y_s", [TS, D])
    o_conv = [do("p_conv", [3, D]), do("s_conv", [3, D])]
    o_h = [do("p_h", [8, 128]), do("s_h", [8, 128])]
    o_S = [do("p_S", [4, 128, 256]), do("s_S", [4, 128, 256])]
    o_ckv = [do("p_ckv", [2048, 256]), do("s_ckv", [TS, 256])]
    o_kpe = [do("p_kpe", [2048, 32]), do("s_kpe", [TS, 32])]
    def MM(ps, out, lhsT, rhs, st, sp, reads, skip=False):
        P.op("pe", lambda e: e.matmul(out, lhsT=lhsT, rhs=rhs, start=st, stop=sp, skip_group_check=skip), reads=reads, writes=[ps])
    def TR(ps, out, in_, idn, reads):
        P.op("pe", lambda e: e.transpose(out, in_, idn), reads=reads, writes=[ps])
    def ACT(out, in_, func, reads, writes, bias=None, scale=None):
        kw = {}
        if bias is not None:
            kw["bias"] = bias
        if scale is not None:
            kw["scale"] = scale
        P.op("act", lambda e: e.activation(out=out, in_=in_, func=func, **kw), reads=reads, writes=writes)
    def TT(eng, out, a, b, op, reads, writes):
        P.op(eng, lambda e: e.tensor_tensor(out=out, in0=a, in1=b, op=op), reads=reads, writes=writes)
    def TS_(eng, out, a, s1, s2, op0, op1, reads, writes):
        if s2 is None:
            P.op(eng, lambda e: e.tensor_scalar(out=out, in0=a, scalar1=s1, scalar2=None, op0=op0), reads=reads, writes=writes)
        else:
            P.op(eng, lambda e: e.tensor_scalar(out=out, in0=a, scalar1=s1, scalar2=s2, op0=op0, op1=op1), reads=reads, writes=writes)
    def STT(out, a, s, b, op0, op1, reads, writes):
        P.op("dve", lambda e: e.scalar_tensor_tensor(out=out, in0=a, scalar=s, in1=b, op0=op0, op1=op1), reads=reads, writes=writes)
    def CP(eng, out, in_, reads, writes):
        if eng == "act":
            P.op("act", lambda e: e.activation(out=out, in_=in_, func=AF.Copy), reads=reads, writes=writes)
        else:
            P.op(eng, lambda e: e.tensor_copy(out=out, in_=in_), reads=reads, writes=writes)
    def RECIP(out, in_, reads, writes):
        P.op("dve", lambda e: e.reciprocal(out=out, in_=in_), reads=reads, writes=writes)
    def SCAN(out, d0, d1, init, reads, writes):
        P.op("dve", lambda e: e.tensor_tensor_scan(out=out, data0=d0, data1=d1, initial=init, op0=OP.mult, op1=OP.add), reads=reads, writes=writes)
    def MEMSET(eng, ap, val, writes):
        P.op(eng, lambda e: e.memset(ap, val), writes=writes)
    def DMA(eng, out, in_, reads, writes, sem):
        P.op(eng, lambda e: e.dma_start(out=out, in_=in_), reads=reads, writes=writes, dma_sem=sem)
    allb = [P.psum("pb%d" % i, [128, 512], F32) for i in range(7)]
    acc = [allb[5], allb[6]]
    psb = P.psum("psb", [128, 1024], BF16)
    rot = [0]
    rot_n = [5]

    def ps_next():
        b = allb[rot[0] % rot_n[0]]
        rot[0] += 1
        return b
    ident = P.sbuf("ident", [128, 128], F32)
    identb = P.sbuf("identb", [128, 128], BF16)
    ones_bf = P.sbuf("ones_bf", [128, 128], BF16)
    ones_f = P.sbuf("ones_f", [128, 64], F32)
    tri4 = P.sbuf("tri4", [128, 512], BF16)
    tri4s = P.sbuf("tri4s", [128, 64], BF16)
    rmask = P.sbuf("rmask", [128, 512], BF16)
    vecs = P.sbuf("vecs", [128, NV], F32)
    drv = P.sbuf("drv", [128, 48], F32)
    wgate = P.sbuf("wgate", [16, 512], BF16)
    wab = P.sbuf("wab", [128, 2 * 8 * 128], BF16)
    diag = P.sbuf("diag", [128, 3 * 4 * 128], BF16)
    DMA("sp", ident[:, :], ident_d[:, :], [], [ident], "C0")
    DMA("pool", tri4[:, :], tri4_d[:, :], [], [tri4], "C1")
    DMA("pool", tri4s[:, :], tri4s_d[:, :], [], [tri4s], "C1")
    DMA("pool", rmask[:, :], rmask_d[:, :], [], [rmask], "C1")
    DMA("sp", vecs[:, :], vecs_d[:, :], [], [vecs], "C0")
    DMA("pool", identb[:, :], ident_d[:, :], [], [identb], "C1")
    DMA("pool", wgate[:, :], w_gate[:, :], [], [wgate], "C1")
    DMA("pool", wab[:, 0:1024].rearrange("p (n d) -> p n d", d=128), lru_wa.rearrange("(n c) d -> c n d", c=128), [], [wab], "C1")
    DMA("pool", wab[:, 1024:2048].rearrange("p (n d) -> p n d", d=128), lru_wx.rearrange("(n c) d -> c n d", c=128), [], [wab], "C1")
    MEMSET("dve", ones_bf[:, :], 1.0, [ones_bf])
    MEMSET("dve", ones_f[:, :], 1.0, [ones_f])
    def vcol(name, i=0):
        o = VOFF[name] + i
        return vecs[:, o:o + 1]
    TS_("dve", drv[:, 0:4], vecs[:, VOFF["b_gate"]:VOFF["b_gate"] + 4], -1.0, None, OP.mult, None, [vecs], [drv])
    TS_("dve", drv[:, 4:12], vecs[:, VOFF["lru_ba"]:VOFF["lru_ba"] + 8], -1.0, None, OP.mult, None, [vecs], [drv])
    TS_("dve", drv[:, 12:20], vecs[:, VOFF["lru_bx"]:VOFF["lru_bx"] + 8], -1.0, None, OP.mult, None, [vecs], [drv])
    ACT(drv[:, 36:44], vecs[:, VOFF["lam"]:VOFF["lam"] + 8], AF.Exp, [vecs], [drv], scale=-1.0)
    ACT(drv[:, 36:44], drv[:, 36:44], AF.Ln, [drv], [drv], bias=1.0)
    TS_("dve", drv[:, 20:28], drv[:, 36:44], -8.0, None, OP.mult, None, [drv], [drv])
    TS_("dve", drv[:, 28:36], drv[:, 36:44], -16.0, None, OP.mult, None, [drv], [drv])
    NSLOT = 3
    SLOT = 4096
    slots = [P.sbuf("wslot%d" % i, [128, SLOT], BF16) for i in range(NSLOT)]
    sl_i = [0]
    def wload(src2d, kp, nk, ncols):
        i = sl_i[0] % NSLOT
        sl_i[0] += 1
        sb = slots[i]
        assert nk * ncols <= SLOT
        view = sb.t[0:kp, 0:nk * ncols].rearrange("p (k n) -> p k n", n=ncols)
        srcv = src2d.rearrange("(k p) n -> p k n", p=kp)
        kstep = max(1, 1024 // kp)
        k0 = 0
        while k0 < nk:
            k1 = min(nk, k0 + kstep)
            DMA("pool", view[:, k0:k1, :], srcv[:, k0:k1, :], [], [(sb, k0)], "W%d" % i)
            k0 = k1
        return sb, view
    def make_ctx(tag, Tn, is_s):
        c = Ctx()
        c.tag, c.T, c.s = tag, Tn, is_s
        c.tw = min(128, Tn)
        c.nt = Tn // c.tw
        c.C = c.tw
        c.pb = 0 if is_s else 64
        c.hT = P.sbuf("hT" + tag, [128, 8 * Tn], F32)
        c.uT = P.sbuf("uT" + tag, [128, 8 * Tn], BF16)
        c.F2 = P.sbuf("F2" + tag, [128, max(8 * Tn, 1024)], F32)
        c.cum = P.sbuf("cum" + tag, [128, 4 * Tn], F32)
        c.b1n = max(22 * Tn, 20 * Tn + 24, 8 * Tn + c.nt * 1536)
        c.B1 = P.sbuf("B1" + tag, [128, c.b1n], BF16)
        c.B2 = P.sbuf("B2" + tag, [128, 16 * Tn], BF16)
        c.tf = [P.sbuf("tf%d%s" % (i, tag), [128, Tn], F32) for i in range(6)]
        c.tb = [P.sbuf("tb%d%s" % (i, tag), [128, Tn], BF16) for i in range(2)]
        c.rstd = P.sbuf("rstd" + tag, [128, Tn], F32)
        c.S = P.sbuf("S" + tag, [128, 4 * 256], F32)
        c.Sbf = P.sbuf("Sbf" + tag, [128, 4 * 256], BF16)
        c.Se = P.sbuf("Se" + tag, [128, 256], F32)
        c.hst = P.sbuf("hst" + tag, [128, 8], F32)
        c.carry = P.sbuf("carry" + tag, [128, 8 * 3], BF16)
        c.convo = P.sbuf("convo" + tag, [128, 8 * 3], F32)
        c.ebl = P.sbuf("ebl" + tag, [128, 4 * c.nt], F32)
        c.rope = P.sbuf("rope" + tag, [96 if not is_s else 32, 2 * Tn], F32)
        c.atb = [P.sbuf("atb%d%s" % (i, tag), [128, 4 * c.tw], BF16) for i in range(2)]
        c.tbi = 0
        return c
    cp = make_ctx("p", T, False)
    cs = make_ctx("s", TS, True)
    ckvnb = P.sbuf("ckvnb", [128, 2 * 2048], BF16)
    KT = [P.sbuf("KT%d" % i, [96, 2048], BF16) for i in range(2)]
    Vp = P.sbuf("Vp", [128, 16 * 2 * 65], BF16)
    PTs = [P.sbuf("PT%d" % i, [128, 512], BF16) for i in range(3)]
    pt_i = [0]
    def pt_next():
        b = PTs[pt_i[0] % 3]
        pt_i[0] += 1
        return b
    MEMSET("pool", Vp[:, :], 1.0, [Vp])
    def v3(buf, off, nk, Tn, p0=0, p1=128):
        return buf.t[p0:p1, off:off + nk * Tn].rearrange("p (k t) -> p k t", t=Tn)
    MEMSET("dve", cp.S[:, :], 0.0, [cp.S])
    MEMSET("dve", cp.Sbf[:, :], 0.0, [cp.Sbf])
    MEMSET("dve", cp.hst[:, :], 0.0, [cp.hst])
    MEMSET("dve", cp.carry[:, :], 0.0, [cp.carry])
    DMA("sp", cs.S[:, :].rearrange("p (h v) -> p h v", v=256), gla_s.rearrange("h k v -> k h v"), [], [cs.S], "C0")
    CP("act", cs.Sbf[:, :], cs.S[:, :], [cs.S], [cs.Sbf])
    CP("dve", cs.hst[:, :], vecs[:, VOFF["h0"]:VOFF["h0"] + 8], [vecs], [cs.hst])
    CP("dve", cs.carry[:, :], vecs[:, VOFF["convs"]:VOFF["convs"] + 24], [vecs], [cs.carry])
    DMA("sp", cs.rope[0:32, :], ropeS_d[:, :], [], [cs.rope], "C0")
    def rms(c, src, off, nk, nfeat, keyed=False):
        Tn = c.T
        ps = ps_next()
        for k in range(nk):
            sq = c.tb[c.tbi % 2]
            c.tbi += 1
            ACT(sq[:, :], src.t[:, off + k * Tn: off + (k + 1) * Tn], AF.Square, [(src, k)] if keyed else [src], [sq])
            MM(ps, ps[:, 0:Tn], ones_bf[:, :], sq[:, :], k == 0, k == nk - 1, [ones_bf, sq])
        ACT(c.rstd[:, :], ps[:, 0:Tn], AF.Ln, [ps], [c.rstd], bias=EPS, scale=1.0 / nfeat)
        ACT(c.rstd[:, :], c.rstd[:, :], AF.Exp, [c.rstd], [c.rstd], scale=-0.5)
    def load_x_dma(c, src, row0):
        tw = c.tw
        xb32 = c.B2.t[:, :].bitcast(F32)
        for t in range(c.nt):
            DMA("sp", xb32[0:tw, t * 1024:(t + 1) * 1024], src[row0 + t * tw: row0 + (t + 1) * tw, :], [], [c.B2], "IO" + c.tag)

    def load_x(c, src, row0, prefetched=False):
        Tn, tw = c.T, c.tw
        if c.s:
            buf, bt = c.F2, c.F2.t
        else:
            buf, bt = c.B2, c.B2.t[:, :].bitcast(F32)
            if not prefetched:
                load_x_dma(c, src, row0)
        for t in range(c.nt):
            if c.s:
                DMA("sp", bt[0:tw, t * 1024:(t + 1) * 1024], src[row0 + t * tw: row0 + (t + 1) * tw, :], [], [buf], "IO" + c.tag)
            for kq in range(2):
                ps = ps_next()
                for kk in range(4):
                    kc = kq * 4 + kk
                    TR(ps, ps[:, kk * tw:(kk + 1) * tw], bt[0:tw, t * 1024 + kc * 128: t * 1024 + (kc + 1) * 128],
                       ident[0:tw, 0:tw], [buf, ident])
                CP("act" if kq == 0 else "dve", v3(c.hT, 0, 8, Tn)[:, kq * 4:(kq + 1) * 4, t * tw:(t + 1) * tw],
                   ps[:, 0:4 * tw].rearrange("p (k t) -> p k t", t=tw), [ps], [c.hT])
    def store_y(c, dst, row0):
        Tn, tw = c.T, c.tw
        fence(c.F2)
        for t in range(c.nt):
            for kq in range(2):
                ps = ps_next()
                for kk in range(4):
                    kc = kq * 4 + kk
                    TR(ps, ps[0:tw, kk * 128:(kk + 1) * 128], c.hT.t[:, kc * Tn + t * tw: kc * Tn + (t + 1) * tw],
                       ident[:, :], [c.hT, ident])
                CP("act" if kq == 0 else "dve", c.F2.t[0:tw, t * 1024 + kq * 512: t * 1024 + (kq + 1) * 512], ps[0:tw, :],
                   [ps], [(c.F2, ("io", t))])
            DMA("sp", dst[row0 + t * tw: row0 + (t + 1) * tw, :], c.F2.t[0:tw, t * 1024:(t + 1) * 1024],
                [(c.F2, ("io", t))], [], "IO" + c.tag)
    def prenorm(c, gname):
        Tn = c.T
        rms(c, c.hT, 0, 8, 1024, keyed=True)
        for k in range(8):
            STT(c.uT.t[:, k * Tn:(k + 1) * Tn], c.hT.t[:, k * Tn:(k + 1) * Tn], vcol(gname, k), c.rstd[:, :],
                OP.mult, OP.mult, [(c.hT, k), c.rstd, vecs], [(c.uT, k)])
    def postnorm_add(c, gname):
        Tn = c.T
        rms(c, c.F2, 0, 8, 1024)
        for k in range(8):
            tmp = c.tf[k % 2]
            TT("dve", tmp[:, :], c.F2.t[:, k * Tn:(k + 1) * Tn], c.rstd[:, :], OP.mult, [c.F2, c.rstd], [tmp])
            STT(c.hT.t[:, k * Tn:(k + 1) * Tn], tmp[:, :], vcol(gname, k), c.hT.t[:, k * Tn:(k + 1) * Tn], OP.mult, OP.add,
                [tmp, (c.hT, k), vecs], [(c.hT, k)])
    def fence(buf):
        P.op("dve", lambda e: e.memset(drv[:, 47:48], 0.0), writes=[buf, (drv, "f")])
    def ffn(ctxs, layer):
        for c in ctxs:
            prenorm(c, "ffn_pre%d" % layer)
            fence(c.B1)
        col = 0
        while col < DFF:
            nc_ = min(512, DFF - col)
            sg, wg = wload(ffn_wg[layer * D:(layer + 1) * D, col:col + nc_], 128, 8, nc_)
            su, wu = wload(ffn_wu[layer * D:(layer + 1) * D, col:col + nc_], 128, 8, nc_)
            for c in ctxs:
                Tn = c.T
                nj = nc_ // 128
                for j in range(nj):
                    pg = ps_next()
                    for k in range(8):
                        MM(pg, pg[:, 0:Tn], wg[:, k, j * 128:(j + 1) * 128], c.uT.t[:, k * Tn:(k + 1) * Tn], k == 0, k == 7, [sg, (c.uT, k)])
                    tmp = c.tf[2 + j]
                    ACT(tmp[:, :], pg[:, 0:Tn], AF.Silu, [pg], [tmp])
                for j in range(nj):
                    fj = col // 128 + j
                    pu = ps_next()
                    for k in range(8):
                        MM(pu, pu[:, 0:Tn], wu[:, k, j * 128:(j + 1) * 128], c.uT.t[:, k * Tn:(k + 1) * Tn], k == 0, k == 7, [su, (c.uT, k)])
                    tmp = c.tf[2 + j]
                    TT("dve", c.B1.t[:, fj * Tn:(fj + 1) * Tn], tmp[:, :], pu[:, 0:Tn], OP.mult, [tmp, pu], [(c.B1, ("ff", fj))])
            col += nc_
        for dc in range(8):
            sd, wd = wload(ffn_wd[layer * DFF:(layer + 1) * DFF, dc * 128:(dc + 1) * 128], 128, NFF, 128)
            for c in ctxs:
                Tn = c.T
                ps = ps_next()
                for f in range(NFF):
                    MM(ps, ps[:, 0:Tn], wd[:, f, :], c.B1.t[:, f * Tn:(f + 1) * Tn], f == 0, f == NFF - 1, [sd, (c.B1, ("ff", f))])
                CP("act", c.F2.t[:, dc * Tn:(dc + 1) * Tn], ps[:, 0:Tn], [ps], [c.F2])
        for c in ctxs:
            postnorm_add(c, "ffn_post%d" % layer)
    def mixer_ab(ctxs, last):
        for c in ctxs:
            prenorm(c, "mix_pre0")
            fence(c.B1)
            fence(c.F2)
        wz = P_wz
        DMA("pool", wz.t[:, :].rearrange("p (k n) -> p k n", n=16), w_in_ab[:, 5120:5136].rearrange("(k p) n -> p k n", p=128), [], [wz], "C1")
        for c in ctxs:
            Tn = c.T
            ps = ps_next()
            for k in range(8):
                MM(ps, ps[0:16, 0:Tn], wz.t[:, k * 16:(k + 1) * 16], c.uT.t[:, k * Tn:(k + 1) * Tn], k == 0, k == 7, [wz, (c.uT, k)])
            zr = c.tb[0]
            CP("act", zr[0:16, :], ps[0:16, 0:Tn], [ps], [zr])
            for h in range(4):
                pz = ps_next()
                MM(pz, pz[:, 0:Tn], wgate[0:16, h * 128:(h + 1) * 128], zr[0:16, :], True, True, [wgate, zr])
                e1 = c.tf[0]
                ACT(e1[:, :], pz[:, 0:Tn], AF.Exp, [pz, drv], [e1], bias=drv[:, h:h + 1], scale=-1.0)
                ACT(e1[:, :], e1[:, :], AF.Ln, [e1], [e1], bias=1.0)
                msk = rmask[:, 0:Tn] if not c.s else rmask[:, 1:1 + Tn]
                SCAN(c.cum.t[:, h * Tn:(h + 1) * Tn], msk, e1[:, :], 0.0, [rmask, e1], [(c.cum, h)])
                ACT(c.ebl.t[:, h * c.nt:(h + 1) * c.nt],
                    c.cum.t[:, h * Tn:(h + 1) * Tn].rearrange("p (n c) -> p n c", c=c.C)[:, :, c.C - 1],
                    AF.Exp, [(c.cum, h)], [(c.ebl, h)], scale=-1.0 / 16.0)
        for half in range(2):
            sx, wx = wload(w_in_ab[:, half * 512:(half + 1) * 512], 128, 8, 512)
            sgw, wgl = wload(w_in_ab[:, 1024 + half * 512:1024 + (half + 1) * 512], 128, 8, 512)
            sqk, wqk = wload(w_in_ab[:, 2048 + half * 512:2560 + half * 512], 128, 8, 512)

            def qk_head(c, h):
                Tn = c.T
                ps = ps_next()
                for k in range(8):
                    MM(ps, ps[:, 0:Tn], wqk[:, k, h * 128:(h + 1) * 128], c.uT.t[:, k * Tn:(k + 1) * Tn], k == 0, k == 7, [sqk, (c.uT, k)])
                e1 = c.rstd
                if half == 0:
                    ACT(e1[:, :], c.cum.t[:, h * Tn:(h + 1) * Tn], AF.Exp, [(c.cum, h)], [e1], scale=-1.0 / 16.0)
                    STT(c.B1.t[:, h * Tn:(h + 1) * Tn], ps[:, 0:Tn], 128.0 ** -0.5, e1[:, :], OP.mult, OP.mult, [ps, e1], [(c.B1, ("q", h))])
                else:
                    ACT(e1[:, :], c.cum.t[:, h * Tn:(h + 1) * Tn], AF.Exp, [(c.cum, h)], [e1], scale=1.0 / 16.0)
                    TT("dve", c.B1.t[:, 4 * Tn + h * Tn: 4 * Tn + (h + 1) * Tn], ps[:, 0:Tn], e1[:, :], OP.mult, [ps, e1], [(c.B1, ("k", h))])
            for c in ctxs:
                Tn = c.T
                XO = 8 * Tn
                GO = 16 * Tn + 24
                for j in range(4):
                    kc = half * 4 + j
                    xo = XO + kc * (Tn + 3)
                    ps = ps_next()
                    for k in range(8):
                        MM(ps, ps[:, 0:Tn], wx[:, k, j * 128:(j + 1) * 128], c.uT.t[:, k * Tn:(k + 1) * Tn], k == 0, k == 7, [sx, (c.uT, k)])
                    CP("act", c.B1.t[:, xo + 3: xo + 3 + Tn], ps[:, 0:Tn], [ps], [(c.B1, ("xa", kc))])
                    CP("act", c.B1.t[:, xo: xo + 3], c.carry.t[:, kc * 3:(kc + 1) * 3], [(c.carry, kc)], [(c.B1, ("xa", kc))])
                    if last or c.s:
                        CP("dve", c.convo.t[:, kc * 3:(kc + 1) * 3], ps[:, Tn - 3:Tn], [ps], [(c.convo, kc)])
                def tv(i):
                    if i < 6:
                        return c.tf[i].t[:, 0:Tn], c.tf[i]
                    return c.F2.t[:, (i - 6) * Tn:(i - 5) * Tn], (c.F2, ("t", i - 6))
                gt = [tv(11), tv(12), tv(13), tv(5)]
                gps = []
                for j in range(4):
                    ps = ps_next()
                    gps.append(ps)
                    for k in range(8):
                        MM(ps, ps[:, 0:Tn], wgl[:, k, j * 128:(j + 1) * 128], c.uT.t[:, k * Tn:(k + 1) * Tn], k == 0, k == 7, [sgw, (c.uT, k)])
                    ACT(c.B1.t[:, GO + j * Tn: GO + (j + 1) * Tn], ps[:, 0:Tn], AF.Gelu_apprx_tanh, [ps], [(c.B1, ("gl", j))])
                st_ = {}
                XS = [0, 6, 5]

                def build_diag(j):
                    kc = half * 4 + j
                    dg = (kc % 3) * 512
                    for tap in range(4):
                        TS_("dve", diag[:, dg + tap * 128: dg + (tap + 1) * 128], identb[:, :], vcol("conv_w", tap * 8 + kc), None, OP.mult, None,
                            [identb, vecs], [(diag, (kc % 3, tap))])

                def prep_conv(j):
                    kc = half * 4 + j
                    xo = XO + kc * (Tn + 3)
                    dg = (kc % 3) * 512
                    pc = ps_next()
                    for tap in range(4):
                        MM(pc, pc[:, 0:Tn], diag[:, dg + tap * 128: dg + (tap + 1) * 128], c.B1.t[:, xo + tap: xo + tap + Tn], tap == 0, tap == 3,
                           [(diag, (kc % 3, tap)), (c.B1, ("xa", kc))])
                    CP("act", c.carry.t[:, kc * 3:(kc + 1) * 3], c.B1.t[:, xo + Tn: xo + Tn + 3], [(c.B1, ("xa", kc))], [(c.carry, kc)])
                    base = 0 if kc % 2 == 0 else 6
                    tvs = [tv(XS[kc % 3])] + [tv(base + q) for q in range(1, 5)]
                    xb = c.tb[kc % 2]
                    ACT(tvs[0][0], pc[:, 0:Tn], AF.Identity, [pc, vecs], [tvs[0][1]], bias=vcol("conv_b", kc))
                    TS_("dve", xb[:, :], pc[:, 0:Tn], vcol("conv_b", kc), None, OP.add, None, [pc, vecs], [xb])
                    st_[j] = [tvs, xb, None, None]

                def gates(j):
                    kc = half * 4 + j
                    xb = st_[j][1]
                    pr = ps_next()
                    pi = ps_next()
                    MM(pr, pr[:, 0:Tn], wab[:, kc * 128:(kc + 1) * 128], xb[:, :], True, True, [wab, xb])
                    MM(pi, pi[:, 0:Tn], wab[:, 1024 + kc * 128:1024 + (kc + 1) * 128], xb[:, :], True, True, [wab, xb])
                    st_[j][2], st_[j][3] = pr, pi

                def chain_head(j):
                    kc = half * 4 + j
                    (X, Xd), (R, Rd), (I, Id), (A, Ad), (M, Md) = st_[j][0]
                    pr, pi = st_[j][2], st_[j][3]
                    ACT(R, pr[:, 0:Tn], AF.Exp, [pr, drv], [Rd], bias=drv[:, 4 + kc:5 + kc], scale=-1.0)
                    ACT(I, pi[:, 0:Tn], AF.Exp, [pi, drv], [Id], bias=drv[:, 12 + kc:13 + kc], scale=-1.0)

                def chain_tail(j):
                    kc = half * 4 + j
                    (X, Xd), (R, Rd), (I, Id), (A, Ad), (M, Md) = st_[j][0]
                    ACT(R, R, AF.Ln, [Rd], [Rd], bias=1.0)
                    ACT(R, R, AF.Exp, [Rd], [Rd], scale=-1.0)
                    ACT(A, R, AF.Exp, [Rd, drv], [Ad], scale=drv[:, 20 + kc:21 + kc])
                    TT("dve", M, A, A, OP.mult, [Ad], [Md])
                    ACT(I, I, AF.Ln, [Id], [Id], bias=1.0)
                    ACT(I, I, AF.Exp, [Id], [Id], scale=-1.0)
                    TT("dve", I, I, X, OP.mult, [Id, Xd], [Id])
                    ACT(M, M, AF.Ln, [Md], [Md], bias=1.0, scale=-1.0)
                    ACT(M, M, AF.Exp, [Md], [Md], scale=0.5)
                    TT("dve", I, I, M, OP.mult, [Id, Md], [Id])
                    SCAN(X, A, I, c.hst.t[:, kc:kc + 1], [Ad, Id, (c.hst, kc), Xd], [Xd])
                    CP("dve", c.hst.t[:, kc:kc + 1], X[:, Tn - 1:Tn], [Xd], [(c.hst, kc)])
                    TT("dve", c.B2.t[:, kc * Tn:(kc + 1) * Tn], X, c.B1.t[:, GO + j * Tn: GO + (j + 1) * Tn], OP.mult,
                       [Xd, (c.B1, ("gl", j))], [(c.B2, kc)])

                build_diag(0)
                build_diag(1)
                prep_conv(0)
                gates(0)
                for j in range(4):
                    if j + 2 < 4:
                        build_diag(j + 2)
                    if j + 1 < 4:
                        prep_conv(j + 1)
                    chain_head(j)
                    if j + 1 < 4:
                        gates(j + 1)
                    qk_head(c, j)
                    chain_tail(j)
        for c in ctxs:
            fence(c.F2)
        for c in ctxs:
            fence(c.B1)
        for c in ctxs:
            Tn, tw = c.T, c.tw
            KO = 4 * Tn
            KTO = 8 * Tn
            for t in range(c.nt):
                for h in range(4):
                    TR(psb, psb[0:tw, h * 128:(h + 1) * 128], c.B1.t[:, KO + h * Tn + t * tw: KO + h * Tn + (t + 1) * tw], identb[:, :],
                       [(c.B1, ("k", h)), identb])
                CP("act", c.B1.t[0:tw, KTO + t * 512: KTO + (t + 1) * 512], psb[0:tw, 0:512], [psb], [(c.B1, ("kt", t))])
        for vb in range(2):
            sv_, wv_ = wload(w_in_ab[:, 3072 + vb * 512:3072 + (vb + 1) * 512], 128, 8, 512)
            for c in ctxs:
                Tn, tw = c.T, c.tw
                VO = 8 * Tn + c.nt * 512
                for t in range(c.nt):
                    ps = ps_next()
                    for k in range(8):
                        MM(ps, ps[0:tw, :], c.uT.t[:, k * Tn + t * tw: k * Tn + (t + 1) * tw], wv_[:, k, :], k == 0, k == 7, [sv_, (c.uT, k)])
                    CP("act" if t % 2 == 0 else "dve", c.B1.t[0:tw, VO + t * 1024 + vb * 512: VO + t * 1024 + (vb + 1) * 512], ps[0:tw, :],
                       [ps], [(c.B1, ("v", t))])
        for c in ctxs:
            Tn, tw, C = c.T, c.tw, c.C
            KO, KTO = 4 * Tn, 8 * Tn
            VO = 8 * Tn + c.nt * 512
            trm = tri4 if not c.s else tri4s
            def a_mask(t):
                pa = ps_next()
                for h in range(4):
                    MM(pa, pa[0:C, h * C:(h + 1) * C], c.B1.t[:, KO + h * Tn + t * C: KO + h * Tn + (t + 1) * C],
                       c.B1.t[:, h * Tn + t * C: h * Tn + (t + 1) * C], True, True, [(c.B1, ("k", h)), (c.B1, ("q", h))])
                at = c.atb[t % 2]
                TT("dve", at[0:C, 0:4 * C], pa[0:C, 0:4 * C], trm[0:C, 0:4 * C], OP.mult, [pa, trm], [at])
                return at
            at = a_mask(0)
            for t in range(c.nt):
                for hp in range(2):
                    po = ps_next()
                    for hh in range(2):
                        h = hp * 2 + hh
                        for j in range(2):
                            c0 = (hh * 2 + j) * C
                            MM(po, po[:, c0:c0 + C], c.B1.t[0:C, VO + t * 1024 + h * 256 + j * 128: VO + t * 1024 + h * 256 + (j + 1) * 128],
                               at[0:C, h * C:(h + 1) * C], (hh == 0 and j == 0), False, [(c.B1, ("v", t)), at], skip=True)
                    for hh in range(2):
                        h = hp * 2 + hh
                        for j in range(2):
                            c0 = (hh * 2 + j) * C
                            MM(po, po[:, c0:c0 + C], c.Sbf.t[:, h * 256 + j * 128: h * 256 + (j + 1) * 128],
                               c.B1.t[:, h * Tn + t * C: h * Tn + (t + 1) * C], False, True, [(c.Sbf, h), (c.B1, ("q", h))], skip=True)
                    CP("act", v3(c.F2, 0, 8, Tn)[:, hp * 4:(hp + 1) * 4, t * C:(t + 1) * C],
                       po[:, 0:4 * C].rearrange("p (k t) -> p k t", t=C), [po], [(c.F2, ("o", hp))])
                pds = []
                for hp in range(2):
                    pd = ps_next()
                    pds.append(pd)
                    for hh in range(2):
                        h = hp * 2 + hh
                        MM(pd, pd[:, hh * 256:(hh + 1) * 256], c.B1.t[0:C, KTO + t * 512 + h * 128: KTO + t * 512 + (h + 1) * 128],
                           c.B1.t[0:C, VO + t * 1024 + h * 256: VO + t * 1024 + (h + 1) * 256], True, True,
                           [(c.B1, ("kt", t)), (c.B1, ("v", t))])
                if t + 1 < c.nt:
                    at = a_mask(t + 1)
                for hp in range(2):
                    pd = pds[hp]
                    for hh in range(2):
                        h = hp * 2 + hh
                        eb = c.ebl.t[:, h * c.nt + t: h * c.nt + t + 1]
                        Sh = c.S.t[:, h * 256:(h + 1) * 256]
                        TT("dve", Sh, Sh, pd[:, hh * 256:(hh + 1) * 256], OP.add, [(c.S, h), pd], [(c.S, h)])
                        ACT(c.Sbf.t[:, h * 256:(h + 1) * 256], Sh, AF.Copy, [(c.S, h), (c.ebl, h)], [(c.Sbf, h)], scale=eb)
                        TS_("dve", Sh, Sh, eb, None, OP.mult, None, [(c.S, h), (c.ebl, h)], [(c.S, h)])
            pss_ = []
            for h in range(4):
                ps = ps_next()
                pss_.append(ps)
                for j in range(2):
                    sq = c.tb[c.tbi % 2]
                    c.tbi += 1
                    ACT(sq[:, :], c.F2.t[:, (2 * h + j) * Tn:(2 * h + j + 1) * Tn], AF.Square, [c.F2], [sq])
                    MM(ps, ps[:, 0:Tn], ones_bf[:, :], sq[:, :], j == 0, j == 1, [ones_bf, sq])
            for h in range(4):
                ACT(c.tf[h][:, :], pss_[h][:, 0:Tn], AF.Ln, [pss_[h]], [c.tf[h]], bias=EPS, scale=1.0 / 256)
            for h in range(4):
                ACT(c.tf[h][:, :], c.tf[h][:, :], AF.Exp, [c.tf[h]], [c.tf[h]], scale=-0.5)
            for h in range(4):
                for j in range(2):
                    hj = 2 * h + j
                    STT(c.F2.t[:, hj * Tn:(hj + 1) * Tn], c.F2.t[:, hj * Tn:(hj + 1) * Tn], vcol("gla_norm", j), c.tf[h][:, :],
                        OP.mult, OP.mult, [c.F2, c.tf[h], vecs], [(c.F2, ("o", h // 2))])
        for gbk in range(2):
            sb_, wb_ = wload(w_in_ab[:, 4096 + gbk * 512:4096 + (gbk + 1) * 512], 128, 8, 512)
            for c in ctxs:
                Tn = c.T
                for j in range(4):
                    hj = gbk * 4 + j
                    ps = ps_next()
                    for k in range(8):
                        MM(ps, ps[:, 0:Tn], wb_[:, k, j * 128:(j + 1) * 128], c.uT.t[:, k * Tn:(k + 1) * Tn], k == 0, k == 7, [sb_, (c.uT, k)])
                    tmp = c.tf[4 + j % 2]
                    ACT(tmp[:, :], ps[:, 0:Tn], AF.Silu, [ps], [tmp])
                    TT("dve", c.B2.t[:, (8 + hj) * Tn:(9 + hj) * Tn], tmp[:, :], c.F2.t[:, hj * Tn:(hj + 1) * Tn], OP.mult,
                       [tmp, c.F2], [(c.B2, 8 + hj)])
        for blk in range(4):
            so_, wo_ = wload(w_out_ab[:, blk * 256:(blk + 1) * 256], 128, 16, 256)
            for c in ctxs:
                Tn = c.T
                for dl in range(2):
                    dc = blk * 2 + dl
                    ps = ps_next()
                    for k in range(16):
                        MM(ps, ps[:, 0:Tn], wo_[:, k, dl * 128:(dl + 1) * 128], c.B2.t[:, k * Tn:(k + 1) * Tn], k == 0, k == 15, [so_, (c.B2, k)])
                    CP("act", c.F2.t[:, dc * Tn:(dc + 1) * Tn], ps[:, 0:Tn], [ps], [c.F2])
        for c in ctxs:
            postnorm_add(c, "mix_post0")
    P_wz = P.sbuf("wz", [128, 8 * 16], BF16)
    def out_states(c, idx):
        for half in range(2):
            ps = ps_next()
            for kk in range(4):
                kc = half * 4 + kk
                TR(ps, ps[0:3, kk * 128:(kk + 1) * 128], c.convo.t[:, kc * 3:(kc + 1) * 3], ident[:, :], [c.convo, ident])
            CP("act", c.F2.t[0:3, half * 512:(half + 1) * 512], ps[0:3, 0:512], [ps], [c.F2])
        DMA("sp", o_conv[idx][:, :], c.F2.t[0:3, 0:1024], [c.F2], [], "O" + c.tag)
        ps = ps_next()
        TR(ps, ps[0:8, 0:128], c.hst.t[:, 0:8], ident[:, :], [c.hst, ident])
        hh = c.Se
        CP("act", hh[0:8, 0:128], ps[0:8, 0:128], [ps], [hh])
        DMA("sp", o_h[idx][:, :], hh[0:8, 0:128], [hh], [], "O" + c.tag)
        DMA("sp", o_S[idx].rearrange("h k v -> k h v"), c.S.t[:, :].rearrange("p (h v) -> p h v", v=256), [c.S], [], "O" + c.tag)
    def mla_proj(ctxs, g):
        for c in ctxs:
            prenorm(c, "mix_pre1")
            fence(c.B1)
        s1, w1 = wload(w_in_c[:, 0:384], 128, 8, 384)
        s2, w2 = wload(w_in_c[:, 384:768], 128, 8, 384)
        for c in ctxs:
            Tn, tw, pb = c.T, c.tw, c.pb
            for k3 in range(3):
                ps = ps_next()
                for k in range(8):
                    MM(ps, ps[:, 0:Tn], w1[:, k, k3 * 128:(k3 + 1) * 128], c.uT.t[:, k * Tn:(k + 1) * Tn], k == 0, k == 7, [s1, (c.uT, k)])
                CP("act", c.F2.t[:, k3 * Tn:(k3 + 1) * Tn], ps[:, 0:Tn], [ps], [(c.F2, ("cq", k3))])
            for k2 in range(2):
                ps = ps_next()
                for k in range(8):
                    MM(ps, ps[:, 0:Tn], w2[:, k, k2 * 128:(k2 + 1) * 128], c.uT.t[:, k * Tn:(k + 1) * Tn], k == 0, k == 7, [s2, (c.uT, k)])
                CP("act", c.F2.t[:, (3 + k2) * Tn:(4 + k2) * Tn], ps[:, 0:Tn], [ps], [(c.F2, ("ckv", k2))])
            if SUB < 1:
                continue
            pA = ps_next()
            pB = ps_next()
            if not c.s:
                cA, cB, M_ = (192, 288), (288, 384), 96
            else:
                cA, cB, M_ = (256, 288), (352, 384), 32
            for k in range(8):
                MM(pA, pA[0:M_, 0:Tn], w2[:, k, cA[0]:cA[1]], c.uT.t[:, k * Tn:(k + 1) * Tn], k == 0, k == 7, [s2, (c.uT, k)])
            for k in range(8):
                MM(pB, pB[0:M_, 0:Tn], w2[:, k, cB[0]:cB[1]], c.uT.t[:, k * Tn:(k + 1) * Tn], k == 0, k == 7, [s2, (c.uT, k)])
            if not c.s:
                DMA("sp", c.rope.t[64:96, :].rearrange("p (a t) -> p a t", t=Tn),
                    ropeP_d.rearrange("p (a t) -> p a t", t=2048)[:, :, g * T:(g + 1) * T], [], [c.rope], "R" + c.tag)
            t1, t2 = c.tf[0], c.tf[1]
            kpr = c.F2.t[pb:pb + 32, 5 * Tn:6 * Tn]
            TT("dve", t1[pb:pb + 32, :], pA[pb:pb + 32, 0:Tn], c.rope.t[pb:pb + 32, 0:Tn], OP.mult, [pA, c.rope], [t1])
            TT("dve", t2[pb:pb + 32, :], pB[pb:pb + 32, 0:Tn], c.rope.t[pb:pb + 32, Tn:2 * Tn], OP.mult, [pB, c.rope], [t2])
            TT("dve", kpr, t1[pb:pb + 32, :], t2[pb:pb + 32, :], OP.add, [t1, t2], [(c.F2, "kpr")])
            if not c.s:
                for i in range(2):
                    CP("act", KT[i].t[64:96, g * T:(g + 1) * T], kpr, [(c.F2, "kpr")], [(KT[i], "pe")])
            else:
                CP("act", c.kprb.t[0:32, 0:Tn], kpr, [(c.F2, "kpr")], [c.kprb])
            if SUB < 2:
                continue
            rms(c, c.F2, 0, 3, 384)
            for k3 in range(3):
                STT(c.uT.t[:, k3 * Tn:(k3 + 1) * Tn], c.F2.t[:, k3 * Tn:(k3 + 1) * Tn], vcol("q_norm", k3), c.rstd[:, :], OP.mult, OP.mult,
                    [(c.F2, ("cq", k3)), c.rstd, vecs], [c.uT])
            rms(c, c.F2, 3 * Tn, 2, 256)
            for k2 in range(2):
                STT(c.F2.t[:, (3 + k2) * Tn:(4 + k2) * Tn], c.F2.t[:, (3 + k2) * Tn:(4 + k2) * Tn], vcol("kv_norm", k2), c.rstd[:, :],
                    OP.mult, OP.mult, [(c.F2, ("ckv", k2)), c.rstd, vecs], [(c.F2, ("ckv", k2))])
                if not c.s:
                    CP("act", ckvnb.t[:, k2 * 2048 + g * T: k2 * 2048 + (g + 1) * T], c.F2.t[:, (3 + k2) * Tn:(4 + k2) * Tn],
                       [(c.F2, ("ckv", k2))], [(ckvnb, g)])
                else:
                    CP("act", c.ckvb.t[:, k2 * Tn:(k2 + 1) * Tn], c.F2.t[:, (3 + k2) * Tn:(4 + k2) * Tn], [(c.F2, ("ckv", k2))], [c.ckvb])
            if SUB < 3:
                continue
            idx = 1 if c.s else 0
            for t in range(c.nt):
                ps = ps_next()
                for k2 in range(2):
                    TR(ps, ps[0:tw, k2 * 128:(k2 + 1) * 128], c.F2.t[:, (3 + k2) * Tn + t * tw:(3 + k2) * Tn + (t + 1) * tw], ident[:, :],
                       [(c.F2, ("ckv", k2)), ident])
                if KPEOUT:
                    MM(ps, ps[0:tw, 256:288], c.F2.t[pb:pb + 32, 5 * Tn + t * tw: 5 * Tn + (t + 1) * tw], ident[pb:pb + 32, pb:pb + 32],
                       True, True, [(c.F2, "kpr"), ident])
                st = c.ost[t % 2]
                CP("act", st[0:tw, 0:288], ps[0:tw, 0:288], [ps], [st])
                r0 = (g * T if not c.s else 0) + t * tw
                if OUTV >= 2:
                    DMA("sp", o_ckv[idx][r0:r0 + tw, :], st[0:tw, 0:256], [st], [], "O" + c.tag)
                if OUTV >= 3:
                    DMA("sp", o_kpe[idx][r0:r0 + tw, :], st[0:tw, 256:288], [st], [], "O" + c.tag)
                if c.s:
                    CP("dve", c.ckvtok.t[0:tw, 0:256], ps[0:tw, 0:256], [ps], [c.ckvtok])
        if SUB < 4:
            return
        for c in ctxs:
            fence(c.B1)
        for hf in range(2):
            sa, wa_ = wload(w_uq[:, hf * 768:(hf + 1) * 768], 128, 3, 768)
            sb2, wb2 = wload(w_uq_sw[:, hf * 768:(hf + 1) * 768], 128, 3, 768)
            for c in ctxs:
                Tn = c.T
                for hl in range(8):
                    h = hf * 8 + hl
                    if not c.s:
                        pA = ps_next()
                        pB = ps_next()
                        for k in range(3):
                            MM(pA, pA[0:96, 0:Tn], wa_[:, k, hl * 96:(hl + 1) * 96], c.uT.t[:, k * Tn:(k + 1) * Tn], k == 0, k == 2, [sa, c.uT])
                        for k in range(3):
                            MM(pB, pB[0:96, 0:Tn], wb2[:, k, hl * 96:(hl + 1) * 96], c.uT.t[:, k * Tn:(k + 1) * Tn], k == 0, k == 2, [sb2, c.uT])
                        CP("act", c.B1.t[0:64, h * Tn:(h + 1) * Tn], pA[0:64, 0:Tn], [pA], [(c.B1, ("Q", h))])
                        t1, t2 = c.tf[0], c.tf[1]
                        TT("dve", t1[64:96, :], pA[64:96, 0:Tn], c.rope.t[64:96, 0:Tn], OP.mult, [pA, c.rope], [t1])
                        TT("dve", t2[64:96, :], pB[64:96, 0:Tn], c.rope.t[64:96, Tn:2 * Tn], OP.mult, [pB, c.rope], [t2])
                        TT("dve", c.B1.t[64:96, h * Tn:(h + 1) * Tn], t1[64:96, :], t2[64:96, :], OP.add, [t1, t2], [(c.B1, ("Q", h))])
                    else:
                        pq = ps_next()
                        for k in range(3):
                            MM(pq, pq[0:64, 0:16], wa_[:, k, hl * 96:hl * 96 + 64], c.uT.t[:, k * Tn:(k + 1) * Tn], k == 0, k == 2, [sa, c.uT])
                        for k in range(3):
                            MM(pq, pq[0:32, 16:32], wa_[:, k, hl * 96 + 64:hl * 96 + 96], c.uT.t[:, k * Tn:(k + 1) * Tn], k == 0, k == 2, [sa, c.uT])
                        for k in range(3):
                            MM(pq, pq[0:32, 32:48], wb2[:, k, hl * 96 + 64:hl * 96 + 96], c.uT.t[:, k * Tn:(k + 1) * Tn], k == 0, k == 2, [sb2, c.uT])
                        CP("act", c.qn.t[0:64, h * 16:(h + 1) * 16], pq[0:64, 0:16], [pq], [c.qn])
                        t1, t2 = c.tf[0], c.tf[1]
                        TT("dve", t1[0:32, :], pq[0:32, 16:32], c.rope.t[0:32, 0:Tn], OP.mult, [pq, c.rope], [t1])
                        TT("dve", t2[0:32, :], pq[0:32, 32:48], c.rope.t[0:32, Tn:2 * Tn], OP.mult, [pq, c.rope], [t2])
                        TT("dve", c.qpe.t[0:32, h * 16:(h + 1) * 16], t1[0:32, :], t2[0:32, :], OP.add, [t1, t2], [c.qpe])
    def mla_attn_prompt(c, g, suk, wuk, suv, wuv):
        Tn = c.T
        nkb = g + 1
        nkt = 4 * (g + 1)
        rot_n[0] = 3
        Vp4 = Vp.t[:, 0:2048].rearrange("p (k m) -> p k m", m=128)

        def kt_recompute(h):
            hh = h % 2
            for kb in range(nkb):
                ps = ps_next()
                for cc in range(2):
                    MM(ps, ps[0:64, 0:512], wuk[:, cc, h * 64:(h + 1) * 64], ckvnb.t[:, cc * 2048 + kb * 512: cc * 2048 + (kb + 1) * 512],
                       cc == 0, cc == 1, [suk, (ckvnb, kb)])
                CP("dve", KT[hh].t[0:64, kb * 512:(kb + 1) * 512], ps[0:64, 0:512], [ps], [(KT[hh], ("n", kb))])

        def v_recompute(hp):
            for k4 in range(nkt // 4):
                ps = ps_next()
                for kk in range(4):
                    kt = k4 * 4 + kk
                    for cc in range(2):
                        MM(ps, ps[:, kk * 128:(kk + 1) * 128], ckvnb.t[:, cc * 2048 + kt * 128: cc * 2048 + (kt + 1) * 128],
                           wuv[:, cc, hp * 128:(hp + 1) * 128], cc == 0, cc == 1, [suv, (ckvnb, kt // 4)])
                CP("dve", Vp4[:, k4 * 4:(k4 + 1) * 4, :], ps[:, :].rearrange("p (k m) -> p k m", m=128), [ps], [(Vp, k4)])

        kt_recompute(0)
        v_recompute(0)
        for h in range(16):
            hp, hh = h // 2, h % 2
            po = allb[3 + 2 * hh]
            pss = allb[4 + 2 * hh]

            def s_exp(kt):
                qlo = max(0, kt - 4 * g) * 128
                nq = Tn - qlo
                pS = ps_next()
                MM(pS, pS[:, 0:nq], KT[hh].t[0:96, kt * 128:(kt + 1) * 128], c.B1.t[0:96, h * Tn + qlo:(h + 1) * Tn], True, True,
                   [(KT[hh], ("n", kt // 4)), (KT[hh], "pe"), (c.B1, ("Q", h))])
                pt = pt_next()
                ACT(pt[:, 0:nq], pS[:, 0:nq], AF.Exp, [pS], [pt], scale=SM_SCALE)
                if kt >= 4 * g:
                    MEMSET("dve", pt[64:128, 0:64], 0.0, [pt])
                return pt, qlo, nq
            q_ = [s_exp(0)]
            if nkt > 1:
                q_.append(s_exp(1))
            for kt in range(nkt):
                pt, qlo, nq = q_.pop(0)
                if kt + 2 < nkt:
                    q_.append(s_exp(kt + 2))
                MM(po, po[:, qlo:Tn], Vp4[:, kt, :], pt[:, 0:nq], kt == 0, kt == nkt - 1, [(Vp, kt // 4), pt], skip=True)
                MM(pss, pss[:, qlo:Tn], ones_bf[:, :], pt[:, 0:nq], kt == 0, kt == nkt - 1, [ones_bf, pt], skip=True)
                if kt == 0 and h + 1 < 16:
                    kt_recompute(h + 1)
            r0 = hh * 64
            rl = c.tf[2 + hh]
            ACT(rl[r0:r0 + 64, :], pss[r0:r0 + 64, 0:Tn], AF.Ln, [pss], [rl])
            ACT(rl[r0:r0 + 64, :], rl[r0:r0 + 64, :], AF.Exp, [rl], [rl], scale=-1.0)
            TT("dve", c.B2.t[r0:r0 + 64, hp * Tn:(hp + 1) * Tn], po[r0:r0 + 64, 0:Tn], rl[r0:r0 + 64, :], OP.mult, [po, rl], [(c.B2, hp)])
            if hh == 1 and hp + 1 < 8:
                v_recompute(hp + 1)
        rot_n[0] = 5
    def mla_attn_sample(c, suk, wuk, suv, wuv):
        Tn = c.T
        skt, wkt = wload(w_ukT[:, :], 64, 1, 4096)
        ps = ps_next()
        for h in range(16):
            for cc in range(2):
                MM(ps, ps[:, cc * 256 + h * 16: cc * 256 + (h + 1) * 16], wkt[0:64, 0, h * 256 + cc * 128: h * 256 + (cc + 1) * 128],
                   c.qn.t[0:64, h * 16:(h + 1) * 16], True, True, [skt, c.qn])
        CP("act", c.qlat.t[:, :], ps[:, :], [ps], [c.qlat])
        olat, sums = acc[0], acc[1]
        first = True
        def s_part(lhs_c0, lhs_c1, lhs_pe, nk, rd):
            pS = ps_next()
            MM(pS, pS[0:nk, 0:256], lhs_c0, c.qlat.t[:, 0:256], True, False, rd + [c.qlat])
            MM(pS, pS[0:nk, 0:256], lhs_c1, c.qlat.t[:, 256:512], False, False, rd + [c.qlat])
            MM(pS, pS[0:nk, 0:256], lhs_pe, c.qpe.t[0:32, :], False, True, rd + [c.qpe])
            pt = pt_next()
            ACT(pt[0:nk, 0:256], pS[0:nk, 0:256], AF.Exp, [pS], [pt], scale=SM_SCALE)
            return pt

        def pv_part(pt, tok_c0, tok_c1, nk, rd, last):
            nonlocal first
            MM(olat, olat[:, 0:256], tok_c0, pt[0:nk, 0:256], first, last, rd + [pt], skip=True)
            MM(olat, olat[:, 256:512], tok_c1, pt[0:nk, 0:256], False, last, rd + [pt], skip=True)
            MM(sums, sums[:, 0:256], ones_bf[0:nk, :], pt[0:nk, 0:256], first, last, [ones_bf, pt])
            first = False

        ct = c.ct[0]

        def prep(blk):
            cb = c.cb[blk % 2]
            DMA("pool", cb.t[:, :].rearrange("p (k c) -> p k c", c=288)[:, :, 0:256],
                cckv[blk * 512:(blk + 1) * 512, :].rearrange("(k p) c -> p k c", p=128), [], [cb], "CB%d" % (blk % 2))
            DMA("pool", cb.t[:, :].rearrange("p (k c) -> p k c", c=288)[:, :, 256:288],
                ckpe[blk * 512:(blk + 1) * 512, :].rearrange("(k p) c -> p k c", p=128), [], [cb], "CB%d" % (blk % 2))
            for cc in range(2):
                for kt in range(4):
                    TR(psb, psb[:, (cc * 4 + kt) * 128:(cc * 4 + kt + 1) * 128], cb.t[:, kt * 288 + cc * 128: kt * 288 + (cc + 1) * 128],
                       identb[:, :], [cb, identb])
            CP("act", ct.t[:, 0:1024], psb[:, 0:1024], [psb], [ct])
            for kt in range(4):
                TR(psb, psb[0:32, kt * 128:(kt + 1) * 128], cb.t[:, kt * 288 + 256: kt * 288 + 288], identb[:, :], [cb, identb])
            CP("dve", ct.t[0:32, 1024:1536], psb[0:32, 0:512], [psb], [ct])

        def s_blk(blk, kt):
            return s_part(ct.t[:, kt * 128:(kt + 1) * 128], ct.t[:, 512 + kt * 128:512 + (kt + 1) * 128],
                          ct.t[0:32, 1024 + kt * 128:1024 + (kt + 1) * 128], 128, [ct])

        prep(0)
        for blk in range(8):
            cb = c.cb[blk % 2]
            q_ = [s_blk(blk, 0), s_blk(blk, 1)]
            for kt in range(4):
                pt = q_.pop(0)
                if kt + 2 < 4:
                    q_.append(s_blk(blk, kt + 2))
                if kt == 1 and blk + 1 < 8:
                    prep(blk + 1)
                pv_part(pt, cb.t[:, kt * 288: kt * 288 + 128], cb.t[:, kt * 288 + 128: kt * 288 + 256], 128, [cb], False)
        CP("dve", c.ckvtokb.t[0:16, :], c.ckvtok.t[0:16, :], [c.ckvtok], [c.ckvtokb])
        pt = s_part(c.ckvb.t[:, 0:16], c.ckvb.t[:, 16:32], c.kprb.t[0:32, 0:16], 16, [c.ckvb, c.kprb])
        pv_part(pt, c.ckvtokb.t[0:16, 0:128], c.ckvtokb.t[0:16, 128:256], 16, [c.ckvtokb], True)
        rs = c.rs
        RECIP(rs.t[:, 0:256], sums[:, 0:256], [sums], [rs])
        for cc in range(2):
            TT("dve", c.olatn.t[:, cc * 256:(cc + 1) * 256], olat[:, cc * 256:(cc + 1) * 256], rs.t[:, 0:256], OP.mult, [olat, rs], [c.olatn])
        ps = ps_next()
        for h in range(16):
            hp = h // 2
            for cc in range(2):
                MM(ps, ps[:, h * 16:(h + 1) * 16], wuv[:, cc, hp * 128:(hp + 1) * 128], c.olatn.t[:, cc * 256 + h * 16: cc * 256 + (h + 1) * 16],
                   cc == 0, cc == 1, [suv, c.olatn])
        for hh in range(2):
            CP("act", c.B2.t[hh * 64:(hh + 1) * 64, 0:128].rearrange("p (k t) -> p k t", t=16),
               ps[hh * 64:(hh + 1) * 64, 0:256].rearrange("p (k h t) -> p k h t", h=2, t=16)[:, :, hh, :], [ps], [c.B2])
    def mla_out(ctxs):
        for blk in range(4):
            so_, wo_ = wload(w_out_c[:, blk * 256:(blk + 1) * 256], 128, 8, 256)
            for c in ctxs:
                Tn = c.T
                for dl in range(2):
                    dc = blk * 2 + dl
                    ps = ps_next()
                    for hp in range(8):
                        MM(ps, ps[:, 0:Tn], wo_[:, hp, dl * 128:(dl + 1) * 128], c.B2.t[:, hp * Tn:(hp + 1) * Tn], hp == 0, hp == 7, [so_, c.B2])
                    CP("act", c.F2.t[:, dc * Tn:(dc + 1) * Tn], ps[:, 0:Tn], [ps], [c.F2])
        for c in ctxs:
            postnorm_add(c, "mix_post1")
    cp.ost = [cp.tf[2], cp.tf[3]]
    _ost = P.sbuf("ost", [16, 288], F32)
    cs.ost = [_ost, _ost]
    cs.kprb = P.sbuf("kprb", [32, 16], BF16)
    cs.ckvb = P.sbuf("ckvb", [128, 32], BF16)
    cs.ckvtok = P.sbuf("ckvtok", [16, 256], F32)
    cs.ckvtokb = P.sbuf("ckvtokb", [16, 256], BF16)
    cs.qn = P.sbuf("qn", [64, 256], BF16)
    cs.qpe = P.sbuf("qpe", [32, 256], BF16)
    cs.qlat = P.sbuf("qlat", [128, 512], BF16)
    cs.olatn = P.sbuf("olatn", [128, 512], BF16)
    cs.rs = P.sbuf("rs", [128, 256], F32)
    cs.cb = [P.sbuf("cb%d" % i, [128, 4 * 288], BF16) for i in range(2)]
    _ct = P.sbuf("ct0", [128, 1536], BF16)
    cs.ct = [_ct, _ct]
    for g in range(ngroups):
        ctxs = [cp] + ([cs] if g == 0 else [])
        load_x(cp, xp, g * T, prefetched=(g > 0))
        if g == 0:
            load_x(cs, xs, 0)
        if stage >= 1:
            mixer_ab(ctxs, g == NG - 1)
            if g == 0:
                out_states(cs, 1)
            if g == NG - 1:
                out_states(cp, 0)
        if stage >= 2:
            ffn(ctxs, 0)
        if stage >= 3:
            mla_proj(ctxs, g)
        if stage >= 4:
            sukv, wukv = wload(w_ukv[:, :], 128, 2, 2048)
            suk, wuk = sukv, wukv[:, :, 0:1024]
            suv, wuv = sukv, wukv[:, :, 1024:2048]
            mla_attn_prompt(cp, g, suk, wuk, suv, wuv)
            if g == 0 and stage >= 5:
                mla_attn_sample(cs, suk, wuk, suv, wuv)
        if stage >= 6:
            mla_out(ctxs)
        if stage >= 7:
            if g + 1 < ngroups:
                load_x_dma(cp, xp, (g + 1) * T)
            ffn(ctxs, 1)
        store_y(cp, y_p, g * T)
        if g == 0:
            store_y(cs, y_s, 0)
    P.emit()
    P.close()
    return nc
_NC = None
SUB = int(os.environ.get('SUB', '99'))
KPEOUT = int(os.environ.get('KPEOUT', '1'))
OUTV = int(os.environ.get('OUTV', '3'))
def _fm(v, n):
    return np.ascontiguousarray(np.asarray(v, np.float32).reshape(n, 128).T)
def prep_inputs(inp, cores=range(8)):
    f = lambda k: np.asarray(inp[k], np.float32)
    w_in_c = f("w_in_c")[0]
    sw = np.concatenate([np.arange(16, 32), np.arange(0, 16)])
    w_in_c_ext = np.ascontiguousarray(np.concatenate(
        [w_in_c, w_in_c[:, 576:640], w_in_c[:, 640:672][:, sw]], axis=1))
    w_uq = f("w_uq")[0]
    idx = np.arange(1536).reshape(16, 96).copy()
    idx[:, 64:96] = idx[:, 64:96][:, sw]
    w_uq_sw = np.ascontiguousarray(w_uq[:, idx.reshape(-1)])
    w_uk = f("w_uk")[0]
    w_ukT = np.ascontiguousarray(w_uk.transpose(2, 1, 0).reshape(64, 16 * 256))
    tri = np.triu(np.ones((128, 128), np.float32))
    tri4 = np.ascontiguousarray(np.tile(tri, (1, 4)))
    tri4s = np.zeros((128, 64), np.float32)
    tri4s[:16] = np.tile(np.triu(np.ones((16, 16), np.float32)), (1, 4))
    rmask = np.ones((128, 512), np.float32)
    rmask[:, ::128] = 0.0
    half = 16
    inv = (10000.0 ** (-np.arange(half, dtype=np.float32) / np.float32(half))).astype(np.float32)
    def rope_tab(pos):
        ang = pos.astype(np.float32)[None, :] * inv[:, None]
        cos = np.cos(ang).astype(np.float32)
        sin = np.sin(ang).astype(np.float32)
        c32 = np.concatenate([cos, cos], 0)
        s32 = np.concatenate([-sin, sin], 0)
        return np.ascontiguousarray(np.concatenate([c32, s32], 1))
    ropeP = rope_tab(np.arange(2048))
    ropeS = rope_tab(4096 + np.arange(TS))
    shared = {
        "w_in_ab": f("w_in_ab")[0], "lru_wa": f("lru_w_a")[0].reshape(1024, 128), "lru_wx": f("lru_w_x")[0].reshape(1024, 128),
        "w_gate": f("gla_w_gate")[0], "w_out_ab": f("w_out_ab")[0], "w_in_c": w_in_c_ext, "w_uq": w_uq, "w_uq_sw": w_uq_sw,
        "w_ukv": np.concatenate([w_uk.reshape(256, 1024), f("w_uv")[0].reshape(256, 1024)], axis=1), "w_ukT": w_ukT, "w_out_c": f("w_out_c")[0],
        "ffn_wg": f("ffn_w_gate").reshape(2 * D, DFF), "ffn_wu": f("ffn_w_up").reshape(2 * D, DFF), "ffn_wd": f("ffn_w_down").reshape(2 * DFF, D),
        "ident": np.eye(128, dtype=np.float32), "tri4": tri4, "tri4s": tri4s, "rmask": rmask, "ropeP": ropeP, "ropeS": ropeS,
    }
    shared = {k: np.ascontiguousarray(v, dtype=np.float32) for k, v in shared.items()}
    vparts = {}
    for l in range(2):
        vparts["mix_pre%d" % l] = _fm(f("norm_mix_pre")[l], 8)
        vparts["mix_post%d" % l] = _fm(f("norm_mix_post")[l], 8)
        vparts["ffn_pre%d" % l] = _fm(f("norm_ffn_pre")[l], 8)
        vparts["ffn_post%d" % l] = _fm(f("norm_ffn_post")[l], 8)
    cw = f("conv_w_a")[0]
    vparts["conv_w"] = np.concatenate([_fm(cw[j], 8) for j in range(4)], 1)
    vparts["conv_b"] = _fm(f("conv_b_a")[0], 8)
    vparts["lru_ba"] = _fm(f("lru_b_a")[0], 8)
    vparts["lru_bx"] = _fm(f("lru_b_x")[0], 8)
    vparts["lam"] = _fm(f("lru_lambda")[0], 8)
    vparts["b_gate"] = _fm(f("gla_b_gate")[0], 4)
    vparts["gla_norm"] = _fm(f("gla_norm")[0], 2)
    vparts["q_norm"] = _fm(f("mla_q_norm")[0], 3)
    vparts["kv_norm"] = _fm(f("mla_kv_norm")[0], 2)
    in_maps = []
    for c in cores:
        vp = dict(vparts)
        vp["h0"] = _fm(f("state_lru_h")[0, c], 8)
        cs_ = f("state_conv_a")[0, c]
        vp["convs"] = np.ascontiguousarray(cs_.reshape(3, 8, 128).transpose(2, 1, 0).reshape(128, 24))
        vecs = np.ascontiguousarray(np.concatenate([vp[n] for n, _ in _VNAMES], 1), dtype=np.float32)
        m = dict(shared)
        m.update({
            "xp": np.ascontiguousarray(f("x_prompt")[c]), "xs": np.ascontiguousarray(f("x_sample")[c]),
            "gla_s": np.ascontiguousarray(f("state_gla_S")[0, c]), "cckv": np.ascontiguousarray(f("cache_mla_ckv")[0, c]),
            "ckpe": np.ascontiguousarray(f("cache_mla_kpe")[0, c]), "vecs": vecs,
        })
        in_maps.append(m)
    return in_maps
def kernel(**inp):
    global _NC
    if _NC is None:
        _NC = build()
    nc = _NC
    in_maps = prep_inputs(inp)
    res = run_bass_kernel_spmd(nc, in_maps, core_ids=list(range(8)))
    R = res.results
    def st(name):
        return np.stack([np.asarray(R[c][name], np.float32) for c in range(8)], 0)
    y_prompt = st("y_p")
    y_sample = st("y_s")
    outs = [y_prompt, y_sample]
    for pfx in ("p", "s"):
        outs.append(st(pfx + "_conv")[None])
        outs.append(st(pfx + "_h").reshape(8, 1024)[None])
        outs.append(st(pfx + "_S")[None])
        outs.append(st(pfx + "_ckv")[None])
        outs.append(st(pfx + "_kpe")[None])
    return tuple(outs)
```

```python
import os
import numpy as np
import concourse.bass as bass
import concourse.mybir as mybir
from concourse.bass_utils import run_bass_kernel_spmd
F32 = mybir.dt.float32
BF16 = mybir.dt.bfloat16
AF = mybir.ActivationFunctionType
OP = mybir.AluOpType
D = 1024
T = 512
NG = 4
TS = 16
DFF = 2816
NFF = 22
EPS = 1e-6
SM_SCALE = 96 ** -0.5
GELU_K = 1.5957691216057308
_VNAMES = []
for _l in range(2):
    _VNAMES += [("mix_pre%d" % _l, 8), ("mix_post%d" % _l, 8), ("ffn_pre%d" % _l, 8), ("ffn_post%d" % _l, 8)]
_VNAMES += [("conv_w", 32), ("conv_b", 8), ("lru_ba", 8), ("lru_bx", 8), ("lam", 8), ("b_gate", 4),
            ("gla_norm", 2), ("q_norm", 3), ("kv_norm", 2), ("h0", 8), ("convs", 24)]
VOFF = {}
_c = 0
for _n, _k in _VNAMES:
    VOFF[_n] = _c
    _c += _k
NV = _c
class Buf:
    def __init__(self, t, name):
        self.t = t
        self.name = name
        self.st = {}
        self.excl = False
    def __getitem__(self, idx):
        return self.t[idx]
class Prog:
    ENGS = ("pe", "act", "dve", "pool", "sp")
    def __init__(self, nc):
        self.nc = nc
        self.ops = {e: [] for e in self.ENGS}
        self.known = {e: {} for e in self.ENGS}
        self.sems = {}
        self.cnt = {}
        self._stack = []
        self.epoch = {e: 0 for e in self.ENGS}
        self.dma_rr = {}
        for e in self.ENGS:
            self.new_sem("E_%s_0" % e)
    def enter(self, cm):
        r = cm.__enter__()
        self._stack.append(cm)
        return r
    def close(self):
        while self._stack:
            self._stack.pop().__exit__(None, None, None)
    def new_sem(self, sid):
        if sid not in self.sems:
            self.sems[sid] = self.enter(self.nc.semaphore(sid))
            self.cnt[sid] = 0
        return sid
    def sbuf(self, name, shape, dt):
        return Buf(self.enter(self.nc.sbuf_tensor("sb_" + name, list(shape), dt)), name)
    def psum(self, name, shape, dt=F32):
        b = Buf(self.enter(self.nc.psum_tensor("ps_" + name, list(shape), dt)), name)
        b.excl = True
        return b
    @staticmethod
    def _norm(lst):
        out = []
        for x in lst or []:
            out.append((x, None) if isinstance(x, Buf) else x)
        return out
    def _deps(self, reads, writes):
        w = {}
        def add(s, v):
            if w.get(s, 0) < v:
                w[s] = v
        for b, k in reads:
            keys = [k, None] if k is not None else list(b.st.keys())
            for kk in keys:
                st = b.st.get(kk)
                if st and st[0] is not None:
                    add(*st[0])
        for b, k in writes:
            keys = [k, None] if k is not None else list(b.st.keys())
            for kk in keys:
                st = b.st.get(kk)
                if st:
                    if st[0] is not None:
                        add(*st[0])
                    for s, v in st[1].items():
                        add(s, v)
        return w
    def _record(self, reads, writes, tok):
        for b, k in writes:
            if k is None:
                b.st = {None: [tok, {}]}
            else:
                b.st[k] = [tok, {}]
        for b, k in reads:
            if k is None:
                b.st.setdefault(None, [None, {}])
                for st in b.st.values():
                    if st[1].get(tok[0], 0) < tok[1]:
                        st[1][tok[0]] = tok[1]
            else:
                st = b.st.setdefault(k, [None, {}])
                if st[1].get(tok[0], 0) < tok[1]:
                    st[1][tok[0]] = tok[1]
    def op(self, eng, fn, reads=None, writes=None, dma_sem=None):
        reads = self._norm(reads)
        writes = self._norm(writes)
        xr = [r for r in reads if r[0].excl]
        if xr:
            reads = [r for r in reads if not r[0].excl]
            writes = writes + [r for r in xr if r not in writes]
        w = self._deps(reads, writes)
        if self.cnt["E_%s_%d" % (eng, self.epoch[eng])] >= 30000:
            self.epoch[eng] += 1
            self.new_sem("E_%s_%d" % (eng, self.epoch[eng]))
        own = "E_%s_%d" % (eng, self.epoch[eng])
        kn = self.known[eng]
        waits = []
        for s, v in w.items():
            if eng == "pe" and s.startswith("E_pe_"):
                continue
            if kn.get(s, 0) >= v:
                continue
            kn[s] = v
            waits.append((s, v))
        if dma_sem is not None:
            npool = 8
            i = self.dma_rr.get(eng, 0)
            self.dma_rr[eng] = i + 1
            dma_sem = "D_%s_%d" % (eng, i % npool)
            self.new_sem(dma_sem)
            if self.cnt[dma_sem] > 0 and kn.get(dma_sem, 0) < self.cnt[dma_sem]:
                kn[dma_sem] = self.cnt[dma_sem]
                waits.append((dma_sem, self.cnt[dma_sem]))
            self.cnt[dma_sem] += 16
            tok = (dma_sem, self.cnt[dma_sem])
            self.ops[eng].append((waits, fn, (dma_sem, 16)))
        else:
            self.cnt[own] += 1
            tok = (own, self.cnt[own])
            self.ops[eng].append((waits, fn, (own, 1)))
        self._record(reads, writes, tok)
        return tok
    def emit(self):
        nc = self.nc
        with nc.Block() as block:
            def body(ename):
                def f(e):
                    for waits, fn, (s, inc) in self.ops[ename]:
                        for ws, wv in waits:
                            e.wait_ge(self.sems[ws], wv)
                        ins = fn(e)
                        ins.then_inc(self.sems[s], inc)
                    if ename == "sp":
                        for s, c in self.cnt.items():
                            if c > 0:
                                e.wait_ge(self.sems[s], c)
                return f
            block.tensor(body("pe"))
            block.scalar(body("act"))
            block.vector(body("dve"))
            block.gpsimd(body("pool"))
            block.sync(body("sp"))
class Ctx:
    pass
def build(stage=99, ngroups=NG):
    nc = bass.Bass("TRN2", target_bir_lowering=False)
    P = Prog(nc)
    def di(n, s):
        return nc.dram_tensor(n, list(s), F32, kind="ExternalInput").ap()
    def do(n, s):
        return nc.dram_tensor(n, list(s), F32, kind="ExternalOutput").ap()
    xp = di("xp", [2048, D])
    xs = di("xs", [TS, D])
    gla_s = di("gla_s", [4, 128, 256])
    cckv = di("cckv", [4096, 256])
    ckpe = di("ckpe", [4096, 32])
    vecs_d = di("vecs", [128, NV])
    w_in_ab = di("w_in_ab", [D, 5136])
    lru_wa = di("lru_wa", [1024, 128])
    lru_wx = di("lru_wx", [1024, 128])
    w_gate = di("w_gate", [16, 512])
    w_out_ab = di("w_out_ab", [2048, D])
    w_in_c = di("w_in_c", [D, 768])
    w_uq_cat = di("w_uq_cat", [384, 3072])
    w_ukv = di("w_ukv", [256, 2048])
    w_ukT = di("w_ukT", [64, 4096])
    w_out_c = di("w_out_c", [1024, D])
    ffn_wg = di("ffn_wg", [2 * D, DFF])
    ffn_wu = di("ffn_wu", [2 * D, DFF])
    ffn_wd = di("ffn_wd", [2 * DFF, D])
    ident_d = di("ident", [128, 128])
    tri4_d = di("tri4", [128, 512])
    tri4s_d = di("tri4s", [128, 64])
    rmask_d = di("rmask", [128, 512])
    ropeP_d = di("ropeP", [32, 2 * 2048])
    ropeS_d = di("ropeS", [32, 2 * TS])
    y_p = do("y_p", [2048, D])
    y_s = do("y_s", [TS, D])
    o_conv = [do("p_conv", [3, D]), do("s_conv", [3, D])]
    o_h = [do("p_h", [8, 128]), do("s_h", [8, 128])]
    o_S = [do("p_S", [4, 128, 256]), do("s_S", [4, 128, 256])]
    o_ckv = [do("p_ckv", [2048, 256]), do("s_ckv", [TS, 256])]
    o_kpe = [do("p_kpe", [2048, 32]), do("s_kpe", [TS, 32])]
    def MM(ps, out, lhsT, rhs, st, sp, reads, skip=False):
        P.op("pe", lambda e: e.matmul(out, lhsT=lhsT, rhs=rhs, start=st, stop=sp, skip_group_check=skip), reads=reads, writes=[ps])
    def TR(ps, out, in_, idn, reads):
        P.op("pe", lambda e: e.transpose(out, in_, idn), reads=reads, writes=[ps])
    def ACT(out, in_, func, reads, writes, bias=None, scale=None):
        kw = {}
        if bias is not None:
            kw["bias"] = bias
        if scale is not None:
            kw["scale"] = scale
        P.op("act", lambda e: e.activation(out=out, in_=in_, func=func, **kw), reads=reads, writes=writes)
    def TT(eng, out, a, b, op, reads, writes):
        P.op(eng, lambda e: e.tensor_tensor(out=out, in0=a, in1=b, op=op), reads=reads, writes=writes)
    def TS_(eng, out, a, s1, s2, op0, op1, reads, writes):
        if s2 is None:
            P.op(eng, lambda e: e.tensor_scalar(out=out, in0=a, scalar1=s1, scalar2=None, op0=op0), reads=reads, writes=writes)
        else:
            P.op(eng, lambda e: e.tensor_scalar(out=out, in0=a, scalar1=s1, scalar2=s2, op0=op0, op1=op1), reads=reads, writes=writes)
    def STT(out, a, s, b, op0, op1, reads, writes):
        P.op("dve", lambda e: e.scalar_tensor_tensor(out=out, in0=a, scalar=s, in1=b, op0=op0, op1=op1), reads=reads, writes=writes)
    def CP(eng, out, in_, reads, writes):
        if eng == "act":
            P.op("act", lambda e: e.activation(out=out, in_=in_, func=AF.Copy), reads=reads, writes=writes)
        else:
            P.op(eng, lambda e: e.tensor_copy(out=out, in_=in_), reads=reads, writes=writes)
    def RECIP(out, in_, reads, writes):
        P.op("dve", lambda e: e.reciprocal(out=out, in_=in_), reads=reads, writes=writes)
    def SCAN(out, d0, d1, init, reads, writes):
        P.op("dve", lambda e: e.tensor_tensor_scan(out=out, data0=d0, data1=d1, initial=init, op0=OP.mult, op1=OP.add), reads=reads, writes=writes)
    def MEMSET(eng, ap, val, writes):
        P.op(eng, lambda e: e.memset(ap, val), writes=writes)
    def DMA(eng, out, in_, reads, writes, sem):
        P.op(eng, lambda e: e.dma_start(out=out, in_=in_), reads=reads, writes=writes, dma_sem=sem)
    allb = [P.psum("pb%d" % i, [128, 512], F32) for i in range(7)]
    acc = [allb[5], allb[6]]
    psb = P.psum("psb", [128, 1024], BF16)
    rot = [0]
    rot_n = [5]

    def ps_next():
        b = allb[rot[0] % rot_n[0]]
        rot[0] += 1
        return b
    ident = P.sbuf("ident", [128, 128], F32)
    identb = P.sbuf("identb", [128, 128], BF16)
    ones_bf = P.sbuf("ones_bf", [128, 128], BF16)
    ones_f = P.sbuf("ones_f", [128, 64], F32)
    tri4 = P.sbuf("tri4", [128, 512], BF16)
    tri4s = P.sbuf("tri4s", [128, 64], BF16)
    rmask = P.sbuf("rmask", [128, 512], BF16)
    vecs = P.sbuf("vecs", [128, NV], F32)
    drv = P.sbuf("drv", [128, 48], F32)
    wgate = P.sbuf("wgate", [16, 512], BF16)
    wab = P.sbuf("wab", [128, 2 * 8 * 128], BF16)
    diag = P.sbuf("diag", [128, 3 * 4 * 128], BF16)
    DMA("sp", ident[:, :], ident_d[:, :], [], [ident], "C0")
    DMA("pool", tri4[:, :], tri4_d[:, :], [], [tri4], "C1")
    DMA("pool", tri4s[:, :], tri4s_d[:, :], [], [tri4s], "C1")
    DMA("pool", rmask[:, :], rmask_d[:, :], [], [rmask], "C1")
    DMA("sp", vecs[:, :], vecs_d[:, :], [], [vecs], "C0")
    DMA("pool", identb[:, :], ident_d[:, :], [], [identb], "C1")
    DMA("pool", wgate[:, :], w_gate[:, :], [], [wgate], "C1")
    DMA("pool", wab[:, 0:1024].rearrange("p (n d) -> p n d", d=128), lru_wa.rearrange("(n c) d -> c n d", c=128), [], [wab], "C1")
    DMA("pool", wab[:, 1024:2048].rearrange("p (n d) -> p n d", d=128), lru_wx.rearrange("(n c) d -> c n d", c=128), [], [wab], "C1")
    MEMSET("dve", ones_bf[:, :], 1.0, [ones_bf])
    MEMSET("dve", ones_f[:, :], 1.0, [ones_f])
    def vcol(name, i=0):
        o = VOFF[name] + i
        return vecs[:, o:o + 1]
    TS_("dve", drv[:, 0:4], vecs[:, VOFF["b_gate"]:VOFF["b_gate"] + 4], -1.0, None, OP.mult, None, [vecs], [drv])
    TS_("dve", drv[:, 4:12], vecs[:, VOFF["lru_ba"]:VOFF["lru_ba"] + 8], -1.0, None, OP.mult, None, [vecs], [drv])
    TS_("dve", drv[:, 12:20], vecs[:, VOFF["lru_bx"]:VOFF["lru_bx"] + 8], -1.0, None, OP.mult, None, [vecs], [drv])
    ACT(drv[:, 36:44], vecs[:, VOFF["lam"]:VOFF["lam"] + 8], AF.Exp, [vecs], [drv], scale=-1.0)
    ACT(drv[:, 36:44], drv[:, 36:44], AF.Ln, [drv], [drv], bias=1.0)
    TS_("dve", drv[:, 20:28], drv[:, 36:44], -8.0, None, OP.mult, None, [drv], [drv])
    TS_("dve", drv[:, 28:36], drv[:, 36:44], -16.0, None, OP.mult, None, [drv], [drv])
    NSLOT = 3
    SLOT = 4096
    slots = [P.sbuf("wslot%d" % i, [128, SLOT], BF16) for i in range(NSLOT)]
    sl_i = [0]
    def wload(src2d, kp, nk, ncols):
        i = sl_i[0] % NSLOT
        sl_i[0] += 1
        sb = slots[i]
        assert nk * ncols <= SLOT
        view = sb.t[0:kp, 0:nk * ncols].rearrange("p (k n) -> p k n", n=ncols)
        srcv = src2d.rearrange("(k p) n -> p k n", p=kp)
        kstep = max(1, 1024 // kp)
        k0 = 0
        while k0 < nk:
            k1 = min(nk, k0 + kstep)
            DMA("pool", view[:, k0:k1, :], srcv[:, k0:k1, :], [], [(sb, k0)], "W%d" % i)
            k0 = k1
        return sb, view
    def make_ctx(tag, Tn, is_s):
        c = Ctx()
        c.tag, c.T, c.s = tag, Tn, is_s
        c.tw = min(128, Tn)
        c.nt = Tn // c.tw
        c.C = c.tw
        c.pb = 0 if is_s else 64
        c.hT = P.sbuf("hT" + tag, [128, 8 * Tn], F32)
        c.uT = P.sbuf("uT" + tag, [128, 8 * Tn], BF16)
        c.F2 = P.sbuf("F2" + tag, [128, max(8 * Tn, 1024)], F32)
        c.cum = P.sbuf("cum" + tag, [128, 4 * Tn], F32)
        c.b1n = max(22 * Tn, 20 * Tn + 24, 8 * Tn + c.nt * 1536)
        c.B1 = P.sbuf("B1" + tag, [128, c.b1n], BF16)
        c.B2 = P.sbuf("B2" + tag, [128, 16 * Tn], BF16)
        c.tf = [P.sbuf("tf%d%s" % (i, tag), [128, Tn], F32) for i in range(6)]
        c.tb = [P.sbuf("tb%d%s" % (i, tag), [128, Tn], BF16) for i in range(2)]
        c.rstd = P.sbuf("rstd" + tag, [128, Tn], F32)
        c.S = P.sbuf("S" + tag, [128, 4 * 256], F32)
        c.Sbf = P.sbuf("Sbf" + tag, [128, 4 * 256], BF16)
        c.Se = P.sbuf("Se" + tag, [128, 256], F32)
        c.hst = P.sbuf("hst" + tag, [128, 8], F32)
        c.carry = P.sbuf("carry" + tag, [128, 8 * 3], BF16)
        c.convo = P.sbuf("convo" + tag, [128, 8 * 3], F32)
        c.ebl = P.sbuf("ebl" + tag, [128, 4 * c.nt], F32)
        c.rope = P.sbuf("rope" + tag, [96 if not is_s else 32, 2 * Tn], F32)
        c.atb = [P.sbuf("atb%d%s" % (i, tag), [128, 4 * c.tw], BF16) for i in range(2)]
        c.tbi = 0
        return c
    cp = make_ctx("p", T, False)
    cs = make_ctx("s", TS, True)
    ckvnb = P.sbuf("ckvnb", [128, 2 * 2048], BF16)
    KT = [P.sbuf("KT%d" % i, [96, 2048], BF16) for i in range(2)]
    Vp = P.sbuf("Vp", [128, 16 * 2 * 65], BF16)
    PTs = [P.sbuf("PT%d" % i, [128, 512], BF16) for i in range(3)]
    pt_i = [0]
    def pt_next():
        b = PTs[pt_i[0] % 3]
        pt_i[0] += 1
        return b
    MEMSET("pool", Vp[:, :], 1.0, [Vp])
    def v3(buf, off, nk, Tn, p0=0, p1=128):
        return buf.t[p0:p1, off:off + nk * Tn].rearrange("p (k t) -> p k t", t=Tn)
    MEMSET("dve", cp.S[:, :], 0.0, [cp.S])
    MEMSET("dve", cp.Sbf[:, :], 0.0, [cp.Sbf])
    MEMSET("dve", cp.hst[:, :], 0.0, [cp.hst])
    MEMSET("dve", cp.carry[:, :], 0.0, [cp.carry])
    DMA("sp", cs.S[:, :].rearrange("p (h v) -> p h v", v=256), gla_s.rearrange("h k v -> k h v"), [], [cs.S], "C0")
    CP("act", cs.Sbf[:, :], cs.S[:, :], [cs.S], [cs.Sbf])
    CP("dve", cs.hst[:, :], vecs[:, VOFF["h0"]:VOFF["h0"] + 8], [vecs], [cs.hst])
    CP("dve", cs.carry[:, :], vecs[:, VOFF["convs"]:VOFF["convs"] + 24], [vecs], [cs.carry])
    DMA("sp", cs.rope[0:32, :], ropeS_d[:, :], [], [cs.rope], "C0")
    def rms(c, src, off, nk, nfeat, keyed=False):
        Tn = c.T
        ps = ps_next()
        for k in range(nk):
            sq = c.tb[c.tbi % 2]
            c.tbi += 1
            ACT(sq[:, :], src.t[:, off + k * Tn: off + (k + 1) * Tn], AF.Square, [(src, k)] if keyed else [src], [sq])
            MM(ps, ps[:, 0:Tn], ones_bf[:, :], sq[:, :], k == 0, k == nk - 1, [ones_bf, sq])
        ACT(c.rstd[:, :], ps[:, 0:Tn], AF.Ln, [ps], [c.rstd], bias=EPS, scale=1.0 / nfeat)
        ACT(c.rstd[:, :], c.rstd[:, :], AF.Exp, [c.rstd], [c.rstd], scale=-0.5)
    def load_x_dma(c, src, row0):
        tw = c.tw
        xb32 = c.B2.t[:, :].bitcast(F32)
        for t in range(c.nt):
            DMA("sp", xb32[0:tw, t * 1024:(t + 1) * 1024], src[row0 + t * tw: row0 + (t + 1) * tw, :], [], [c.B2], "IO" + c.tag)

    def load_x(c, src, row0, prefetched=False):
        Tn, tw = c.T, c.tw
        if c.s:
            buf, bt = c.F2, c.F2.t
        else:
            buf, bt = c.B2, c.B2.t[:, :].bitcast(F32)
            if not prefetched:
                load_x_dma(c, src, row0)
        for t in range(c.nt):
            if c.s:
                DMA("sp", bt[0:tw, t * 1024:(t + 1) * 1024], src[row0 + t * tw: row0 + (t + 1) * tw, :], [], [buf], "IO" + c.tag)
            for kq in range(2):
                ps = ps_next()
                for kk in range(4):
                    kc = kq * 4 + kk
                    TR(ps, ps[:, kk * tw:(kk + 1) * tw], bt[0:tw, t * 1024 + kc * 128: t * 1024 + (kc + 1) * 128],
                       ident[0:tw, 0:tw], [buf, ident])
                CP("act" if kq == 0 else "dve", v3(c.hT, 0, 8, Tn)[:, kq * 4:(kq + 1) * 4, t * tw:(t + 1) * tw],
                   ps[:, 0:4 * tw].rearrange("p (k t) -> p k t", t=tw), [ps], [c.hT])
    def store_y(c, dst, row0):
        Tn, tw = c.T, c.tw
        fence(c.F2)
        for t in range(c.nt):
            for kq in range(2):
                ps = ps_next()
                for kk in range(4):
                    kc = kq * 4 + kk
                    TR(ps, ps[0:tw, kk * 128:(kk + 1) * 128], c.hT.t[:, kc * Tn + t * tw: kc * Tn + (t + 1) * tw],
                       ident[:, :], [c.hT, ident])
                CP("act" if kq == 0 else "dve", c.F2.t[0:tw, t * 1024 + kq * 512: t * 1024 + (kq + 1) * 512], ps[0:tw, :],
                   [ps], [(c.F2, ("io", t))])
            DMA("sp", dst[row0 + t * tw: row0 + (t + 1) * tw, :], c.F2.t[0:tw, t * 1024:(t + 1) * 1024],
                [(c.F2, ("io", t))], [], "IO" + c.tag)
    def prenorm(c, gname):
        Tn = c.T
        rms(c, c.hT, 0, 8, 1024, keyed=True)
        for k in range(8):
            STT(c.uT.t[:, k * Tn:(k + 1) * Tn], c.hT.t[:, k * Tn:(k + 1) * Tn], vcol(gname, k), c.rstd[:, :],
                OP.mult, OP.mult, [(c.hT, k), c.rstd, vecs], [(c.uT, k)])
    def postnorm_add(c, gname):
        Tn = c.T
        rms(c, c.F2, 0, 8, 1024)
        for k in range(8):
            tmp = c.tf[k % 2]
            TT("dve", tmp[:, :], c.F2.t[:, k * Tn:(k + 1) * Tn], c.rstd[:, :], OP.mult, [c.F2, c.rstd], [tmp])
            STT(c.hT.t[:, k * Tn:(k + 1) * Tn], tmp[:, :], vcol(gname, k), c.hT.t[:, k * Tn:(k + 1) * Tn], OP.mult, OP.add,
                [tmp, (c.hT, k), vecs], [(c.hT, k)])
    def fence(buf):
        P.op("dve", lambda e: e.memset(drv[:, 47:48], 0.0), writes=[buf, (drv, "f")])
    def ffn(ctxs, layer):
        for c in ctxs:
            prenorm(c, "ffn_pre%d" % layer)
            fence(c.B1)
        col = 0
        while col < DFF:
            nc_ = min(512, DFF - col)
            sg, wg = wload(ffn_wg[layer * D:(layer + 1) * D, col:col + nc_], 128, 8, nc_)
            su, wu = wload(ffn_wu[layer * D:(layer + 1) * D, col:col + nc_], 128, 8, nc_)
            for c in ctxs:
                Tn = c.T
                nj = nc_ // 128
                for j in range(nj):
                    pg = ps_next()
                    for k in range(8):
                        MM(pg, pg[:, 0:Tn], wg[:, k, j * 128:(j + 1) * 128], c.uT.t[:, k * Tn:(k + 1) * Tn], k == 0, k == 7, [sg, (c.uT, k)])
                    tmp = c.tf[2 + j]
                    ACT(tmp[:, :], pg[:, 0:Tn], AF.Silu, [pg], [tmp])
                for j in range(nj):
                    fj = col // 128 + j
                    pu = ps_next()
                    for k in range(8):
                        MM(pu, pu[:, 0:Tn], wu[:, k, j * 128:(j + 1) * 128], c.uT.t[:, k * Tn:(k + 1) * Tn], k == 0, k == 7, [su, (c.uT, k)])
                    tmp = c.tf[2 + j]
                    TT("dve", c.B1.t[:, fj * Tn:(fj + 1) * Tn], tmp[:, :], pu[:, 0:Tn], OP.mult, [tmp, pu], [(c.B1, ("ff", fj))])
            col += nc_
        for dc in range(8):
            sd, wd = wload(ffn_wd[layer * DFF:(layer + 1) * DFF, dc * 128:(dc + 1) * 128], 128, NFF, 128)
            for c in ctxs:
                Tn = c.T
                ps = ps_next()
                for f in range(NFF):
                    MM(ps, ps[:, 0:Tn], wd[:, f, :], c.B1.t[:, f * Tn:(f + 1) * Tn], f == 0, f == NFF - 1, [sd, (c.B1, ("ff", f))])
                CP("act", c.F2.t[:, dc * Tn:(dc + 1) * Tn], ps[:, 0:Tn], [ps], [c.F2])
        for c in ctxs:
            postnorm_add(c, "ffn_post%d" % layer)
    def mixer_ab(ctxs, last):
        for c in ctxs:
            prenorm(c, "mix_pre0")
            fence(c.B1)
            fence(c.F2)
        wz = P_wz
        DMA("pool", wz.t[:, :].rearrange("p (k n) -> p k n", n=16), w_in_ab[:, 5120:5136].rearrange("(k p) n -> p k n", p=128), [], [wz], "C1")
        for c in ctxs:
            Tn = c.T
            ps = ps_next()
            for k in range(8):
                MM(ps, ps[0:16, 0:Tn], wz.t[:, k * 16:(k + 1) * 16], c.uT.t[:, k * Tn:(k + 1) * Tn], k == 0, k == 7, [wz, (c.uT, k)])
            zr = c.tb[0]
            CP("act", zr[0:16, :], ps[0:16, 0:Tn], [ps], [zr])
            for h in range(4):
                pz = ps_next()
                MM(pz, pz[:, 0:Tn], wgate[0:16, h * 128:(h + 1) * 128], zr[0:16, :], True, True, [wgate, zr])
                e1 = c.tf[0]
                ACT(e1[:, :], pz[:, 0:Tn], AF.Exp, [pz, drv], [e1], bias=drv[:, h:h + 1], scale=-1.0)
                ACT(e1[:, :], e1[:, :], AF.Ln, [e1], [e1], bias=1.0)
                msk = rmask[:, 0:Tn] if not c.s else rmask[:, 1:1 + Tn]
                SCAN(c.cum.t[:, h * Tn:(h + 1) * Tn], msk, e1[:, :], 0.0, [rmask, e1], [(c.cum, h)])
                ACT(c.ebl.t[:, h * c.nt:(h + 1) * c.nt],
                    c.cum.t[:, h * Tn:(h + 1) * Tn].rearrange("p (n c) -> p n c", c=c.C)[:, :, c.C - 1],
                    AF.Exp, [(c.cum, h)], [(c.ebl, h)], scale=-1.0 / 16.0)
        for half in range(2):
            sx, wx = wload(w_in_ab[:, half * 512:(half + 1) * 512], 128, 8, 512)
            sgw, wgl = wload(w_in_ab[:, 1024 + half * 512:1024 + (half + 1) * 512], 128, 8, 512)
            sqk, wqk = wload(w_in_ab[:, 2048 + half * 512:2560 + half * 512], 128, 8, 512)

            def qk_head(c, h):
                Tn = c.T
                ps = ps_next()
                for k in range(8):
                    MM(ps, ps[:, 0:Tn], wqk[:, k, h * 128:(h + 1) * 128], c.uT.t[:, k * Tn:(k + 1) * Tn], k == 0, k == 7, [sqk, (c.uT, k)])
                e1 = c.rstd
                if half == 0:
                    ACT(e1[:, :], c.cum.t[:, h * Tn:(h + 1) * Tn], AF.Exp, [(c.cum, h)], [e1], scale=-1.0 / 16.0)
                    STT(c.B1.t[:, h * Tn:(h + 1) * Tn], ps[:, 0:Tn], 128.0 ** -0.5, e1[:, :], OP.mult, OP.mult, [ps, e1], [(c.B1, ("q", h))])
                else:
                    ACT(e1[:, :], c.cum.t[:, h * Tn:(h + 1) * Tn], AF.Exp, [(c.cum, h)], [e1], scale=1.0 / 16.0)
                    TT("dve", c.B1.t[:, 4 * Tn + h * Tn: 4 * Tn + (h + 1) * Tn], ps[:, 0:Tn], e1[:, :], OP.mult, [ps, e1], [(c.B1, ("k", h))])
            for c in ctxs:
                Tn = c.T
                XO = 8 * Tn
                GO = 16 * Tn + 24
                for j in range(4):
                    kc = half * 4 + j
                    xo = XO + kc * (Tn + 3)
                    ps = ps_next()
                    for k in range(8):
                        MM(ps, ps[:, 0:Tn], wx[:, k, j * 128:(j + 1) * 128], c.uT.t[:, k * Tn:(k + 1) * Tn], k == 0, k == 7, [sx, (c.uT, k)])
                    CP("act", c.B1.t[:, xo + 3: xo + 3 + Tn], ps[:, 0:Tn], [ps], [(c.B1, ("xa", kc))])
                    CP("act", c.B1.t[:, xo: xo + 3], c.carry.t[:, kc * 3:(kc + 1) * 3], [(c.carry, kc)], [(c.B1, ("xa", kc))])
                    if last or c.s:
                        CP("dve", c.convo.t[:, kc * 3:(kc + 1) * 3], ps[:, Tn - 3:Tn], [ps], [(c.convo, kc)])
                def tv(i):
                    if i < 6:
                        return c.tf[i].t[:, 0:Tn], c.tf[i]
                    return c.F2.t[:, (i - 6) * Tn:(i - 5) * Tn], (c.F2, ("t", i - 6))
                gt = [tv(11), tv(12), tv(13), tv(5)]
                gps = []
                for j in range(4):
                    ps = ps_next()
                    gps.append(ps)
                    for k in range(8):
                        MM(ps, ps[:, 0:Tn], wgl[:, k, j * 128:(j + 1) * 128], c.uT.t[:, k * Tn:(k + 1) * Tn], k == 0, k == 7, [sgw, (c.uT, k)])
                    ACT(c.B1.t[:, GO + j * Tn: GO + (j + 1) * Tn], ps[:, 0:Tn], AF.Gelu_apprx_tanh, [ps], [(c.B1, ("gl", j))])
                st_ = {}
                XS = [0, 6, 5]

                def build_diag(j):
                    kc = half * 4 + j
                    dg = (kc % 3) * 512
                    for tap in range(4):
                        TS_("dve", diag[:, dg + tap * 128: dg + (tap + 1) * 128], identb[:, :], vcol("conv_w", tap * 8 + kc), None, OP.mult, None,
                            [identb, vecs], [(diag, (kc % 3, tap))])

                def prep_conv(j):
                    kc = half * 4 + j
                    xo = XO + kc * (Tn + 3)
                    dg = (kc % 3) * 512
                    pc = ps_next()
                    for tap in range(4):
                        MM(pc, pc[:, 0:Tn], diag[:, dg + tap * 128: dg + (tap + 1) * 128], c.B1.t[:, xo + tap: xo + tap + Tn], tap == 0, tap == 3,
                           [(diag, (kc % 3, tap)), (c.B1, ("xa", kc))])
                    CP("act", c.carry.t[:, kc * 3:(kc + 1) * 3], c.B1.t[:, xo + Tn: xo + Tn + 3], [(c.B1, ("xa", kc))], [(c.carry, kc)])
                    base = 0 if kc % 2 == 0 else 6
                    tvs = [tv(XS[kc % 3])] + [tv(base + q) for q in range(1, 5)]
                    xb = c.tb[kc % 2]
                    ACT(tvs[0][0], pc[:, 0:Tn], AF.Identity, [pc, vecs], [tvs[0][1]], bias=vcol("conv_b", kc))
                    TS_("dve", xb[:, :], pc[:, 0:Tn], vcol("conv_b", kc), None, OP.add, None, [pc, vecs], [xb])
                    st_[j] = [tvs, xb, None, None]

                def gates(j):
                    kc = half * 4 + j
                    xb = st_[j][1]
                    pr = ps_next()
                    pi = ps_next()
                    MM(pr, pr[:, 0:Tn], wab[:, kc * 128:(kc + 1) * 128], xb[:, :], True, True, [wab, xb])
                    MM(pi, pi[:, 0:Tn], wab[:, 1024 + kc * 128:1024 + (kc + 1) * 128], xb[:, :], True, True, [wab, xb])
                    st_[j][2], st_[j][3] = pr, pi

                def chain_head(j):
                    kc = half * 4 + j
                    (X, Xd), (R, Rd), (I, Id), (A, Ad), (M, Md) = st_[j][0]
                    pr, pi = st_[j][2], st_[j][3]
                    ACT(R, pr[:, 0:Tn], AF.Exp, [pr, drv], [Rd], bias=drv[:, 4 + kc:5 + kc], scale=-1.0)
                    ACT(I, pi[:, 0:Tn], AF.Exp, [pi, drv], [Id], bias=drv[:, 12 + kc:13 + kc], scale=-1.0)

                def chain_tail(j):
                    kc = half * 4 + j
                    (X, Xd), (R, Rd), (I, Id), (A, Ad), (M, Md) = st_[j][0]
                    ACT(R, R, AF.Ln, [Rd], [Rd], bias=1.0)
                    ACT(R, R, AF.Exp, [Rd], [Rd], scale=-1.0)
                    ACT(A, R, AF.Exp, [Rd, drv], [Ad], scale=drv[:, 20 + kc:21 + kc])
                    TT("dve", M, A, A, OP.mult, [Ad], [Md])
                    ACT(I, I, AF.Ln, [Id], [Id], bias=1.0)
                    ACT(I, I, AF.Exp, [Id], [Id], scale=-1.0)
                    TT("dve", I, I, X, OP.mult, [Id, Xd], [Id])
                    ACT(M, M, AF.Ln, [Md], [Md], bias=1.0, scale=-1.0)
                    ACT(M, M, AF.Exp, [Md], [Md], scale=0.5)
                    TT("dve", I, I, M, OP.mult, [Id, Md], [Id])
                    SCAN(X, A, I, c.hst.t[:, kc:kc + 1], [Ad, Id, (c.hst, kc), Xd], [Xd])
                    CP("dve", c.hst.t[:, kc:kc + 1], X[:, Tn - 1:Tn], [Xd], [(c.hst, kc)])
                    TT("dve", c.B2.t[:, kc * Tn:(kc + 1) * Tn], X, c.B1.t[:, GO + j * Tn: GO + (j + 1) * Tn], OP.mult,
                       [Xd, (c.B1, ("gl", j))], [(c.B2, kc)])

                build_diag(0)
                build_diag(1)
                prep_conv(0)
                gates(0)
                for j in range(4):
                    if j + 2 < 4:
                        build_diag(j + 2)
                    if j + 1 < 4:
                        prep_conv(j + 1)
                    chain_head(j)
                    if j + 1 < 4:
                        gates(j + 1)
                    qk_head(c, j)
                    chain_tail(j)
        for c in ctxs:
            fence(c.F2)
        for c in ctxs:
            fence(c.B1)
        for c in ctxs:
            Tn, tw = c.T, c.tw
            KO = 4 * Tn
            KTO = 8 * Tn
            for t in range(c.nt):
                for h in range(4):
                    TR(psb, psb[0:tw, h * 128:(h + 1) * 128], c.B1.t[:, KO + h * Tn + t * tw: KO + h * Tn + (t + 1) * tw], identb[:, :],
                       [(c.B1, ("k", h)), identb])
                CP("act", c.B1.t[0:tw, KTO + t * 512: KTO + (t + 1) * 512], psb[0:tw, 0:512], [psb], [(c.B1, ("kt", t))])
        for vb in range(2):
            sv_, wv_ = wload(w_in_ab[:, 3072 + vb * 512:3072 + (vb + 1) * 512], 128, 8, 512)
            for c in ctxs:
                Tn, tw = c.T, c.tw
                VO = 8 * Tn + c.nt * 512
                for t in range(c.nt):
                    ps = ps_next()
                    for k in range(8):
                        MM(ps, ps[0:tw, :], c.uT.t[:, k * Tn + t * tw: k * Tn + (t + 1) * tw], wv_[:, k, :], k == 0, k == 7, [sv_, (c.uT, k)])
                    CP("act" if t % 2 == 0 else "dve", c.B1.t[0:tw, VO + t * 1024 + vb * 512: VO + t * 1024 + (vb + 1) * 512], ps[0:tw, :],
                       [ps], [(c.B1, ("v", t))])
        for c in ctxs:
            Tn, tw, C = c.T, c.tw, c.C
            KO, KTO = 4 * Tn, 8 * Tn
            VO = 8 * Tn + c.nt * 512
            trm = tri4 if not c.s else tri4s
            def a_mask(t):
                pa = ps_next()
                for h in range(4):
                    MM(pa, pa[0:C, h * C:(h + 1) * C], c.B1.t[:, KO + h * Tn + t * C: KO + h * Tn + (t + 1) * C],
                       c.B1.t[:, h * Tn + t * C: h * Tn + (t + 1) * C], True, True, [(c.B1, ("k", h)), (c.B1, ("q", h))])
                at = c.atb[t % 2]
                TT("dve", at[0:C, 0:4 * C], pa[0:C, 0:4 * C], trm[0:C, 0:4 * C], OP.mult, [pa, trm], [at])
                return at
            at = a_mask(0)
            for t in range(c.nt):
                for hp in range(2):
                    po = ps_next()
                    for hh in range(2):
                        h = hp * 2 + hh
                        for j in range(2):
                            c0 = (hh * 2 + j) * C
                            MM(po, po[:, c0:c0 + C], c.B1.t[0:C, VO + t * 1024 + h * 256 + j * 128: VO + t * 1024 + h * 256 + (j + 1) * 128],
                               at[0:C, h * C:(h + 1) * C], (hh == 0 and j == 0), False, [(c.B1, ("v", t)), at], skip=True)
                    for hh in range(2):
                        h = hp * 2 + hh
                        for j in range(2):
                            c0 = (hh * 2 + j) * C
                            MM(po, po[:, c0:c0 + C], c.Sbf.t[:, h * 256 + j * 128: h * 256 + (j + 1) * 128],
                               c.B1.t[:, h * Tn + t * C: h * Tn + (t + 1) * C], False, True, [(c.Sbf, h), (c.B1, ("q", h))], skip=True)
                    CP("act", v3(c.F2, 0, 8, Tn)[:, hp * 4:(hp + 1) * 4, t * C:(t + 1) * C],
                       po[:, 0:4 * C].rearrange("p (k t) -> p k t", t=C), [po], [(c.F2, ("o", hp))])
                pds = []
                for hp in range(2):
                    pd = ps_next()
                    pds.append(pd)
                    for hh in range(2):
                        h = hp * 2 + hh
                        MM(pd, pd[:, hh * 256:(hh + 1) * 256], c.B1.t[0:C, KTO + t * 512 + h * 128: KTO + t * 512 + (h + 1) * 128],
                           c.B1.t[0:C, VO + t * 1024 + h * 256: VO + t * 1024 + (h + 1) * 256], True, True,
                           [(c.B1, ("kt", t)), (c.B1, ("v", t))])
                if t + 1 < c.nt:
                    at = a_mask(t + 1)
                for hp in range(2):
                    pd = pds[hp]
                    for hh in range(2):
                        h = hp * 2 + hh
                        eb = c.ebl.t[:, h * c.nt + t: h * c.nt + t + 1]
                        Sh = c.S.t[:, h * 256:(h + 1) * 256]
                        TT("dve", Sh, Sh, pd[:, hh * 256:(hh + 1) * 256], OP.add, [(c.S, h), pd], [(c.S, h)])
                        ACT(c.Sbf.t[:, h * 256:(h + 1) * 256], Sh, AF.Copy, [(c.S, h), (c.ebl, h)], [(c.Sbf, h)], scale=eb)
                        TS_("dve", Sh, Sh, eb, None, OP.mult, None, [(c.S, h), (c.ebl, h)], [(c.S, h)])
            pss_ = []
            for h in range(4):
                ps = ps_next()
                pss_.append(ps)
                for j in range(2):
                    sq = c.tb[c.tbi % 2]
                    c.tbi += 1
                    ACT(sq[:, :], c.F2.t[:, (2 * h + j) * Tn:(2 * h + j + 1) * Tn], AF.Square, [c.F2], [sq])
                    MM(ps, ps[:, 0:Tn], ones_bf[:, :], sq[:, :], j == 0, j == 1, [ones_bf, sq])
            for h in range(4):
                ACT(c.tf[h][:, :], pss_[h][:, 0:Tn], AF.Ln, [pss_[h]], [c.tf[h]], bias=EPS, scale=1.0 / 256)
            for h in range(4):
                ACT(c.tf[h][:, :], c.tf[h][:, :], AF.Exp, [c.tf[h]], [c.tf[h]], scale=-0.5)
            for h in range(4):
                for j in range(2):
                    hj = 2 * h + j
                    STT(c.F2.t[:, hj * Tn:(hj + 1) * Tn], c.F2.t[:, hj * Tn:(hj + 1) * Tn], vcol("gla_norm", j), c.tf[h][:, :],
                        OP.mult, OP.mult, [c.F2, c.tf[h], vecs], [(c.F2, ("o", h // 2))])
        for gbk in range(2):
            sb_, wb_ = wload(w_in_ab[:, 4096 + gbk * 512:4096 + (gbk + 1) * 512], 128, 8, 512)
            for c in ctxs:
                Tn = c.T
                for j in range(4):
                    hj = gbk * 4 + j
                    ps = ps_next()
                    for k in range(8):
                        MM(ps, ps[:, 0:Tn], wb_[:, k, j * 128:(j + 1) * 128], c.uT.t[:, k * Tn:(k + 1) * Tn], k == 0, k == 7, [sb_, (c.uT, k)])
                    tmp = c.tf[4 + j % 2]
                    ACT(tmp[:, :], ps[:, 0:Tn], AF.Silu, [ps], [tmp])
                    TT("dve", c.B2.t[:, (8 + hj) * Tn:(9 + hj) * Tn], tmp[:, :], c.F2.t[:, hj * Tn:(hj + 1) * Tn], OP.mult,
                       [tmp, c.F2], [(c.B2, 8 + hj)])
        for blk in range(4):
            so_, wo_ = wload(w_out_ab[:, blk * 256:(blk + 1) * 256], 128, 16, 256)
            for c in ctxs:
                Tn = c.T
                for dl in range(2):
                    dc = blk * 2 + dl
                    ps = ps_next()
                    for k in range(16):
                        MM(ps, ps[:, 0:Tn], wo_[:, k, dl * 128:(dl + 1) * 128], c.B2.t[:, k * Tn:(k + 1) * Tn], k == 0, k == 15, [so_, (c.B2, k)])
                    CP("act", c.F2.t[:, dc * Tn:(dc + 1) * Tn], ps[:, 0:Tn], [ps], [c.F2])
        for c in ctxs:
            postnorm_add(c, "mix_post0")
    P_wz = P.sbuf("wz", [128, 8 * 16], BF16)
    def out_states(c, idx):
        for half in range(2):
            ps = ps_next()
            for kk in range(4):
                kc = half * 4 + kk
                TR(ps, ps[0:3, kk * 128:(kk + 1) * 128], c.convo.t[:, kc * 3:(kc + 1) * 3], ident[:, :], [c.convo, ident])
            CP("act", c.F2.t[0:3, half * 512:(half + 1) * 512], ps[0:3, 0:512], [ps], [c.F2])
        DMA("sp", o_conv[idx][:, :], c.F2.t[0:3, 0:1024], [c.F2], [], "O" + c.tag)
        ps = ps_next()
        TR(ps, ps[0:8, 0:128], c.hst.t[:, 0:8], ident[:, :], [c.hst, ident])
        hh = c.Se
        CP("act", hh[0:8, 0:128], ps[0:8, 0:128], [ps], [hh])
        DMA("sp", o_h[idx][:, :], hh[0:8, 0:128], [hh], [], "O" + c.tag)
        DMA("sp", o_S[idx].rearrange("h k v -> k h v"), c.S.t[:, :].rearrange("p (h v) -> p h v", v=256), [c.S], [], "O" + c.tag)
    def mla_proj(ctxs, g):
        for c in ctxs:
            prenorm(c, "mix_pre1")
            fence(c.B1)
        s1, w1 = wload(w_in_c[:, 0:384], 128, 8, 384)
        s2, w2 = wload(w_in_c[:, 384:768], 128, 8, 384)
        for c in ctxs:
            Tn, tw, pb = c.T, c.tw, c.pb
            for k3 in range(3):
                ps = ps_next()
                for k in range(8):
                    MM(ps, ps[:, 0:Tn], w1[:, k, k3 * 128:(k3 + 1) * 128], c.uT.t[:, k * Tn:(k + 1) * Tn], k == 0, k == 7, [s1, (c.uT, k)])
                CP("act", c.F2.t[:, k3 * Tn:(k3 + 1) * Tn], ps[:, 0:Tn], [ps], [(c.F2, ("cq", k3))])
            for k2 in range(2):
                ps = ps_next()
                for k in range(8):
                    MM(ps, ps[:, 0:Tn], w2[:, k, k2 * 128:(k2 + 1) * 128], c.uT.t[:, k * Tn:(k + 1) * Tn], k == 0, k == 7, [s2, (c.uT, k)])
                CP("act", c.F2.t[:, (3 + k2) * Tn:(4 + k2) * Tn], ps[:, 0:Tn], [ps], [(c.F2, ("ckv", k2))])
            if SUB < 1:
                continue
            pA = ps_next()
            pB = ps_next()
            if not c.s:
                cA, cB, M_ = (192, 288), (288, 384), 96
            else:
                cA, cB, M_ = (256, 288), (352, 384), 32
            for k in range(8):
                MM(pA, pA[0:M_, 0:Tn], w2[:, k, cA[0]:cA[1]], c.uT.t[:, k * Tn:(k + 1) * Tn], k == 0, k == 7, [s2, (c.uT, k)])
            for k in range(8):
                MM(pB, pB[0:M_, 0:Tn], w2[:, k, cB[0]:cB[1]], c.uT.t[:, k * Tn:(k + 1) * Tn], k == 0, k == 7, [s2, (c.uT, k)])
            if not c.s:
                DMA("sp", c.rope.t[64:96, :].rearrange("p (a t) -> p a t", t=Tn),
                    ropeP_d.rearrange("p (a t) -> p a t", t=2048)[:, :, g * T:(g + 1) * T], [], [c.rope], "R" + c.tag)
            t1, t2 = c.tf[0], c.tf[1]
            kpr = c.F2.t[pb:pb + 32, 5 * Tn:6 * Tn]
            TT("dve", t1[pb:pb + 32, :], pA[pb:pb + 32, 0:Tn], c.rope.t[pb:pb + 32, 0:Tn], OP.mult, [pA, c.rope], [t1])
            TT("dve", t2[pb:pb + 32, :], pB[pb:pb + 32, 0:Tn], c.rope.t[pb:pb + 32, Tn:2 * Tn], OP.mult, [pB, c.rope], [t2])
            TT("dve", kpr, t1[pb:pb + 32, :], t2[pb:pb + 32, :], OP.add, [t1, t2], [(c.F2, "kpr")])
            if not c.s:
                for i in range(2):
                    CP("act", KT[i].t[64:96, g * T:(g + 1) * T], kpr, [(c.F2, "kpr")], [(KT[i], "pe")])
            else:
                CP("act", c.kprb.t[0:32, 0:Tn], kpr, [(c.F2, "kpr")], [c.kprb])
            if SUB < 2:
                continue
            rms(c, c.F2, 0, 3, 384)
            for k3 in range(3):
                STT(c.uT.t[:, k3 * Tn:(k3 + 1) * Tn], c.F2.t[:, k3 * Tn:(k3 + 1) * Tn], vcol("q_norm", k3), c.rstd[:, :], OP.mult, OP.mult,
                    [(c.F2, ("cq", k3)), c.rstd, vecs], [c.uT])
            rms(c, c.F2, 3 * Tn, 2, 256)
            for k2 in range(2):
                STT(c.F2.t[:, (3 + k2) * Tn:(4 + k2) * Tn], c.F2.t[:, (3 + k2) * Tn:(4 + k2) * Tn], vcol("kv_norm", k2), c.rstd[:, :],
                    OP.mult, OP.mult, [(c.F2, ("ckv", k2)), c.rstd, vecs], [(c.F2, ("ckv", k2))])
                if not c.s:
                    CP("act", ckvnb.t[:, k2 * 2048 + g * T: k2 * 2048 + (g + 1) * T], c.F2.t[:, (3 + k2) * Tn:(4 + k2) * Tn],
                       [(c.F2, ("ckv", k2))], [(ckvnb, g)])
                else:
                    CP("act", c.ckvb.t[:, k2 * Tn:(k2 + 1) * Tn], c.F2.t[:, (3 + k2) * Tn:(4 + k2) * Tn], [(c.F2, ("ckv", k2))], [c.ckvb])
            if SUB < 3:
                continue
            idx = 1 if c.s else 0
            for t in range(c.nt):
                ps = ps_next()
                for k2 in range(2):
                    TR(ps, ps[0:tw, k2 * 128:(k2 + 1) * 128], c.F2.t[:, (3 + k2) * Tn + t * tw:(3 + k2) * Tn + (t + 1) * tw], ident[:, :],
                       [(c.F2, ("ckv", k2)), ident])
                if KPEOUT:
                    MM(ps, ps[0:tw, 256:288], c.F2.t[pb:pb + 32, 5 * Tn + t * tw: 5 * Tn + (t + 1) * tw], ident[pb:pb + 32, pb:pb + 32],
                       True, True, [(c.F2, "kpr"), ident])
                st = c.ost[t % 2]
                CP("act", st[0:tw, 0:288], ps[0:tw, 0:288], [ps], [st])
                r0 = (g * T if not c.s else 0) + t * tw
                if OUTV >= 2:
                    DMA("sp", o_ckv[idx][r0:r0 + tw, :], st[0:tw, 0:256], [st], [], "O" + c.tag)
                if OUTV >= 3:
                    DMA("sp", o_kpe[idx][r0:r0 + tw, :], st[0:tw, 256:288], [st], [], "O" + c.tag)
                if c.s:
                    CP("dve", c.ckvtok.t[0:tw, 0:256], ps[0:tw, 0:256], [ps], [c.ckvtok])
        if SUB < 4:
            return
        for c in ctxs:
            fence(c.B1)
        for hf in range(4):
            sa, wcat = wload(w_uq_cat[:, hf * 768:(hf + 1) * 768], 128, 3, 768)
            sb2 = sa
            wa_ = wcat[:, :, 0:384]
            wb2 = wcat[:, :, 384:768]
            for c in ctxs:
                Tn = c.T
                for hl in range(4):
                    h = hf * 4 + hl
                    if not c.s:
                        pA = ps_next()
                        pB = ps_next()
                        for k in range(3):
                            MM(pA, pA[0:96, 0:Tn], wa_[:, k, hl * 96:(hl + 1) * 96], c.uT.t[:, k * Tn:(k + 1) * Tn], k == 0, k == 2, [sa, c.uT])
                        for k in range(3):
                            MM(pB, pB[0:96, 0:Tn], wb2[:, k, hl * 96:(hl + 1) * 96], c.uT.t[:, k * Tn:(k + 1) * Tn], k == 0, k == 2, [sb2, c.uT])
                        CP("act", c.B1.t[0:64, h * Tn:(h + 1) * Tn], pA[0:64, 0:Tn], [pA], [(c.B1, ("Q", h))])
                        t1, t2 = c.tf[0], c.tf[1]
                        TT("dve", t1[64:96, :], pA[64:96, 0:Tn], c.rope.t[64:96, 0:Tn], OP.mult, [pA, c.rope], [t1])
                        TT("dve", t2[64:96, :], pB[64:96, 0:Tn], c.rope.t[64:96, Tn:2 * Tn], OP.mult, [pB, c.rope], [t2])
                        TT("dve", c.B1.t[64:96, h * Tn:(h + 1) * Tn], t1[64:96, :], t2[64:96, :], OP.add, [t1, t2], [(c.B1, ("Q", h))])
                    else:
                        pq = ps_next()
                        for k in range(3):
                            MM(pq, pq[0:64, 0:16], wa_[:, k, hl * 96:hl * 96 + 64], c.uT.t[:, k * Tn:(k + 1) * Tn], k == 0, k == 2, [sa, c.uT])
                        for k in range(3):
                            MM(pq, pq[0:32, 16:32], wa_[:, k, hl * 96 + 64:hl * 96 + 96], c.uT.t[:, k * Tn:(k + 1) * Tn], k == 0, k == 2, [sa, c.uT])
                        for k in range(3):
                            MM(pq, pq[0:32, 32:48], wb2[:, k, hl * 96 + 64:hl * 96 + 96], c.uT.t[:, k * Tn:(k + 1) * Tn], k == 0, k == 2, [sb2, c.uT])
                        CP("act", c.qn.t[0:64, h * 16:(h + 1) * 16], pq[0:64, 0:16], [pq], [c.qn])
                        t1, t2 = c.tf[0], c.tf[1]
                        TT("dve", t1[0:32, :], pq[0:32, 16:32], c.rope.t[0:32, 0:Tn], OP.mult, [pq, c.rope], [t1])
                        TT("dve", t2[0:32, :], pq[0:32, 32:48], c.rope.t[0:32, Tn:2 * Tn], OP.mult, [pq, c.rope], [t2])
                        TT("dve", c.qpe.t[0:32, h * 16:(h + 1) * 16], t1[0:32, :], t2[0:32, :], OP.add, [t1, t2], [c.qpe])
    def mla_attn_prompt(c, g, suk, wuk, suv, wuv):
        Tn = c.T
        nkb = g + 1
        nkt = 4 * (g + 1)
        rot_n[0] = 3
        Vp4 = Vp.t[:, 0:2048].rearrange("p (k m) -> p k m", m=128)

        def kt_recompute(h):
            hh = h % 2
            for kb in range(nkb):
                ps = ps_next()
                for cc in range(2):
                    MM(ps, ps[0:64, 0:512], wuk[:, cc, h * 64:(h + 1) * 64], ckvnb.t[:, cc * 2048 + kb * 512: cc * 2048 + (kb + 1) * 512],
                       cc == 0, cc == 1, [suk, (ckvnb, kb)])
                CP("dve", KT[hh].t[0:64, kb * 512:(kb + 1) * 512], ps[0:64, 0:512], [ps], [(KT[hh], ("n", kb))])

        def v_recompute(hp):
            for k4 in range(nkt // 4):
                ps = ps_next()
                for kk in range(4):
                    kt = k4 * 4 + kk
                    for cc in range(2):
                        MM(ps, ps[:, kk * 128:(kk + 1) * 128], ckvnb.t[:, cc * 2048 + kt * 128: cc * 2048 + (kt + 1) * 128],
                           wuv[:, cc, hp * 128:(hp + 1) * 128], cc == 0, cc == 1, [suv, (ckvnb, kt // 4)])
                CP("dve", Vp4[:, k4 * 4:(k4 + 1) * 4, :], ps[:, :].rearrange("p (k m) -> p k m", m=128), [ps], [(Vp, k4)])

        kt_recompute(0)
        v_recompute(0)
        for h in range(16):
            hp, hh = h // 2, h % 2
            po = allb[3 + 2 * hh]
            pss = allb[4 + 2 * hh]

            def s_exp(kt):
                qlo = max(0, kt - 4 * g) * 128
                nq = Tn - qlo
                pS = ps_next()
                MM(pS, pS[:, 0:nq], KT[hh].t[0:96, kt * 128:(kt + 1) * 128], c.B1.t[0:96, h * Tn + qlo:(h + 1) * Tn], True, True,
                   [(KT[hh], ("n", kt // 4)), (KT[hh], "pe"), (c.B1, ("Q", h))])
                pt = pt_next()
                ACT(pt[:, 0:nq], pS[:, 0:nq], AF.Exp, [pS], [pt], scale=SM_SCALE)
                if kt >= 4 * g:
                    MEMSET("dve", pt[64:128, 0:64], 0.0, [pt])
                return pt, qlo, nq
            q_ = [s_exp(0)]
            if nkt > 1:
                q_.append(s_exp(1))
            for kt in range(nkt):
                pt, qlo, nq = q_.pop(0)
                if kt + 2 < nkt:
                    q_.append(s_exp(kt + 2))
                MM(po, po[:, qlo:Tn], Vp4[:, kt, :], pt[:, 0:nq], kt == 0, kt == nkt - 1, [(Vp, kt // 4), pt], skip=True)
                MM(pss, pss[:, qlo:Tn], ones_bf[:, :], pt[:, 0:nq], kt == 0, kt == nkt - 1, [ones_bf, pt], skip=True)
                if kt == 0 and h + 1 < 16:
                    kt_recompute(h + 1)
            r0 = hh * 64
            rl = c.tf[2 + hh]
            ACT(rl[r0:r0 + 64, :], pss[r0:r0 + 64, 0:Tn], AF.Ln, [pss], [rl])
            ACT(rl[r0:r0 + 64, :], rl[r0:r0 + 64, :], AF.Exp, [rl], [rl], scale=-1.0)
            TT("dve", c.B2.t[r0:r0 + 64, hp * Tn:(hp + 1) * Tn], po[r0:r0 + 64, 0:Tn], rl[r0:r0 + 64, :], OP.mult, [po, rl], [(c.B2, hp)])
            if hh == 1 and hp + 1 < 8:
                v_recompute(hp + 1)
        rot_n[0] = 5
    def mla_attn_sample(c, suk, wuk, suv, wuv):
        Tn = c.T
        skt, wkt = wload(w_ukT[:, :], 64, 1, 4096)
        ps = ps_next()
        for h in range(16):
            for cc in range(2):
                MM(ps, ps[:, cc * 256 + h * 16: cc * 256 + (h + 1) * 16], wkt[0:64, 0, h * 256 + cc * 128: h * 256 + (cc + 1) * 128],
                   c.qn.t[0:64, h * 16:(h + 1) * 16], True, True, [skt, c.qn])
        CP("act", c.qlat.t[:, :], ps[:, :], [ps], [c.qlat])
        olat, sums = acc[0], acc[1]
        first = True
        def s_part(lhs_c0, lhs_c1, lhs_pe, nk, rd):
            pS = ps_next()
            MM(pS, pS[0:nk, 0:256], lhs_c0, c.qlat.t[:, 0:256], True, False, rd + [c.qlat])
            MM(pS, pS[0:nk, 0:256], lhs_c1, c.qlat.t[:, 256:512], False, False, rd + [c.qlat])
            MM(pS, pS[0:nk, 0:256], lhs_pe, c.qpe.t[0:32, :], False, True, rd + [c.qpe])
            pt = pt_next()
            ACT(pt[0:nk, 0:256], pS[0:nk, 0:256], AF.Exp, [pS], [pt], scale=SM_SCALE)
            return pt

        def pv_part(pt, tok_c0, tok_c1, nk, rd, last):
            nonlocal first
            MM(olat, olat[:, 0:256], tok_c0, pt[0:nk, 0:256], first, last, rd + [pt], skip=True)
            MM(olat, olat[:, 256:512], tok_c1, pt[0:nk, 0:256], False, last, rd + [pt], skip=True)
            MM(sums, sums[:, 0:256], ones_bf[0:nk, :], pt[0:nk, 0:256], first, last, [ones_bf, pt])
            first = False

        ct = c.ct[0]

        def prep(blk):
            cb = c.cb[blk % 2]
            DMA("pool", cb.t[:, :].rearrange("p (k c) -> p k c", c=288)[:, :, 0:256],
                cckv[blk * 512:(blk + 1) * 512, :].rearrange("(k p) c -> p k c", p=128), [], [cb], "CB%d" % (blk % 2))
            DMA("pool", cb.t[:, :].rearrange("p (k c) -> p k c", c=288)[:, :, 256:288],
                ckpe[blk * 512:(blk + 1) * 512, :].rearrange("(k p) c -> p k c", p=128), [], [cb], "CB%d" % (blk % 2))
            for cc in range(2):
                for kt in range(4):
                    TR(psb, psb[:, (cc * 4 + kt) * 128:(cc * 4 + kt + 1) * 128], cb.t[:, kt * 288 + cc * 128: kt * 288 + (cc + 1) * 128],
                       identb[:, :], [cb, identb])
            CP("act", ct.t[:, 0:1024], psb[:, 0:1024], [psb], [ct])
            for kt in range(4):
                TR(psb, psb[0:32, kt * 128:(kt + 1) * 128], cb.t[:, kt * 288 + 256: kt * 288 + 288], identb[:, :], [cb, identb])
            CP("dve", ct.t[0:32, 1024:1536], psb[0:32, 0:512], [psb], [ct])

        def s_blk(blk, kt):
            return s_part(ct.t[:, kt * 128:(kt + 1) * 128], ct.t[:, 512 + kt * 128:512 + (kt + 1) * 128],
                          ct.t[0:32, 1024 + kt * 128:1024 + (kt + 1) * 128], 128, [ct])

        prep(0)
        for blk in range(8):
            cb = c.cb[blk % 2]
            q_ = [s_blk(blk, 0), s_blk(blk, 1)]
            for kt in range(4):
                pt = q_.pop(0)
                if kt + 2 < 4:
                    q_.append(s_blk(blk, kt + 2))
                if kt == 1 and blk + 1 < 8:
                    prep(blk + 1)
                pv_part(pt, cb.t[:, kt * 288: kt * 288 + 128], cb.t[:, kt * 288 + 128: kt * 288 + 256], 128, [cb], False)
        CP("dve", c.ckvtokb.t[0:16, :], c.ckvtok.t[0:16, :], [c.ckvtok], [c.ckvtokb])
        pt = s_part(c.ckvb.t[:, 0:16], c.ckvb.t[:, 16:32], c.kprb.t[0:32, 0:16], 16, [c.ckvb, c.kprb])
        pv_part(pt, c.ckvtokb.t[0:16, 0:128], c.ckvtokb.t[0:16, 128:256], 16, [c.ckvtokb], True)
        rs = c.rs
        RECIP(rs.t[:, 0:256], sums[:, 0:256], [sums], [rs])
        for cc in range(2):
            TT("dve", c.olatn.t[:, cc * 256:(cc + 1) * 256], olat[:, cc * 256:(cc + 1) * 256], rs.t[:, 0:256], OP.mult, [olat, rs], [c.olatn])
        ps = ps_next()
        for h in range(16):
            hp = h // 2
            for cc in range(2):
                MM(ps, ps[:, h * 16:(h + 1) * 16], wuv[:, cc, hp * 128:(hp + 1) * 128], c.olatn.t[:, cc * 256 + h * 16: cc * 256 + (h + 1) * 16],
                   cc == 0, cc == 1, [suv, c.olatn])
        for hh in range(2):
            CP("act", c.B2.t[hh * 64:(hh + 1) * 64, 0:128].rearrange("p (k t) -> p k t", t=16),
               ps[hh * 64:(hh + 1) * 64, 0:256].rearrange("p (k h t) -> p k h t", h=2, t=16)[:, :, hh, :], [ps], [c.B2])
    def mla_out(ctxs):
        for blk in range(4):
            so_, wo_ = wload(w_out_c[:, blk * 256:(blk + 1) * 256], 128, 8, 256)
            for c in ctxs:
                Tn = c.T
                for dl in range(2):
                    dc = blk * 2 + dl
                    ps = ps_next()
                    for hp in range(8):
                        MM(ps, ps[:, 0:Tn], wo_[:, hp, dl * 128:(dl + 1) * 128], c.B2.t[:, hp * Tn:(hp + 1) * Tn], hp == 0, hp == 7, [so_, c.B2])
                    CP("act", c.F2.t[:, dc * Tn:(dc + 1) * Tn], ps[:, 0:Tn], [ps], [c.F2])
        for c in ctxs:
            postnorm_add(c, "mix_post1")
    cp.ost = [cp.tf[2], cp.tf[3]]
    _ost = P.sbuf("ost", [16, 288], F32)
    cs.ost = [_ost, _ost]
    cs.kprb = P.sbuf("kprb", [32, 16], BF16)
    cs.ckvb = P.sbuf("ckvb", [128, 32], BF16)
    cs.ckvtok = P.sbuf("ckvtok", [16, 256], F32)
    cs.ckvtokb = P.sbuf("ckvtokb", [16, 256], BF16)
    cs.qn = P.sbuf("qn", [64, 256], BF16)
    cs.qpe = P.sbuf("qpe", [32, 256], BF16)
    cs.qlat = P.sbuf("qlat", [128, 512], BF16)
    cs.olatn = P.sbuf("olatn", [128, 512], BF16)
    cs.rs = P.sbuf("rs", [128, 256], F32)
    cs.cb = [P.sbuf("cb%d" % i, [128, 4 * 288], BF16) for i in range(2)]
    _ct = P.sbuf("ct0", [128, 1536], BF16)
    cs.ct = [_ct, _ct]
    for g in range(ngroups):
        ctxs = [cp] + ([cs] if g == 0 else [])
        load_x(cp, xp, g * T, prefetched=(g > 0))
        if g == 0:
            load_x(cs, xs, 0)
        if stage >= 1:
            mixer_ab(ctxs, g == NG - 1)
            if g == 0:
                out_states(cs, 1)
            if g == NG - 1:
                out_states(cp, 0)
        if stage >= 2:
            ffn(ctxs, 0)
        if stage >= 3:
            mla_proj(ctxs, g)
        if stage >= 4:
            sukv, wukv = wload(w_ukv[:, :], 128, 2, 2048)
            suk, wuk = sukv, wukv[:, :, 0:1024]
            suv, wuv = sukv, wukv[:, :, 1024:2048]
            mla_attn_prompt(cp, g, suk, wuk, suv, wuv)
            if g == 0 and stage >= 5:
                mla_attn_sample(cs, suk, wuk, suv, wuv)
        if stage >= 6:
            mla_out(ctxs)
        if stage >= 7:
            if g + 1 < ngroups:
                load_x_dma(cp, xp, (g + 1) * T)
            ffn(ctxs, 1)
        store_y(cp, y_p, g * T)
        if g == 0:
            store_y(cs, y_s, 0)
    P.emit()
    P.close()
    return nc
_NC = None
SUB = int(os.environ.get('SUB', '99'))
KPEOUT = int(os.environ.get('KPEOUT', '1'))
OUTV = int(os.environ.get('OUTV', '3'))
def _fm(v, n):
    return np.ascontiguousarray(np.asarray(v, np.float32).reshape(n, 128).T)
def prep_inputs(inp, cores=range(8)):
    f = lambda k: np.asarray(inp[k], np.float32)
    w_in_c = f("w_in_c")[0]
    sw = np.concatenate([np.arange(16, 32), np.arange(0, 16)])
    w_in_c_ext = np.ascontiguousarray(np.concatenate(
        [w_in_c, w_in_c[:, 576:640], w_in_c[:, 640:672][:, sw]], axis=1))
    w_uq = f("w_uq")[0]
    idx = np.arange(1536).reshape(16, 96).copy()
    idx[:, 64:96] = idx[:, 64:96][:, sw]
    w_uq_sw = np.ascontiguousarray(w_uq[:, idx.reshape(-1)])
    w_uk = f("w_uk")[0]
    w_ukT = np.ascontiguousarray(w_uk.transpose(2, 1, 0).reshape(64, 16 * 256))
    tri = np.triu(np.ones((128, 128), np.float32))
    tri4 = np.ascontiguousarray(np.tile(tri, (1, 4)))
    tri4s = np.zeros((128, 64), np.float32)
    tri4s[:16] = np.tile(np.triu(np.ones((16, 16), np.float32)), (1, 4))
    rmask = np.ones((128, 512), np.float32)
    rmask[:, ::128] = 0.0
    half = 16
    inv = (10000.0 ** (-np.arange(half, dtype=np.float32) / np.float32(half))).astype(np.float32)
    def rope_tab(pos):
        ang = pos.astype(np.float32)[None, :] * inv[:, None]
        cos = np.cos(ang).astype(np.float32)
        sin = np.sin(ang).astype(np.float32)
        c32 = np.concatenate([cos, cos], 0)
        s32 = np.concatenate([-sin, sin], 0)
        return np.ascontiguousarray(np.concatenate([c32, s32], 1))
    ropeP = rope_tab(np.arange(2048))
    ropeS = rope_tab(4096 + np.arange(TS))
    shared = {
        "w_in_ab": f("w_in_ab")[0], "lru_wa": f("lru_w_a")[0].reshape(1024, 128), "lru_wx": f("lru_w_x")[0].reshape(1024, 128),
        "w_gate": f("gla_w_gate")[0], "w_out_ab": f("w_out_ab")[0], "w_in_c": w_in_c_ext, "w_uq_cat": np.concatenate([np.concatenate([w_uq[:, b * 384:(b + 1) * 384], w_uq_sw[:, b * 384:(b + 1) * 384]], axis=1) for b in range(4)], axis=1),
        "w_ukv": np.concatenate([w_uk.reshape(256, 1024), f("w_uv")[0].reshape(256, 1024)], axis=1), "w_ukT": w_ukT, "w_out_c": f("w_out_c")[0],
        "ffn_wg": f("ffn_w_gate").reshape(2 * D, DFF), "ffn_wu": f("ffn_w_up").reshape(2 * D, DFF), "ffn_wd": f("ffn_w_down").reshape(2 * DFF, D),
        "ident": np.eye(128, dtype=np.float32), "tri4": tri4, "tri4s": tri4s, "rmask": rmask, "ropeP": ropeP, "ropeS": ropeS,
    }
    shared = {k: np.ascontiguousarray(v, dtype=np.float32) for k, v in shared.items()}
    vparts = {}
    for l in range(2):
        vparts["mix_pre%d" % l] = _fm(f("norm_mix_pre")[l], 8)
        vparts["mix_post%d" % l] = _fm(f("norm_mix_post")[l], 8)
        vparts["ffn_pre%d" % l] = _fm(f("norm_ffn_pre")[l], 8)
        vparts["ffn_post%d" % l] = _fm(f("norm_ffn_post")[l], 8)
    cw = f("conv_w_a")[0]
    vparts["conv_w"] = np.concatenate([_fm(cw[j], 8) for j in range(4)], 1)
    vparts["conv_b"] = _fm(f("conv_b_a")[0], 8)
    vparts["lru_ba"] = _fm(f("lru_b_a")[0], 8)
    vparts["lru_bx"] = _fm(f("lru_b_x")[0], 8)
    vparts["lam"] = _fm(f("lru_lambda")[0], 8)
    vparts["b_gate"] = _fm(f("gla_b_gate")[0], 4)
    vparts["gla_norm"] = _fm(f("gla_norm")[0], 2)
    vparts["q_norm"] = _fm(f("mla_q_norm")[0], 3)
    vparts["kv_norm"] = _fm(f("mla_kv_norm")[0], 2)
    in_maps = []
    for c in cores:
        vp = dict(vparts)
        vp["h0"] = _fm(f("state_lru_h")[0, c], 8)
        cs_ = f("state_conv_a")[0, c]
        vp["convs"] = np.ascontiguousarray(cs_.reshape(3, 8, 128).transpose(2, 1, 0).reshape(128, 24))
        vecs = np.ascontiguousarray(np.concatenate([vp[n] for n, _ in _VNAMES], 1), dtype=np.float32)
        m = dict(shared)
        m.update({
            "xp": np.ascontiguousarray(f("x_prompt")[c]), "xs": np.ascontiguousarray(f("x_sample")[c]),
            "gla_s": np.ascontiguousarray(f("state_gla_S")[0, c]), "cckv": np.ascontiguousarray(f("cache_mla_ckv")[0, c]),
            "ckpe": np.ascontiguousarray(f("cache_mla_kpe")[0, c]), "vecs": vecs,
        })
        in_maps.append(m)
    return in_maps
def kernel(**inp):
    global _NC
    if _NC is None:
        _NC = build()
    nc = _NC
    in_maps = prep_inputs(inp)
    res = run_bass_kernel_spmd(nc, in_maps, core_ids=list(range(8)))
    R = res.results
    def st(name):
        return np.stack([np.asarray(R[c][name], np.float32) for c in range(8)], 0)
    y_prompt = st("y_p")
    y_sample = st("y_s")
    outs = [y_prompt, y_sample]
    for pfx in ("p", "s"):
        outs.append(st(pfx + "_conv")[None])
        outs.append(st(pfx + "_h").reshape(8, 1024)[None])
        outs.append(st(pfx + "_S")[None])
        outs.append(st(pfx + "_ckv")[None])
        outs.append(st(pfx + "_kpe")[None])
    return tuple(outs)
```

```python
import os
import numpy as np
import concourse.bass as bass
import concourse.mybir as mybir
from concourse.bass_utils import run_bass_kernel_spmd
F32 = mybir.dt.float32
BF16 = mybir.dt.bfloat16
AF = mybir.ActivationFunctionType
OP = mybir.AluOpType
D = 1024
T = 512
NG = 4
TS = 16
DFF = 2816
NFF = 22
EPS = 1e-6
SM_SCALE = 96 ** -0.5
GELU_K = 1.5957691216057308
_VNAMES = []
for _l in range(2):
    _VNAMES += [("mix_pre%d" % _l, 8), ("mix_post%d" % _l, 8), ("ffn_pre%d" % _l, 8), ("ffn_post%d" % _l, 8)]
_VNAMES += [("conv_w", 32), ("conv_b", 8), ("lru_ba", 8), ("lru_bx", 8), ("lam", 8), ("b_gate", 4),
            ("gla_norm", 2), ("q_norm", 3), ("kv_norm", 2), ("h0", 8), ("convs", 24)]
VOFF = {}
_c = 0
for _n, _k in _VNAMES:
    VOFF[_n] = _c
    _c += _k
NV = _c
class Buf:
    def __init__(self, t, name):
        self.t = t
        self.name = name
        self.st = {}
        self.excl = False
    def __getitem__(self, idx):
        return self.t[idx]
class Prog:
    ENGS = ("pe", "act", "dve", "pool", "sp")
    def __init__(self, nc):
        self.nc = nc
        self.ops = {e: [] for e in self.ENGS}
        self.known = {e: {} for e in self.ENGS}
        self.sems = {}
        self.cnt = {}
        self._stack = []
        self.epoch = {e: 0 for e in self.ENGS}
        self.dma_rr = {}
        for e in self.ENGS:
            self.new_sem("E_%s_0" % e)
    def enter(self, cm):
        r = cm.__enter__()
        self._stack.append(cm)
        return r
    def close(self):
        while self._stack:
            self._stack.pop().__exit__(None, None, None)
    def new_sem(self, sid):
        if sid not in self.sems:
            self.sems[sid] = self.enter(self.nc.semaphore(sid))
            self.cnt[sid] = 0
        return sid
    def sbuf(self, name, shape, dt):
        return Buf(self.enter(self.nc.sbuf_tensor("sb_" + name, list(shape), dt)), name)
    def psum(self, name, shape, dt=F32):
        b = Buf(self.enter(self.nc.psum_tensor("ps_" + name, list(shape), dt)), name)
        b.excl = True
        return b
    @staticmethod
    def _norm(lst):
        out = []
        for x in lst or []:
            out.append((x, None) if isinstance(x, Buf) else x)
        return out
    def _deps(self, reads, writes):
        w = {}
        def add(s, v):
            if w.get(s, 0) < v:
                w[s] = v
        for b, k in reads:
            keys = [k, None] if k is not None else list(b.st.keys())
            for kk in keys:
                st = b.st.get(kk)
                if st and st[0] is not None:
                    add(*st[0])
        for b, k in writes:
            keys = [k, None] if k is not None else list(b.st.keys())
            for kk in keys:
                st = b.st.get(kk)
                if st:
                    if st[0] is not None:
                        add(*st[0])
                    for s, v in st[1].items():
                        add(s, v)
        return w
    def _record(self, reads, writes, tok):
        for b, k in writes:
            if k is None:
                b.st = {None: [tok, {}]}
            else:
                b.st[k] = [tok, {}]
        for b, k in reads:
            if k is None:
                b.st.setdefault(None, [None, {}])
                for st in b.st.values():
                    if st[1].get(tok[0], 0) < tok[1]:
                        st[1][tok[0]] = tok[1]
            else:
                st = b.st.setdefault(k, [None, {}])
                if st[1].get(tok[0], 0) < tok[1]:
                    st[1][tok[0]] = tok[1]
    def op(self, eng, fn, reads=None, writes=None, dma_sem=None):
        reads = self._norm(reads)
        writes = self._norm(writes)
        xr = [r for r in reads if r[0].excl]
        if xr:
            reads = [r for r in reads if not r[0].excl]
            writes = writes + [r for r in xr if r not in writes]
        w = self._deps(reads, writes)
        if self.cnt["E_%s_%d" % (eng, self.epoch[eng])] >= 30000:
            self.epoch[eng] += 1
            self.new_sem("E_%s_%d" % (eng, self.epoch[eng]))
        own = "E_%s_%d" % (eng, self.epoch[eng])
        kn = self.known[eng]
        waits = []
        for s, v in w.items():
            if eng == "pe" and s.startswith("E_pe_"):
                continue
            if kn.get(s, 0) >= v:
                continue
            kn[s] = v
            waits.append((s, v))
        if dma_sem is not None:
            npool = 8
            i = self.dma_rr.get(eng, 0)
            self.dma_rr[eng] = i + 1
            dma_sem = "D_%s_%d" % (eng, i % npool)
            self.new_sem(dma_sem)
            if self.cnt[dma_sem] > 0 and kn.get(dma_sem, 0) < self.cnt[dma_sem]:
                kn[dma_sem] = self.cnt[dma_sem]
                waits.append((dma_sem, self.cnt[dma_sem]))
            self.cnt[dma_sem] += 16
            tok = (dma_sem, self.cnt[dma_sem])
            self.ops[eng].append((waits, fn, (dma_sem, 16)))
        else:
            self.cnt[own] += 1
            tok = (own, self.cnt[own])
            self.ops[eng].append((waits, fn, (own, 1)))
        self._record(reads, writes, tok)
        return tok
    def emit(self):
        nc = self.nc
        with nc.Block() as block:
            def body(ename):
                def f(e):
                    for waits, fn, (s, inc) in self.ops[ename]:
                        for ws, wv in waits:
                            e.wait_ge(self.sems[ws], wv)
                        ins = fn(e)
                        ins.then_inc(self.sems[s], inc)
                    if ename == "sp":
                        for s, c in self.cnt.items():
                            if c > 0:
                                e.wait_ge(self.sems[s], c)
                return f
            block.tensor(body("pe"))
            block.scalar(body("act"))
            block.vector(body("dve"))
            block.gpsimd(body("pool"))
            block.sync(body("sp"))
class Ctx:
    pass
def build(stage=99, ngroups=NG):
    nc = bass.Bass("TRN2", target_bir_lowering=False)
    P = Prog(nc)
    def di(n, s):
        return nc.dram_tensor(n, list(s), F32, kind="ExternalInput").ap()
    def do(n, s):
        return nc.dram_tensor(n, list(s), F32, kind="ExternalOutput").ap()
    xp = di("xp", [2048, D])
    xs = di("xs", [TS, D])
    gla_s = di("gla_s", [4, 128, 256])
    cckv = di("cckv", [4096, 256])
    ckpe = di("ckpe", [4096, 32])
    vecs_d = di("vecs", [128, NV])
    w_in_ab = di("w_in_ab", [D, 5136])
    lru_wa = di("lru_wa", [1024, 128])
    lru_wx = di("lru_wx", [1024, 128])
    w_gate = di("w_gate", [16, 512])
    w_out_ab = di("w_out_ab", [2048, D])
    w_in_c = di("w_in_c", [D, 768])
    w_uq_cat = di("w_uq_cat", [384, 3072])
    w_ukv = di("w_ukv", [256, 2048])
    w_ukT = di("w_ukT", [64, 4096])
    w_out_c = di("w_out_c", [1024, D])
    ffn_wg = di("ffn_wg", [2 * D, DFF])
    ffn_wu = di("ffn_wu", [2 * D, DFF])
    ffn_wd = di("ffn_wd", [2 * DFF, D])
    ident_d = di("ident", [128, 128])
    tri4_d = di("tri4", [128, 512])
    tri4s_d = di("tri4s", [128, 64])
    rmask_d = di("rmask", [128, 512])
    ropeP_d = di("ropeP", [32, 2 * 2048])
    ropeS_d = di("ropeS", [32, 2 * TS])
    y_p = do("y_p", [2048, D])
    y_s = do("y_s", [TS, D])
    o_conv = [do("p_conv", [3, D]), do("s_conv", [3, D])]
    o_h = [do("p_h", [8, 128]), do("s_h", [8, 128])]
    o_S = [do("p_S", [4, 128, 256]), do("s_S", [4, 128, 256])]
    o_ckv = [do("p_ckv", [2048, 256]), do("s_ckv", [TS, 256])]
    o_kpe = [do("p_kpe", [2048, 32]), do("s_kpe", [TS, 32])]
    def MM(ps, out, lhsT, rhs, st, sp, reads, skip=False):
        P.op("pe", lambda e: e.matmul(out, lhsT=lhsT, rhs=rhs, start=st, stop=sp, skip_group_check=skip), reads=reads, writes=[ps])
    def TR(ps, out, in_, idn, reads):
        P.op("pe", lambda e: e.transpose(out, in_, idn), reads=reads, writes=[ps])
    def ACT(out, in_, func, reads, writes, bias=None, scale=None):
        kw = {}
        if bias is not None:
            kw["bias"] = bias
        if scale is not None:
            kw["scale"] = scale
        P.op("act", lambda e: e.activation(out=out, in_=in_, func=func, **kw), reads=reads, writes=writes)
    def TT(eng, out, a, b, op, reads, writes):
        P.op(eng, lambda e: e.tensor_tensor(out=out, in0=a, in1=b, op=op), reads=reads, writes=writes)
    def TS_(eng, out, a, s1, s2, op0, op1, reads, writes):
        if s2 is None:
            P.op(eng, lambda e: e.tensor_scalar(out=out, in0=a, scalar1=s1, scalar2=None, op0=op0), reads=reads, writes=writes)
        else:
            P.op(eng, lambda e: e.tensor_scalar(out=out, in0=a, scalar1=s1, scalar2=s2, op0=op0, op1=op1), reads=reads, writes=writes)
    def STT(out, a, s, b, op0, op1, reads, writes):
        P.op("dve", lambda e: e.scalar_tensor_tensor(out=out, in0=a, scalar=s, in1=b, op0=op0, op1=op1), reads=reads, writes=writes)
    def CP(eng, out, in_, reads, writes):
        if eng == "act":
            P.op("act", lambda e: e.activation(out=out, in_=in_, func=AF.Copy), reads=reads, writes=writes)
        else:
            P.op(eng, lambda e: e.tensor_copy(out=out, in_=in_), reads=reads, writes=writes)
    def RECIP(out, in_, reads, writes):
        P.op("dve", lambda e: e.reciprocal(out=out, in_=in_), reads=reads, writes=writes)
    def SCAN(out, d0, d1, init, reads, writes):
        P.op("dve", lambda e: e.tensor_tensor_scan(out=out, data0=d0, data1=d1, initial=init, op0=OP.mult, op1=OP.add), reads=reads, writes=writes)
    def MEMSET(eng, ap, val, writes):
        P.op(eng, lambda e: e.memset(ap, val), writes=writes)
    def DMA(eng, out, in_, reads, writes, sem):
        P.op(eng, lambda e: e.dma_start(out=out, in_=in_), reads=reads, writes=writes, dma_sem=sem)
    allb = [P.psum("pb%d" % i, [128, 512], F32) for i in range(7)]
    acc = [allb[5], allb[6]]
    psb = P.psum("psb", [128, 1024], BF16)
    rot = [0]
    rot_n = [5]

    def ps_next():
        b = allb[rot[0] % rot_n[0]]
        rot[0] += 1
        return b
    ident = P.sbuf("ident", [128, 128], F32)
    identb = P.sbuf("identb", [128, 128], BF16)
    ones_bf = P.sbuf("ones_bf", [128, 128], BF16)
    ones_f = P.sbuf("ones_f", [128, 64], F32)
    tri4 = P.sbuf("tri4", [128, 512], BF16)
    tri4s = P.sbuf("tri4s", [128, 64], BF16)
    rmask = P.sbuf("rmask", [128, 512], BF16)
    vecs = P.sbuf("vecs", [128, NV], F32)
    drv = P.sbuf("drv", [128, 48], F32)
    wgate = P.sbuf("wgate", [16, 512], BF16)
    wab = P.sbuf("wab", [128, 2 * 8 * 128], BF16)
    diag = P.sbuf("diag", [128, 3 * 4 * 128], BF16)
    DMA("sp", ident[:, :], ident_d[:, :], [], [ident], "C0")
    DMA("pool", tri4[:, :], tri4_d[:, :], [], [tri4], "C1")
    DMA("pool", tri4s[:, :], tri4s_d[:, :], [], [tri4s], "C1")
    DMA("pool", rmask[:, :], rmask_d[:, :], [], [rmask], "C1")
    DMA("sp", vecs[:, :], vecs_d[:, :], [], [vecs], "C0")
    DMA("pool", identb[:, :], ident_d[:, :], [], [identb], "C1")
    DMA("pool", wgate[:, :], w_gate[:, :], [], [wgate], "C1")
    DMA("pool", wab[:, 0:1024].rearrange("p (n d) -> p n d", d=128), lru_wa.rearrange("(n c) d -> c n d", c=128), [], [wab], "C1")
    DMA("pool", wab[:, 1024:2048].rearrange("p (n d) -> p n d", d=128), lru_wx.rearrange("(n c) d -> c n d", c=128), [], [wab], "C1")
    MEMSET("dve", ones_bf[:, :], 1.0, [ones_bf])
    MEMSET("dve", ones_f[:, :], 1.0, [ones_f])
    def vcol(name, i=0):
        o = VOFF[name] + i
        return vecs[:, o:o + 1]
    TS_("dve", drv[:, 0:4], vecs[:, VOFF["b_gate"]:VOFF["b_gate"] + 4], -1.0, None, OP.mult, None, [vecs], [drv])
    TS_("dve", drv[:, 4:12], vecs[:, VOFF["lru_ba"]:VOFF["lru_ba"] + 8], -1.0, None, OP.mult, None, [vecs], [drv])
    TS_("dve", drv[:, 12:20], vecs[:, VOFF["lru_bx"]:VOFF["lru_bx"] + 8], -1.0, None, OP.mult, None, [vecs], [drv])
    ACT(drv[:, 36:44], vecs[:, VOFF["lam"]:VOFF["lam"] + 8], AF.Exp, [vecs], [drv], scale=-1.0)
    ACT(drv[:, 36:44], drv[:, 36:44], AF.Ln, [drv], [drv], bias=1.0)
    TS_("dve", drv[:, 20:28], drv[:, 36:44], -8.0, None, OP.mult, None, [drv], [drv])
    TS_("dve", drv[:, 28:36], drv[:, 36:44], -16.0, None, OP.mult, None, [drv], [drv])
    NSLOT = 3
    SLOT = 4096
    slots = [P.sbuf("wslot%d" % i, [128, SLOT], BF16) for i in range(NSLOT)]
    sl_i = [0]
    def wload(src2d, kp, nk, ncols):
        i = sl_i[0] % NSLOT
        sl_i[0] += 1
        sb = slots[i]
        assert nk * ncols <= SLOT
        view = sb.t[0:kp, 0:nk * ncols].rearrange("p (k n) -> p k n", n=ncols)
        srcv = src2d.rearrange("(k p) n -> p k n", p=kp)
        kstep = max(1, 1024 // kp)
        k0 = 0
        while k0 < nk:
            k1 = min(nk, k0 + kstep)
            DMA("pool", view[:, k0:k1, :], srcv[:, k0:k1, :], [], [(sb, k0)], "W%d" % i)
            k0 = k1
        return sb, view
    def make_ctx(tag, Tn, is_s):
        c = Ctx()
        c.tag, c.T, c.s = tag, Tn, is_s
        c.tw = min(128, Tn)
        c.nt = Tn // c.tw
        c.C = c.tw
        c.pb = 0 if is_s else 64
        c.hT = P.sbuf("hT" + tag, [128, 8 * Tn], F32)
        c.uT = P.sbuf("uT" + tag, [128, 8 * Tn], BF16)
        c.F2 = P.sbuf("F2" + tag, [128, max(8 * Tn, 1024)], F32)
        c.cum = P.sbuf("cum" + tag, [128, 4 * Tn], F32)
        c.b1n = max(22 * Tn, 20 * Tn + 24, 8 * Tn + c.nt * 1536)
        c.B1 = P.sbuf("B1" + tag, [128, c.b1n], BF16)
        c.B2 = P.sbuf("B2" + tag, [128, 16 * Tn], BF16)
        c.tf = [P.sbuf("tf%d%s" % (i, tag), [128, Tn], F32) for i in range(6)]
        c.tb = [P.sbuf("tb%d%s" % (i, tag), [128, Tn], BF16) for i in range(2)]
        c.rstd = P.sbuf("rstd" + tag, [128, Tn], F32)
        c.S = P.sbuf("S" + tag, [128, 4 * 256], F32)
        c.Sbf = P.sbuf("Sbf" + tag, [128, 4 * 256], BF16)
        c.Se = P.sbuf("Se" + tag, [128, 256], F32)
        c.hst = P.sbuf("hst" + tag, [128, 8], F32)
        c.carry = P.sbuf("carry" + tag, [128, 8 * 3], BF16)
        c.convo = P.sbuf("convo" + tag, [128, 8 * 3], F32)
        c.ebl = P.sbuf("ebl" + tag, [128, 4 * c.nt], F32)
        c.rope = P.sbuf("rope" + tag, [96 if not is_s else 32, 2 * Tn], F32)
        c.atb = [P.sbuf("atb%d%s" % (i, tag), [128, 4 * c.tw], BF16) for i in range(2)]
        c.tbi = 0
        return c
    cp = make_ctx("p", T, False)
    cp.nb, cp.pend = allb[5], []
    cs = make_ctx("s", TS, True)
    cs.nb, cs.pend = allb[6], []
    ckvnb = P.sbuf("ckvnb", [128, 2 * 2048], BF16)
    KT = [P.sbuf("KT%d" % i, [96, 2048], BF16) for i in range(2)]
    Vp = P.sbuf("Vp", [128, 16 * 2 * 65], BF16)
    PTs = [P.sbuf("PT%d" % i, [128, 512], BF16) for i in range(3)]
    pt_i = [0]
    def pt_next():
        b = PTs[pt_i[0] % 3]
        pt_i[0] += 1
        return b
    MEMSET("pool", Vp[:, :], 1.0, [Vp])
    def v3(buf, off, nk, Tn, p0=0, p1=128):
        return buf.t[p0:p1, off:off + nk * Tn].rearrange("p (k t) -> p k t", t=Tn)
    MEMSET("dve", cp.S[:, :], 0.0, [cp.S])
    MEMSET("dve", cp.Sbf[:, :], 0.0, [cp.Sbf])
    MEMSET("dve", cp.hst[:, :], 0.0, [cp.hst])
    MEMSET("dve", cp.carry[:, :], 0.0, [cp.carry])
    DMA("sp", cs.S[:, :].rearrange("p (h v) -> p h v", v=256), gla_s.rearrange("h k v -> k h v"), [], [cs.S], "C0")
    CP("act", cs.Sbf[:, :], cs.S[:, :], [cs.S], [cs.Sbf])
    CP("dve", cs.hst[:, :], vecs[:, VOFF["h0"]:VOFF["h0"] + 8], [vecs], [cs.hst])
    CP("dve", cs.carry[:, :], vecs[:, VOFF["convs"]:VOFF["convs"] + 24], [vecs], [cs.carry])
    DMA("sp", cs.rope[0:32, :], ropeS_d[:, :], [], [cs.rope], "C0")
    def rms(c, src, off, nk, nfeat, keyed=False):
        Tn = c.T
        ps = ps_next()
        for k in range(nk):
            sq = c.tb[c.tbi % 2]
            c.tbi += 1
            ACT(sq[:, :], src.t[:, off + k * Tn: off + (k + 1) * Tn], AF.Square, [(src, k)] if keyed else [src], [sq])
            MM(ps, ps[:, 0:Tn], ones_bf[:, :], sq[:, :], k == 0, k == nk - 1, [ones_bf, sq])
        ACT(c.rstd[:, :], ps[:, 0:Tn], AF.Ln, [ps], [c.rstd], bias=EPS, scale=1.0 / nfeat)
        ACT(c.rstd[:, :], c.rstd[:, :], AF.Exp, [c.rstd], [c.rstd], scale=-0.5)
    def load_x_dma(c, src, row0):
        tw = c.tw
        xb32 = c.B2.t[:, :].bitcast(F32)
        for t in range(c.nt):
            DMA("sp", xb32[0:tw, t * 1024:(t + 1) * 1024], src[row0 + t * tw: row0 + (t + 1) * tw, :], [], [c.B2], "IO" + c.tag)

    def load_x(c, src, row0, prefetched=False):
        Tn, tw = c.T, c.tw
        if c.s:
            buf, bt = c.F2, c.F2.t
        else:
            buf, bt = c.B2, c.B2.t[:, :].bitcast(F32)
            if not prefetched:
                load_x_dma(c, src, row0)
        for t in range(c.nt):
            if c.s:
                DMA("sp", bt[0:tw, t * 1024:(t + 1) * 1024], src[row0 + t * tw: row0 + (t + 1) * tw, :], [], [buf], "IO" + c.tag)
            for kq in range(2):
                ps = ps_next()
                for kk in range(4):
                    kc = kq * 4 + kk
                    TR(ps, ps[:, kk * tw:(kk + 1) * tw], bt[0:tw, t * 1024 + kc * 128: t * 1024 + (kc + 1) * 128],
                       ident[0:tw, 0:tw], [buf, ident])
                CP("act" if kq == 0 else "dve", v3(c.hT, 0, 8, Tn)[:, kq * 4:(kq + 1) * 4, t * tw:(t + 1) * tw],
                   ps[:, 0:4 * tw].rearrange("p (k t) -> p k t", t=tw), [ps], [c.hT])
    def store_y(c, dst, row0):
        Tn, tw = c.T, c.tw
        fence(c.F2)
        for t in range(c.nt):
            for kq in range(2):
                ps = ps_next()
                for kk in range(4):
                    kc = kq * 4 + kk
                    TR(ps, ps[0:tw, kk * 128:(kk + 1) * 128], c.hT.t[:, kc * Tn + t * tw: kc * Tn + (t + 1) * tw],
                       ident[:, :], [c.hT, ident])
                CP("act" if kq == 0 else "dve", c.F2.t[0:tw, t * 1024 + kq * 512: t * 1024 + (kq + 1) * 512], ps[0:tw, :],
                   [ps], [(c.F2, ("io", t))])
            DMA("sp", dst[row0 + t * tw: row0 + (t + 1) * tw, :], c.F2.t[0:tw, t * 1024:(t + 1) * 1024],
                [(c.F2, ("io", t))], [], "IO" + c.tag)
    def prenorm(c, gname):
        Tn = c.T
        rms(c, c.hT, 0, 8, 1024, keyed=True)
        for k in range(8):
            STT(c.uT.t[:, k * Tn:(k + 1) * Tn], c.hT.t[:, k * Tn:(k + 1) * Tn], vcol(gname, k), c.rstd[:, :],
                OP.mult, OP.mult, [(c.hT, k), c.rstd, vecs], [(c.uT, k)])
    def evac_sq(c, ps, dc):
        Tn = c.T
        CP("act", c.F2.t[:, dc * Tn:(dc + 1) * Tn], ps[:, 0:Tn], [ps], [c.F2])
        sq = c.tb[c.tbi % 2]
        c.tbi += 1
        ACT(sq[:, :], ps[:, 0:Tn], AF.Square, [ps], [sq])
        c.pend.append((sq, dc == 0, dc == 7))
        flush_sq(c, 1)

    def flush_sq(c, keep):
        Tn = c.T
        while len(c.pend) > keep:
            sq, f_, l_ = c.pend.pop(0)
            MM(c.nb, c.nb[:, 0:Tn], ones_bf[:, :], sq[:, :], f_, l_, [ones_bf, sq])

    def postnorm_add(c, gname):
        Tn = c.T
        flush_sq(c, 0)
        ACT(c.rstd[:, :], c.nb[:, 0:Tn], AF.Ln, [c.nb], [c.rstd], bias=EPS, scale=1.0 / 1024)
        ACT(c.rstd[:, :], c.rstd[:, :], AF.Exp, [c.rstd], [c.rstd], scale=-0.5)
        for k in range(8):
            tmp = c.tf[k % 2]
            TT("dve", tmp[:, :], c.F2.t[:, k * Tn:(k + 1) * Tn], c.rstd[:, :], OP.mult, [c.F2, c.rstd], [tmp])
            STT(c.hT.t[:, k * Tn:(k + 1) * Tn], tmp[:, :], vcol(gname, k), c.hT.t[:, k * Tn:(k + 1) * Tn], OP.mult, OP.add,
                [tmp, (c.hT, k), vecs], [(c.hT, k)])
    def fence(buf):
        P.op("dve", lambda e: e.memset(drv[:, 47:48], 0.0), writes=[buf, (drv, "f")])
    def ffn(ctxs, layer):
        for c in ctxs:
            prenorm(c, "ffn_pre%d" % layer)
            fence(c.B1)
        col = 0
        while col < DFF:
            nc_ = min(512, DFF - col)
            sg, wg = wload(ffn_wg[layer * D:(layer + 1) * D, col:col + nc_], 128, 8, nc_)
            su, wu = wload(ffn_wu[layer * D:(layer + 1) * D, col:col + nc_], 128, 8, nc_)
            for c in ctxs:
                Tn = c.T
                nj = nc_ // 128
                for j in range(nj):
                    pg = ps_next()
                    for k in range(8):
                        MM(pg, pg[:, 0:Tn], wg[:, k, j * 128:(j + 1) * 128], c.uT.t[:, k * Tn:(k + 1) * Tn], k == 0, k == 7, [sg, (c.uT, k)])
                    tmp = c.tf[2 + j]
                    ACT(tmp[:, :], pg[:, 0:Tn], AF.Silu, [pg], [tmp])
                for j in range(nj):
                    fj = col // 128 + j
                    pu = ps_next()
                    for k in range(8):
                        MM(pu, pu[:, 0:Tn], wu[:, k, j * 128:(j + 1) * 128], c.uT.t[:, k * Tn:(k + 1) * Tn], k == 0, k == 7, [su, (c.uT, k)])
                    tmp = c.tf[2 + j]
                    TT("dve", c.B1.t[:, fj * Tn:(fj + 1) * Tn], tmp[:, :], pu[:, 0:Tn], OP.mult, [tmp, pu], [(c.B1, ("ff", fj))])
            col += nc_
        for dc in range(8):
            sd, wd = wload(ffn_wd[layer * DFF:(layer + 1) * DFF, dc * 128:(dc + 1) * 128], 128, NFF, 128)
            for c in ctxs:
                Tn = c.T
                ps = ps_next()
                for f in range(NFF):
                    MM(ps, ps[:, 0:Tn], wd[:, f, :], c.B1.t[:, f * Tn:(f + 1) * Tn], f == 0, f == NFF - 1, [sd, (c.B1, ("ff", f))])
                evac_sq(c, ps, dc)
        for c in ctxs:
            postnorm_add(c, "ffn_post%d" % layer)
    def mixer_ab(ctxs, last):
        for c in ctxs:
            prenorm(c, "mix_pre0")
            fence(c.B1)
            fence(c.F2)
        wz = P_wz
        DMA("pool", wz.t[:, :].rearrange("p (k n) -> p k n", n=16), w_in_ab[:, 5120:5136].rearrange("(k p) n -> p k n", p=128), [], [wz], "C1")
        for c in ctxs:
            Tn = c.T
            ps = ps_next()
            for k in range(8):
                MM(ps, ps[0:16, 0:Tn], wz.t[:, k * 16:(k + 1) * 16], c.uT.t[:, k * Tn:(k + 1) * Tn], k == 0, k == 7, [wz, (c.uT, k)])
            zr = c.tb[0]
            CP("act", zr[0:16, :], ps[0:16, 0:Tn], [ps], [zr])
            for h in range(4):
                pz = ps_next()
                MM(pz, pz[:, 0:Tn], wgate[0:16, h * 128:(h + 1) * 128], zr[0:16, :], True, True, [wgate, zr])
                e1 = c.tf[0]
                ACT(e1[:, :], pz[:, 0:Tn], AF.Exp, [pz, drv], [e1], bias=drv[:, h:h + 1], scale=-1.0)
                ACT(e1[:, :], e1[:, :], AF.Ln, [e1], [e1], bias=1.0)
                msk = rmask[:, 0:Tn] if not c.s else rmask[:, 1:1 + Tn]
                SCAN(c.cum.t[:, h * Tn:(h + 1) * Tn], msk, e1[:, :], 0.0, [rmask, e1], [(c.cum, h)])
                ACT(c.ebl.t[:, h * c.nt:(h + 1) * c.nt],
                    c.cum.t[:, h * Tn:(h + 1) * Tn].rearrange("p (n c) -> p n c", c=c.C)[:, :, c.C - 1],
                    AF.Exp, [(c.cum, h)], [(c.ebl, h)], scale=-1.0 / 16.0)
        for half in range(2):
            sx, wx = wload(w_in_ab[:, half * 512:(half + 1) * 512], 128, 8, 512)
            sgw, wgl = wload(w_in_ab[:, 1024 + half * 512:1024 + (half + 1) * 512], 128, 8, 512)
            sqk, wqk = wload(w_in_ab[:, 2048 + half * 512:2560 + half * 512], 128, 8, 512)

            def qk_head(c, h):
                Tn = c.T
                ps = ps_next()
                for k in range(8):
                    MM(ps, ps[:, 0:Tn], wqk[:, k, h * 128:(h + 1) * 128], c.uT.t[:, k * Tn:(k + 1) * Tn], k == 0, k == 7, [sqk, (c.uT, k)])
                e1 = c.rstd
                if half == 0:
                    ACT(e1[:, :], c.cum.t[:, h * Tn:(h + 1) * Tn], AF.Exp, [(c.cum, h)], [e1], scale=-1.0 / 16.0)
                    STT(c.B1.t[:, h * Tn:(h + 1) * Tn], ps[:, 0:Tn], 128.0 ** -0.5, e1[:, :], OP.mult, OP.mult, [ps, e1], [(c.B1, ("q", h))])
                else:
                    ACT(e1[:, :], c.cum.t[:, h * Tn:(h + 1) * Tn], AF.Exp, [(c.cum, h)], [e1], scale=1.0 / 16.0)
                    TT("dve", c.B1.t[:, 4 * Tn + h * Tn: 4 * Tn + (h + 1) * Tn], ps[:, 0:Tn], e1[:, :], OP.mult, [ps, e1], [(c.B1, ("k", h))])
            for c in ctxs:
                Tn = c.T
                XO = 8 * Tn
                GO = 16 * Tn + 24
                for j in range(4):
                    kc = half * 4 + j
                    xo = XO + kc * (Tn + 3)
                    ps = ps_next()
                    for k in range(8):
                        MM(ps, ps[:, 0:Tn], wx[:, k, j * 128:(j + 1) * 128], c.uT.t[:, k * Tn:(k + 1) * Tn], k == 0, k == 7, [sx, (c.uT, k)])
                    CP("act", c.B1.t[:, xo + 3: xo + 3 + Tn], ps[:, 0:Tn], [ps], [(c.B1, ("xa", kc))])
                    CP("act", c.B1.t[:, xo: xo + 3], c.carry.t[:, kc * 3:(kc + 1) * 3], [(c.carry, kc)], [(c.B1, ("xa", kc))])
                    if last or c.s:
                        CP("dve", c.convo.t[:, kc * 3:(kc + 1) * 3], ps[:, Tn - 3:Tn], [ps], [(c.convo, kc)])
                def tv(i):
                    if i < 6:
                        return c.tf[i].t[:, 0:Tn], c.tf[i]
                    return c.F2.t[:, (i - 6) * Tn:(i - 5) * Tn], (c.F2, ("t", i - 6))
                gt = [tv(11), tv(12), tv(13), tv(5)]
                gps = []
                for j in range(4):
                    ps = ps_next()
                    gps.append(ps)
                    for k in range(8):
                        MM(ps, ps[:, 0:Tn], wgl[:, k, j * 128:(j + 1) * 128], c.uT.t[:, k * Tn:(k + 1) * Tn], k == 0, k == 7, [sgw, (c.uT, k)])
                    ACT(c.B1.t[:, GO + j * Tn: GO + (j + 1) * Tn], ps[:, 0:Tn], AF.Gelu_apprx_tanh, [ps], [(c.B1, ("gl", j))])
                st_ = {}
                XS = [0, 6, 5]

                def build_diag(j):
                    kc = half * 4 + j
                    dg = (kc % 3) * 512
                    for tap in range(4):
                        TS_("dve", diag[:, dg + tap * 128: dg + (tap + 1) * 128], identb[:, :], vcol("conv_w", tap * 8 + kc), None, OP.mult, None,
                            [identb, vecs], [(diag, (kc % 3, tap))])

                def prep_conv(j):
                    kc = half * 4 + j
                    xo = XO + kc * (Tn + 3)
                    dg = (kc % 3) * 512
                    pc = ps_next()
                    for tap in range(4):
                        MM(pc, pc[:, 0:Tn], diag[:, dg + tap * 128: dg + (tap + 1) * 128], c.B1.t[:, xo + tap: xo + tap + Tn], tap == 0, tap == 3,
                           [(diag, (kc % 3, tap)), (c.B1, ("xa", kc))])
                    CP("act", c.carry.t[:, kc * 3:(kc + 1) * 3], c.B1.t[:, xo + Tn: xo + Tn + 3], [(c.B1, ("xa", kc))], [(c.carry, kc)])
                    base = 0 if kc % 2 == 0 else 6
                    tvs = [tv(XS[kc % 3])] + [tv(base + q) for q in range(1, 5)]
                    xb = c.tb[kc % 2]
                    ACT(tvs[0][0], pc[:, 0:Tn], AF.Identity, [pc, vecs], [tvs[0][1]], bias=vcol("conv_b", kc))
                    TS_("dve", xb[:, :], pc[:, 0:Tn], vcol("conv_b", kc), None, OP.add, None, [pc, vecs], [xb])
                    st_[j] = [tvs, xb, None, None]

                def gates(j):
                    kc = half * 4 + j
                    xb = st_[j][1]
                    pr = ps_next()
                    pi = ps_next()
                    MM(pr, pr[:, 0:Tn], wab[:, kc * 128:(kc + 1) * 128], xb[:, :], True, True, [wab, xb])
                    MM(pi, pi[:, 0:Tn], wab[:, 1024 + kc * 128:1024 + (kc + 1) * 128], xb[:, :], True, True, [wab, xb])
                    st_[j][2], st_[j][3] = pr, pi

                def chain_head(j):
                    kc = half * 4 + j
                    (X, Xd), (R, Rd), (I, Id), (A, Ad), (M, Md) = st_[j][0]
                    pr, pi = st_[j][2], st_[j][3]
                    ACT(R, pr[:, 0:Tn], AF.Exp, [pr, drv], [Rd], bias=drv[:, 4 + kc:5 + kc], scale=-1.0)
                    ACT(I, pi[:, 0:Tn], AF.Exp, [pi, drv], [Id], bias=drv[:, 12 + kc:13 + kc], scale=-1.0)

                def chain_tail(j):
                    kc = half * 4 + j
                    (X, Xd), (R, Rd), (I, Id), (A, Ad), (M, Md) = st_[j][0]
                    ACT(R, R, AF.Ln, [Rd], [Rd], bias=1.0)
                    ACT(R, R, AF.Exp, [Rd], [Rd], scale=-1.0)
                    ACT(A, R, AF.Exp, [Rd, drv], [Ad], scale=drv[:, 20 + kc:21 + kc])
                    TT("dve", M, A, A, OP.mult, [Ad], [Md])
                    ACT(I, I, AF.Ln, [Id], [Id], bias=1.0)
                    ACT(I, I, AF.Exp, [Id], [Id], scale=-1.0)
                    TT("dve", I, I, X, OP.mult, [Id, Xd], [Id])
                    ACT(M, M, AF.Ln, [Md], [Md], bias=1.0, scale=-1.0)
                    ACT(M, M, AF.Exp, [Md], [Md], scale=0.5)
                    TT("dve", I, I, M, OP.mult, [Id, Md], [Id])
                    SCAN(X, A, I, c.hst.t[:, kc:kc + 1], [Ad, Id, (c.hst, kc), Xd], [Xd])
                    CP("dve", c.hst.t[:, kc:kc + 1], X[:, Tn - 1:Tn], [Xd], [(c.hst, kc)])
                    TT("dve", c.B2.t[:, kc * Tn:(kc + 1) * Tn], X, c.B1.t[:, GO + j * Tn: GO + (j + 1) * Tn], OP.mult,
                       [Xd, (c.B1, ("gl", j))], [(c.B2, kc)])

                build_diag(0)
                build_diag(1)
                prep_conv(0)
                gates(0)
                for j in range(4):
                    if j + 2 < 4:
                        build_diag(j + 2)
                    if j + 1 < 4:
                        prep_conv(j + 1)
                    chain_head(j)
                    if j + 1 < 4:
                        gates(j + 1)
                    qk_head(c, j)
                    chain_tail(j)
        for c in ctxs:
            fence(c.F2)
        for c in ctxs:
            fence(c.B1)
        for c in ctxs:
            Tn, tw = c.T, c.tw
            KO = 4 * Tn
            KTO = 8 * Tn
            for t in range(c.nt):
                for h in range(4):
                    TR(psb, psb[0:tw, h * 128:(h + 1) * 128], c.B1.t[:, KO + h * Tn + t * tw: KO + h * Tn + (t + 1) * tw], identb[:, :],
                       [(c.B1, ("k", h)), identb])
                CP("act", c.B1.t[0:tw, KTO + t * 512: KTO + (t + 1) * 512], psb[0:tw, 0:512], [psb], [(c.B1, ("kt", t))])
        for vb in range(2):
            sv_, wv_ = wload(w_in_ab[:, 3072 + vb * 512:3072 + (vb + 1) * 512], 128, 8, 512)
            for c in ctxs:
                Tn, tw = c.T, c.tw
                VO = 8 * Tn + c.nt * 512
                for t in range(c.nt):
                    ps = ps_next()
                    for k in range(8):
                        MM(ps, ps[0:tw, :], c.uT.t[:, k * Tn + t * tw: k * Tn + (t + 1) * tw], wv_[:, k, :], k == 0, k == 7, [sv_, (c.uT, k)])
                    CP("act" if t % 2 == 0 else "dve", c.B1.t[0:tw, VO + t * 1024 + vb * 512: VO + t * 1024 + (vb + 1) * 512], ps[0:tw, :],
                       [ps], [(c.B1, ("v", t))])
        for c in ctxs:
            Tn, tw, C = c.T, c.tw, c.C
            KO, KTO = 4 * Tn, 8 * Tn
            VO = 8 * Tn + c.nt * 512
            trm = tri4 if not c.s else tri4s
            def a_mask(t):
                pa = ps_next()
                for h in range(4):
                    MM(pa, pa[0:C, h * C:(h + 1) * C], c.B1.t[:, KO + h * Tn + t * C: KO + h * Tn + (t + 1) * C],
                       c.B1.t[:, h * Tn + t * C: h * Tn + (t + 1) * C], True, True, [(c.B1, ("k", h)), (c.B1, ("q", h))])
                at = c.atb[t % 2]
                TT("dve", at[0:C, 0:4 * C], pa[0:C, 0:4 * C], trm[0:C, 0:4 * C], OP.mult, [pa, trm], [at])
                return at
            at = a_mask(0)
            for t in range(c.nt):
                for hp in range(2):
                    po = ps_next()
                    for hh in range(2):
                        h = hp * 2 + hh
                        for j in range(2):
                            c0 = (hh * 2 + j) * C
                            MM(po, po[:, c0:c0 + C], c.B1.t[0:C, VO + t * 1024 + h * 256 + j * 128: VO + t * 1024 + h * 256 + (j + 1) * 128],
                               at[0:C, h * C:(h + 1) * C], (hh == 0 and j == 0), False, [(c.B1, ("v", t)), at], skip=True)
                    for hh in range(2):
                        h = hp * 2 + hh
                        for j in range(2):
                            c0 = (hh * 2 + j) * C
                            MM(po, po[:, c0:c0 + C], c.Sbf.t[:, h * 256 + j * 128: h * 256 + (j + 1) * 128],
                               c.B1.t[:, h * Tn + t * C: h * Tn + (t + 1) * C], False, True, [(c.Sbf, h), (c.B1, ("q", h))], skip=True)
                    CP("act", v3(c.F2, 0, 8, Tn)[:, hp * 4:(hp + 1) * 4, t * C:(t + 1) * C],
                       po[:, 0:4 * C].rearrange("p (k t) -> p k t", t=C), [po], [(c.F2, ("o", hp))])
                pds = []
                for hp in range(2):
                    pd = ps_next()
                    pds.append(pd)
                    for hh in range(2):
                        h = hp * 2 + hh
                        MM(pd, pd[:, hh * 256:(hh + 1) * 256], c.B1.t[0:C, KTO + t * 512 + h * 128: KTO + t * 512 + (h + 1) * 128],
                           c.B1.t[0:C, VO + t * 1024 + h * 256: VO + t * 1024 + (h + 1) * 256], True, True,
                           [(c.B1, ("kt", t)), (c.B1, ("v", t))])
                if t + 1 < c.nt:
                    at = a_mask(t + 1)
                for hp in range(2):
                    pd = pds[hp]
                    for hh in range(2):
                        h = hp * 2 + hh
                        eb = c.ebl.t[:, h * c.nt + t: h * c.nt + t + 1]
                        Sh = c.S.t[:, h * 256:(h + 1) * 256]
                        TT("dve", Sh, Sh, pd[:, hh * 256:(hh + 1) * 256], OP.add, [(c.S, h), pd], [(c.S, h)])
                        ACT(c.Sbf.t[:, h * 256:(h + 1) * 256], Sh, AF.Copy, [(c.S, h), (c.ebl, h)], [(c.Sbf, h)], scale=eb)
                        TS_("dve", Sh, Sh, eb, None, OP.mult, None, [(c.S, h), (c.ebl, h)], [(c.S, h)])
            pss_ = []
            for h in range(4):
                ps = ps_next()
                pss_.append(ps)
                for j in range(2):
                    sq = c.tb[c.tbi % 2]
                    c.tbi += 1
                    ACT(sq[:, :], c.F2.t[:, (2 * h + j) * Tn:(2 * h + j + 1) * Tn], AF.Square, [c.F2], [sq])
                    MM(ps, ps[:, 0:Tn], ones_bf[:, :], sq[:, :], j == 0, j == 1, [ones_bf, sq])
            for h in range(4):
                ACT(c.tf[h][:, :], pss_[h][:, 0:Tn], AF.Ln, [pss_[h]], [c.tf[h]], bias=EPS, scale=1.0 / 256)
            for h in range(4):
                ACT(c.tf[h][:, :], c.tf[h][:, :], AF.Exp, [c.tf[h]], [c.tf[h]], scale=-0.5)
            for h in range(4):
                for j in range(2):
                    hj = 2 * h + j
                    STT(c.F2.t[:, hj * Tn:(hj + 1) * Tn], c.F2.t[:, hj * Tn:(hj + 1) * Tn], vcol("gla_norm", j), c.tf[h][:, :],
                        OP.mult, OP.mult, [c.F2, c.tf[h], vecs], [(c.F2, ("o", h // 2))])
        for gbk in range(2):
            sb_, wb_ = wload(w_in_ab[:, 4096 + gbk * 512:4096 + (gbk + 1) * 512], 128, 8, 512)
            for c in ctxs:
                Tn = c.T
                for j in range(4):
                    hj = gbk * 4 + j
                    ps = ps_next()
                    for k in range(8):
                        MM(ps, ps[:, 0:Tn], wb_[:, k, j * 128:(j + 1) * 128], c.uT.t[:, k * Tn:(k + 1) * Tn], k == 0, k == 7, [sb_, (c.uT, k)])
                    tmp = c.tf[4 + j % 2]
                    ACT(tmp[:, :], ps[:, 0:Tn], AF.Silu, [ps], [tmp])
                    TT("dve", c.B2.t[:, (8 + hj) * Tn:(9 + hj) * Tn], tmp[:, :], c.F2.t[:, hj * Tn:(hj + 1) * Tn], OP.mult,
                       [tmp, c.F2], [(c.B2, 8 + hj)])
        for blk in range(4):
            so_, wo_ = wload(w_out_ab[:, blk * 256:(blk + 1) * 256], 128, 16, 256)
            for c in ctxs:
                Tn = c.T
                for dl in range(2):
                    dc = blk * 2 + dl
                    ps = ps_next()
                    for k in range(16):
                        MM(ps, ps[:, 0:Tn], wo_[:, k, dl * 128:(dl + 1) * 128], c.B2.t[:, k * Tn:(k + 1) * Tn], k == 0, k == 15, [so_, (c.B2, k)])
                    evac_sq(c, ps, dc)
        for c in ctxs:
            postnorm_add(c, "mix_post0")
    P_wz = P.sbuf("wz", [128, 8 * 16], BF16)
    def out_states(c, idx):
        for half in range(2):
            ps = ps_next()
            for kk in range(4):
                kc = half * 4 + kk
                TR(ps, ps[0:3, kk * 128:(kk + 1) * 128], c.convo.t[:, kc * 3:(kc + 1) * 3], ident[:, :], [c.convo, ident])
            CP("act", c.F2.t[0:3, half * 512:(half + 1) * 512], ps[0:3, 0:512], [ps], [c.F2])
        DMA("sp", o_conv[idx][:, :], c.F2.t[0:3, 0:1024], [c.F2], [], "O" + c.tag)
        ps = ps_next()
        TR(ps, ps[0:8, 0:128], c.hst.t[:, 0:8], ident[:, :], [c.hst, ident])
        hh = c.Se
        CP("act", hh[0:8, 0:128], ps[0:8, 0:128], [ps], [hh])
        DMA("sp", o_h[idx][:, :], hh[0:8, 0:128], [hh], [], "O" + c.tag)
        DMA("sp", o_S[idx].rearrange("h k v -> k h v"), c.S.t[:, :].rearrange("p (h v) -> p h v", v=256), [c.S], [], "O" + c.tag)
    def mla_proj(ctxs, g):
        for c in ctxs:
            prenorm(c, "mix_pre1")
            fence(c.B1)
        s1, w1 = wload(w_in_c[:, 0:384], 128, 8, 384)
        s2, w2 = wload(w_in_c[:, 384:768], 128, 8, 384)
        for c in ctxs:
            Tn, tw, pb = c.T, c.tw, c.pb
            for k3 in range(3):
                ps = ps_next()
                for k in range(8):
                    MM(ps, ps[:, 0:Tn], w1[:, k, k3 * 128:(k3 + 1) * 128], c.uT.t[:, k * Tn:(k + 1) * Tn], k == 0, k == 7, [s1, (c.uT, k)])
                CP("act", c.F2.t[:, k3 * Tn:(k3 + 1) * Tn], ps[:, 0:Tn], [ps], [(c.F2, ("cq", k3))])
            for k2 in range(2):
                ps = ps_next()
                for k in range(8):
                    MM(ps, ps[:, 0:Tn], w2[:, k, k2 * 128:(k2 + 1) * 128], c.uT.t[:, k * Tn:(k + 1) * Tn], k == 0, k == 7, [s2, (c.uT, k)])
                CP("act", c.F2.t[:, (3 + k2) * Tn:(4 + k2) * Tn], ps[:, 0:Tn], [ps], [(c.F2, ("ckv", k2))])
            if SUB < 1:
                continue
            pA = ps_next()
            pB = ps_next()
            if not c.s:
                cA, cB, M_ = (192, 288), (288, 384), 96
            else:
                cA, cB, M_ = (256, 288), (352, 384), 32
            for k in range(8):
                MM(pA, pA[0:M_, 0:Tn], w2[:, k, cA[0]:cA[1]], c.uT.t[:, k * Tn:(k + 1) * Tn], k == 0, k == 7, [s2, (c.uT, k)])
            for k in range(8):
                MM(pB, pB[0:M_, 0:Tn], w2[:, k, cB[0]:cB[1]], c.uT.t[:, k * Tn:(k + 1) * Tn], k == 0, k == 7, [s2, (c.uT, k)])
            if not c.s:
                DMA("sp", c.rope.t[64:96, :].rearrange("p (a t) -> p a t", t=Tn),
                    ropeP_d.rearrange("p (a t) -> p a t", t=2048)[:, :, g * T:(g + 1) * T], [], [c.rope], "R" + c.tag)
            t1, t2 = c.tf[0], c.tf[1]
            kpr = c.F2.t[pb:pb + 32, 5 * Tn:6 * Tn]
            TT("dve", t1[pb:pb + 32, :], pA[pb:pb + 32, 0:Tn], c.rope.t[pb:pb + 32, 0:Tn], OP.mult, [pA, c.rope], [t1])
            TT("dve", t2[pb:pb + 32, :], pB[pb:pb + 32, 0:Tn], c.rope.t[pb:pb + 32, Tn:2 * Tn], OP.mult, [pB, c.rope], [t2])
            TT("dve", kpr, t1[pb:pb + 32, :], t2[pb:pb + 32, :], OP.add, [t1, t2], [(c.F2, "kpr")])
            if not c.s:
                for i in range(2):
                    CP("act", KT[i].t[64:96, g * T:(g + 1) * T], kpr, [(c.F2, "kpr")], [(KT[i], "pe")])
            else:
                CP("act", c.kprb.t[0:32, 0:Tn], kpr, [(c.F2, "kpr")], [c.kprb])
            if SUB < 2:
                continue
            rms(c, c.F2, 0, 3, 384)
            for k3 in range(3):
                STT(c.uT.t[:, k3 * Tn:(k3 + 1) * Tn], c.F2.t[:, k3 * Tn:(k3 + 1) * Tn], vcol("q_norm", k3), c.rstd[:, :], OP.mult, OP.mult,
                    [(c.F2, ("cq", k3)), c.rstd, vecs], [c.uT])
            rms(c, c.F2, 3 * Tn, 2, 256)
            for k2 in range(2):
                STT(c.F2.t[:, (3 + k2) * Tn:(4 + k2) * Tn], c.F2.t[:, (3 + k2) * Tn:(4 + k2) * Tn], vcol("kv_norm", k2), c.rstd[:, :],
                    OP.mult, OP.mult, [(c.F2, ("ckv", k2)), c.rstd, vecs], [(c.F2, ("ckv", k2))])
                if not c.s:
                    CP("act", ckvnb.t[:, k2 * 2048 + g * T: k2 * 2048 + (g + 1) * T], c.F2.t[:, (3 + k2) * Tn:(4 + k2) * Tn],
                       [(c.F2, ("ckv", k2))], [(ckvnb, g)])
                else:
                    CP("act", c.ckvb.t[:, k2 * Tn:(k2 + 1) * Tn], c.F2.t[:, (3 + k2) * Tn:(4 + k2) * Tn], [(c.F2, ("ckv", k2))], [c.ckvb])
            if SUB < 3:
                continue
            idx = 1 if c.s else 0
            for t in range(c.nt):
                ps = ps_next()
                for k2 in range(2):
                    TR(ps, ps[0:tw, k2 * 128:(k2 + 1) * 128], c.F2.t[:, (3 + k2) * Tn + t * tw:(3 + k2) * Tn + (t + 1) * tw], ident[:, :],
                       [(c.F2, ("ckv", k2)), ident])
                if KPEOUT:
                    MM(ps, ps[0:tw, 256:288], c.F2.t[pb:pb + 32, 5 * Tn + t * tw: 5 * Tn + (t + 1) * tw], ident[pb:pb + 32, pb:pb + 32],
                       True, True, [(c.F2, "kpr"), ident])
                st = c.ost[t % 2]
                CP("act", st[0:tw, 0:288], ps[0:tw, 0:288], [ps], [st])
                r0 = (g * T if not c.s else 0) + t * tw
                if OUTV >= 2:
                    DMA("sp", o_ckv[idx][r0:r0 + tw, :], st[0:tw, 0:256], [st], [], "O" + c.tag)
                if OUTV >= 3:
                    DMA("sp", o_kpe[idx][r0:r0 + tw, :], st[0:tw, 256:288], [st], [], "O" + c.tag)
                if c.s:
                    CP("dve", c.ckvtok.t[0:tw, 0:256], ps[0:tw, 0:256], [ps], [c.ckvtok])
        if SUB < 4:
            return
        for c in ctxs:
            fence(c.B1)
        for hf in range(4):
            sa, wcat = wload(w_uq_cat[:, hf * 768:(hf + 1) * 768], 128, 3, 768)
            sb2 = sa
            wa_ = wcat[:, :, 0:384]
            wb2 = wcat[:, :, 384:768]
            for c in ctxs:
                Tn = c.T
                for hl in range(4):
                    h = hf * 4 + hl
                    if not c.s:
                        pA = ps_next()
                        pB = ps_next()
                        for k in range(3):
                            MM(pA, pA[0:96, 0:Tn], wa_[:, k, hl * 96:(hl + 1) * 96], c.uT.t[:, k * Tn:(k + 1) * Tn], k == 0, k == 2, [sa, c.uT])
                        for k in range(3):
                            MM(pB, pB[0:96, 0:Tn], wb2[:, k, hl * 96:(hl + 1) * 96], c.uT.t[:, k * Tn:(k + 1) * Tn], k == 0, k == 2, [sb2, c.uT])
                        CP("act", c.B1.t[0:64, h * Tn:(h + 1) * Tn], pA[0:64, 0:Tn], [pA], [(c.B1, ("Q", h))])
                        t1, t2 = c.tf[0], c.tf[1]
                        TT("dve", t1[64:96, :], pA[64:96, 0:Tn], c.rope.t[64:96, 0:Tn], OP.mult, [pA, c.rope], [t1])
                        TT("dve", t2[64:96, :], pB[64:96, 0:Tn], c.rope.t[64:96, Tn:2 * Tn], OP.mult, [pB, c.rope], [t2])
                        TT("dve", c.B1.t[64:96, h * Tn:(h + 1) * Tn], t1[64:96, :], t2[64:96, :], OP.add, [t1, t2], [(c.B1, ("Q", h))])
                    else:
                        pq = ps_next()
                        for k in range(3):
                            MM(pq, pq[0:64, 0:16], wa_[:, k, hl * 96:hl * 96 + 64], c.uT.t[:, k * Tn:(k + 1) * Tn], k == 0, k == 2, [sa, c.uT])
                        for k in range(3):
                            MM(pq, pq[0:32, 16:32], wa_[:, k, hl * 96 + 64:hl * 96 + 96], c.uT.t[:, k * Tn:(k + 1) * Tn], k == 0, k == 2, [sa, c.uT])
                        for k in range(3):
                            MM(pq, pq[0:32, 32:48], wb2[:, k, hl * 96 + 64:hl * 96 + 96], c.uT.t[:, k * Tn:(k + 1) * Tn], k == 0, k == 2, [sb2, c.uT])
                        CP("act", c.qn.t[0:64, h * 16:(h + 1) * 16], pq[0:64, 0:16], [pq], [c.qn])
                        t1, t2 = c.tf[0], c.tf[1]
                        TT("dve", t1[0:32, :], pq[0:32, 16:32], c.rope.t[0:32, 0:Tn], OP.mult, [pq, c.rope], [t1])
                        TT("dve", t2[0:32, :], pq[0:32, 32:48], c.rope.t[0:32, Tn:2 * Tn], OP.mult, [pq, c.rope], [t2])
                        TT("dve", c.qpe.t[0:32, h * 16:(h + 1) * 16], t1[0:32, :], t2[0:32, :], OP.add, [t1, t2], [c.qpe])
    def mla_attn_prompt(c, g, suk, wuk, suv, wuv):
        Tn = c.T
        nkb = g + 1
        nkt = 4 * (g + 1)
        rot_n[0] = 3
        Vp4 = Vp.t[:, 0:2048].rearrange("p (k m) -> p k m", m=128)

        def kt_recompute(h):
            hh = h % 2
            for kb in range(nkb):
                ps = ps_next()
                for cc in range(2):
                    MM(ps, ps[0:64, 0:512], wuk[:, cc, h * 64:(h + 1) * 64], ckvnb.t[:, cc * 2048 + kb * 512: cc * 2048 + (kb + 1) * 512],
                       cc == 0, cc == 1, [suk, (ckvnb, kb)])
                CP("dve", KT[hh].t[0:64, kb * 512:(kb + 1) * 512], ps[0:64, 0:512], [ps], [(KT[hh], ("n", kb))])

        def v_recompute(hp):
            for k4 in range(nkt // 4):
                ps = ps_next()
                for kk in range(4):
                    kt = k4 * 4 + kk
                    for cc in range(2):
                        MM(ps, ps[:, kk * 128:(kk + 1) * 128], ckvnb.t[:, cc * 2048 + kt * 128: cc * 2048 + (kt + 1) * 128],
                           wuv[:, cc, hp * 128:(hp + 1) * 128], cc == 0, cc == 1, [suv, (ckvnb, kt // 4)])
                CP("dve", Vp4[:, k4 * 4:(k4 + 1) * 4, :], ps[:, :].rearrange("p (k m) -> p k m", m=128), [ps], [(Vp, k4)])

        kt_recompute(0)
        v_recompute(0)
        for h in range(16):
            hp, hh = h // 2, h % 2
            po = allb[3 + 2 * hh]
            pss = allb[4 + 2 * hh]

            def s_exp(kt):
                qlo = max(0, kt - 4 * g) * 128
                nq = Tn - qlo
                pS = ps_next()
                MM(pS, pS[:, 0:nq], KT[hh].t[0:96, kt * 128:(kt + 1) * 128], c.B1.t[0:96, h * Tn + qlo:(h + 1) * Tn], True, True,
                   [(KT[hh], ("n", kt // 4)), (KT[hh], "pe"), (c.B1, ("Q", h))])
                pt = pt_next()
                ACT(pt[:, 0:nq], pS[:, 0:nq], AF.Exp, [pS], [pt], scale=SM_SCALE)
                if kt >= 4 * g:
                    MEMSET("dve", pt[64:128, 0:64], 0.0, [pt])
                return pt, qlo, nq
            q_ = [s_exp(0)]
            if nkt > 1:
                q_.append(s_exp(1))
            for kt in range(nkt):
                pt, qlo, nq = q_.pop(0)
                if kt + 2 < nkt:
                    q_.append(s_exp(kt + 2))
                MM(po, po[:, qlo:Tn], Vp4[:, kt, :], pt[:, 0:nq], kt == 0, kt == nkt - 1, [(Vp, kt // 4), pt], skip=True)
                MM(pss, pss[:, qlo:Tn], ones_bf[:, :], pt[:, 0:nq], kt == 0, kt == nkt - 1, [ones_bf, pt], skip=True)
                if kt == 0 and h + 1 < 16:
                    kt_recompute(h + 1)
            r0 = hh * 64
            rl = c.tf[2 + hh]
            ACT(rl[r0:r0 + 64, :], pss[r0:r0 + 64, 0:Tn], AF.Ln, [pss], [rl])
            ACT(rl[r0:r0 + 64, :], rl[r0:r0 + 64, :], AF.Exp, [rl], [rl], scale=-1.0)
            TT("dve", c.B2.t[r0:r0 + 64, hp * Tn:(hp + 1) * Tn], po[r0:r0 + 64, 0:Tn], rl[r0:r0 + 64, :], OP.mult, [po, rl], [(c.B2, hp)])
            if hh == 1 and hp + 1 < 8:
                v_recompute(hp + 1)
        rot_n[0] = 5
    def mla_attn_sample(c, suk, wuk, suv, wuv):
        Tn = c.T
        skt, wkt = wload(w_ukT[:, :], 64, 1, 4096)
        ps = ps_next()
        for h in range(16):
            for cc in range(2):
                MM(ps, ps[:, cc * 256 + h * 16: cc * 256 + (h + 1) * 16], wkt[0:64, 0, h * 256 + cc * 128: h * 256 + (cc + 1) * 128],
                   c.qn.t[0:64, h * 16:(h + 1) * 16], True, True, [skt, c.qn])
        CP("act", c.qlat.t[:, :], ps[:, :], [ps], [c.qlat])
        olat, sums = acc[0], acc[1]
        first = True
        def s_part(lhs_c0, lhs_c1, lhs_pe, nk, rd):
            pS = ps_next()
            MM(pS, pS[0:nk, 0:256], lhs_c0, c.qlat.t[:, 0:256], True, False, rd + [c.qlat])
            MM(pS, pS[0:nk, 0:256], lhs_c1, c.qlat.t[:, 256:512], False, False, rd + [c.qlat])
            MM(pS, pS[0:nk, 0:256], lhs_pe, c.qpe.t[0:32, :], False, True, rd + [c.qpe])
            pt = pt_next()
            ACT(pt[0:nk, 0:256], pS[0:nk, 0:256], AF.Exp, [pS], [pt], scale=SM_SCALE)
            return pt

        def pv_part(pt, tok_c0, tok_c1, nk, rd, last):
            nonlocal first
            MM(olat, olat[:, 0:256], tok_c0, pt[0:nk, 0:256], first, last, rd + [pt], skip=True)
            MM(olat, olat[:, 256:512], tok_c1, pt[0:nk, 0:256], False, last, rd + [pt], skip=True)
            MM(sums, sums[:, 0:256], ones_bf[0:nk, :], pt[0:nk, 0:256], first, last, [ones_bf, pt])
            first = False

        ct = c.ct[0]

        def prep(blk):
            cb = c.cb[blk % 2]
            DMA("pool", cb.t[:, :].rearrange("p (k c) -> p k c", c=288)[:, :, 0:256],
                cckv[blk * 512:(blk + 1) * 512, :].rearrange("(k p) c -> p k c", p=128), [], [cb], "CB%d" % (blk % 2))
            DMA("pool", cb.t[:, :].rearrange("p (k c) -> p k c", c=288)[:, :, 256:288],
                ckpe[blk * 512:(blk + 1) * 512, :].rearrange("(k p) c -> p k c", p=128), [], [cb], "CB%d" % (blk % 2))
            for cc in range(2):
                for kt in range(4):
                    TR(psb, psb[:, (cc * 4 + kt) * 128:(cc * 4 + kt + 1) * 128], cb.t[:, kt * 288 + cc * 128: kt * 288 + (cc + 1) * 128],
                       identb[:, :], [cb, identb])
            CP("act", ct.t[:, 0:1024], psb[:, 0:1024], [psb], [ct])
            for kt in range(4):
                TR(psb, psb[0:32, kt * 128:(kt + 1) * 128], cb.t[:, kt * 288 + 256: kt * 288 + 288], identb[:, :], [cb, identb])
            CP("dve", ct.t[0:32, 1024:1536], psb[0:32, 0:512], [psb], [ct])

        def s_blk(blk, kt):
            return s_part(ct.t[:, kt * 128:(kt + 1) * 128], ct.t[:, 512 + kt * 128:512 + (kt + 1) * 128],
                          ct.t[0:32, 1024 + kt * 128:1024 + (kt + 1) * 128], 128, [ct])

        prep(0)
        for blk in range(8):
            cb = c.cb[blk % 2]
            q_ = [s_blk(blk, 0), s_blk(blk, 1)]
            for kt in range(4):
                pt = q_.pop(0)
                if kt + 2 < 4:
                    q_.append(s_blk(blk, kt + 2))
                if kt == 1 and blk + 1 < 8:
                    prep(blk + 1)
                pv_part(pt, cb.t[:, kt * 288: kt * 288 + 128], cb.t[:, kt * 288 + 128: kt * 288 + 256], 128, [cb], False)
        CP("dve", c.ckvtokb.t[0:16, :], c.ckvtok.t[0:16, :], [c.ckvtok], [c.ckvtokb])
        pt = s_part(c.ckvb.t[:, 0:16], c.ckvb.t[:, 16:32], c.kprb.t[0:32, 0:16], 16, [c.ckvb, c.kprb])
        pv_part(pt, c.ckvtokb.t[0:16, 0:128], c.ckvtokb.t[0:16, 128:256], 16, [c.ckvtokb], True)
        rs = c.rs
        RECIP(rs.t[:, 0:256], sums[:, 0:256], [sums], [rs])
        for cc in range(2):
            TT("dve", c.olatn.t[:, cc * 256:(cc + 1) * 256], olat[:, cc * 256:(cc + 1) * 256], rs.t[:, 0:256], OP.mult, [olat, rs], [c.olatn])
        ps = ps_next()
        for h in range(16):
            hp = h // 2
            for cc in range(2):
                MM(ps, ps[:, h * 16:(h + 1) * 16], wuv[:, cc, hp * 128:(hp + 1) * 128], c.olatn.t[:, cc * 256 + h * 16: cc * 256 + (h + 1) * 16],
                   cc == 0, cc == 1, [suv, c.olatn])
        for hh in range(2):
            CP("act", c.B2.t[hh * 64:(hh + 1) * 64, 0:128].rearrange("p (k t) -> p k t", t=16),
               ps[hh * 64:(hh + 1) * 64, 0:256].rearrange("p (k h t) -> p k h t", h=2, t=16)[:, :, hh, :], [ps], [c.B2])
    def mla_out(ctxs):
        for blk in range(4):
            so_, wo_ = wload(w_out_c[:, blk * 256:(blk + 1) * 256], 128, 8, 256)
            for c in ctxs:
                Tn = c.T
                for dl in range(2):
                    dc = blk * 2 + dl
                    ps = ps_next()
                    for hp in range(8):
                        MM(ps, ps[:, 0:Tn], wo_[:, hp, dl * 128:(dl + 1) * 128], c.B2.t[:, hp * Tn:(hp + 1) * Tn], hp == 0, hp == 7, [so_, c.B2])
                    evac_sq(c, ps, dc)
        for c in ctxs:
            postnorm_add(c, "mix_post1")
    cp.ost = [cp.tf[2], cp.tf[3]]
    _ost = P.sbuf("ost", [16, 288], F32)
    cs.ost = [_ost, _ost]
    cs.kprb = P.sbuf("kprb", [32, 16], BF16)
    cs.ckvb = P.sbuf("ckvb", [128, 32], BF16)
    cs.ckvtok = P.sbuf("ckvtok", [16, 256], F32)
    cs.ckvtokb = P.sbuf("ckvtokb", [16, 256], BF16)
    cs.qn = P.sbuf("qn", [64, 256], BF16)
    cs.qpe = P.sbuf("qpe", [32, 256], BF16)
    cs.qlat = P.sbuf("qlat", [128, 512], BF16)
    cs.olatn = P.sbuf("olatn", [128, 512], BF16)
    cs.rs = P.sbuf("rs", [128, 256], F32)
    cs.cb = [P.sbuf("cb%d" % i, [128, 4 * 288], BF16) for i in range(2)]
    _ct = P.sbuf("ct0", [128, 1536], BF16)
    cs.ct = [_ct, _ct]
    for g in range(ngroups):
        ctxs = [cp] + ([cs] if g == 0 else [])
        load_x(cp, xp, g * T, prefetched=(g > 0))
        if g == 0:
            load_x(cs, xs, 0)
        if stage >= 1:
            mixer_ab(ctxs, g == NG - 1)
            if g == 0:
                out_states(cs, 1)
            if g == NG - 1:
                out_states(cp, 0)
        if stage >= 2:
            ffn(ctxs, 0)
        if stage >= 3:
            mla_proj(ctxs, g)
        if stage >= 4:
            sukv, wukv = wload(w_ukv[:, :], 128, 2, 2048)
            suk, wuk = sukv, wukv[:, :, 0:1024]
            suv, wuv = sukv, wukv[:, :, 1024:2048]
            mla_attn_prompt(cp, g, suk, wuk, suv, wuv)
            if g == 0 and stage >= 5:
                mla_attn_sample(cs, suk, wuk, suv, wuv)
        if stage >= 6:
            mla_out(ctxs)
        if stage >= 7:
            if g + 1 < ngroups:
                load_x_dma(cp, xp, (g + 1) * T)
            ffn(ctxs, 1)
        store_y(cp, y_p, g * T)
        if g == 0:
            store_y(cs, y_s, 0)
    P.emit()
    P.close()
    return nc
_NC = None
SUB = int(os.environ.get('SUB', '99'))
KPEOUT = int(os.environ.get('KPEOUT', '1'))
OUTV = int(os.environ.get('OUTV', '3'))
def _fm(v, n):
    return np.ascontiguousarray(np.asarray(v, np.float32).reshape(n, 128).T)
def prep_inputs(inp, cores=range(8)):
    f = lambda k: np.asarray(inp[k], np.float32)
    w_in_c = f("w_in_c")[0]
    sw = np.concatenate([np.arange(16, 32), np.arange(0, 16)])
    w_in_c_ext = np.ascontiguousarray(np.concatenate(
        [w_in_c, w_in_c[:, 576:640], w_in_c[:, 640:672][:, sw]], axis=1))
    w_uq = f("w_uq")[0]
    idx = np.arange(1536).reshape(16, 96).copy()
    idx[:, 64:96] = idx[:, 64:96][:, sw]
    w_uq_sw = np.ascontiguousarray(w_uq[:, idx.reshape(-1)])
    w_uk = f("w_uk")[0]
    w_ukT = np.ascontiguousarray(w_uk.transpose(2, 1, 0).reshape(64, 16 * 256))
    tri = np.triu(np.ones((128, 128), np.float32))
    tri4 = np.ascontiguousarray(np.tile(tri, (1, 4)))
    tri4s = np.zeros((128, 64), np.float32)
    tri4s[:16] = np.tile(np.triu(np.ones((16, 16), np.float32)), (1, 4))
    rmask = np.ones((128, 512), np.float32)
    rmask[:, ::128] = 0.0
    half = 16
    inv = (10000.0 ** (-np.arange(half, dtype=np.float32) / np.float32(half))).astype(np.float32)
    def rope_tab(pos):
        ang = pos.astype(np.float32)[None, :] * inv[:, None]
        cos = np.cos(ang).astype(np.float32)
        sin = np.sin(ang).astype(np.float32)
        c32 = np.concatenate([cos, cos], 0)
        s32 = np.concatenate([-sin, sin], 0)
        return np.ascontiguousarray(np.concatenate([c32, s32], 1))
    ropeP = rope_tab(np.arange(2048))
    ropeS = rope_tab(4096 + np.arange(TS))
    shared = {
        "w_in_ab": f("w_in_ab")[0], "lru_wa": f("lru_w_a")[0].reshape(1024, 128), "lru_wx": f("lru_w_x")[0].reshape(1024, 128),
        "w_gate": f("gla_w_gate")[0], "w_out_ab": f("w_out_ab")[0], "w_in_c": w_in_c_ext, "w_uq_cat": np.concatenate([np.concatenate([w_uq[:, b * 384:(b + 1) * 384], w_uq_sw[:, b * 384:(b + 1) * 384]], axis=1) for b in range(4)], axis=1),
        "w_ukv": np.concatenate([w_uk.reshape(256, 1024), f("w_uv")[0].reshape(256, 1024)], axis=1), "w_ukT": w_ukT, "w_out_c": f("w_out_c")[0],
        "ffn_wg": f("ffn_w_gate").reshape(2 * D, DFF), "ffn_wu": f("ffn_w_up").reshape(2 * D, DFF), "ffn_wd": f("ffn_w_down").reshape(2 * DFF, D),
        "ident": np.eye(128, dtype=np.float32), "tri4": tri4, "tri4s": tri4s, "rmask": rmask, "ropeP": ropeP, "ropeS": ropeS,
    }
    shared = {k: np.ascontiguousarray(v, dtype=np.float32) for k, v in shared.items()}
    vparts = {}
    for l in range(2):
        vparts["mix_pre%d" % l] = _fm(f("norm_mix_pre")[l], 8)
        vparts["mix_post%d" % l] = _fm(f("norm_mix_post")[l], 8)
        vparts["ffn_pre%d" % l] = _fm(f("norm_ffn_pre")[l], 8)
        vparts["ffn_post%d" % l] = _fm(f("norm_ffn_post")[l], 8)
    cw = f("conv_w_a")[0]
    vparts["conv_w"] = np.concatenate([_fm(cw[j], 8) for j in range(4)], 1)
    vparts["conv_b"] = _fm(f("conv_b_a")[0], 8)
    vparts["lru_ba"] = _fm(f("lru_b_a")[0], 8)
    vparts["lru_bx"] = _fm(f("lru_b_x")[0], 8)
    vparts["lam"] = _fm(f("lru_lambda")[0], 8)
    vparts["b_gate"] = _fm(f("gla_b_gate")[0], 4)
    vparts["gla_norm"] = _fm(f("gla_norm")[0], 2)
    vparts["q_norm"] = _fm(f("mla_q_norm")[0], 3)
    vparts["kv_norm"] = _fm(f("mla_kv_norm")[0], 2)
    in_maps = []
    for c in cores:
        vp = dict(vparts)
        vp["h0"] = _fm(f("state_lru_h")[0, c], 8)
        cs_ = f("state_conv_a")[0, c]
        vp["convs"] = np.ascontiguousarray(cs_.reshape(3, 8, 128).transpose(2, 1, 0).reshape(128, 24))
        vecs = np.ascontiguousarray(np.concatenate([vp[n] for n, _ in _VNAMES], 1), dtype=np.float32)
        m = dict(shared)
        m.update({
            "xp": np.ascontiguousarray(f("x_prompt")[c]), "xs": np.ascontiguousarray(f("x_sample")[c]),
            "gla_s": np.ascontiguousarray(f("state_gla_S")[0, c]), "cckv": np.ascontiguousarray(f("cache_mla_ckv")[0, c]),
            "ckpe": np.ascontiguousarray(f("cache_mla_kpe")[0, c]), "vecs": vecs,
        })
        in_maps.append(m)
    return in_maps
def kernel(**inp):
    global _NC
    if _NC is None:
        _NC = build()
    nc = _NC
    in_maps = prep_inputs(inp)
    res = run_bass_kernel_spmd(nc, in_maps, core_ids=list(range(8)))
    R = res.results
    def st(name):
        return np.stack([np.asarray(R[c][name], np.float32) for c in range(8)], 0)
    y_prompt = st("y_p")
    y_sample = st("y_s")
    outs = [y_prompt, y_sample]
    for pfx in ("p", "s"):
        outs.append(st(pfx + "_conv")[None])
        outs.append(st(pfx + "_h").reshape(8, 1024)[None])
        outs.append(st(pfx + "_S")[None])
        outs.append(st(pfx + "_ckv")[None])
        outs.append(st(pfx + "_kpe")[None])
    return tuple(outs)
```

```python
import os
import numpy as np
import concourse.bass as bass
import concourse.mybir as mybir
from concourse.bass_utils import run_bass_kernel_spmd
F32 = mybir.dt.float32
BF16 = mybir.dt.bfloat16
AF = mybir.ActivationFunctionType
OP = mybir.AluOpType
D = 1024
T = 512
NG = 4
TS = 16
DFF = 2816
NFF = 22
EPS = 1e-6
SM_SCALE = 96 ** -0.5
GELU_K = 1.5957691216057308
_VNAMES = []
for _l in range(2):
    _VNAMES += [("mix_pre%d" % _l, 8), ("mix_post%d" % _l, 8), ("ffn_pre%d" % _l, 8), ("ffn_post%d" % _l, 8)]
_VNAMES += [("conv_w", 32), ("conv_b", 8), ("lru_ba", 8), ("lru_bx", 8), ("lam", 8), ("b_gate", 4),
            ("gla_norm", 2), ("q_norm", 3), ("kv_norm", 2), ("h0", 8), ("convs", 24)]
VOFF = {}
_c = 0
for _n, _k in _VNAMES:
    VOFF[_n] = _c
    _c += _k
NV = _c
class Buf:
    def __init__(self, t, name):
        self.t = t
        self.name = name
        self.st = {}
        self.excl = False
    def __getitem__(self, idx):
        return self.t[idx]
class Prog:
    ENGS = ("pe", "act", "dve", "pool", "sp")
    def __init__(self, nc):
        self.nc = nc
        self.ops = {e: [] for e in self.ENGS}
        self.known = {e: {} for e in self.ENGS}
        self.sems = {}
        self.cnt = {}
        self._stack = []
        self.epoch = {e: 0 for e in self.ENGS}
        self.dma_rr = {}
        for e in self.ENGS:
            self.new_sem("E_%s_0" % e)
    def enter(self, cm):
        r = cm.__enter__()
        self._stack.append(cm)
        return r
    def close(self):
        while self._stack:
            self._stack.pop().__exit__(None, None, None)
    def new_sem(self, sid):
        if sid not in self.sems:
            self.sems[sid] = self.enter(self.nc.semaphore(sid))
            self.cnt[sid] = 0
        return sid
    def sbuf(self, name, shape, dt):
        return Buf(self.enter(self.nc.sbuf_tensor("sb_" + name, list(shape), dt)), name)
    def psum(self, name, shape, dt=F32):
        b = Buf(self.enter(self.nc.psum_tensor("ps_" + name, list(shape), dt)), name)
        b.excl = True
        return b
    @staticmethod
    def _norm(lst):
        out = []
        for x in lst or []:
            out.append((x, None) if isinstance(x, Buf) else x)
        return out
    def _deps(self, reads, writes):
        w = {}
        def add(s, v):
            if w.get(s, 0) < v:
                w[s] = v
        for b, k in reads:
            keys = [k, None] if k is not None else list(b.st.keys())
            for kk in keys:
                st = b.st.get(kk)
                if st and st[0] is not None:
                    add(*st[0])
        for b, k in writes:
            keys = [k, None] if k is not None else list(b.st.keys())
            for kk in keys:
                st = b.st.get(kk)
                if st:
                    if st[0] is not None:
                        add(*st[0])
                    for s, v in st[1].items():
                        add(s, v)
        return w
    def _record(self, reads, writes, tok):
        for b, k in writes:
            if k is None:
                b.st = {None: [tok, {}]}
            else:
                b.st[k] = [tok, {}]
        for b, k in reads:
            if k is None:
                b.st.setdefault(None, [None, {}])
                for st in b.st.values():
                    if st[1].get(tok[0], 0) < tok[1]:
                        st[1][tok[0]] = tok[1]
            else:
                st = b.st.setdefault(k, [None, {}])
                if st[1].get(tok[0], 0) < tok[1]:
                    st[1][tok[0]] = tok[1]
    def op(self, eng, fn, reads=None, writes=None, dma_sem=None):
        reads = self._norm(reads)
        writes = self._norm(writes)
        xr = [r for r in reads if r[0].excl]
        if xr:
            reads = [r for r in reads if not r[0].excl]
            writes = writes + [r for r in xr if r not in writes]
        w = self._deps(reads, writes)
        if self.cnt["E_%s_%d" % (eng, self.epoch[eng])] >= 30000:
            self.epoch[eng] += 1
            self.new_sem("E_%s_%d" % (eng, self.epoch[eng]))
        own = "E_%s_%d" % (eng, self.epoch[eng])
        kn = self.known[eng]
        waits = []
        for s, v in w.items():
            if eng == "pe" and s.startswith("E_pe_"):
                continue
            if kn.get(s, 0) >= v:
                continue
            kn[s] = v
            waits.append((s, v))
        if dma_sem is not None:
            npool = 8
            i = self.dma_rr.get(eng, 0)
            self.dma_rr[eng] = i + 1
            dma_sem = "D_%s_%d" % (eng, i % npool)
            self.new_sem(dma_sem)
            if self.cnt[dma_sem] > 0 and kn.get(dma_sem, 0) < self.cnt[dma_sem]:
                kn[dma_sem] = self.cnt[dma_sem]
                waits.append((dma_sem, self.cnt[dma_sem]))
            self.cnt[dma_sem] += 16
            tok = (dma_sem, self.cnt[dma_sem])
            self.ops[eng].append((waits, fn, (dma_sem, 16)))
        else:
            self.cnt[own] += 1
            tok = (own, self.cnt[own])
            self.ops[eng].append((waits, fn, (own, 1)))
        self._record(reads, writes, tok)
        return tok
    def emit(self):
        nc = self.nc
        with nc.Block() as block:
            def body(ename):
                def f(e):
                    for waits, fn, (s, inc) in self.ops[ename]:
                        for ws, wv in waits:
                            e.wait_ge(self.sems[ws], wv)
                        ins = fn(e)
                        ins.then_inc(self.sems[s], inc)
                    if ename == "sp":
                        for s, c in self.cnt.items():
                            if c > 0:
                                e.wait_ge(self.sems[s], c)
                return f
            block.tensor(body("pe"))
            block.scalar(body("act"))
            block.vector(body("dve"))
            block.gpsimd(body("pool"))
            block.sync(body("sp"))
class Ctx:
    pass
def build(stage=99, ngroups=NG):
    nc = bass.Bass("TRN2", target_bir_lowering=False)
    P = Prog(nc)
    def di(n, s):
        return nc.dram_tensor(n, list(s), F32, kind="ExternalInput").ap()
    def do(n, s):
        return nc.dram_tensor(n, list(s), F32, kind="ExternalOutput").ap()
    xp = di("xp", [2048, D])
    xs = di("xs", [TS, D])
    gla_s = di("gla_s", [4, 128, 256])
    cckv = di("cckv", [4096, 256])
    ckpe = di("ckpe", [4096, 32])
    vecs_d = di("vecs", [128, NV])
    w_in_ab = di("w_in_ab", [D, 5136])
    lru_wa = di("lru_wa", [1024, 128])
    lru_wx = di("lru_wx", [1024, 128])
    w_gate = di("w_gate", [16, 512])
    w_out_ab = di("w_out_ab", [2048, D])
    w_in_c = di("w_in_c", [D, 768])
    w_uq_cat = di("w_uq_cat", [384, 3072])
    w_ukv = di("w_ukv", [256, 2048])
    w_ukT = di("w_ukT", [64, 4096])
    w_out_c = di("w_out_c", [1024, D])
    ffn_wg = di("ffn_wg", [2 * D, DFF])
    ffn_wu = di("ffn_wu", [2 * D, DFF])
    ffn_wd = di("ffn_wd", [2 * DFF, D])
    ident_d = di("ident", [128, 128])
    tri4_d = di("tri4", [128, 512])
    tri4s_d = di("tri4s", [128, 64])
    rmask_d = di("rmask", [128, 512])
    ropeP_d = di("ropeP", [32, 2 * 2048])
    ropeS_d = di("ropeS", [32, 2 * TS])
    y_p = do("y_p", [2048, D])
    y_s = do("y_s", [TS, D])
    o_conv = [do("p_conv", [3, D]), do("s_conv", [3, D])]
    o_h = [do("p_h", [8, 128]), do("s_h", [8, 128])]
    o_S = [do("p_S", [4, 128, 256]), do("s_S", [4, 128, 256])]
    o_ckv = [do("p_ckv", [2048, 256]), do("s_ckv", [TS, 256])]
    o_kpe = [do("p_kpe", [2048, 32]), do("s_kpe", [TS, 32])]
    def MM(ps, out, lhsT, rhs, st, sp, reads, skip=False):
        P.op("pe", lambda e: e.matmul(out, lhsT=lhsT, rhs=rhs, start=st, stop=sp, skip_group_check=skip), reads=reads, writes=[ps])
    def TR(ps, out, in_, idn, reads):
        P.op("pe", lambda e: e.transpose(out, in_, idn), reads=reads, writes=[ps])
    def ACT(out, in_, func, reads, writes, bias=None, scale=None):
        kw = {}
        if bias is not None:
            kw["bias"] = bias
        if scale is not None:
            kw["scale"] = scale
        P.op("act", lambda e: e.activation(out=out, in_=in_, func=func, **kw), reads=reads, writes=writes)
    def TT(eng, out, a, b, op, reads, writes):
        P.op(eng, lambda e: e.tensor_tensor(out=out, in0=a, in1=b, op=op), reads=reads, writes=writes)
    def TS_(eng, out, a, s1, s2, op0, op1, reads, writes):
        if s2 is None:
            P.op(eng, lambda e: e.tensor_scalar(out=out, in0=a, scalar1=s1, scalar2=None, op0=op0), reads=reads, writes=writes)
        else:
            P.op(eng, lambda e: e.tensor_scalar(out=out, in0=a, scalar1=s1, scalar2=s2, op0=op0, op1=op1), reads=reads, writes=writes)
    def STT(out, a, s, b, op0, op1, reads, writes):
        P.op("dve", lambda e: e.scalar_tensor_tensor(out=out, in0=a, scalar=s, in1=b, op0=op0, op1=op1), reads=reads, writes=writes)
    def CP(eng, out, in_, reads, writes):
        if eng == "act":
            P.op("act", lambda e: e.activation(out=out, in_=in_, func=AF.Copy), reads=reads, writes=writes)
        else:
            P.op(eng, lambda e: e.tensor_copy(out=out, in_=in_), reads=reads, writes=writes)
    def RECIP(out, in_, reads, writes):
        P.op("dve", lambda e: e.reciprocal(out=out, in_=in_), reads=reads, writes=writes)
    def SCAN(out, d0, d1, init, reads, writes):
        P.op("dve", lambda e: e.tensor_tensor_scan(out=out, data0=d0, data1=d1, initial=init, op0=OP.mult, op1=OP.add), reads=reads, writes=writes)
    def MEMSET(eng, ap, val, writes):
        P.op(eng, lambda e: e.memset(ap, val), writes=writes)
    def DMA(eng, out, in_, reads, writes, sem):
        P.op(eng, lambda e: e.dma_start(out=out, in_=in_), reads=reads, writes=writes, dma_sem=sem)
    allb = [P.psum("pb%d" % i, [128, 512], F32) for i in range(7)]
    acc = [allb[5], allb[6]]
    psb = P.psum("psb", [128, 1024], BF16)
    rot = [0]
    rot_n = [5]

    def ps_next():
        b = allb[rot[0] % rot_n[0]]
        rot[0] += 1
        return b
    ident = P.sbuf("ident", [128, 128], F32)
    identb = P.sbuf("identb", [128, 128], BF16)
    ones_bf = P.sbuf("ones_bf", [128, 128], BF16)
    ones_f = P.sbuf("ones_f", [128, 64], F32)
    tri4 = P.sbuf("tri4", [128, 512], BF16)
    tri4s = P.sbuf("tri4s", [128, 64], BF16)
    rmask = P.sbuf("rmask", [128, 512], BF16)
    vecs = P.sbuf("vecs", [128, NV], F32)
    drv = P.sbuf("drv", [128, 48], F32)
    wgate = P.sbuf("wgate", [16, 512], BF16)
    wab = P.sbuf("wab", [128, 2 * 8 * 128], BF16)
    diag = P.sbuf("diag", [128, 3 * 4 * 128], BF16)
    DMA("sp", ident[:, :], ident_d[:, :], [], [ident], "C0")
    DMA("pool", tri4[:, :], tri4_d[:, :], [], [tri4], "C1")
    DMA("pool", tri4s[:, :], tri4s_d[:, :], [], [tri4s], "C1")
    DMA("pool", rmask[:, :], rmask_d[:, :], [], [rmask], "C1")
    DMA("sp", vecs[:, :], vecs_d[:, :], [], [vecs], "C0")
    DMA("pool", identb[:, :], ident_d[:, :], [], [identb], "C1")
    DMA("pool", wgate[:, :], w_gate[:, :], [], [wgate], "C1")
    DMA("pool", wab[:, 0:1024].rearrange("p (n d) -> p n d", d=128), lru_wa.rearrange("(n c) d -> c n d", c=128), [], [wab], "C1")
    DMA("pool", wab[:, 1024:2048].rearrange("p (n d) -> p n d", d=128), lru_wx.rearrange("(n c) d -> c n d", c=128), [], [wab], "C1")
    MEMSET("dve", ones_bf[:, :], 1.0, [ones_bf])
    MEMSET("dve", ones_f[:, :], 1.0, [ones_f])
    def vcol(name, i=0):
        o = VOFF[name] + i
        return vecs[:, o:o + 1]
    TS_("dve", drv[:, 0:4], vecs[:, VOFF["b_gate"]:VOFF["b_gate"] + 4], -1.0, None, OP.mult, None, [vecs], [drv])
    TS_("dve", drv[:, 4:12], vecs[:, VOFF["lru_ba"]:VOFF["lru_ba"] + 8], -1.0, None, OP.mult, None, [vecs], [drv])
    TS_("dve", drv[:, 12:20], vecs[:, VOFF["lru_bx"]:VOFF["lru_bx"] + 8], -1.0, None, OP.mult, None, [vecs], [drv])
    ACT(drv[:, 36:44], vecs[:, VOFF["lam"]:VOFF["lam"] + 8], AF.Exp, [vecs], [drv], scale=-1.0)
    ACT(drv[:, 36:44], drv[:, 36:44], AF.Ln, [drv], [drv], bias=1.0)
    TS_("dve", drv[:, 20:28], drv[:, 36:44], -8.0, None, OP.mult, None, [drv], [drv])
    TS_("dve", drv[:, 28:36], drv[:, 36:44], -16.0, None, OP.mult, None, [drv], [drv])
    NSLOT = 3
    SLOT = 4096
    slots = [P.sbuf("wslot%d" % i, [128, SLOT], BF16) for i in range(NSLOT)]
    sl_i = [0]
    def wload(src2d, kp, nk, ncols):
        i = sl_i[0] % NSLOT
        sl_i[0] += 1
        sb = slots[i]
        assert nk * ncols <= SLOT
        view = sb.t[0:kp, 0:nk * ncols].rearrange("p (k n) -> p k n", n=ncols)
        srcv = src2d.rearrange("(k p) n -> p k n", p=kp)
        kstep = max(1, 1024 // kp)
        k0 = 0
        while k0 < nk:
            k1 = min(nk, k0 + kstep)
            DMA("pool", view[:, k0:k1, :], srcv[:, k0:k1, :], [], [(sb, k0)], "W%d" % i)
            k0 = k1
        return sb, view
    def make_ctx(tag, Tn, is_s):
        c = Ctx()
        c.tag, c.T, c.s = tag, Tn, is_s
        c.tw = min(128, Tn)
        c.nt = Tn // c.tw
        c.C = c.tw
        c.pb = 0 if is_s else 64
        c.hT = P.sbuf("hT" + tag, [128, 8 * Tn], F32)
        c.uT = P.sbuf("uT" + tag, [128, 8 * Tn], BF16)
        c.F2 = P.sbuf("F2" + tag, [128, max(8 * Tn, 1024)], F32)
        c.cum = P.sbuf("cum" + tag, [128, 4 * Tn], F32)
        c.b1n = max(22 * Tn, 20 * Tn + 24, 8 * Tn + c.nt * 1536)
        c.B1 = P.sbuf("B1" + tag, [128, c.b1n], BF16)
        c.B2 = P.sbuf("B2" + tag, [128, 16 * Tn], BF16)
        c.tf = [P.sbuf("tf%d%s" % (i, tag), [128, Tn], F32) for i in range(6)]
        c.tb = [P.sbuf("tb%d%s" % (i, tag), [128, Tn], BF16) for i in range(2)]
        c.rstd = P.sbuf("rstd" + tag, [128, Tn], F32)
        c.S = P.sbuf("S" + tag, [128, 4 * 256], F32)
        c.Sbf = P.sbuf("Sbf" + tag, [128, 4 * 256], BF16)
        c.Se = P.sbuf("Se" + tag, [128, 256], F32)
        c.hst = P.sbuf("hst" + tag, [128, 8], F32)
        c.carry = P.sbuf("carry" + tag, [128, 8 * 3], BF16)
        c.convo = P.sbuf("convo" + tag, [128, 8 * 3], F32)
        c.ebl = P.sbuf("ebl" + tag, [128, 4 * c.nt], F32)
        c.rope = P.sbuf("rope" + tag, [96 if not is_s else 32, 2 * Tn], F32)
        c.atb = [P.sbuf("atb%d%s" % (i, tag), [128, 4 * c.tw], BF16) for i in range(2)]
        c.tbi = 0
        return c
    cp = make_ctx("p", T, False)
    cp.nb, cp.pend = allb[5], []
    cs = make_ctx("s", TS, True)
    cs.nb, cs.pend = allb[6], []
    ckvnb = P.sbuf("ckvnb", [128, 2 * 2048], BF16)
    KT = [P.sbuf("KT%d" % i, [96, 2048], BF16) for i in range(2)]
    Vp = P.sbuf("Vp", [128, 16 * 2 * 65], BF16)
    PTs = [P.sbuf("PT%d" % i, [128, 512], BF16) for i in range(3)]
    pt_i = [0]
    def pt_next():
        b = PTs[pt_i[0] % 3]
        pt_i[0] += 1
        return b
    MEMSET("pool", Vp[:, :], 1.0, [Vp])
    def v3(buf, off, nk, Tn, p0=0, p1=128):
        return buf.t[p0:p1, off:off + nk * Tn].rearrange("p (k t) -> p k t", t=Tn)
    MEMSET("dve", cp.S[:, :], 0.0, [cp.S])
    MEMSET("dve", cp.Sbf[:, :], 0.0, [cp.Sbf])
    MEMSET("dve", cp.hst[:, :], 0.0, [cp.hst])
    MEMSET("dve", cp.carry[:, :], 0.0, [cp.carry])
    DMA("sp", cs.S[:, :].rearrange("p (h v) -> p h v", v=256), gla_s.rearrange("h k v -> k h v"), [], [cs.S], "C0")
    CP("act", cs.Sbf[:, :], cs.S[:, :], [cs.S], [cs.Sbf])
    CP("dve", cs.hst[:, :], vecs[:, VOFF["h0"]:VOFF["h0"] + 8], [vecs], [cs.hst])
    CP("dve", cs.carry[:, :], vecs[:, VOFF["convs"]:VOFF["convs"] + 24], [vecs], [cs.carry])
    DMA("sp", cs.rope[0:32, :], ropeS_d[:, :], [], [cs.rope], "C0")
    def rms(c, src, off, nk, nfeat, keyed=False):
        Tn = c.T
        ps = ps_next()
        for k in range(nk):
            sq = c.tb[c.tbi % 2]
            c.tbi += 1
            ACT(sq[:, :], src.t[:, off + k * Tn: off + (k + 1) * Tn], AF.Square, [(src, k)] if keyed else [src], [sq])
            MM(ps, ps[:, 0:Tn], ones_bf[:, :], sq[:, :], k == 0, k == nk - 1, [ones_bf, sq])
        ACT(c.rstd[:, :], ps[:, 0:Tn], AF.Ln, [ps], [c.rstd], bias=EPS, scale=1.0 / nfeat)
        ACT(c.rstd[:, :], c.rstd[:, :], AF.Exp, [c.rstd], [c.rstd], scale=-0.5)
    def load_x_dma(c, src, row0):
        tw = c.tw
        xb32 = c.B2.t[:, :].bitcast(F32)
        for t in range(c.nt):
            DMA("sp", xb32[0:tw, t * 1024:(t + 1) * 1024], src[row0 + t * tw: row0 + (t + 1) * tw, :], [], [c.B2], "IO" + c.tag)

    def load_x(c, src, row0, prefetched=False):
        Tn, tw = c.T, c.tw
        if c.s:
            buf, bt = c.F2, c.F2.t
        else:
            buf, bt = c.B2, c.B2.t[:, :].bitcast(F32)
            if not prefetched:
                load_x_dma(c, src, row0)
        for t in range(c.nt):
            if c.s:
                DMA("sp", bt[0:tw, t * 1024:(t + 1) * 1024], src[row0 + t * tw: row0 + (t + 1) * tw, :], [], [buf], "IO" + c.tag)
            for kq in range(2):
                ps = ps_next()
                for kk in range(4):
                    kc = kq * 4 + kk
                    TR(ps, ps[:, kk * tw:(kk + 1) * tw], bt[0:tw, t * 1024 + kc * 128: t * 1024 + (kc + 1) * 128],
                       ident[0:tw, 0:tw], [buf, ident])
                CP("act" if kq == 0 else "dve", v3(c.hT, 0, 8, Tn)[:, kq * 4:(kq + 1) * 4, t * tw:(t + 1) * tw],
                   ps[:, 0:4 * tw].rearrange("p (k t) -> p k t", t=tw), [ps], [c.hT])
    def store_y(c, dst, row0):
        Tn, tw = c.T, c.tw
        fence(c.F2)
        for t in range(c.nt):
            for kq in range(2):
                ps = ps_next()
                for kk in range(4):
                    kc = kq * 4 + kk
                    TR(ps, ps[0:tw, kk * 128:(kk + 1) * 128], c.hT.t[:, kc * Tn + t * tw: kc * Tn + (t + 1) * tw],
                       ident[:, :], [c.hT, ident])
                CP("act" if kq == 0 else "dve", c.F2.t[0:tw, t * 1024 + kq * 512: t * 1024 + (kq + 1) * 512], ps[0:tw, :],
                   [ps], [(c.F2, ("io", t))])
            DMA("sp", dst[row0 + t * tw: row0 + (t + 1) * tw, :], c.F2.t[0:tw, t * 1024:(t + 1) * 1024],
                [(c.F2, ("io", t))], [], "IO" + c.tag)
    def prenorm(c, gname):
        Tn = c.T
        rms(c, c.hT, 0, 8, 1024, keyed=True)
        for k in range(8):
            STT(c.uT.t[:, k * Tn:(k + 1) * Tn], c.hT.t[:, k * Tn:(k + 1) * Tn], vcol(gname, k), c.rstd[:, :],
                OP.mult, OP.mult, [(c.hT, k), c.rstd, vecs], [(c.uT, k)])
    def evac_sq(c, ps, dc):
        Tn = c.T
        CP("act", c.F2.t[:, dc * Tn:(dc + 1) * Tn], ps[:, 0:Tn], [ps], [c.F2])
        sq = c.tb[c.tbi % 2]
        c.tbi += 1
        ACT(sq[:, :], ps[:, 0:Tn], AF.Square, [ps], [sq])
        c.pend.append((sq, dc == 0, dc == 7))
        flush_sq(c, 1)

    def flush_sq(c, keep):
        Tn = c.T
        while len(c.pend) > keep:
            sq, f_, l_ = c.pend.pop(0)
            MM(c.nb, c.nb[:, 0:Tn], ones_bf[:, :], sq[:, :], f_, l_, [ones_bf, sq])

    def postnorm_add(c, gname):
        Tn = c.T
        flush_sq(c, 0)
        ACT(c.rstd[:, :], c.nb[:, 0:Tn], AF.Ln, [c.nb], [c.rstd], bias=EPS, scale=1.0 / 1024)
        ACT(c.rstd[:, :], c.rstd[:, :], AF.Exp, [c.rstd], [c.rstd], scale=-0.5)
        for k in range(8):
            tmp = c.tf[k % 2]
            TT("dve", tmp[:, :], c.F2.t[:, k * Tn:(k + 1) * Tn], c.rstd[:, :], OP.mult, [c.F2, c.rstd], [tmp])
            STT(c.hT.t[:, k * Tn:(k + 1) * Tn], tmp[:, :], vcol(gname, k), c.hT.t[:, k * Tn:(k + 1) * Tn], OP.mult, OP.add,
                [tmp, (c.hT, k), vecs], [(c.hT, k)])
    def fence(buf):
        P.op("dve", lambda e: e.memset(drv[:, 47:48], 0.0), writes=[buf, (drv, "f")])
    def ffn(ctxs, layer):
        for c in ctxs:
            prenorm(c, "ffn_pre%d" % layer)
            fence(c.B1)
        col = 0
        while col < DFF:
            nc_ = min(512, DFF - col)
            sg, wg = wload(ffn_wg[layer * D:(layer + 1) * D, col:col + nc_], 128, 8, nc_)
            su, wu = wload(ffn_wu[layer * D:(layer + 1) * D, col:col + nc_], 128, 8, nc_)
            for c in ctxs:
                Tn = c.T
                nj = nc_ // 128
                for j in range(nj):
                    pg = ps_next()
                    for k in range(8):
                        MM(pg, pg[:, 0:Tn], wg[:, k, j * 128:(j + 1) * 128], c.uT.t[:, k * Tn:(k + 1) * Tn], k == 0, k == 7, [sg, (c.uT, k)])
                    tmp = c.tf[2 + j]
                    ACT(tmp[:, :], pg[:, 0:Tn], AF.Silu, [pg], [tmp])
                for j in range(nj):
                    fj = col // 128 + j
                    pu = ps_next()
                    for k in range(8):
                        MM(pu, pu[:, 0:Tn], wu[:, k, j * 128:(j + 1) * 128], c.uT.t[:, k * Tn:(k + 1) * Tn], k == 0, k == 7, [su, (c.uT, k)])
                    tmp = c.tf[2 + j]
                    TT("dve", c.B1.t[:, fj * Tn:(fj + 1) * Tn], tmp[:, :], pu[:, 0:Tn], OP.mult, [tmp, pu], [(c.B1, ("ff", fj))])
            col += nc_
        for dc in range(8):
            sd, wd = wload(ffn_wd[layer * DFF:(layer + 1) * DFF, dc * 128:(dc + 1) * 128], 128, NFF, 128)
            for c in ctxs:
                Tn = c.T
                ps = ps_next()
                for f in range(NFF):
                    MM(ps, ps[:, 0:Tn], wd[:, f, :], c.B1.t[:, f * Tn:(f + 1) * Tn], f == 0, f == NFF - 1, [sd, (c.B1, ("ff", f))])
                evac_sq(c, ps, dc)
        for c in ctxs:
            postnorm_add(c, "ffn_post%d" % layer)
    def mixer_ab(ctxs, last):
        for c in ctxs:
            prenorm(c, "mix_pre0")
            fence(c.B1)
            fence(c.F2)
        wz = P_wz
        DMA("pool", wz.t[:, :].rearrange("p (k n) -> p k n", n=16), w_in_ab[:, 5120:5136].rearrange("(k p) n -> p k n", p=128), [], [wz], "C1")
        for c in ctxs:
            Tn = c.T
            ps = ps_next()
            for k in range(8):
                MM(ps, ps[0:16, 0:Tn], wz.t[:, k * 16:(k + 1) * 16], c.uT.t[:, k * Tn:(k + 1) * Tn], k == 0, k == 7, [wz, (c.uT, k)])
            zr = c.tb[0]
            CP("act", zr[0:16, :], ps[0:16, 0:Tn], [ps], [zr])
            for h in range(4):
                pz = ps_next()
                MM(pz, pz[:, 0:Tn], wgate[0:16, h * 128:(h + 1) * 128], zr[0:16, :], True, True, [wgate, zr])
                e1 = c.tf[0]
                ACT(e1[:, :], pz[:, 0:Tn], AF.Exp, [pz, drv], [e1], bias=drv[:, h:h + 1], scale=-1.0)
                ACT(e1[:, :], e1[:, :], AF.Ln, [e1], [e1], bias=1.0)
                msk = rmask[:, 0:Tn] if not c.s else rmask[:, 1:1 + Tn]
                SCAN(c.cum.t[:, h * Tn:(h + 1) * Tn], msk, e1[:, :], 0.0, [rmask, e1], [(c.cum, h)])
                ACT(c.ebl.t[:, h * c.nt:(h + 1) * c.nt],
                    c.cum.t[:, h * Tn:(h + 1) * Tn].rearrange("p (n c) -> p n c", c=c.C)[:, :, c.C - 1],
                    AF.Exp, [(c.cum, h)], [(c.ebl, h)], scale=-1.0 / 16.0)
        for half in range(2):
            sx, wx = wload(w_in_ab[:, half * 512:(half + 1) * 512], 128, 8, 512)
            sgw, wgl = wload(w_in_ab[:, 1024 + half * 512:1024 + (half + 1) * 512], 128, 8, 512)
            sqk, wqk = wload(w_in_ab[:, 2048 + half * 512:2560 + half * 512], 128, 8, 512)

            def qk_head(c, h):
                Tn = c.T
                ps = ps_next()
                for k in range(8):
                    MM(ps, ps[:, 0:Tn], wqk[:, k, h * 128:(h + 1) * 128], c.uT.t[:, k * Tn:(k + 1) * Tn], k == 0, k == 7, [sqk, (c.uT, k)])
                e1 = c.rstd
                if half == 0:
                    ACT(e1[:, :], c.cum.t[:, h * Tn:(h + 1) * Tn], AF.Exp, [(c.cum, h)], [e1], scale=-1.0 / 16.0)
                    STT(c.B1.t[:, h * Tn:(h + 1) * Tn], ps[:, 0:Tn], 128.0 ** -0.5, e1[:, :], OP.mult, OP.mult, [ps, e1], [(c.B1, ("q", h))])
                else:
                    ACT(e1[:, :], c.cum.t[:, h * Tn:(h + 1) * Tn], AF.Exp, [(c.cum, h)], [e1], scale=1.0 / 16.0)
                    TT("dve", c.B1.t[:, 4 * Tn + h * Tn: 4 * Tn + (h + 1) * Tn], ps[:, 0:Tn], e1[:, :], OP.mult, [ps, e1], [(c.B1, ("k", h))])
            for c in ctxs:
                Tn = c.T
                XO = 8 * Tn
                GO = 16 * Tn + 24
                for j in range(4):
                    kc = half * 4 + j
                    xo = XO + kc * (Tn + 3)
                    ps = ps_next()
                    for k in range(8):
                        MM(ps, ps[:, 0:Tn], wx[:, k, j * 128:(j + 1) * 128], c.uT.t[:, k * Tn:(k + 1) * Tn], k == 0, k == 7, [sx, (c.uT, k)])
                    CP("act", c.B1.t[:, xo + 3: xo + 3 + Tn], ps[:, 0:Tn], [ps], [(c.B1, ("xa", kc))])
                    CP("act", c.B1.t[:, xo: xo + 3], c.carry.t[:, kc * 3:(kc + 1) * 3], [(c.carry, kc)], [(c.B1, ("xa", kc))])
                    if last or c.s:
                        CP("dve", c.convo.t[:, kc * 3:(kc + 1) * 3], ps[:, Tn - 3:Tn], [ps], [(c.convo, kc)])
                def tv(i):
                    if i < 6:
                        return c.tf[i].t[:, 0:Tn], c.tf[i]
                    return c.F2.t[:, (i - 6) * Tn:(i - 5) * Tn], (c.F2, ("t", i - 6))
                gt = [tv(11), tv(12), tv(13), tv(5)]
                gps = []
                for j in range(4):
                    ps = ps_next()
                    gps.append(ps)
                    for k in range(8):
                        MM(ps, ps[:, 0:Tn], wgl[:, k, j * 128:(j + 1) * 128], c.uT.t[:, k * Tn:(k + 1) * Tn], k == 0, k == 7, [sgw, (c.uT, k)])
                    ACT(c.B1.t[:, GO + j * Tn: GO + (j + 1) * Tn], ps[:, 0:Tn], AF.Gelu_apprx_tanh, [ps], [(c.B1, ("gl", j))])
                st_ = {}
                XS = [0, 6, 5]

                def build_diag(j):
                    kc = half * 4 + j
                    dg = (kc % 3) * 512
                    for tap in range(4):
                        TS_("dve", diag[:, dg + tap * 128: dg + (tap + 1) * 128], identb[:, :], vcol("conv_w", tap * 8 + kc), None, OP.mult, None,
                            [identb, vecs], [(diag, (kc % 3, tap))])

                def prep_conv(j):
                    kc = half * 4 + j
                    xo = XO + kc * (Tn + 3)
                    dg = (kc % 3) * 512
                    pc = allb[5 + (kc % 2)]
                    for tap in range(4):
                        MM(pc, pc[:, 0:Tn], diag[:, dg + tap * 128: dg + (tap + 1) * 128], c.B1.t[:, xo + tap: xo + tap + Tn], tap == 0, tap == 3,
                           [(diag, (kc % 3, tap)), (c.B1, ("xa", kc))])
                    CP("act", c.carry.t[:, kc * 3:(kc + 1) * 3], c.B1.t[:, xo + Tn: xo + Tn + 3], [(c.B1, ("xa", kc))], [(c.carry, kc)])
                    base = 0 if kc % 2 == 0 else 6
                    tvs = [tv(XS[kc % 3])] + [tv(base + q) for q in range(1, 5)]
                    xb = c.tb[kc % 2]
                    TS_("dve", xb[:, :], pc[:, 0:Tn], vcol("conv_b", kc), None, OP.add, None, [pc, vecs], [xb])
                    st_[j] = [tvs, xb, None, None, pc]

                def gates(j):
                    kc = half * 4 + j
                    xb = st_[j][1]
                    pr = ps_next()
                    pi = ps_next()
                    MM(pr, pr[:, 0:Tn], wab[:, kc * 128:(kc + 1) * 128], xb[:, :], True, True, [wab, xb])
                    MM(pi, pi[:, 0:Tn], wab[:, 1024 + kc * 128:1024 + (kc + 1) * 128], xb[:, :], True, True, [wab, xb])
                    st_[j][2], st_[j][3] = pr, pi

                def chain_head(j):
                    kc = half * 4 + j
                    (X, Xd), (R, Rd), (I, Id), (A, Ad), (M, Md) = st_[j][0]
                    pr, pi = st_[j][2], st_[j][3]
                    ACT(R, pr[:, 0:Tn], AF.Exp, [pr, drv], [Rd], bias=drv[:, 4 + kc:5 + kc], scale=-1.0)
                    ACT(I, pi[:, 0:Tn], AF.Exp, [pi, drv], [Id], bias=drv[:, 12 + kc:13 + kc], scale=-1.0)

                def chain_tail(j):
                    kc = half * 4 + j
                    (X, Xd), (R, Rd), (I, Id), (A, Ad), (M, Md) = st_[j][0]
                    ACT(R, R, AF.Ln, [Rd], [Rd], bias=1.0)
                    ACT(R, R, AF.Exp, [Rd], [Rd], scale=-1.0)
                    ACT(A, R, AF.Exp, [Rd, drv], [Ad], scale=drv[:, 20 + kc:21 + kc])
                    TT("dve", M, A, A, OP.mult, [Ad], [Md])
                    ACT(I, I, AF.Ln, [Id], [Id], bias=1.0)
                    ACT(I, I, AF.Exp, [Id], [Id], scale=-1.0)
                    pc = st_[j][4]
                    STT(I, pc[:, 0:Tn], vcol("conv_b", kc), I, OP.add, OP.mult, [pc, vecs, Id], [Id])
                    ACT(M, M, AF.Ln, [Md], [Md], bias=1.0, scale=-1.0)
                    ACT(M, M, AF.Exp, [Md], [Md], scale=0.5)
                    TT("dve", I, I, M, OP.mult, [Id, Md], [Id])
                    SCAN(X, A, I, c.hst.t[:, kc:kc + 1], [Ad, Id, (c.hst, kc), Xd], [Xd])
                    CP("dve", c.hst.t[:, kc:kc + 1], X[:, Tn - 1:Tn], [Xd], [(c.hst, kc)])
                    TT("dve", c.B2.t[:, kc * Tn:(kc + 1) * Tn], X, c.B1.t[:, GO + j * Tn: GO + (j + 1) * Tn], OP.mult,
                       [Xd, (c.B1, ("gl", j))], [(c.B2, kc)])

                build_diag(0)
                build_diag(1)
                prep_conv(0)
                gates(0)
                for j in range(4):
                    if j + 2 < 4:
                        build_diag(j + 2)
                    if j + 1 < 4:
                        prep_conv(j + 1)
                    chain_head(j)
                    if j + 1 < 4:
                        gates(j + 1)
                    qk_head(c, j)
                    chain_tail(j)
        for c in ctxs:
            fence(c.F2)
        for c in ctxs:
            fence(c.B1)
        for c in ctxs:
            Tn, tw = c.T, c.tw
            KO = 4 * Tn
            KTO = 8 * Tn
            for t in range(c.nt):
                for h in range(4):
                    TR(psb, psb[0:tw, h * 128:(h + 1) * 128], c.B1.t[:, KO + h * Tn + t * tw: KO + h * Tn + (t + 1) * tw], identb[:, :],
                       [(c.B1, ("k", h)), identb])
                CP("act", c.B1.t[0:tw, KTO + t * 512: KTO + (t + 1) * 512], psb[0:tw, 0:512], [psb], [(c.B1, ("kt", t))])
        for vb in range(2):
            sv_, wv_ = wload(w_in_ab[:, 3072 + vb * 512:3072 + (vb + 1) * 512], 128, 8, 512)
            for c in ctxs:
                Tn, tw = c.T, c.tw
                VO = 8 * Tn + c.nt * 512
                for t in range(c.nt):
                    ps = ps_next()
                    for k in range(8):
                        MM(ps, ps[0:tw, :], c.uT.t[:, k * Tn + t * tw: k * Tn + (t + 1) * tw], wv_[:, k, :], k == 0, k == 7, [sv_, (c.uT, k)])
                    CP("act" if t % 2 == 0 else "dve", c.B1.t[0:tw, VO + t * 1024 + vb * 512: VO + t * 1024 + (vb + 1) * 512], ps[0:tw, :],
                       [ps], [(c.B1, ("v", t))])
        for c in ctxs:
            Tn, tw, C = c.T, c.tw, c.C
            KO, KTO = 4 * Tn, 8 * Tn
            VO = 8 * Tn + c.nt * 512
            trm = tri4 if not c.s else tri4s
            def a_mask(t):
                pa = ps_next()
                for h in range(4):
                    MM(pa, pa[0:C, h * C:(h + 1) * C], c.B1.t[:, KO + h * Tn + t * C: KO + h * Tn + (t + 1) * C],
                       c.B1.t[:, h * Tn + t * C: h * Tn + (t + 1) * C], True, True, [(c.B1, ("k", h)), (c.B1, ("q", h))])
                at = c.atb[t % 2]
                TT("dve", at[0:C, 0:4 * C], pa[0:C, 0:4 * C], trm[0:C, 0:4 * C], OP.mult, [pa, trm], [at])
                return at
            at = a_mask(0)
            for t in range(c.nt):
                for hp in range(2):
                    po = ps_next()
                    for hh in range(2):
                        h = hp * 2 + hh
                        for j in range(2):
                            c0 = (hh * 2 + j) * C
                            MM(po, po[:, c0:c0 + C], c.B1.t[0:C, VO + t * 1024 + h * 256 + j * 128: VO + t * 1024 + h * 256 + (j + 1) * 128],
                               at[0:C, h * C:(h + 1) * C], (hh == 0 and j == 0), False, [(c.B1, ("v", t)), at], skip=True)
                    for hh in range(2):
                        h = hp * 2 + hh
                        for j in range(2):
                            c0 = (hh * 2 + j) * C
                            MM(po, po[:, c0:c0 + C], c.Sbf.t[:, h * 256 + j * 128: h * 256 + (j + 1) * 128],
                               c.B1.t[:, h * Tn + t * C: h * Tn + (t + 1) * C], False, True, [(c.Sbf, h), (c.B1, ("q", h))], skip=True)
                    CP("act", v3(c.F2, 0, 8, Tn)[:, hp * 4:(hp + 1) * 4, t * C:(t + 1) * C],
                       po[:, 0:4 * C].rearrange("p (k t) -> p k t", t=C), [po], [(c.F2, ("o", hp))])
                pds = []
                for hp in range(2):
                    pd = ps_next()
                    pds.append(pd)
                    for hh in range(2):
                        h = hp * 2 + hh
                        MM(pd, pd[:, hh * 256:(hh + 1) * 256], c.B1.t[0:C, KTO + t * 512 + h * 128: KTO + t * 512 + (h + 1) * 128],
                           c.B1.t[0:C, VO + t * 1024 + h * 256: VO + t * 1024 + (h + 1) * 256], True, True,
                           [(c.B1, ("kt", t)), (c.B1, ("v", t))])
                if t + 1 < c.nt:
                    at = a_mask(t + 1)
                for hp in range(2):
                    pd = pds[hp]
                    for hh in range(2):
                        h = hp * 2 + hh
                        eb = c.ebl.t[:, h * c.nt + t: h * c.nt + t + 1]
                        Sh = c.S.t[:, h * 256:(h + 1) * 256]
                        TT("dve", Sh, Sh, pd[:, hh * 256:(hh + 1) * 256], OP.add, [(c.S, h), pd], [(c.S, h)])
                        ACT(c.Sbf.t[:, h * 256:(h + 1) * 256], Sh, AF.Copy, [(c.S, h), (c.ebl, h)], [(c.Sbf, h)], scale=eb)
                        TS_("dve", Sh, Sh, eb, None, OP.mult, None, [(c.S, h), (c.ebl, h)], [(c.S, h)])
            pss_ = []
            for h in range(4):
                ps = ps_next()
                pss_.append(ps)
                for j in range(2):
                    sq = c.tb[c.tbi % 2]
                    c.tbi += 1
                    ACT(sq[:, :], c.F2.t[:, (2 * h + j) * Tn:(2 * h + j + 1) * Tn], AF.Square, [c.F2], [sq])
                    MM(ps, ps[:, 0:Tn], ones_bf[:, :], sq[:, :], j == 0, j == 1, [ones_bf, sq])
            for h in range(4):
                ACT(c.tf[h][:, :], pss_[h][:, 0:Tn], AF.Ln, [pss_[h]], [c.tf[h]], bias=EPS, scale=1.0 / 256)
            for h in range(4):
                ACT(c.tf[h][:, :], c.tf[h][:, :], AF.Exp, [c.tf[h]], [c.tf[h]], scale=-0.5)
            for h in range(4):
                for j in range(2):
                    hj = 2 * h + j
                    STT(c.F2.t[:, hj * Tn:(hj + 1) * Tn], c.F2.t[:, hj * Tn:(hj + 1) * Tn], vcol("gla_norm", j), c.tf[h][:, :],
                        OP.mult, OP.mult, [c.F2, c.tf[h], vecs], [(c.F2, ("o", h // 2))])
        for gbk in range(2):
            sb_, wb_ = wload(w_in_ab[:, 4096 + gbk * 512:4096 + (gbk + 1) * 512], 128, 8, 512)
            for c in ctxs:
                Tn = c.T
                for j in range(4):
                    hj = gbk * 4 + j
                    ps = ps_next()
                    for k in range(8):
                        MM(ps, ps[:, 0:Tn], wb_[:, k, j * 128:(j + 1) * 128], c.uT.t[:, k * Tn:(k + 1) * Tn], k == 0, k == 7, [sb_, (c.uT, k)])
                    tmp = c.tf[4 + j % 2]
                    ACT(tmp[:, :], ps[:, 0:Tn], AF.Silu, [ps], [tmp])
                    TT("dve", c.B2.t[:, (8 + hj) * Tn:(9 + hj) * Tn], tmp[:, :], c.F2.t[:, hj * Tn:(hj + 1) * Tn], OP.mult,
                       [tmp, c.F2], [(c.B2, 8 + hj)])
        for blk in range(4):
            so_, wo_ = wload(w_out_ab[:, blk * 256:(blk + 1) * 256], 128, 16, 256)
            for c in ctxs:
                Tn = c.T
                for dl in range(2):
                    dc = blk * 2 + dl
                    ps = ps_next()
                    for k in range(16):
                        MM(ps, ps[:, 0:Tn], wo_[:, k, dl * 128:(dl + 1) * 128], c.B2.t[:, k * Tn:(k + 1) * Tn], k == 0, k == 15, [so_, (c.B2, k)])
                    evac_sq(c, ps, dc)
        for c in ctxs:
            postnorm_add(c, "mix_post0")
    P_wz = P.sbuf("wz", [128, 8 * 16], BF16)
    def out_states(c, idx):
        for half in range(2):
            ps = ps_next()
            for kk in range(4):
                kc = half * 4 + kk
                TR(ps, ps[0:3, kk * 128:(kk + 1) * 128], c.convo.t[:, kc * 3:(kc + 1) * 3], ident[:, :], [c.convo, ident])
            CP("act", c.F2.t[0:3, half * 512:(half + 1) * 512], ps[0:3, 0:512], [ps], [c.F2])
        DMA("sp", o_conv[idx][:, :], c.F2.t[0:3, 0:1024], [c.F2], [], "O" + c.tag)
        ps = ps_next()
        TR(ps, ps[0:8, 0:128], c.hst.t[:, 0:8], ident[:, :], [c.hst, ident])
        hh = c.Se
        CP("act", hh[0:8, 0:128], ps[0:8, 0:128], [ps], [hh])
        DMA("sp", o_h[idx][:, :], hh[0:8, 0:128], [hh], [], "O" + c.tag)
        DMA("sp", o_S[idx].rearrange("h k v -> k h v"), c.S.t[:, :].rearrange("p (h v) -> p h v", v=256), [c.S], [], "O" + c.tag)
    def mla_proj(ctxs, g):
        for c in ctxs:
            prenorm(c, "mix_pre1")
            fence(c.B1)
        s1, w1 = wload(w_in_c[:, 0:384], 128, 8, 384)
        s2, w2 = wload(w_in_c[:, 384:768], 128, 8, 384)
        for c in ctxs:
            Tn, tw, pb = c.T, c.tw, c.pb
            for k3 in range(3):
                ps = ps_next()
                for k in range(8):
                    MM(ps, ps[:, 0:Tn], w1[:, k, k3 * 128:(k3 + 1) * 128], c.uT.t[:, k * Tn:(k + 1) * Tn], k == 0, k == 7, [s1, (c.uT, k)])
                CP("act", c.F2.t[:, k3 * Tn:(k3 + 1) * Tn], ps[:, 0:Tn], [ps], [(c.F2, ("cq", k3))])
            for k2 in range(2):
                ps = ps_next()
                for k in range(8):
                    MM(ps, ps[:, 0:Tn], w2[:, k, k2 * 128:(k2 + 1) * 128], c.uT.t[:, k * Tn:(k + 1) * Tn], k == 0, k == 7, [s2, (c.uT, k)])
                CP("act", c.F2.t[:, (3 + k2) * Tn:(4 + k2) * Tn], ps[:, 0:Tn], [ps], [(c.F2, ("ckv", k2))])
            if SUB < 1:
                continue
            pA = ps_next()
            pB = ps_next()
            if not c.s:
                cA, cB, M_ = (192, 288), (288, 384), 96
            else:
                cA, cB, M_ = (256, 288), (352, 384), 32
            for k in range(8):
                MM(pA, pA[0:M_, 0:Tn], w2[:, k, cA[0]:cA[1]], c.uT.t[:, k * Tn:(k + 1) * Tn], k == 0, k == 7, [s2, (c.uT, k)])
            for k in range(8):
                MM(pB, pB[0:M_, 0:Tn], w2[:, k, cB[0]:cB[1]], c.uT.t[:, k * Tn:(k + 1) * Tn], k == 0, k == 7, [s2, (c.uT, k)])
            if not c.s:
                DMA("sp", c.rope.t[64:96, :].rearrange("p (a t) -> p a t", t=Tn),
                    ropeP_d.rearrange("p (a t) -> p a t", t=2048)[:, :, g * T:(g + 1) * T], [], [c.rope], "R" + c.tag)
            t1, t2 = c.tf[0], c.tf[1]
            kpr = c.F2.t[pb:pb + 32, 5 * Tn:6 * Tn]
            TT("dve", t1[pb:pb + 32, :], pA[pb:pb + 32, 0:Tn], c.rope.t[pb:pb + 32, 0:Tn], OP.mult, [pA, c.rope], [t1])
            TT("dve", t2[pb:pb + 32, :], pB[pb:pb + 32, 0:Tn], c.rope.t[pb:pb + 32, Tn:2 * Tn], OP.mult, [pB, c.rope], [t2])
            TT("dve", kpr, t1[pb:pb + 32, :], t2[pb:pb + 32, :], OP.add, [t1, t2], [(c.F2, "kpr")])
            if not c.s:
                for i in range(2):
                    CP("act", KT[i].t[64:96, g * T:(g + 1) * T], kpr, [(c.F2, "kpr")], [(KT[i], "pe")])
            else:
                CP("act", c.kprb.t[0:32, 0:Tn], kpr, [(c.F2, "kpr")], [c.kprb])
            if SUB < 2:
                continue
            rms(c, c.F2, 0, 3, 384)
            for k3 in range(3):
                STT(c.uT.t[:, k3 * Tn:(k3 + 1) * Tn], c.F2.t[:, k3 * Tn:(k3 + 1) * Tn], vcol("q_norm", k3), c.rstd[:, :], OP.mult, OP.mult,
                    [(c.F2, ("cq", k3)), c.rstd, vecs], [c.uT])
            rms(c, c.F2, 3 * Tn, 2, 256)
            for k2 in range(2):
                STT(c.F2.t[:, (3 + k2) * Tn:(4 + k2) * Tn], c.F2.t[:, (3 + k2) * Tn:(4 + k2) * Tn], vcol("kv_norm", k2), c.rstd[:, :],
                    OP.mult, OP.mult, [(c.F2, ("ckv", k2)), c.rstd, vecs], [(c.F2, ("ckv", k2))])
                if not c.s:
                    CP("act", ckvnb.t[:, k2 * 2048 + g * T: k2 * 2048 + (g + 1) * T], c.F2.t[:, (3 + k2) * Tn:(4 + k2) * Tn],
                       [(c.F2, ("ckv", k2))], [(ckvnb, g)])
                else:
                    CP("act", c.ckvb.t[:, k2 * Tn:(k2 + 1) * Tn], c.F2.t[:, (3 + k2) * Tn:(4 + k2) * Tn], [(c.F2, ("ckv", k2))], [c.ckvb])
            if SUB < 3:
                continue
            idx = 1 if c.s else 0
            for t in range(c.nt):
                ps = ps_next()
                for k2 in range(2):
                    TR(ps, ps[0:tw, k2 * 128:(k2 + 1) * 128], c.F2.t[:, (3 + k2) * Tn + t * tw:(3 + k2) * Tn + (t + 1) * tw], ident[:, :],
                       [(c.F2, ("ckv", k2)), ident])
                if KPEOUT:
                    MM(ps, ps[0:tw, 256:288], c.F2.t[pb:pb + 32, 5 * Tn + t * tw: 5 * Tn + (t + 1) * tw], ident[pb:pb + 32, pb:pb + 32],
                       True, True, [(c.F2, "kpr"), ident])
                st = c.ost[t % 2]
                CP("act", st[0:tw, 0:288], ps[0:tw, 0:288], [ps], [st])
                r0 = (g * T if not c.s else 0) + t * tw
                if OUTV >= 2:
                    DMA("sp", o_ckv[idx][r0:r0 + tw, :], st[0:tw, 0:256], [st], [], "O" + c.tag)
                if OUTV >= 3:
                    DMA("sp", o_kpe[idx][r0:r0 + tw, :], st[0:tw, 256:288], [st], [], "O" + c.tag)
                if c.s:
                    CP("dve", c.ckvtok.t[0:tw, 0:256], ps[0:tw, 0:256], [ps], [c.ckvtok])
        if SUB < 4:
            return
        for c in ctxs:
            fence(c.B1)
        for hf in range(4):
            sa, wcat = wload(w_uq_cat[:, hf * 768:(hf + 1) * 768], 128, 3, 768)
            sb2 = sa
            wa_ = wcat[:, :, 0:384]
            wb2 = wcat[:, :, 384:768]
            for c in ctxs:
                Tn = c.T
                for hl in range(4):
                    h = hf * 4 + hl
                    if not c.s:
                        pA = ps_next()
                        pB = ps_next()
                        for k in range(3):
                            MM(pA, pA[0:96, 0:Tn], wa_[:, k, hl * 96:(hl + 1) * 96], c.uT.t[:, k * Tn:(k + 1) * Tn], k == 0, k == 2, [sa, c.uT])
                        for k in range(3):
                            MM(pB, pB[0:96, 0:Tn], wb2[:, k, hl * 96:(hl + 1) * 96], c.uT.t[:, k * Tn:(k + 1) * Tn], k == 0, k == 2, [sb2, c.uT])
                        CP("act", c.B1.t[0:64, h * Tn:(h + 1) * Tn], pA[0:64, 0:Tn], [pA], [(c.B1, ("Q", h))])
                        t1, t2 = c.tf[0], c.tf[1]
                        TT("dve", t1[64:96, :], pA[64:96, 0:Tn], c.rope.t[64:96, 0:Tn], OP.mult, [pA, c.rope], [t1])
                        TT("dve", t2[64:96, :], pB[64:96, 0:Tn], c.rope.t[64:96, Tn:2 * Tn], OP.mult, [pB, c.rope], [t2])
                        TT("dve", c.B1.t[64:96, h * Tn:(h + 1) * Tn], t1[64:96, :], t2[64:96, :], OP.add, [t1, t2], [(c.B1, ("Q", h))])
                    else:
                        pq = ps_next()
                        for k in range(3):
                            MM(pq, pq[0:64, 0:16], wa_[:, k, hl * 96:hl * 96 + 64], c.uT.t[:, k * Tn:(k + 1) * Tn], k == 0, k == 2, [sa, c.uT])
                        for k in range(3):
                            MM(pq, pq[0:32, 16:32], wa_[:, k, hl * 96 + 64:hl * 96 + 96], c.uT.t[:, k * Tn:(k + 1) * Tn], k == 0, k == 2, [sa, c.uT])
                        for k in range(3):
                            MM(pq, pq[0:32, 32:48], wb2[:, k, hl * 96 + 64:hl * 96 + 96], c.uT.t[:, k * Tn:(k + 1) * Tn], k == 0, k == 2, [sb2, c.uT])
                        CP("act", c.qn.t[0:64, h * 16:(h + 1) * 16], pq[0:64, 0:16], [pq], [c.qn])
                        t1, t2 = c.tf[0], c.tf[1]
                        TT("dve", t1[0:32, :], pq[0:32, 16:32], c.rope.t[0:32, 0:Tn], OP.mult, [pq, c.rope], [t1])
                        TT("dve", t2[0:32, :], pq[0:32, 32:48], c.rope.t[0:32, Tn:2 * Tn], OP.mult, [pq, c.rope], [t2])
                        TT("dve", c.qpe.t[0:32, h * 16:(h + 1) * 16], t1[0:32, :], t2[0:32, :], OP.add, [t1, t2], [c.qpe])
    def mla_attn_prompt(c, g, suk, wuk, suv, wuv):
        Tn = c.T
        nkb = g + 1
        nkt = 4 * (g + 1)
        rot_n[0] = 3
        Vp4 = Vp.t[:, 0:2048].rearrange("p (k m) -> p k m", m=128)

        def kt_recompute(h):
            hh = h % 2
            for kb in range(nkb):
                ps = ps_next()
                for cc in range(2):
                    MM(ps, ps[0:64, 0:512], wuk[:, cc, h * 64:(h + 1) * 64], ckvnb.t[:, cc * 2048 + kb * 512: cc * 2048 + (kb + 1) * 512],
                       cc == 0, cc == 1, [suk, (ckvnb, kb)])
                CP("dve", KT[hh].t[0:64, kb * 512:(kb + 1) * 512], ps[0:64, 0:512], [ps], [(KT[hh], ("n", kb))])

        def v_recompute(hp):
            for k4 in range(nkt // 4):
                ps = ps_next()
                for kk in range(4):
                    kt = k4 * 4 + kk
                    for cc in range(2):
                        MM(ps, ps[:, kk * 128:(kk + 1) * 128], ckvnb.t[:, cc * 2048 + kt * 128: cc * 2048 + (kt + 1) * 128],
                           wuv[:, cc, hp * 128:(hp + 1) * 128], cc == 0, cc == 1, [suv, (ckvnb, kt // 4)])
                CP("dve", Vp4[:, k4 * 4:(k4 + 1) * 4, :], ps[:, :].rearrange("p (k m) -> p k m", m=128), [ps], [(Vp, k4)])

        kt_recompute(0)
        v_recompute(0)
        for h in range(16):
            hp, hh = h // 2, h % 2
            po = allb[3 + 2 * hh]
            pss = allb[4 + 2 * hh]

            def s_exp(kt):
                qlo = max(0, kt - 4 * g) * 128
                nq = Tn - qlo
                pS = ps_next()
                MM(pS, pS[:, 0:nq], KT[hh].t[0:96, kt * 128:(kt + 1) * 128], c.B1.t[0:96, h * Tn + qlo:(h + 1) * Tn], True, True,
                   [(KT[hh], ("n", kt // 4)), (KT[hh], "pe"), (c.B1, ("Q", h))])
                pt = pt_next()
                ACT(pt[:, 0:nq], pS[:, 0:nq], AF.Exp, [pS], [pt], scale=SM_SCALE)
                if kt >= 4 * g:
                    MEMSET("dve", pt[64:128, 0:64], 0.0, [pt])
                return pt, qlo, nq
            q_ = [s_exp(0)]
            if nkt > 1:
                q_.append(s_exp(1))
            for kt in range(nkt):
                pt, qlo, nq = q_.pop(0)
                if kt + 2 < nkt:
                    q_.append(s_exp(kt + 2))
                MM(po, po[:, qlo:Tn], Vp4[:, kt, :], pt[:, 0:nq], kt == 0, kt == nkt - 1, [(Vp, kt // 4), pt], skip=True)
                MM(pss, pss[:, qlo:Tn], ones_bf[:, :], pt[:, 0:nq], kt == 0, kt == nkt - 1, [ones_bf, pt], skip=True)
                if kt == 0 and h + 1 < 16:
                    kt_recompute(h + 1)
            r0 = hh * 64
            rl = c.tf[2 + hh]
            ACT(rl[r0:r0 + 64, :], pss[r0:r0 + 64, 0:Tn], AF.Ln, [pss], [rl])
            ACT(rl[r0:r0 + 64, :], rl[r0:r0 + 64, :], AF.Exp, [rl], [rl], scale=-1.0)
            TT("dve", c.B2.t[r0:r0 + 64, hp * Tn:(hp + 1) * Tn], po[r0:r0 + 64, 0:Tn], rl[r0:r0 + 64, :], OP.mult, [po, rl], [(c.B2, hp)])
            if hh == 1 and hp + 1 < 8:
                v_recompute(hp + 1)
        rot_n[0] = 5
    def mla_attn_sample(c, suk, wuk, suv, wuv):
        Tn = c.T
        skt, wkt = wload(w_ukT[:, :], 64, 1, 4096)
        ps = ps_next()
        for h in range(16):
            for cc in range(2):
                MM(ps, ps[:, cc * 256 + h * 16: cc * 256 + (h + 1) * 16], wkt[0:64, 0, h * 256 + cc * 128: h * 256 + (cc + 1) * 128],
                   c.qn.t[0:64, h * 16:(h + 1) * 16], True, True, [skt, c.qn])
        CP("act", c.qlat.t[:, :], ps[:, :], [ps], [c.qlat])
        olat, sums = acc[0], acc[1]
        first = True
        def s_part(lhs_c0, lhs_c1, lhs_pe, nk, rd):
            pS = ps_next()
            MM(pS, pS[0:nk, 0:256], lhs_c0, c.qlat.t[:, 0:256], True, False, rd + [c.qlat])
            MM(pS, pS[0:nk, 0:256], lhs_c1, c.qlat.t[:, 256:512], False, False, rd + [c.qlat])
            MM(pS, pS[0:nk, 0:256], lhs_pe, c.qpe.t[0:32, :], False, True, rd + [c.qpe])
            pt = pt_next()
            ACT(pt[0:nk, 0:256], pS[0:nk, 0:256], AF.Exp, [pS], [pt], scale=SM_SCALE)
            return pt

        def pv_part(pt, tok_c0, tok_c1, nk, rd, last):
            nonlocal first
            MM(olat, olat[:, 0:256], tok_c0, pt[0:nk, 0:256], first, last, rd + [pt], skip=True)
            MM(olat, olat[:, 256:512], tok_c1, pt[0:nk, 0:256], False, last, rd + [pt], skip=True)
            MM(sums, sums[:, 0:256], ones_bf[0:nk, :], pt[0:nk, 0:256], first, last, [ones_bf, pt])
            first = False

        ct = c.ct[0]

        def prep(blk):
            cb = c.cb[blk % 2]
            DMA("pool", cb.t[:, :].rearrange("p (k c) -> p k c", c=288)[:, :, 0:256],
                cckv[blk * 512:(blk + 1) * 512, :].rearrange("(k p) c -> p k c", p=128), [], [cb], "CB%d" % (blk % 2))
            DMA("pool", cb.t[:, :].rearrange("p (k c) -> p k c", c=288)[:, :, 256:288],
                ckpe[blk * 512:(blk + 1) * 512, :].rearrange("(k p) c -> p k c", p=128), [], [cb], "CB%d" % (blk % 2))
            for cc in range(2):
                for kt in range(4):
                    TR(psb, psb[:, (cc * 4 + kt) * 128:(cc * 4 + kt + 1) * 128], cb.t[:, kt * 288 + cc * 128: kt * 288 + (cc + 1) * 128],
                       identb[:, :], [cb, identb])
            CP("act", ct.t[:, 0:1024], psb[:, 0:1024], [psb], [ct])
            for kt in range(4):
                TR(psb, psb[0:32, kt * 128:(kt + 1) * 128], cb.t[:, kt * 288 + 256: kt * 288 + 288], identb[:, :], [cb, identb])
            CP("dve", ct.t[0:32, 1024:1536], psb[0:32, 0:512], [psb], [ct])

        def s_blk(blk, kt):
            return s_part(ct.t[:, kt * 128:(kt + 1) * 128], ct.t[:, 512 + kt * 128:512 + (kt + 1) * 128],
                          ct.t[0:32, 1024 + kt * 128:1024 + (kt + 1) * 128], 128, [ct])

        prep(0)
        for blk in range(8):
            cb = c.cb[blk % 2]
            q_ = [s_blk(blk, 0), s_blk(blk, 1)]
            for kt in range(4):
                pt = q_.pop(0)
                if kt + 2 < 4:
                    q_.append(s_blk(blk, kt + 2))
                if kt == 1 and blk + 1 < 8:
                    prep(blk + 1)
                pv_part(pt, cb.t[:, kt * 288: kt * 288 + 128], cb.t[:, kt * 288 + 128: kt * 288 + 256], 128, [cb], False)
        CP("dve", c.ckvtokb.t[0:16, :], c.ckvtok.t[0:16, :], [c.ckvtok], [c.ckvtokb])
        pt = s_part(c.ckvb.t[:, 0:16], c.ckvb.t[:, 16:32], c.kprb.t[0:32, 0:16], 16, [c.ckvb, c.kprb])
        pv_part(pt, c.ckvtokb.t[0:16, 0:128], c.ckvtokb.t[0:16, 128:256], 16, [c.ckvtokb], True)
        rs = c.rs
        RECIP(rs.t[:, 0:256], sums[:, 0:256], [sums], [rs])
        for cc in range(2):
            TT("dve", c.olatn.t[:, cc * 256:(cc + 1) * 256], olat[:, cc * 256:(cc + 1) * 256], rs.t[:, 0:256], OP.mult, [olat, rs], [c.olatn])
        ps = ps_next()
        for h in range(16):
            hp = h // 2
            for cc in range(2):
                MM(ps, ps[:, h * 16:(h + 1) * 16], wuv[:, cc, hp * 128:(hp + 1) * 128], c.olatn.t[:, cc * 256 + h * 16: cc * 256 + (h + 1) * 16],
                   cc == 0, cc == 1, [suv, c.olatn])
        for hh in range(2):
            CP("act", c.B2.t[hh * 64:(hh + 1) * 64, 0:128].rearrange("p (k t) -> p k t", t=16),
               ps[hh * 64:(hh + 1) * 64, 0:256].rearrange("p (k h t) -> p k h t", h=2, t=16)[:, :, hh, :], [ps], [c.B2])
    def mla_out(ctxs):
        for blk in range(4):
            so_, wo_ = wload(w_out_c[:, blk * 256:(blk + 1) * 256], 128, 8, 256)
            for c in ctxs:
                Tn = c.T
                for dl in range(2):
                    dc = blk * 2 + dl
                    ps = ps_next()
                    for hp in range(8):
                        MM(ps, ps[:, 0:Tn], wo_[:, hp, dl * 128:(dl + 1) * 128], c.B2.t[:, hp * Tn:(hp + 1) * Tn], hp == 0, hp == 7, [so_, c.B2])
                    evac_sq(c, ps, dc)
        for c in ctxs:
            postnorm_add(c, "mix_post1")
    cp.ost = [cp.tf[2], cp.tf[3]]
    _ost = P.sbuf("ost", [16, 288], F32)
    cs.ost = [_ost, _ost]
    cs.kprb = P.sbuf("kprb", [32, 16], BF16)
    cs.ckvb = P.sbuf("ckvb", [128, 32], BF16)
    cs.ckvtok = P.sbuf("ckvtok", [16, 256], F32)
    cs.ckvtokb = P.sbuf("ckvtokb", [16, 256], BF16)
    cs.qn = P.sbuf("qn", [64, 256], BF16)
    cs.qpe = P.sbuf("qpe", [32, 256], BF16)
    cs.qlat = P.sbuf("qlat", [128, 512], BF16)
    cs.olatn = P.sbuf("olatn", [128, 512], BF16)
    cs.rs = P.sbuf("rs", [128, 256], F32)
    cs.cb = [P.sbuf("cb%d" % i, [128, 4 * 288], BF16) for i in range(2)]
    _ct = P.sbuf("ct0", [128, 1536], BF16)
    cs.ct = [_ct, _ct]
    for g in range(ngroups):
        ctxs = [cp] + ([cs] if g == 0 else [])
        load_x(cp, xp, g * T, prefetched=(g > 0))
        if g == 0:
            load_x(cs, xs, 0)
        if stage >= 1:
            mixer_ab(ctxs, g == NG - 1)
            if g == 0:
                out_states(cs, 1)
            if g == NG - 1:
                out_states(cp, 0)
        if stage >= 2:
            ffn(ctxs, 0)
        if stage >= 3:
            mla_proj(ctxs, g)
        if stage >= 4:
            sukv, wukv = wload(w_ukv[:, :], 128, 2, 2048)
            suk, wuk = sukv, wukv[:, :, 0:1024]
            suv, wuv = sukv, wukv[:, :, 1024:2048]
            mla_attn_prompt(cp, g, suk, wuk, suv, wuv)
            if g == 0 and stage >= 5:
                mla_attn_sample(cs, suk, wuk, suv, wuv)
        if stage >= 6:
            mla_out(ctxs)
        if stage >= 7:
            if g + 1 < ngroups:
                load_x_dma(cp, xp, (g + 1) * T)
            ffn(ctxs, 1)
        store_y(cp, y_p, g * T)
        if g == 0:
            store_y(cs, y_s, 0)
    P.emit()
    P.close()
    return nc
_NC = None
SUB = int(os.environ.get('SUB', '99'))
KPEOUT = int(os.environ.get('KPEOUT', '1'))
OUTV = int(os.environ.get('OUTV', '3'))
def _fm(v, n):
    return np.ascontiguousarray(np.asarray(v, np.float32).reshape(n, 128).T)
def prep_inputs(inp, cores=range(8)):
    f = lambda k: np.asarray(inp[k], np.float32)
    w_in_c = f("w_in_c")[0]
    sw = np.concatenate([np.arange(16, 32), np.arange(0, 16)])
    w_in_c_ext = np.ascontiguousarray(np.concatenate(
        [w_in_c, w_in_c[:, 576:640], w_in_c[:, 640:672][:, sw]], axis=1))
    w_uq = f("w_uq")[0]
    idx = np.arange(1536).reshape(16, 96).copy()
    idx[:, 64:96] = idx[:, 64:96][:, sw]
    w_uq_sw = np.ascontiguousarray(w_uq[:, idx.reshape(-1)])
    w_uk = f("w_uk")[0]
    w_ukT = np.ascontiguousarray(w_uk.transpose(2, 1, 0).reshape(64, 16 * 256))
    tri = np.triu(np.ones((128, 128), np.float32))
    tri4 = np.ascontiguousarray(np.tile(tri, (1, 4)))
    tri4s = np.zeros((128, 64), np.float32)
    tri4s[:16] = np.tile(np.triu(np.ones((16, 16), np.float32)), (1, 4))
    rmask = np.ones((128, 512), np.float32)
    rmask[:, ::128] = 0.0
    half = 16
    inv = (10000.0 ** (-np.arange(half, dtype=np.float32) / np.float32(half))).astype(np.float32)
    def rope_tab(pos):
        ang = pos.astype(np.float32)[None, :] * inv[:, None]
        cos = np.cos(ang).astype(np.float32)
        sin = np.sin(ang).astype(np.float32)
        c32 = np.concatenate([cos, cos], 0)
        s32 = np.concatenate([-sin, sin], 0)
        return np.ascontiguousarray(np.concatenate([c32, s32], 1))
    ropeP = rope_tab(np.arange(2048))
    ropeS = rope_tab(4096 + np.arange(TS))
    shared = {
        "w_in_ab": f("w_in_ab")[0], "lru_wa": f("lru_w_a")[0].reshape(1024, 128), "lru_wx": f("lru_w_x")[0].reshape(1024, 128),
        "w_gate": f("gla_w_gate")[0], "w_out_ab": f("w_out_ab")[0], "w_in_c": w_in_c_ext, "w_uq_cat": np.concatenate([np.concatenate([w_uq[:, b * 384:(b + 1) * 384], w_uq_sw[:, b * 384:(b + 1) * 384]], axis=1) for b in range(4)], axis=1),
        "w_ukv": np.concatenate([w_uk.reshape(256, 1024), f("w_uv")[0].reshape(256, 1024)], axis=1), "w_ukT": w_ukT, "w_out_c": f("w_out_c")[0],
        "ffn_wg": f("ffn_w_gate").reshape(2 * D, DFF), "ffn_wu": f("ffn_w_up").reshape(2 * D, DFF), "ffn_wd": f("ffn_w_down").reshape(2 * DFF, D),
        "ident": np.eye(128, dtype=np.float32), "tri4": tri4, "tri4s": tri4s, "rmask": rmask, "ropeP": ropeP, "ropeS": ropeS,
    }
    shared = {k: np.ascontiguousarray(v, dtype=np.float32) for k, v in shared.items()}
    vparts = {}
    for l in range(2):
        vparts["mix_pre%d" % l] = _fm(f("norm_mix_pre")[l], 8)
        vparts["mix_post%d" % l] = _fm(f("norm_mix_post")[l], 8)
        vparts["ffn_pre%d" % l] = _fm(f("norm_ffn_pre")[l], 8)
        vparts["ffn_post%d" % l] = _fm(f("norm_ffn_post")[l], 8)
    cw = f("conv_w_a")[0]
    vparts["conv_w"] = np.concatenate([_fm(cw[j], 8) for j in range(4)], 1)
    vparts["conv_b"] = _fm(f("conv_b_a")[0], 8)
    vparts["lru_ba"] = _fm(f("lru_b_a")[0], 8)
    vparts["lru_bx"] = _fm(f("lru_b_x")[0], 8)
    vparts["lam"] = _fm(f("lru_lambda")[0], 8)
    vparts["b_gate"] = _fm(f("gla_b_gate")[0], 4)
    vparts["gla_norm"] = _fm(f("gla_norm")[0], 2)
    vparts["q_norm"] = _fm(f("mla_q_norm")[0], 3)
    vparts["kv_norm"] = _fm(f("mla_kv_norm")[0], 2)
    in_maps = []
    for c in cores:
        vp = dict(vparts)
        vp["h0"] = _fm(f("state_lru_h")[0, c], 8)
        cs_ = f("state_conv_a")[0, c]
        vp["convs"] = np.ascontiguousarray(cs_.reshape(3, 8, 128).transpose(2, 1, 0).reshape(128, 24))
        vecs = np.ascontiguousarray(np.concatenate([vp[n] for n, _ in _VNAMES], 1), dtype=np.float32)
        m = dict(shared)
        m.update({
            "xp": np.ascontiguousarray(f("x_prompt")[c]), "xs": np.ascontiguousarray(f("x_sample")[c]),
            "gla_s": np.ascontiguousarray(f("state_gla_S")[0, c]), "cckv": np.ascontiguousarray(f("cache_mla_ckv")[0, c]),
            "ckpe": np.ascontiguousarray(f("cache_mla_kpe")[0, c]), "vecs": vecs,
        })
        in_maps.append(m)
    return in_maps
def kernel(**inp):
    global _NC
    if _NC is None:
        _NC = build()
    nc = _NC
    in_maps = prep_inputs(inp)
    res = run_bass_kernel_spmd(nc, in_maps, core_ids=list(range(8)))
    R = res.results
    def st(name):
        return np.stack([np.asarray(R[c][name], np.float32) for c in range(8)], 0)
    y_prompt = st("y_p")
    y_sample = st("y_s")
    outs = [y_prompt, y_sample]
    for pfx in ("p", "s"):
        outs.append(st(pfx + "_conv")[None])
        outs.append(st(pfx + "_h").reshape(8, 1024)[None])
        outs.append(st(pfx + "_S")[None])
        outs.append(st(pfx + "_ckv")[None])
        outs.append(st(pfx + "_kpe")[None])
    return tuple(outs)
```

```python
import os
import numpy as np
import concourse.bass as bass
import concourse.mybir as mybir
from concourse.bass_utils import run_bass_kernel_spmd
F32 = mybir.dt.float32
BF16 = mybir.dt.bfloat16
AF = mybir.ActivationFunctionType
OP = mybir.AluOpType
D = 1024
T = 512
NG = 4
TS = 16
DFF = 2816
NFF = 22
EPS = 1e-6
SM_SCALE = 96 ** -0.5
GELU_K = 1.5957691216057308
_VNAMES = []
for _l in range(2):
    _VNAMES += [("mix_pre%d" % _l, 8), ("mix_post%d" % _l, 8), ("ffn_pre%d" % _l, 8), ("ffn_post%d" % _l, 8)]
_VNAMES += [("conv_w", 32), ("conv_b", 8), ("lru_ba", 8), ("lru_bx", 8), ("lam", 8), ("b_gate", 4),
            ("gla_norm", 2), ("q_norm", 3), ("kv_norm", 2), ("h0", 8), ("convs", 24)]
VOFF = {}
_c = 0
for _n, _k in _VNAMES:
    VOFF[_n] = _c
    _c += _k
NV = _c
class Buf:
    def __init__(self, t, name):
        self.t = t
        self.name = name
        self.st = {}
        self.excl = False
    def __getitem__(self, idx):
        return self.t[idx]
class Prog:
    ENGS = ("pe", "act", "dve", "pool", "sp")
    def __init__(self, nc):
        self.nc = nc
        self.ops = {e: [] for e in self.ENGS}
        self.known = {e: {} for e in self.ENGS}
        self.sems = {}
        self.cnt = {}
        self._stack = []
        self.epoch = {e: 0 for e in self.ENGS}
        self.dma_rr = {}
        self.clock = {}
        self.dma_direct = {}
        for e in self.ENGS:
            self.new_sem("E_%s_0" % e)
    def enter(self, cm):
        r = cm.__enter__()
        self._stack.append(cm)
        return r
    def close(self):
        while self._stack:
            self._stack.pop().__exit__(None, None, None)
    def new_sem(self, sid):
        if sid not in self.sems:
            self.sems[sid] = self.enter(self.nc.semaphore(sid))
            self.cnt[sid] = 0
        return sid
    def sbuf(self, name, shape, dt):
        return Buf(self.enter(self.nc.sbuf_tensor("sb_" + name, list(shape), dt)), name)
    def psum(self, name, shape, dt=F32):
        b = Buf(self.enter(self.nc.psum_tensor("ps_" + name, list(shape), dt)), name)
        b.excl = True
        return b
    @staticmethod
    def _norm(lst):
        out = []
        for x in lst or []:
            out.append((x, None) if isinstance(x, Buf) else x)
        return out
    def _deps(self, reads, writes):
        w = {}
        def add(s, v):
            if w.get(s, 0) < v:
                w[s] = v
        for b, k in reads:
            keys = [k, None] if k is not None else list(b.st.keys())
            for kk in keys:
                st = b.st.get(kk)
                if st and st[0] is not None:
                    add(*st[0])
        for b, k in writes:
            keys = [k, None] if k is not None else list(b.st.keys())
            for kk in keys:
                st = b.st.get(kk)
                if st:
                    if st[0] is not None:
                        add(*st[0])
                    for s, v in st[1].items():
                        add(s, v)
        return w
    def _record(self, reads, writes, tok):
        for b, k in writes:
            if k is None:
                b.st = {None: [tok, {}]}
            else:
                b.st[k] = [tok, {}]
        for b, k in reads:
            if k is None:
                b.st.setdefault(None, [None, {}])
                for st in b.st.values():
                    if st[1].get(tok[0], 0) < tok[1]:
                        st[1][tok[0]] = tok[1]
            else:
                st = b.st.setdefault(k, [None, {}])
                if st[1].get(tok[0], 0) < tok[1]:
                    st[1][tok[0]] = tok[1]
    def op(self, eng, fn, reads=None, writes=None, dma_sem=None):
        reads = self._norm(reads)
        writes = self._norm(writes)
        xr = [r for r in reads if r[0].excl]
        if xr:
            reads = [r for r in reads if not r[0].excl]
            writes = writes + [r for r in xr if r not in writes]
        w = self._deps(reads, writes)
        if self.cnt["E_%s_%d" % (eng, self.epoch[eng])] >= 30000:
            self.epoch[eng] += 1
            self.new_sem("E_%s_%d" % (eng, self.epoch[eng]))
        own = "E_%s_%d" % (eng, self.epoch[eng])
        kn = self.known[eng]
        waits = []

        def learn(s, v):
            kn[s] = v
            for s2, v2 in self.clock.get((s, v), {}).items():
                if kn.get(s2, 0) < v2:
                    kn[s2] = v2
        for s, v in sorted(w.items(), key=lambda x: -x[1]):
            if eng == "pe" and s.startswith("E_pe_"):
                continue
            if kn.get(s, 0) >= v:
                continue
            waits.append((s, v))
            learn(s, v)
        if dma_sem is not None:
            npool = 8
            i = self.dma_rr.get(eng, 0)
            self.dma_rr[eng] = i + 1
            dma_sem = "D_%s_%d" % (eng, i % npool)
            self.new_sem(dma_sem)
            dd = self.dma_direct.setdefault(eng, {})
            if self.cnt[dma_sem] > 0 and dd.get(dma_sem, 0) < self.cnt[dma_sem]:
                dd[dma_sem] = self.cnt[dma_sem]
                if (dma_sem, self.cnt[dma_sem]) not in waits:
                    waits.append((dma_sem, self.cnt[dma_sem]))
                learn(dma_sem, self.cnt[dma_sem])
            self.cnt[dma_sem] += 16
            tok = (dma_sem, self.cnt[dma_sem])
            self.ops[eng].append((waits, fn, (dma_sem, 16)))
        else:
            self.cnt[own] += 1
            tok = (own, self.cnt[own])
            self.ops[eng].append((waits, fn, (own, 1)))
        self.clock[tok] = dict(kn)
        self._record(reads, writes, tok)
        return tok
    def emit(self):
        nc = self.nc
        with nc.Block() as block:
            def body(ename):
                def f(e):
                    for waits, fn, (s, inc) in self.ops[ename]:
                        for ws, wv in waits:
                            e.wait_ge(self.sems[ws], wv)
                        ins = fn(e)
                        ins.then_inc(self.sems[s], inc)
                    if ename == "sp":
                        for s, c in self.cnt.items():
                            if c > 0:
                                e.wait_ge(self.sems[s], c)
                return f
            block.tensor(body("pe"))
            block.scalar(body("act"))
            block.vector(body("dve"))
            block.gpsimd(body("pool"))
            block.sync(body("sp"))
class Ctx:
    pass
def build(stage=99, ngroups=NG):
    nc = bass.Bass("TRN2", target_bir_lowering=False)
    P = Prog(nc)
    def di(n, s):
        return nc.dram_tensor(n, list(s), F32, kind="ExternalInput").ap()
    def do(n, s):
        return nc.dram_tensor(n, list(s), F32, kind="ExternalOutput").ap()
    xp = di("xp", [2048, D])
    xs = di("xs", [TS, D])
    gla_s = di("gla_s", [4, 128, 256])
    cckv = di("cckv", [4096, 256])
    ckpe = di("ckpe", [4096, 32])
    vecs_d = di("vecs", [128, NV])
    w_in_ab = di("w_in_ab", [D, 5136])
    lru_wa = di("lru_wa", [1024, 128])
    lru_wx = di("lru_wx", [1024, 128])
    w_gate = di("w_gate", [16, 512])
    w_out_ab = di("w_out_ab", [2048, D])
    w_in_c = di("w_in_c", [D, 768])
    w_uq_cat = di("w_uq_cat", [384, 3072])
    w_ukv = di("w_ukv", [256, 2048])
    w_ukT = di("w_ukT", [64, 4096])
    w_out_c = di("w_out_c", [1024, D])
    ffn_wg = di("ffn_wg", [2 * D, DFF])
    ffn_wu = di("ffn_wu", [2 * D, DFF])
    ffn_wd = di("ffn_wd", [2 * DFF, D])
    ident_d = di("ident", [128, 128])
    tri4_d = di("tri4", [128, 512])
    tri4s_d = di("tri4s", [128, 64])
    rmask_d = di("rmask", [128, 512])
    ropeP_d = di("ropeP", [32, 2 * 2048])
    ropeS_d = di("ropeS", [32, 2 * TS])
    y_p = do("y_p", [2048, D])
    y_s = do("y_s", [TS, D])
    o_conv = [do("p_conv", [3, D]), do("s_conv", [3, D])]
    o_h = [do("p_h", [8, 128]), do("s_h", [8, 128])]
    o_S = [do("p_S", [4, 128, 256]), do("s_S", [4, 128, 256])]
    o_ckv = [do("p_ckv", [2048, 256]), do("s_ckv", [TS, 256])]
    o_kpe = [do("p_kpe", [2048, 32]), do("s_kpe", [TS, 32])]
    def MM(ps, out, lhsT, rhs, st, sp, reads, skip=False):
        P.op("pe", lambda e: e.matmul(out, lhsT=lhsT, rhs=rhs, start=st, stop=sp, skip_group_check=skip), reads=reads, writes=[ps])
    def TR(ps, out, in_, idn, reads):
        P.op("pe", lambda e: e.transpose(out, in_, idn), reads=reads, writes=[ps])
    def ACT(out, in_, func, reads, writes, bias=None, scale=None):
        kw = {}
        if bias is not None:
            kw["bias"] = bias
        if scale is not None:
            kw["scale"] = scale
        P.op("act", lambda e: e.activation(out=out, in_=in_, func=func, **kw), reads=reads, writes=writes)
    def TT(eng, out, a, b, op, reads, writes):
        P.op(eng, lambda e: e.tensor_tensor(out=out, in0=a, in1=b, op=op), reads=reads, writes=writes)
    def TS_(eng, out, a, s1, s2, op0, op1, reads, writes):
        if s2 is None:
            P.op(eng, lambda e: e.tensor_scalar(out=out, in0=a, scalar1=s1, scalar2=None, op0=op0), reads=reads, writes=writes)
        else:
            P.op(eng, lambda e: e.tensor_scalar(out=out, in0=a, scalar1=s1, scalar2=s2, op0=op0, op1=op1), reads=reads, writes=writes)
    def STT(out, a, s, b, op0, op1, reads, writes):
        P.op("dve", lambda e: e.scalar_tensor_tensor(out=out, in0=a, scalar=s, in1=b, op0=op0, op1=op1), reads=reads, writes=writes)
    def CP(eng, out, in_, reads, writes):
        if eng == "act":
            P.op("act", lambda e: e.activation(out=out, in_=in_, func=AF.Copy), reads=reads, writes=writes)
        else:
            P.op(eng, lambda e: e.tensor_copy(out=out, in_=in_), reads=reads, writes=writes)
    def RECIP(out, in_, reads, writes):
        P.op("dve", lambda e: e.reciprocal(out=out, in_=in_), reads=reads, writes=writes)
    def SCAN(out, d0, d1, init, reads, writes):
        P.op("dve", lambda e: e.tensor_tensor_scan(out=out, data0=d0, data1=d1, initial=init, op0=OP.mult, op1=OP.add), reads=reads, writes=writes)
    def MEMSET(eng, ap, val, writes):
        P.op(eng, lambda e: e.memset(ap, val), writes=writes)
    def DMA(eng, out, in_, reads, writes, sem):
        P.op(eng, lambda e: e.dma_start(out=out, in_=in_), reads=reads, writes=writes, dma_sem=sem)
    allb = [P.psum("pb%d" % i, [128, 512], F32) for i in range(7)]
    acc = [allb[5], allb[6]]
    psb = P.psum("psb", [128, 1024], BF16)
    rot = [0]
    rot_n = [5]

    def ps_next():
        b = allb[rot[0] % rot_n[0]]
        rot[0] += 1
        return b
    ident = P.sbuf("ident", [128, 128], F32)
    identb = P.sbuf("identb", [128, 128], BF16)
    ones_bf = P.sbuf("ones_bf", [128, 128], BF16)
    ones_f = P.sbuf("ones_f", [128, 64], F32)
    tri4 = P.sbuf("tri4", [128, 512], BF16)
    tri4s = P.sbuf("tri4s", [128, 64], BF16)
    rmask = P.sbuf("rmask", [128, 512], BF16)
    vecs = P.sbuf("vecs", [128, NV], F32)
    drv = P.sbuf("drv", [128, 48], F32)
    wgate = P.sbuf("wgate", [16, 512], BF16)
    wab = P.sbuf("wab", [128, 2 * 8 * 128], BF16)
    diag = P.sbuf("diag", [128, 3 * 4 * 128], BF16)
    DMA("sp", ident[:, :], ident_d[:, :], [], [ident], "C0")
    DMA("pool", tri4[:, :], tri4_d[:, :], [], [tri4], "C1")
    DMA("pool", tri4s[:, :], tri4s_d[:, :], [], [tri4s], "C1")
    DMA("pool", rmask[:, :], rmask_d[:, :], [], [rmask], "C1")
    DMA("sp", vecs[:, :], vecs_d[:, :], [], [vecs], "C0")
    DMA("pool", identb[:, :], ident_d[:, :], [], [identb], "C1")
    DMA("pool", wgate[:, :], w_gate[:, :], [], [wgate], "C1")
    DMA("pool", wab[:, 0:1024].rearrange("p (n d) -> p n d", d=128), lru_wa.rearrange("(n c) d -> c n d", c=128), [], [wab], "C1")
    DMA("pool", wab[:, 1024:2048].rearrange("p (n d) -> p n d", d=128), lru_wx.rearrange("(n c) d -> c n d", c=128), [], [wab], "C1")
    MEMSET("dve", ones_bf[:, :], 1.0, [ones_bf])
    MEMSET("dve", ones_f[:, :], 1.0, [ones_f])
    def vcol(name, i=0):
        o = VOFF[name] + i
        return vecs[:, o:o + 1]
    TS_("dve", drv[:, 0:4], vecs[:, VOFF["b_gate"]:VOFF["b_gate"] + 4], -1.0, None, OP.mult, None, [vecs], [drv])
    TS_("dve", drv[:, 4:12], vecs[:, VOFF["lru_ba"]:VOFF["lru_ba"] + 8], -1.0, None, OP.mult, None, [vecs], [drv])
    TS_("dve", drv[:, 12:20], vecs[:, VOFF["lru_bx"]:VOFF["lru_bx"] + 8], -1.0, None, OP.mult, None, [vecs], [drv])
    ACT(drv[:, 36:44], vecs[:, VOFF["lam"]:VOFF["lam"] + 8], AF.Exp, [vecs], [drv], scale=-1.0)
    ACT(drv[:, 36:44], drv[:, 36:44], AF.Ln, [drv], [drv], bias=1.0)
    TS_("dve", drv[:, 20:28], drv[:, 36:44], -8.0, None, OP.mult, None, [drv], [drv])
    TS_("dve", drv[:, 28:36], drv[:, 36:44], -16.0, None, OP.mult, None, [drv], [drv])
    NSLOT = 3
    SLOT = 4096
    slots = [P.sbuf("wslot%d" % i, [128, SLOT], BF16) for i in range(NSLOT)]
    sl_i = [0]
    def wload(src2d, kp, nk, ncols):
        i = sl_i[0] % NSLOT
        sl_i[0] += 1
        sb = slots[i]
        assert nk * ncols <= SLOT
        view = sb.t[0:kp, 0:nk * ncols].rearrange("p (k n) -> p k n", n=ncols)
        srcv = src2d.rearrange("(k p) n -> p k n", p=kp)
        kstep = max(1, 1024 // kp)
        k0 = 0
        while k0 < nk:
            k1 = min(nk, k0 + kstep)
            DMA("pool", view[:, k0:k1, :], srcv[:, k0:k1, :], [], [(sb, k0)], "W%d" % i)
            k0 = k1
        return sb, view
    def make_ctx(tag, Tn, is_s):
        c = Ctx()
        c.tag, c.T, c.s = tag, Tn, is_s
        c.tw = min(128, Tn)
        c.nt = Tn // c.tw
        c.C = c.tw
        c.pb = 0 if is_s else 64
        c.hT = P.sbuf("hT" + tag, [128, 8 * Tn], F32)
        c.uT = P.sbuf("uT" + tag, [128, 8 * Tn], BF16)
        c.F2 = P.sbuf("F2" + tag, [128, max(8 * Tn, 1024)], F32)
        c.cum = P.sbuf("cum" + tag, [128, 4 * Tn], F32)
        c.b1n = max(22 * Tn, 20 * Tn + 24, 8 * Tn + c.nt * 1536)
        c.B1 = P.sbuf("B1" + tag, [128, c.b1n], BF16)
        c.B2 = P.sbuf("B2" + tag, [128, 16 * Tn], BF16)
        c.tf = [P.sbuf("tf%d%s" % (i, tag), [128, Tn], F32) for i in range(6)]
        c.tb = [P.sbuf("tb%d%s" % (i, tag), [128, Tn], BF16) for i in range(2)]
        c.rstd = P.sbuf("rstd" + tag, [128, Tn], F32)
        c.S = P.sbuf("S" + tag, [128, 4 * 256], F32)
        c.Sbf = P.sbuf("Sbf" + tag, [128, 4 * 256], BF16)
        c.Se = P.sbuf("Se" + tag, [128, 256], F32)
        c.hst = P.sbuf("hst" + tag, [128, 8], F32)
        c.carry = P.sbuf("carry" + tag, [128, 8 * 3], BF16)
        c.convo = P.sbuf("convo" + tag, [128, 8 * 3], F32)
        c.ebl = P.sbuf("ebl" + tag, [128, 4 * c.nt], F32)
        c.rope = P.sbuf("rope" + tag, [96 if not is_s else 32, 2 * Tn], F32)
        c.atb = [P.sbuf("atb%d%s" % (i, tag), [128, 4 * c.tw], BF16) for i in range(2)]
        c.tbi = 0
        return c
    cp = make_ctx("p", T, False)
    cp.nb, cp.pend = allb[5], []
    cs = make_ctx("s", TS, True)
    cs.nb, cs.pend = allb[6], []
    ckvnb = P.sbuf("ckvnb", [128, 2 * 2048], BF16)
    KT = [P.sbuf("KT%d" % i, [96, 2048], BF16) for i in range(2)]
    Vp = P.sbuf("Vp", [128, 16 * 2 * 65], BF16)
    PTs = [P.sbuf("PT%d" % i, [128, 512], BF16) for i in range(3)]
    pt_i = [0]
    def pt_next():
        b = PTs[pt_i[0] % 3]
        pt_i[0] += 1
        return b
    MEMSET("pool", Vp[:, :], 1.0, [Vp])
    def v3(buf, off, nk, Tn, p0=0, p1=128):
        return buf.t[p0:p1, off:off + nk * Tn].rearrange("p (k t) -> p k t", t=Tn)
    MEMSET("dve", cp.S[:, :], 0.0, [cp.S])
    MEMSET("dve", cp.Sbf[:, :], 0.0, [cp.Sbf])
    MEMSET("dve", cp.hst[:, :], 0.0, [cp.hst])
    MEMSET("dve", cp.carry[:, :], 0.0, [cp.carry])
    DMA("sp", cs.S[:, :].rearrange("p (h v) -> p h v", v=256), gla_s.rearrange("h k v -> k h v"), [], [cs.S], "C0")
    CP("act", cs.Sbf[:, :], cs.S[:, :], [cs.S], [cs.Sbf])
    CP("dve", cs.hst[:, :], vecs[:, VOFF["h0"]:VOFF["h0"] + 8], [vecs], [cs.hst])
    CP("dve", cs.carry[:, :], vecs[:, VOFF["convs"]:VOFF["convs"] + 24], [vecs], [cs.carry])
    DMA("sp", cs.rope[0:32, :], ropeS_d[:, :], [], [cs.rope], "C0")
    def rms(c, src, off, nk, nfeat, keyed=False):
        Tn = c.T
        ps = ps_next()
        for k in range(nk):
            sq = c.tb[c.tbi % 2]
            c.tbi += 1
            ACT(sq[:, :], src.t[:, off + k * Tn: off + (k + 1) * Tn], AF.Square, [(src, k)] if keyed else [src], [sq])
            MM(ps, ps[:, 0:Tn], ones_bf[:, :], sq[:, :], k == 0, k == nk - 1, [ones_bf, sq])
        ACT(c.rstd[:, :], ps[:, 0:Tn], AF.Ln, [ps], [c.rstd], bias=EPS, scale=1.0 / nfeat)
        ACT(c.rstd[:, :], c.rstd[:, :], AF.Exp, [c.rstd], [c.rstd], scale=-0.5)
    def load_x_dma(c, src, row0):
        tw = c.tw
        xb32 = c.B2.t[:, :].bitcast(F32)
        for t in range(c.nt):
            DMA("sp", xb32[0:tw, t * 1024:(t + 1) * 1024], src[row0 + t * tw: row0 + (t + 1) * tw, :], [], [c.B2], "IO" + c.tag)

    def load_x(c, src, row0, prefetched=False):
        Tn, tw = c.T, c.tw
        if c.s:
            buf, bt = c.F2, c.F2.t
        else:
            buf, bt = c.B2, c.B2.t[:, :].bitcast(F32)
            if not prefetched:
                load_x_dma(c, src, row0)
        for t in range(c.nt):
            if c.s:
                DMA("sp", bt[0:tw, t * 1024:(t + 1) * 1024], src[row0 + t * tw: row0 + (t + 1) * tw, :], [], [buf], "IO" + c.tag)
            for kq in range(2):
                ps = ps_next()
                for kk in range(4):
                    kc = kq * 4 + kk
                    TR(ps, ps[:, kk * tw:(kk + 1) * tw], bt[0:tw, t * 1024 + kc * 128: t * 1024 + (kc + 1) * 128],
                       ident[0:tw, 0:tw], [buf, ident])
                CP("act" if kq == 0 else "dve", v3(c.hT, 0, 8, Tn)[:, kq * 4:(kq + 1) * 4, t * tw:(t + 1) * tw],
                   ps[:, 0:4 * tw].rearrange("p (k t) -> p k t", t=tw), [ps], [c.hT])
    def store_y(c, dst, row0):
        Tn, tw = c.T, c.tw
        fence(c.F2)
        for t in range(c.nt):
            for kq in range(2):
                ps = ps_next()
                for kk in range(4):
                    kc = kq * 4 + kk
                    TR(ps, ps[0:tw, kk * 128:(kk + 1) * 128], c.hT.t[:, kc * Tn + t * tw: kc * Tn + (t + 1) * tw],
                       ident[:, :], [c.hT, ident])
                CP("act" if kq == 0 else "dve", c.F2.t[0:tw, t * 1024 + kq * 512: t * 1024 + (kq + 1) * 512], ps[0:tw, :],
                   [ps], [(c.F2, ("io", t))])
            DMA("sp", dst[row0 + t * tw: row0 + (t + 1) * tw, :], c.F2.t[0:tw, t * 1024:(t + 1) * 1024],
                [(c.F2, ("io", t))], [], "IO" + c.tag)
    def prenorm(c, gname):
        Tn = c.T
        rms(c, c.hT, 0, 8, 1024, keyed=True)
        for k in range(8):
            STT(c.uT.t[:, k * Tn:(k + 1) * Tn], c.hT.t[:, k * Tn:(k + 1) * Tn], vcol(gname, k), c.rstd[:, :],
                OP.mult, OP.mult, [(c.hT, k), c.rstd, vecs], [(c.uT, k)])
    def evac_sq(c, ps, dc):
        Tn = c.T
        CP("act", c.F2.t[:, dc * Tn:(dc + 1) * Tn], ps[:, 0:Tn], [ps], [c.F2])
        sq = c.tb[c.tbi % 2]
        c.tbi += 1
        ACT(sq[:, :], ps[:, 0:Tn], AF.Square, [ps], [sq])
        c.pend.append((sq, dc == 0, dc == 7))
        flush_sq(c, 1)

    def flush_sq(c, keep):
        Tn = c.T
        while len(c.pend) > keep:
            sq, f_, l_ = c.pend.pop(0)
            MM(c.nb, c.nb[:, 0:Tn], ones_bf[:, :], sq[:, :], f_, l_, [ones_bf, sq])

    def postnorm_add(c, gname):
        Tn = c.T
        flush_sq(c, 0)
        ACT(c.rstd[:, :], c.nb[:, 0:Tn], AF.Ln, [c.nb], [c.rstd], bias=EPS, scale=1.0 / 1024)
        ACT(c.rstd[:, :], c.rstd[:, :], AF.Exp, [c.rstd], [c.rstd], scale=-0.5)
        for k in range(8):
            tmp = c.tf[k % 2]
            TT("dve", tmp[:, :], c.F2.t[:, k * Tn:(k + 1) * Tn], c.rstd[:, :], OP.mult, [c.F2, c.rstd], [tmp])
            STT(c.hT.t[:, k * Tn:(k + 1) * Tn], tmp[:, :], vcol(gname, k), c.hT.t[:, k * Tn:(k + 1) * Tn], OP.mult, OP.add,
                [tmp, (c.hT, k), vecs], [(c.hT, k)])
    def fence(buf):
        P.op("dve", lambda e: e.memset(drv[:, 47:48], 0.0), writes=[buf, (drv, "f")])
    def ffn(ctxs, layer):
        for c in ctxs:
            prenorm(c, "ffn_pre%d" % layer)
            fence(c.B1)
        col = 0
        while col < DFF:
            nc_ = min(512, DFF - col)
            sg, wg = wload(ffn_wg[layer * D:(layer + 1) * D, col:col + nc_], 128, 8, nc_)
            su, wu = wload(ffn_wu[layer * D:(layer + 1) * D, col:col + nc_], 128, 8, nc_)
            for c in ctxs:
                Tn = c.T
                nj = nc_ // 128
                for j in range(nj):
                    pg = ps_next()
                    for k in range(8):
                        MM(pg, pg[:, 0:Tn], wg[:, k, j * 128:(j + 1) * 128], c.uT.t[:, k * Tn:(k + 1) * Tn], k == 0, k == 7, [sg, (c.uT, k)])
                    tmp = c.tf[2 + j]
                    ACT(tmp[:, :], pg[:, 0:Tn], AF.Silu, [pg], [tmp])
                for j in range(nj):
                    fj = col // 128 + j
                    pu = ps_next()
                    for k in range(8):
                        MM(pu, pu[:, 0:Tn], wu[:, k, j * 128:(j + 1) * 128], c.uT.t[:, k * Tn:(k + 1) * Tn], k == 0, k == 7, [su, (c.uT, k)])
                    tmp = c.tf[2 + j]
                    TT("dve", c.B1.t[:, fj * Tn:(fj + 1) * Tn], tmp[:, :], pu[:, 0:Tn], OP.mult, [tmp, pu], [(c.B1, ("ff", fj))])
            col += nc_
        for dc in range(8):
            sd, wd = wload(ffn_wd[layer * DFF:(layer + 1) * DFF, dc * 128:(dc + 1) * 128], 128, NFF, 128)
            for c in ctxs:
                Tn = c.T
                ps = ps_next()
                for f in range(NFF):
                    MM(ps, ps[:, 0:Tn], wd[:, f, :], c.B1.t[:, f * Tn:(f + 1) * Tn], f == 0, f == NFF - 1, [sd, (c.B1, ("ff", f))])
                evac_sq(c, ps, dc)
        for c in ctxs:
            postnorm_add(c, "ffn_post%d" % layer)
    def mixer_ab(ctxs, last):
        for c in ctxs:
            prenorm(c, "mix_pre0")
            fence(c.B1)
            fence(c.F2)
        wz = P_wz
        DMA("pool", wz.t[:, :].rearrange("p (k n) -> p k n", n=16), w_in_ab[:, 5120:5136].rearrange("(k p) n -> p k n", p=128), [], [wz], "C1")
        for c in ctxs:
            Tn = c.T
            ps = ps_next()
            for k in range(8):
                MM(ps, ps[0:16, 0:Tn], wz.t[:, k * 16:(k + 1) * 16], c.uT.t[:, k * Tn:(k + 1) * Tn], k == 0, k == 7, [wz, (c.uT, k)])
            zr = c.tb[0]
            CP("act", zr[0:16, :], ps[0:16, 0:Tn], [ps], [zr])
            for h in range(4):
                pz = ps_next()
                MM(pz, pz[:, 0:Tn], wgate[0:16, h * 128:(h + 1) * 128], zr[0:16, :], True, True, [wgate, zr])
                e1 = c.tf[0]
                ACT(e1[:, :], pz[:, 0:Tn], AF.Exp, [pz, drv], [e1], bias=drv[:, h:h + 1], scale=-1.0)
                ACT(e1[:, :], e1[:, :], AF.Ln, [e1], [e1], bias=1.0)
                msk = rmask[:, 0:Tn] if not c.s else rmask[:, 1:1 + Tn]
                SCAN(c.cum.t[:, h * Tn:(h + 1) * Tn], msk, e1[:, :], 0.0, [rmask, e1], [(c.cum, h)])
                ACT(c.ebl.t[:, h * c.nt:(h + 1) * c.nt],
                    c.cum.t[:, h * Tn:(h + 1) * Tn].rearrange("p (n c) -> p n c", c=c.C)[:, :, c.C - 1],
                    AF.Exp, [(c.cum, h)], [(c.ebl, h)], scale=-1.0 / 16.0)
        for half in range(2):
            sx, wx = wload(w_in_ab[:, half * 512:(half + 1) * 512], 128, 8, 512)
            sgw, wgl = wload(w_in_ab[:, 1024 + half * 512:1024 + (half + 1) * 512], 128, 8, 512)
            sqk, wqk = wload(w_in_ab[:, 2048 + half * 512:2560 + half * 512], 128, 8, 512)

            def qk_head(c, h):
                Tn = c.T
                ps = ps_next()
                for k in range(8):
                    MM(ps, ps[:, 0:Tn], wqk[:, k, h * 128:(h + 1) * 128], c.uT.t[:, k * Tn:(k + 1) * Tn], k == 0, k == 7, [sqk, (c.uT, k)])
                e1 = c.rstd
                if half == 0:
                    ACT(e1[:, :], c.cum.t[:, h * Tn:(h + 1) * Tn], AF.Exp, [(c.cum, h)], [e1], scale=-1.0 / 16.0)
                    STT(c.B1.t[:, h * Tn:(h + 1) * Tn], ps[:, 0:Tn], 128.0 ** -0.5, e1[:, :], OP.mult, OP.mult, [ps, e1], [(c.B1, ("q", h))])
                else:
                    ACT(e1[:, :], c.cum.t[:, h * Tn:(h + 1) * Tn], AF.Exp, [(c.cum, h)], [e1], scale=1.0 / 16.0)
                    TT("dve", c.B1.t[:, 4 * Tn + h * Tn: 4 * Tn + (h + 1) * Tn], ps[:, 0:Tn], e1[:, :], OP.mult, [ps, e1], [(c.B1, ("k", h))])
            for c in ctxs:
                Tn = c.T
                XO = 8 * Tn
                GO = 16 * Tn + 24
                for j in range(4):
                    kc = half * 4 + j
                    xo = XO + kc * (Tn + 3)
                    ps = ps_next()
                    for k in range(8):
                        MM(ps, ps[:, 0:Tn], wx[:, k, j * 128:(j + 1) * 128], c.uT.t[:, k * Tn:(k + 1) * Tn], k == 0, k == 7, [sx, (c.uT, k)])
                    CP("act", c.B1.t[:, xo + 3: xo + 3 + Tn], ps[:, 0:Tn], [ps], [(c.B1, ("xa", kc))])
                    CP("act", c.B1.t[:, xo: xo + 3], c.carry.t[:, kc * 3:(kc + 1) * 3], [(c.carry, kc)], [(c.B1, ("xa", kc))])
                    if last or c.s:
                        CP("dve", c.convo.t[:, kc * 3:(kc + 1) * 3], ps[:, Tn - 3:Tn], [ps], [(c.convo, kc)])
                def tv(i):
                    if i < 6:
                        return c.tf[i].t[:, 0:Tn], c.tf[i]
                    return c.F2.t[:, (i - 6) * Tn:(i - 5) * Tn], (c.F2, ("t", i - 6))
                gt = [tv(11), tv(12), tv(13), tv(5)]
                gps = []
                for j in range(4):
                    ps = ps_next()
                    gps.append(ps)
                    for k in range(8):
                        MM(ps, ps[:, 0:Tn], wgl[:, k, j * 128:(j + 1) * 128], c.uT.t[:, k * Tn:(k + 1) * Tn], k == 0, k == 7, [sgw, (c.uT, k)])
                    ACT(c.B1.t[:, GO + j * Tn: GO + (j + 1) * Tn], ps[:, 0:Tn], AF.Gelu_apprx_tanh, [ps], [(c.B1, ("gl", j))])
                st_ = {}
                XS = [0, 6, 5]

                def build_diag(j):
                    kc = half * 4 + j
                    dg = (kc % 3) * 512
                    for tap in range(4):
                        TS_("dve", diag[:, dg + tap * 128: dg + (tap + 1) * 128], identb[:, :], vcol("conv_w", tap * 8 + kc), None, OP.mult, None,
                            [identb, vecs], [(diag, (kc % 3, tap))])

                def prep_conv(j):
                    kc = half * 4 + j
                    xo = XO + kc * (Tn + 3)
                    dg = (kc % 3) * 512
                    pc = allb[5 + (kc % 2)]
                    for tap in range(4):
                        MM(pc, pc[:, 0:Tn], diag[:, dg + tap * 128: dg + (tap + 1) * 128], c.B1.t[:, xo + tap: xo + tap + Tn], tap == 0, tap == 3,
                           [(diag, (kc % 3, tap)), (c.B1, ("xa", kc))])
                    CP("act", c.carry.t[:, kc * 3:(kc + 1) * 3], c.B1.t[:, xo + Tn: xo + Tn + 3], [(c.B1, ("xa", kc))], [(c.carry, kc)])
                    base = 0 if kc % 2 == 0 else 6
                    tvs = [tv(XS[kc % 3])] + [tv(base + q) for q in range(1, 5)]
                    xb = c.tb[kc % 2]
                    TS_("dve", xb[:, :], pc[:, 0:Tn], vcol("conv_b", kc), None, OP.add, None, [pc, vecs], [xb])
                    st_[j] = [tvs, xb, None, None, pc]

                def gates(j):
                    kc = half * 4 + j
                    xb = st_[j][1]
                    pr = ps_next()
                    pi = ps_next()
                    MM(pr, pr[:, 0:Tn], wab[:, kc * 128:(kc + 1) * 128], xb[:, :], True, True, [wab, xb])
                    MM(pi, pi[:, 0:Tn], wab[:, 1024 + kc * 128:1024 + (kc + 1) * 128], xb[:, :], True, True, [wab, xb])
                    st_[j][2], st_[j][3] = pr, pi

                def chain_head(j):
                    kc = half * 4 + j
                    (X, Xd), (R, Rd), (I, Id), (A, Ad), (M, Md) = st_[j][0]
                    pr, pi = st_[j][2], st_[j][3]
                    ACT(R, pr[:, 0:Tn], AF.Exp, [pr, drv], [Rd], bias=drv[:, 4 + kc:5 + kc], scale=-1.0)
                    ACT(I, pi[:, 0:Tn], AF.Exp, [pi, drv], [Id], bias=drv[:, 12 + kc:13 + kc], scale=-1.0)

                def chain_tail(j):
                    kc = half * 4 + j
                    (X, Xd), (R, Rd), (I, Id), (A, Ad), (M, Md) = st_[j][0]
                    ACT(R, R, AF.Ln, [Rd], [Rd], bias=1.0)
                    ACT(R, R, AF.Exp, [Rd], [Rd], scale=-1.0)
                    ACT(A, R, AF.Exp, [Rd, drv], [Ad], scale=drv[:, 20 + kc:21 + kc])
                    TT("dve", M, A, A, OP.mult, [Ad], [Md])
                    ACT(I, I, AF.Ln, [Id], [Id], bias=1.0)
                    ACT(I, I, AF.Exp, [Id], [Id], scale=-1.0)
                    pc = st_[j][4]
                    STT(I, pc[:, 0:Tn], vcol("conv_b", kc), I, OP.add, OP.mult, [pc, vecs, Id], [Id])
                    ACT(M, M, AF.Ln, [Md], [Md], bias=1.0, scale=-1.0)
                    ACT(M, M, AF.Exp, [Md], [Md], scale=0.5)
                    TT("dve", I, I, M, OP.mult, [Id, Md], [Id])
                    SCAN(X, A, I, c.hst.t[:, kc:kc + 1], [Ad, Id, (c.hst, kc), Xd], [Xd])
                    CP("dve", c.hst.t[:, kc:kc + 1], X[:, Tn - 1:Tn], [Xd], [(c.hst, kc)])
                    TT("dve", c.B2.t[:, kc * Tn:(kc + 1) * Tn], X, c.B1.t[:, GO + j * Tn: GO + (j + 1) * Tn], OP.mult,
                       [Xd, (c.B1, ("gl", j))], [(c.B2, kc)])

                build_diag(0)
                build_diag(1)
                prep_conv(0)
                gates(0)
                for j in range(4):
                    if j + 2 < 4:
                        build_diag(j + 2)
                    if j + 1 < 4:
                        prep_conv(j + 1)
                    chain_head(j)
                    if j + 1 < 4:
                        gates(j + 1)
                    qk_head(c, j)
                    chain_tail(j)
        for c in ctxs:
            fence(c.F2)
        for c in ctxs:
            fence(c.B1)
        for c in ctxs:
            Tn, tw = c.T, c.tw
            KO = 4 * Tn
            KTO = 8 * Tn
            for t in range(c.nt):
                for h in range(4):
                    TR(psb, psb[0:tw, h * 128:(h + 1) * 128], c.B1.t[:, KO + h * Tn + t * tw: KO + h * Tn + (t + 1) * tw], identb[:, :],
                       [(c.B1, ("k", h)), identb])
                CP("act", c.B1.t[0:tw, KTO + t * 512: KTO + (t + 1) * 512], psb[0:tw, 0:512], [psb], [(c.B1, ("kt", t))])
        for vb in range(2):
            sv_, wv_ = wload(w_in_ab[:, 3072 + vb * 512:3072 + (vb + 1) * 512], 128, 8, 512)
            for c in ctxs:
                Tn, tw = c.T, c.tw
                VO = 8 * Tn + c.nt * 512
                for t in range(c.nt):
                    ps = ps_next()
                    for k in range(8):
                        MM(ps, ps[0:tw, :], c.uT.t[:, k * Tn + t * tw: k * Tn + (t + 1) * tw], wv_[:, k, :], k == 0, k == 7, [sv_, (c.uT, k)])
                    CP("act" if t % 2 == 0 else "dve", c.B1.t[0:tw, VO + t * 1024 + vb * 512: VO + t * 1024 + (vb + 1) * 512], ps[0:tw, :],
                       [ps], [(c.B1, ("v", t))])
        for c in ctxs:
            Tn, tw, C = c.T, c.tw, c.C
            KO, KTO = 4 * Tn, 8 * Tn
            VO = 8 * Tn + c.nt * 512
            trm = tri4 if not c.s else tri4s
            def a_mask(t):
                pa = ps_next()
                for h in range(4):
                    MM(pa, pa[0:C, h * C:(h + 1) * C], c.B1.t[:, KO + h * Tn + t * C: KO + h * Tn + (t + 1) * C],
                       c.B1.t[:, h * Tn + t * C: h * Tn + (t + 1) * C], True, True, [(c.B1, ("k", h)), (c.B1, ("q", h))])
                at = c.atb[t % 2]
                TT("dve", at[0:C, 0:4 * C], pa[0:C, 0:4 * C], trm[0:C, 0:4 * C], OP.mult, [pa, trm], [at])
                return at
            at = a_mask(0)
            for t in range(c.nt):
                for hp in range(2):
                    po = ps_next()
                    for hh in range(2):
                        h = hp * 2 + hh
                        for j in range(2):
                            c0 = (hh * 2 + j) * C
                            MM(po, po[:, c0:c0 + C], c.B1.t[0:C, VO + t * 1024 + h * 256 + j * 128: VO + t * 1024 + h * 256 + (j + 1) * 128],
                               at[0:C, h * C:(h + 1) * C], (hh == 0 and j == 0), False, [(c.B1, ("v", t)), at], skip=True)
                    for hh in range(2):
                        h = hp * 2 + hh
                        for j in range(2):
                            c0 = (hh * 2 + j) * C
                            MM(po, po[:, c0:c0 + C], c.Sbf.t[:, h * 256 + j * 128: h * 256 + (j + 1) * 128],
                               c.B1.t[:, h * Tn + t * C: h * Tn + (t + 1) * C], False, True, [(c.Sbf, h), (c.B1, ("q", h))], skip=True)
                    CP("act", v3(c.F2, 0, 8, Tn)[:, hp * 4:(hp + 1) * 4, t * C:(t + 1) * C],
                       po[:, 0:4 * C].rearrange("p (k t) -> p k t", t=C), [po], [(c.F2, ("o", hp))])
                pds = []
                for hp in range(2):
                    pd = ps_next()
                    pds.append(pd)
                    for hh in range(2):
                        h = hp * 2 + hh
                        MM(pd, pd[:, hh * 256:(hh + 1) * 256], c.B1.t[0:C, KTO + t * 512 + h * 128: KTO + t * 512 + (h + 1) * 128],
                           c.B1.t[0:C, VO + t * 1024 + h * 256: VO + t * 1024 + (h + 1) * 256], True, True,
                           [(c.B1, ("kt", t)), (c.B1, ("v", t))])
                if t + 1 < c.nt:
                    at = a_mask(t + 1)
                for hp in range(2):
                    pd = pds[hp]
                    for hh in range(2):
                        h = hp * 2 + hh
                        eb = c.ebl.t[:, h * c.nt + t: h * c.nt + t + 1]
                        Sh = c.S.t[:, h * 256:(h + 1) * 256]
                        TT("dve", Sh, Sh, pd[:, hh * 256:(hh + 1) * 256], OP.add, [(c.S, h), pd], [(c.S, h)])
                        ACT(c.Sbf.t[:, h * 256:(h + 1) * 256], Sh, AF.Copy, [(c.S, h), (c.ebl, h)], [(c.Sbf, h)], scale=eb)
                        TS_("dve", Sh, Sh, eb, None, OP.mult, None, [(c.S, h), (c.ebl, h)], [(c.S, h)])
            pss_ = []
            for h in range(4):
                ps = ps_next()
                pss_.append(ps)
                for j in range(2):
                    sq = c.tb[c.tbi % 2]
                    c.tbi += 1
                    ACT(sq[:, :], c.F2.t[:, (2 * h + j) * Tn:(2 * h + j + 1) * Tn], AF.Square, [c.F2], [sq])
                    MM(ps, ps[:, 0:Tn], ones_bf[:, :], sq[:, :], j == 0, j == 1, [ones_bf, sq])
            for h in range(4):
                ACT(c.tf[h][:, :], pss_[h][:, 0:Tn], AF.Ln, [pss_[h]], [c.tf[h]], bias=EPS, scale=1.0 / 256)
            for h in range(4):
                ACT(c.tf[h][:, :], c.tf[h][:, :], AF.Exp, [c.tf[h]], [c.tf[h]], scale=-0.5)
            for h in range(4):
                for j in range(2):
                    hj = 2 * h + j
                    STT(c.F2.t[:, hj * Tn:(hj + 1) * Tn], c.F2.t[:, hj * Tn:(hj + 1) * Tn], vcol("gla_norm", j), c.tf[h][:, :],
                        OP.mult, OP.mult, [c.F2, c.tf[h], vecs], [(c.F2, ("o", h // 2))])
        for gbk in range(2):
            sb_, wb_ = wload(w_in_ab[:, 4096 + gbk * 512:4096 + (gbk + 1) * 512], 128, 8, 512)
            for c in ctxs:
                Tn = c.T
                for j in range(4):
                    hj = gbk * 4 + j
                    ps = ps_next()
                    for k in range(8):
                        MM(ps, ps[:, 0:Tn], wb_[:, k, j * 128:(j + 1) * 128], c.uT.t[:, k * Tn:(k + 1) * Tn], k == 0, k == 7, [sb_, (c.uT, k)])
                    tmp = c.tf[4 + j % 2]
                    ACT(tmp[:, :], ps[:, 0:Tn], AF.Silu, [ps], [tmp])
                    TT("dve", c.B2.t[:, (8 + hj) * Tn:(9 + hj) * Tn], tmp[:, :], c.F2.t[:, hj * Tn:(hj + 1) * Tn], OP.mult,
                       [tmp, c.F2], [(c.B2, 8 + hj)])
        for blk in range(4):
            so_, wo_ = wload(w_out_ab[:, blk * 256:(blk + 1) * 256], 128, 16, 256)
            for c in ctxs:
                Tn = c.T
                for dl in range(2):
                    dc = blk * 2 + dl
                    ps = ps_next()
                    for k in range(16):
                        MM(ps, ps[:, 0:Tn], wo_[:, k, dl * 128:(dl + 1) * 128], c.B2.t[:, k * Tn:(k + 1) * Tn], k == 0, k == 15, [so_, (c.B2, k)])
                    evac_sq(c, ps, dc)
        for c in ctxs:
            postnorm_add(c, "mix_post0")
    P_wz = P.sbuf("wz", [128, 8 * 16], BF16)
    def out_states(c, idx):
        for half in range(2):
            ps = ps_next()
            for kk in range(4):
                kc = half * 4 + kk
                TR(ps, ps[0:3, kk * 128:(kk + 1) * 128], c.convo.t[:, kc * 3:(kc + 1) * 3], ident[:, :], [c.convo, ident])
            CP("act", c.F2.t[0:3, half * 512:(half + 1) * 512], ps[0:3, 0:512], [ps], [c.F2])
        DMA("sp", o_conv[idx][:, :], c.F2.t[0:3, 0:1024], [c.F2], [], "O" + c.tag)
        ps = ps_next()
        TR(ps, ps[0:8, 0:128], c.hst.t[:, 0:8], ident[:, :], [c.hst, ident])
        hh = c.Se
        CP("act", hh[0:8, 0:128], ps[0:8, 0:128], [ps], [hh])
        DMA("sp", o_h[idx][:, :], hh[0:8, 0:128], [hh], [], "O" + c.tag)
        DMA("sp", o_S[idx].rearrange("h k v -> k h v"), c.S.t[:, :].rearrange("p (h v) -> p h v", v=256), [c.S], [], "O" + c.tag)
    def mla_proj(ctxs, g):
        for c in ctxs:
            prenorm(c, "mix_pre1")
            fence(c.B1)
        s1, w1 = wload(w_in_c[:, 0:384], 128, 8, 384)
        s2, w2 = wload(w_in_c[:, 384:768], 128, 8, 384)
        for c in ctxs:
            Tn, tw, pb = c.T, c.tw, c.pb
            for k3 in range(3):
                ps = ps_next()
                for k in range(8):
                    MM(ps, ps[:, 0:Tn], w1[:, k, k3 * 128:(k3 + 1) * 128], c.uT.t[:, k * Tn:(k + 1) * Tn], k == 0, k == 7, [s1, (c.uT, k)])
                CP("act", c.F2.t[:, k3 * Tn:(k3 + 1) * Tn], ps[:, 0:Tn], [ps], [(c.F2, ("cq", k3))])
            for k2 in range(2):
                ps = ps_next()
                for k in range(8):
                    MM(ps, ps[:, 0:Tn], w2[:, k, k2 * 128:(k2 + 1) * 128], c.uT.t[:, k * Tn:(k + 1) * Tn], k == 0, k == 7, [s2, (c.uT, k)])
                CP("act", c.F2.t[:, (3 + k2) * Tn:(4 + k2) * Tn], ps[:, 0:Tn], [ps], [(c.F2, ("ckv", k2))])
            if SUB < 1:
                continue
            pA = ps_next()
            pB = ps_next()
            if not c.s:
                cA, cB, M_ = (192, 288), (288, 384), 96
            else:
                cA, cB, M_ = (256, 288), (352, 384), 32
            for k in range(8):
                MM(pA, pA[0:M_, 0:Tn], w2[:, k, cA[0]:cA[1]], c.uT.t[:, k * Tn:(k + 1) * Tn], k == 0, k == 7, [s2, (c.uT, k)])
            for k in range(8):
                MM(pB, pB[0:M_, 0:Tn], w2[:, k, cB[0]:cB[1]], c.uT.t[:, k * Tn:(k + 1) * Tn], k == 0, k == 7, [s2, (c.uT, k)])
            if not c.s:
                DMA("sp", c.rope.t[64:96, :].rearrange("p (a t) -> p a t", t=Tn),
                    ropeP_d.rearrange("p (a t) -> p a t", t=2048)[:, :, g * T:(g + 1) * T], [], [c.rope], "R" + c.tag)
            t1, t2 = c.tf[0], c.tf[1]
            kpr = c.F2.t[pb:pb + 32, 5 * Tn:6 * Tn]
            TT("dve", t1[pb:pb + 32, :], pA[pb:pb + 32, 0:Tn], c.rope.t[pb:pb + 32, 0:Tn], OP.mult, [pA, c.rope], [t1])
            TT("dve", t2[pb:pb + 32, :], pB[pb:pb + 32, 0:Tn], c.rope.t[pb:pb + 32, Tn:2 * Tn], OP.mult, [pB, c.rope], [t2])
            TT("dve", kpr, t1[pb:pb + 32, :], t2[pb:pb + 32, :], OP.add, [t1, t2], [(c.F2, "kpr")])
            if not c.s:
                for i in range(2):
                    CP("act", KT[i].t[64:96, g * T:(g + 1) * T], kpr, [(c.F2, "kpr")], [(KT[i], "pe")])
            else:
                CP("act", c.kprb.t[0:32, 0:Tn], kpr, [(c.F2, "kpr")], [c.kprb])
            if SUB < 2:
                continue
            rms(c, c.F2, 0, 3, 384)
            for k3 in range(3):
                STT(c.uT.t[:, k3 * Tn:(k3 + 1) * Tn], c.F2.t[:, k3 * Tn:(k3 + 1) * Tn], vcol("q_norm", k3), c.rstd[:, :], OP.mult, OP.mult,
                    [(c.F2, ("cq", k3)), c.rstd, vecs], [c.uT])
            rms(c, c.F2, 3 * Tn, 2, 256)
            for k2 in range(2):
                STT(c.F2.t[:, (3 + k2) * Tn:(4 + k2) * Tn], c.F2.t[:, (3 + k2) * Tn:(4 + k2) * Tn], vcol("kv_norm", k2), c.rstd[:, :],
                    OP.mult, OP.mult, [(c.F2, ("ckv", k2)), c.rstd, vecs], [(c.F2, ("ckv", k2))])
                if not c.s:
                    CP("act", ckvnb.t[:, k2 * 2048 + g * T: k2 * 2048 + (g + 1) * T], c.F2.t[:, (3 + k2) * Tn:(4 + k2) * Tn],
                       [(c.F2, ("ckv", k2))], [(ckvnb, g)])
                else:
                    CP("act", c.ckvb.t[:, k2 * Tn:(k2 + 1) * Tn], c.F2.t[:, (3 + k2) * Tn:(4 + k2) * Tn], [(c.F2, ("ckv", k2))], [c.ckvb])
            if SUB < 3:
                continue
            idx = 1 if c.s else 0
            for t in range(c.nt):
                ps = ps_next()
                for k2 in range(2):
                    TR(ps, ps[0:tw, k2 * 128:(k2 + 1) * 128], c.F2.t[:, (3 + k2) * Tn + t * tw:(3 + k2) * Tn + (t + 1) * tw], ident[:, :],
                       [(c.F2, ("ckv", k2)), ident])
                if KPEOUT:
                    MM(ps, ps[0:tw, 256:288], c.F2.t[pb:pb + 32, 5 * Tn + t * tw: 5 * Tn + (t + 1) * tw], ident[pb:pb + 32, pb:pb + 32],
                       True, True, [(c.F2, "kpr"), ident])
                st = c.ost[t % 2]
                CP("act", st[0:tw, 0:288], ps[0:tw, 0:288], [ps], [st])
                r0 = (g * T if not c.s else 0) + t * tw
                if OUTV >= 2:
                    DMA("sp", o_ckv[idx][r0:r0 + tw, :], st[0:tw, 0:256], [st], [], "O" + c.tag)
                if OUTV >= 3:
                    DMA("sp", o_kpe[idx][r0:r0 + tw, :], st[0:tw, 256:288], [st], [], "O" + c.tag)
                if c.s:
                    CP("dve", c.ckvtok.t[0:tw, 0:256], ps[0:tw, 0:256], [ps], [c.ckvtok])
        if SUB < 4:
            return
        for c in ctxs:
            fence(c.B1)
        for hf in range(4):
            sa, wcat = wload(w_uq_cat[:, hf * 768:(hf + 1) * 768], 128, 3, 768)
            sb2 = sa
            wa_ = wcat[:, :, 0:384]
            wb2 = wcat[:, :, 384:768]
            for c in ctxs:
                Tn = c.T
                for hl in range(4):
                    h = hf * 4 + hl
                    if not c.s:
                        pA = ps_next()
                        pB = ps_next()
                        for k in range(3):
                            MM(pA, pA[0:96, 0:Tn], wa_[:, k, hl * 96:(hl + 1) * 96], c.uT.t[:, k * Tn:(k + 1) * Tn], k == 0, k == 2, [sa, c.uT])
                        for k in range(3):
                            MM(pB, pB[0:96, 0:Tn], wb2[:, k, hl * 96:(hl + 1) * 96], c.uT.t[:, k * Tn:(k + 1) * Tn], k == 0, k == 2, [sb2, c.uT])
                        CP("act", c.B1.t[0:64, h * Tn:(h + 1) * Tn], pA[0:64, 0:Tn], [pA], [(c.B1, ("Q", h))])
                        t1, t2 = c.tf[0], c.tf[1]
                        TT("dve", t1[64:96, :], pA[64:96, 0:Tn], c.rope.t[64:96, 0:Tn], OP.mult, [pA, c.rope], [t1])
                        TT("dve", t2[64:96, :], pB[64:96, 0:Tn], c.rope.t[64:96, Tn:2 * Tn], OP.mult, [pB, c.rope], [t2])
                        TT("dve", c.B1.t[64:96, h * Tn:(h + 1) * Tn], t1[64:96, :], t2[64:96, :], OP.add, [t1, t2], [(c.B1, ("Q", h))])
                    else:
                        pq = ps_next()
                        for k in range(3):
                            MM(pq, pq[0:64, 0:16], wa_[:, k, hl * 96:hl * 96 + 64], c.uT.t[:, k * Tn:(k + 1) * Tn], k == 0, k == 2, [sa, c.uT])
                        for k in range(3):
                            MM(pq, pq[0:32, 16:32], wa_[:, k, hl * 96 + 64:hl * 96 + 96], c.uT.t[:, k * Tn:(k + 1) * Tn], k == 0, k == 2, [sa, c.uT])
                        for k in range(3):
                            MM(pq, pq[0:32, 32:48], wb2[:, k, hl * 96 + 64:hl * 96 + 96], c.uT.t[:, k * Tn:(k + 1) * Tn], k == 0, k == 2, [sb2, c.uT])
                        CP("act", c.qn.t[0:64, h * 16:(h + 1) * 16], pq[0:64, 0:16], [pq], [c.qn])
                        t1, t2 = c.tf[0], c.tf[1]
                        TT("dve", t1[0:32, :], pq[0:32, 16:32], c.rope.t[0:32, 0:Tn], OP.mult, [pq, c.rope], [t1])
                        TT("dve", t2[0:32, :], pq[0:32, 32:48], c.rope.t[0:32, Tn:2 * Tn], OP.mult, [pq, c.rope], [t2])
                        TT("dve", c.qpe.t[0:32, h * 16:(h + 1) * 16], t1[0:32, :], t2[0:32, :], OP.add, [t1, t2], [c.qpe])
    def mla_attn_prompt(c, g, suk, wuk, suv, wuv):
        Tn = c.T
        nkb = g + 1
        nkt = 4 * (g + 1)
        rot_n[0] = 3
        Vp4 = Vp.t[:, 0:2048].rearrange("p (k m) -> p k m", m=128)

        def kt_recompute(h):
            hh = h % 2
            for kb in range(nkb):
                ps = ps_next()
                for cc in range(2):
                    MM(ps, ps[0:64, 0:512], wuk[:, cc, h * 64:(h + 1) * 64], ckvnb.t[:, cc * 2048 + kb * 512: cc * 2048 + (kb + 1) * 512],
                       cc == 0, cc == 1, [suk, (ckvnb, kb)])
                CP("dve", KT[hh].t[0:64, kb * 512:(kb + 1) * 512], ps[0:64, 0:512], [ps], [(KT[hh], ("n", kb))])

        def v_recompute(hp):
            for k4 in range(nkt // 4):
                ps = ps_next()
                for kk in range(4):
                    kt = k4 * 4 + kk
                    for cc in range(2):
                        MM(ps, ps[:, kk * 128:(kk + 1) * 128], ckvnb.t[:, cc * 2048 + kt * 128: cc * 2048 + (kt + 1) * 128],
                           wuv[:, cc, hp * 128:(hp + 1) * 128], cc == 0, cc == 1, [suv, (ckvnb, kt // 4)])
                CP("dve", Vp4[:, k4 * 4:(k4 + 1) * 4, :], ps[:, :].rearrange("p (k m) -> p k m", m=128), [ps], [(Vp, k4)])

        kt_recompute(0)
        v_recompute(0)
        for h in range(16):
            hp, hh = h // 2, h % 2
            po = allb[3 + 2 * hh]
            pss = allb[4 + 2 * hh]

            def s_exp(kt):
                qlo = max(0, kt - 4 * g) * 128
                nq = Tn - qlo
                pS = ps_next()
                MM(pS, pS[:, 0:nq], KT[hh].t[0:96, kt * 128:(kt + 1) * 128], c.B1.t[0:96, h * Tn + qlo:(h + 1) * Tn], True, True,
                   [(KT[hh], ("n", kt // 4)), (KT[hh], "pe"), (c.B1, ("Q", h))])
                pt = pt_next()
                ACT(pt[:, 0:nq], pS[:, 0:nq], AF.Exp, [pS], [pt], scale=SM_SCALE)
                if kt >= 4 * g:
                    MEMSET("dve", pt[64:128, 0:64], 0.0, [pt])
                return pt, qlo, nq
            q_ = [s_exp(0)]
            if nkt > 1:
                q_.append(s_exp(1))
            for kt in range(nkt):
                pt, qlo, nq = q_.pop(0)
                if kt + 2 < nkt:
                    q_.append(s_exp(kt + 2))
                MM(po, po[:, qlo:Tn], Vp4[:, kt, :], pt[:, 0:nq], kt == 0, kt == nkt - 1, [(Vp, kt // 4), pt], skip=True)
                MM(pss, pss[:, qlo:Tn], ones_bf[:, :], pt[:, 0:nq], kt == 0, kt == nkt - 1, [ones_bf, pt], skip=True)
                if kt == 0 and h + 1 < 16:
                    kt_recompute(h + 1)
            r0 = hh * 64
            rl = c.tf[2 + hh]
            ACT(rl[r0:r0 + 64, :], pss[r0:r0 + 64, 0:Tn], AF.Ln, [pss], [rl])
            ACT(rl[r0:r0 + 64, :], rl[r0:r0 + 64, :], AF.Exp, [rl], [rl], scale=-1.0)
            TT("dve", c.B2.t[r0:r0 + 64, hp * Tn:(hp + 1) * Tn], po[r0:r0 + 64, 0:Tn], rl[r0:r0 + 64, :], OP.mult, [po, rl], [(c.B2, hp)])
            if hh == 1 and hp + 1 < 8:
                v_recompute(hp + 1)
        rot_n[0] = 5
    def mla_attn_sample(c, suk, wuk, suv, wuv):
        Tn = c.T
        skt, wkt = wload(w_ukT[:, :], 64, 1, 4096)
        ps = ps_next()
        for h in range(16):
            for cc in range(2):
                MM(ps, ps[:, cc * 256 + h * 16: cc * 256 + (h + 1) * 16], wkt[0:64, 0, h * 256 + cc * 128: h * 256 + (cc + 1) * 128],
                   c.qn.t[0:64, h * 16:(h + 1) * 16], True, True, [skt, c.qn])
        CP("act", c.qlat.t[:, :], ps[:, :], [ps], [c.qlat])
        olat, sums = acc[0], acc[1]
        first = True
        def s_part(lhs_c0, lhs_c1, lhs_pe, nk, rd):
            pS = ps_next()
            MM(pS, pS[0:nk, 0:256], lhs_c0, c.qlat.t[:, 0:256], True, False, rd + [c.qlat])
            MM(pS, pS[0:nk, 0:256], lhs_c1, c.qlat.t[:, 256:512], False, False, rd + [c.qlat])
            MM(pS, pS[0:nk, 0:256], lhs_pe, c.qpe.t[0:32, :], False, True, rd + [c.qpe])
            pt = pt_next()
            ACT(pt[0:nk, 0:256], pS[0:nk, 0:256], AF.Exp, [pS], [pt], scale=SM_SCALE)
            return pt

        def pv_part(pt, tok_c0, tok_c1, nk, rd, last):
            nonlocal first
            MM(olat, olat[:, 0:256], tok_c0, pt[0:nk, 0:256], first, last, rd + [pt], skip=True)
            MM(olat, olat[:, 256:512], tok_c1, pt[0:nk, 0:256], False, last, rd + [pt], skip=True)
            MM(sums, sums[:, 0:256], ones_bf[0:nk, :], pt[0:nk, 0:256], first, last, [ones_bf, pt])
            first = False

        ct = c.ct[0]

        def prep(blk):
            cb = c.cb[blk % 2]
            DMA("pool", cb.t[:, :].rearrange("p (k c) -> p k c", c=288)[:, :, 0:256],
                cckv[blk * 512:(blk + 1) * 512, :].rearrange("(k p) c -> p k c", p=128), [], [cb], "CB%d" % (blk % 2))
            DMA("pool", cb.t[:, :].rearrange("p (k c) -> p k c", c=288)[:, :, 256:288],
                ckpe[blk * 512:(blk + 1) * 512, :].rearrange("(k p) c -> p k c", p=128), [], [cb], "CB%d" % (blk % 2))
            for cc in range(2):
                for kt in range(4):
                    TR(psb, psb[:, (cc * 4 + kt) * 128:(cc * 4 + kt + 1) * 128], cb.t[:, kt * 288 + cc * 128: kt * 288 + (cc + 1) * 128],
                       identb[:, :], [cb, identb])
            CP("act", ct.t[:, 0:1024], psb[:, 0:1024], [psb], [ct])
            for kt in range(4):
                TR(psb, psb[0:32, kt * 128:(kt + 1) * 128], cb.t[:, kt * 288 + 256: kt * 288 + 288], identb[:, :], [cb, identb])
            CP("dve", ct.t[0:32, 1024:1536], psb[0:32, 0:512], [psb], [ct])

        def s_blk(blk, kt):
            return s_part(ct.t[:, kt * 128:(kt + 1) * 128], ct.t[:, 512 + kt * 128:512 + (kt + 1) * 128],
                          ct.t[0:32, 1024 + kt * 128:1024 + (kt + 1) * 128], 128, [ct])

        prep(0)
        for blk in range(8):
            cb = c.cb[blk % 2]
            q_ = [s_blk(blk, 0), s_blk(blk, 1)]
            for kt in range(4):
                pt = q_.pop(0)
                if kt + 2 < 4:
                    q_.append(s_blk(blk, kt + 2))
                if kt == 1 and blk + 1 < 8:
                    prep(blk + 1)
                pv_part(pt, cb.t[:, kt * 288: kt * 288 + 128], cb.t[:, kt * 288 + 128: kt * 288 + 256], 128, [cb], False)
        CP("dve", c.ckvtokb.t[0:16, :], c.ckvtok.t[0:16, :], [c.ckvtok], [c.ckvtokb])
        pt = s_part(c.ckvb.t[:, 0:16], c.ckvb.t[:, 16:32], c.kprb.t[0:32, 0:16], 16, [c.ckvb, c.kprb])
        pv_part(pt, c.ckvtokb.t[0:16, 0:128], c.ckvtokb.t[0:16, 128:256], 16, [c.ckvtokb], True)
        rs = c.rs
        RECIP(rs.t[:, 0:256], sums[:, 0:256], [sums], [rs])
        for cc in range(2):
            TT("dve", c.olatn.t[:, cc * 256:(cc + 1) * 256], olat[:, cc * 256:(cc + 1) * 256], rs.t[:, 0:256], OP.mult, [olat, rs], [c.olatn])
        ps = ps_next()
        for h in range(16):
            hp = h // 2
            for cc in range(2):
                MM(ps, ps[:, h * 16:(h + 1) * 16], wuv[:, cc, hp * 128:(hp + 1) * 128], c.olatn.t[:, cc * 256 + h * 16: cc * 256 + (h + 1) * 16],
                   cc == 0, cc == 1, [suv, c.olatn])
        for hh in range(2):
            CP("act", c.B2.t[hh * 64:(hh + 1) * 64, 0:128].rearrange("p (k t) -> p k t", t=16),
               ps[hh * 64:(hh + 1) * 64, 0:256].rearrange("p (k h t) -> p k h t", h=2, t=16)[:, :, hh, :], [ps], [c.B2])
    def mla_out(ctxs):
        for blk in range(4):
            so_, wo_ = wload(w_out_c[:, blk * 256:(blk + 1) * 256], 128, 8, 256)
            for c in ctxs:
                Tn = c.T
                for dl in range(2):
                    dc = blk * 2 + dl
                    ps = ps_next()
                    for hp in range(8):
                        MM(ps, ps[:, 0:Tn], wo_[:, hp, dl * 128:(dl + 1) * 128], c.B2.t[:, hp * Tn:(hp + 1) * Tn], hp == 0, hp == 7, [so_, c.B2])
                    evac_sq(c, ps, dc)
        for c in ctxs:
            postnorm_add(c, "mix_post1")
    cp.ost = [cp.tf[2], cp.tf[3]]
    _ost = P.sbuf("ost", [16, 288], F32)
    cs.ost = [_ost, _ost]
    cs.kprb = P.sbuf("kprb", [32, 16], BF16)
    cs.ckvb = P.sbuf("ckvb", [128, 32], BF16)
    cs.ckvtok = P.sbuf("ckvtok", [16, 256], F32)
    cs.ckvtokb = P.sbuf("ckvtokb", [16, 256], BF16)
    cs.qn = P.sbuf("qn", [64, 256], BF16)
    cs.qpe = P.sbuf("qpe", [32, 256], BF16)
    cs.qlat = P.sbuf("qlat", [128, 512], BF16)
    cs.olatn = P.sbuf("olatn", [128, 512], BF16)
    cs.rs = P.sbuf("rs", [128, 256], F32)
    cs.cb = [P.sbuf("cb%d" % i, [128, 4 * 288], BF16) for i in range(2)]
    _ct = P.sbuf("ct0", [128, 1536], BF16)
    cs.ct = [_ct, _ct]
    for g in range(ngroups):
        ctxs = [cp] + ([cs] if g == 0 else [])
        load_x(cp, xp, g * T, prefetched=(g > 0))
        if g == 0:
            load_x(cs, xs, 0)
        if stage >= 1:
            mixer_ab(ctxs, g == NG - 1)
            if g == 0:
                out_states(cs, 1)
            if g == NG - 1:
                out_states(cp, 0)
        if stage >= 2:
            ffn(ctxs, 0)
        if stage >= 3:
            mla_proj(ctxs, g)
        if stage >= 4:
            sukv, wukv = wload(w_ukv[:, :], 128, 2, 2048)
            suk, wuk = sukv, wukv[:, :, 0:1024]
            suv, wuv = sukv, wukv[:, :, 1024:2048]
            mla_attn_prompt(cp, g, suk, wuk, suv, wuv)
            if g == 0 and stage >= 5:
                mla_attn_sample(cs, suk, wuk, suv, wuv)
        if stage >= 6:
            mla_out(ctxs)
        if stage >= 7:
            if g + 1 < ngroups:
                load_x_dma(cp, xp, (g + 1) * T)
            ffn(ctxs, 1)
        store_y(cp, y_p, g * T)
        if g == 0:
            store_y(cs, y_s, 0)
    P.emit()
    P.close()
    return nc
_NC = None
SUB = int(os.environ.get('SUB', '99'))
KPEOUT = int(os.environ.get('KPEOUT', '1'))
OUTV = int(os.environ.get('OUTV', '3'))
def _fm(v, n):
    return np.ascontiguousarray(np.asarray(v, np.float32).reshape(n, 128).T)
def prep_inputs(inp, cores=range(8)):
    f = lambda k: np.asarray(inp[k], np.float32)
    w_in_c = f("w_in_c")[0]
    sw = np.concatenate([np.arange(16, 32), np.arange(0, 16)])
    w_in_c_ext = np.ascontiguousarray(np.concatenate(
        [w_in_c, w_in_c[:, 576:640], w_in_c[:, 640:672][:, sw]], axis=1))
    w_uq = f("w_uq")[0]
    idx = np.arange(1536).reshape(16, 96).copy()
    idx[:, 64:96] = idx[:, 64:96][:, sw]
    w_uq_sw = np.ascontiguousarray(w_uq[:, idx.reshape(-1)])
    w_uk = f("w_uk")[0]
    w_ukT = np.ascontiguousarray(w_uk.transpose(2, 1, 0).reshape(64, 16 * 256))
    tri = np.triu(np.ones((128, 128), np.float32))
    tri4 = np.ascontiguousarray(np.tile(tri, (1, 4)))
    tri4s = np.zeros((128, 64), np.float32)
    tri4s[:16] = np.tile(np.triu(np.ones((16, 16), np.float32)), (1, 4))
    rmask = np.ones((128, 512), np.float32)
    rmask[:, ::128] = 0.0
    half = 16
    inv = (10000.0 ** (-np.arange(half, dtype=np.float32) / np.float32(half))).astype(np.float32)
    def rope_tab(pos):
        ang = pos.astype(np.float32)[None, :] * inv[:, None]
        cos = np.cos(ang).astype(np.float32)
        sin = np.sin(ang).astype(np.float32)
        c32 = np.concatenate([cos, cos], 0)
        s32 = np.concatenate([-sin, sin], 0)
        return np.ascontiguousarray(np.concatenate([c32, s32], 1))
    ropeP = rope_tab(np.arange(2048))
    ropeS = rope_tab(4096 + np.arange(TS))
    shared = {
        "w_in_ab": f("w_in_ab")[0], "lru_wa": f("lru_w_a")[0].reshape(1024, 128), "lru_wx": f("lru_w_x")[0].reshape(1024, 128),
        "w_gate": f("gla_w_gate")[0], "w_out_ab": f("w_out_ab")[0], "w_in_c": w_in_c_ext, "w_uq_cat": np.concatenate([np.concatenate([w_uq[:, b * 384:(b + 1) * 384], w_uq_sw[:, b * 384:(b + 1) * 384]], axis=1) for b in range(4)], axis=1),
        "w_ukv": np.concatenate([w_uk.reshape(256, 1024), f("w_uv")[0].reshape(256, 1024)], axis=1), "w_ukT": w_ukT, "w_out_c": f("w_out_c")[0],
        "ffn_wg": f("ffn_w_gate").reshape(2 * D, DFF), "ffn_wu": f("ffn_w_up").reshape(2 * D, DFF), "ffn_wd": f("ffn_w_down").reshape(2 * DFF, D),
        "ident": np.eye(128, dtype=np.float32), "tri4": tri4, "tri4s": tri4s, "rmask": rmask, "ropeP": ropeP, "ropeS": ropeS,
    }
    shared = {k: np.ascontiguousarray(v, dtype=np.float32) for k, v in shared.items()}
    vparts = {}
    for l in range(2):
        vparts["mix_pre%d" % l] = _fm(f("norm_mix_pre")[l], 8)
        vparts["mix_post%d" % l] = _fm(f("norm_mix_post")[l], 8)
        vparts["ffn_pre%d" % l] = _fm(f("norm_ffn_pre")[l], 8)
        vparts["ffn_post%d" % l] = _fm(f("norm_ffn_post")[l], 8)
    cw = f("conv_w_a")[0]
    vparts["conv_w"] = np.concatenate([_fm(cw[j], 8) for j in range(4)], 1)
    vparts["conv_b"] = _fm(f("conv_b_a")[0], 8)
    vparts["lru_ba"] = _fm(f("lru_b_a")[0], 8)
    vparts["lru_bx"] = _fm(f("lru_b_x")[0], 8)
    vparts["lam"] = _fm(f("lru_lambda")[0], 8)
    vparts["b_gate"] = _fm(f("gla_b_gate")[0], 4)
    vparts["gla_norm"] = _fm(f("gla_norm")[0], 2)
    vparts["q_norm"] = _fm(f("mla_q_norm")[0], 3)
    vparts["kv_norm"] = _fm(f("mla_kv_norm")[0], 2)
    in_maps = []
    for c in cores:
        vp = dict(vparts)
        vp["h0"] = _fm(f("state_lru_h")[0, c], 8)
        cs_ = f("state_conv_a")[0, c]
        vp["convs"] = np.ascontiguousarray(cs_.reshape(3, 8, 128).transpose(2, 1, 0).reshape(128, 24))
        vecs = np.ascontiguousarray(np.concatenate([vp[n] for n, _ in _VNAMES], 1), dtype=np.float32)
        m = dict(shared)
        m.update({
            "xp": np.ascontiguousarray(f("x_prompt")[c]), "xs": np.ascontiguousarray(f("x_sample")[c]),
            "gla_s": np.ascontiguousarray(f("state_gla_S")[0, c]), "cckv": np.ascontiguousarray(f("cache_mla_ckv")[0, c]),
            "ckpe": np.ascontiguousarray(f("cache_mla_kpe")[0, c]), "vecs": vecs,
        })
        in_maps.append(m)
    return in_maps
def kernel(**inp):
    global _NC
    if _NC is None:
        _NC = build()
    nc = _NC
    in_maps = prep_inputs(inp)
    res = run_bass_kernel_spmd(nc, in_maps, core_ids=list(range(8)))
    R = res.results
    def st(name):
        return np.stack([np.asarray(R[c][name], np.float32) for c in range(8)], 0)
    y_prompt = st("y_p")
    y_sample = st("y_s")
    outs = [y_prompt, y_sample]
    for pfx in ("p", "s"):
        outs.append(st(pfx + "_conv")[None])
        outs.append(st(pfx + "_h").reshape(8, 1024)[None])
        outs.append(st(pfx + "_S")[None])
        outs.append(st(pfx + "_ckv")[None])
        outs.append(st(pfx + "_kpe")[None])
    return tuple(outs)
```

```python
import os
import numpy as np
import concourse.bass as bass
import concourse.mybir as mybir
from concourse.bass_utils import run_bass_kernel_spmd
F32 = mybir.dt.float32
BF16 = mybir.dt.bfloat16
AF = mybir.ActivationFunctionType
OP = mybir.AluOpType
D = 1024
T = 512
NG = 4
TS = 16
DFF = 2816
NFF = 22
EPS = 1e-6
SM_SCALE = 96 ** -0.5
GELU_K = 1.5957691216057308
_VNAMES = []
for _l in range(2):
    _VNAMES += [("mix_pre%d" % _l, 8), ("mix_post%d" % _l, 8), ("ffn_pre%d" % _l, 8), ("ffn_post%d" % _l, 8)]
_VNAMES += [("conv_w", 32), ("conv_b", 8), ("lru_ba", 8), ("lru_bx", 8), ("lam", 8), ("b_gate", 4),
            ("gla_norm", 2), ("q_norm", 3), ("kv_norm", 2), ("h0", 8), ("convs", 24)]
VOFF = {}
_c = 0
for _n, _k in _VNAMES:
    VOFF[_n] = _c
    _c += _k
NV = _c
class Buf:
    def __init__(self, t, name):
        self.t = t
        self.name = name
        self.st = {}
        self.excl = False
    def __getitem__(self, idx):
        return self.t[idx]
class Prog:
    ENGS = ("pe", "act", "dve", "pool", "sp")
    def __init__(self, nc):
        self.nc = nc
        self.ops = {e: [] for e in self.ENGS}
        self.known = {e: {} for e in self.ENGS}
        self.sems = {}
        self.cnt = {}
        self._stack = []
        self.epoch = {e: 0 for e in self.ENGS}
        self.dma_rr = {}
        self.clock = {}
        self.dma_direct = {}
        for e in self.ENGS:
            self.new_sem("E_%s_0" % e)
    def enter(self, cm):
        r = cm.__enter__()
        self._stack.append(cm)
        return r
    def close(self):
        while self._stack:
            self._stack.pop().__exit__(None, None, None)
    def new_sem(self, sid):
        if sid not in self.sems:
            self.sems[sid] = self.enter(self.nc.semaphore(sid))
            self.cnt[sid] = 0
        return sid
    def sbuf(self, name, shape, dt):
        return Buf(self.enter(self.nc.sbuf_tensor("sb_" + name, list(shape), dt)), name)
    def psum(self, name, shape, dt=F32):
        b = Buf(self.enter(self.nc.psum_tensor("ps_" + name, list(shape), dt)), name)
        b.excl = True
        return b
    @staticmethod
    def _norm(lst):
        out = []
        for x in lst or []:
            out.append((x, None) if isinstance(x, Buf) else x)
        return out
    def _deps(self, reads, writes):
        w = {}
        def add(s, v):
            if w.get(s, 0) < v:
                w[s] = v
        for b, k in reads:
            keys = [k, None] if k is not None else list(b.st.keys())
            for kk in keys:
                st = b.st.get(kk)
                if st and st[0] is not None:
                    add(*st[0])
        for b, k in writes:
            keys = [k, None] if k is not None else list(b.st.keys())
            for kk in keys:
                st = b.st.get(kk)
                if st:
                    if st[0] is not None:
                        add(*st[0])
                    for s, v in st[1].items():
                        add(s, v)
        return w
    def _record(self, reads, writes, tok):
        for b, k in writes:
            if k is None:
                b.st = {None: [tok, {}]}
            else:
                b.st[k] = [tok, {}]
        for b, k in reads:
            if k is None:
                b.st.setdefault(None, [None, {}])
                for st in b.st.values():
                    if st[1].get(tok[0], 0) < tok[1]:
                        st[1][tok[0]] = tok[1]
            else:
                st = b.st.setdefault(k, [None, {}])
                if st[1].get(tok[0], 0) < tok[1]:
                    st[1][tok[0]] = tok[1]
    def op(self, eng, fn, reads=None, writes=None, dma_sem=None):
        reads = self._norm(reads)
        writes = self._norm(writes)
        xr = [r for r in reads if r[0].excl]
        if xr:
            reads = [r for r in reads if not r[0].excl]
            writes = writes + [r for r in xr if r not in writes]
        w = self._deps(reads, writes)
        if self.cnt["E_%s_%d" % (eng, self.epoch[eng])] >= 30000:
            self.epoch[eng] += 1
            self.new_sem("E_%s_%d" % (eng, self.epoch[eng]))
        own = "E_%s_%d" % (eng, self.epoch[eng])
        kn = self.known[eng]
        waits = []

        def learn(s, v):
            kn[s] = v
            for s2, v2 in self.clock.get((s, v), {}).items():
                if kn.get(s2, 0) < v2:
                    kn[s2] = v2
        for s, v in sorted(w.items(), key=lambda x: -x[1]):
            if eng == "pe" and s.startswith("E_pe_"):
                continue
            if kn.get(s, 0) >= v:
                continue
            waits.append((s, v))
            learn(s, v)
        if dma_sem is not None:
            npool = 8
            i = self.dma_rr.get(eng, 0)
            self.dma_rr[eng] = i + 1
            dma_sem = "D_%s_%d" % (eng, i % npool)
            self.new_sem(dma_sem)
            dd = self.dma_direct.setdefault(eng, {})
            if self.cnt[dma_sem] > 0 and dd.get(dma_sem, 0) < self.cnt[dma_sem]:
                dd[dma_sem] = self.cnt[dma_sem]
                if (dma_sem, self.cnt[dma_sem]) not in waits:
                    waits.append((dma_sem, self.cnt[dma_sem]))
                learn(dma_sem, self.cnt[dma_sem])
            self.cnt[dma_sem] += 16
            tok = (dma_sem, self.cnt[dma_sem])
            self.ops[eng].append((waits, fn, (dma_sem, 16)))
        else:
            self.cnt[own] += 1
            tok = (own, self.cnt[own])
            self.ops[eng].append((waits, fn, (own, 1)))
        self.clock[tok] = dict(kn)
        self._record(reads, writes, tok)
        return tok
    def emit(self):
        nc = self.nc
        waited = {}
        for ename in self.ENGS:
            for waits, fn, (s, inc) in self.ops[ename]:
                for ws, wv in waits:
                    waited.setdefault(ws, set()).add(wv)
        for s, c in self.cnt.items():
            if c > 0:
                waited.setdefault(s, set()).add(c)
        rank = {s: {v: i + 1 for i, v in enumerate(sorted(vs))} for s, vs in waited.items() if s.startswith("E_")}
        self.n_inc = {e: 0 for e in self.ENGS}

        def wval(s, v):
            return rank[s][v] if s in rank else v
        with nc.Block() as block:
            def body(ename):
                def f(e):
                    run = {}
                    for waits, fn, (s, inc) in self.ops[ename]:
                        for ws, wv in waits:
                            e.wait_ge(self.sems[ws], wval(ws, wv))
                        ins = fn(e)
                        run[s] = run.get(s, 0) + inc
                        if s in rank:
                            if run[s] in rank[s]:
                                ins.then_inc(self.sems[s], 1)
                                self.n_inc[ename] += 1
                        else:
                            ins.then_inc(self.sems[s], inc)
                            self.n_inc[ename] += 1
                    if ename == "sp":
                        for s, c in self.cnt.items():
                            if c > 0:
                                e.wait_ge(self.sems[s], wval(s, c))
                return f
            block.tensor(body("pe"))
            block.scalar(body("act"))
            block.vector(body("dve"))
            block.gpsimd(body("pool"))
            block.sync(body("sp"))


class Ctx:
    pass
def build(stage=99, ngroups=NG):
    nc = bass.Bass("TRN2", target_bir_lowering=False)
    P = Prog(nc)
    def di(n, s):
        return nc.dram_tensor(n, list(s), F32, kind="ExternalInput").ap()
    def do(n, s):
        return nc.dram_tensor(n, list(s), F32, kind="ExternalOutput").ap()
    xp = di("xp", [2048, D])
    xs = di("xs", [TS, D])
    gla_s = di("gla_s", [4, 128, 256])
    cckv = di("cckv", [4096, 256])
    ckpe = di("ckpe", [4096, 32])
    vecs_d = di("vecs", [128, NV])
    w_in_ab = di("w_in_ab", [D, 5136])
    lru_wa = di("lru_wa", [1024, 128])
    lru_wx = di("lru_wx", [1024, 128])
    w_gate = di("w_gate", [16, 512])
    w_out_ab = di("w_out_ab", [2048, D])
    w_in_c = di("w_in_c", [D, 768])
    w_uq_cat = di("w_uq_cat", [384, 3072])
    w_ukv = di("w_ukv", [256, 2048])
    w_ukT = di("w_ukT", [64, 4096])
    w_out_c = di("w_out_c", [1024, D])
    ffn_wg = di("ffn_wg", [2 * D, DFF])
    ffn_wu = di("ffn_wu", [2 * D, DFF])
    ffn_wd = di("ffn_wd", [2 * DFF, D])
    ident_d = di("ident", [128, 128])
    tri4_d = di("tri4", [128, 512])
    tri4s_d = di("tri4s", [128, 64])
    rmask_d = di("rmask", [128, 512])
    ropeP_d = di("ropeP", [32, 2 * 2048])
    ropeS_d = di("ropeS", [32, 2 * TS])
    y_p = do("y_p", [2048, D])
    y_s = do("y_s", [TS, D])
    o_conv = [do("p_conv", [3, D]), do("s_conv", [3, D])]
    o_h = [do("p_h", [8, 128]), do("s_h", [8, 128])]
    o_S = [do("p_S", [4, 128, 256]), do("s_S", [4, 128, 256])]
    o_ckv = [do("p_ckv", [2048, 256]), do("s_ckv", [TS, 256])]
    o_kpe = [do("p_kpe", [2048, 32]), do("s_kpe", [TS, 32])]
    def MM(ps, out, lhsT, rhs, st, sp, reads, skip=False):
        P.op("pe", lambda e: e.matmul(out, lhsT=lhsT, rhs=rhs, start=st, stop=sp, skip_group_check=skip), reads=reads, writes=[ps])
    def TR(ps, out, in_, idn, reads):
        P.op("pe", lambda e: e.transpose(out, in_, idn), reads=reads, writes=[ps])
    def ACT(out, in_, func, reads, writes, bias=None, scale=None):
        kw = {}
        if bias is not None:
            kw["bias"] = bias
        if scale is not None:
            kw["scale"] = scale
        P.op("act", lambda e: e.activation(out=out, in_=in_, func=func, **kw), reads=reads, writes=writes)
    def TT(eng, out, a, b, op, reads, writes):
        P.op(eng, lambda e: e.tensor_tensor(out=out, in0=a, in1=b, op=op), reads=reads, writes=writes)
    def TS_(eng, out, a, s1, s2, op0, op1, reads, writes):
        if s2 is None:
            P.op(eng, lambda e: e.tensor_scalar(out=out, in0=a, scalar1=s1, scalar2=None, op0=op0), reads=reads, writes=writes)
        else:
            P.op(eng, lambda e: e.tensor_scalar(out=out, in0=a, scalar1=s1, scalar2=s2, op0=op0, op1=op1), reads=reads, writes=writes)
    def STT(out, a, s, b, op0, op1, reads, writes):
        P.op("dve", lambda e: e.scalar_tensor_tensor(out=out, in0=a, scalar=s, in1=b, op0=op0, op1=op1), reads=reads, writes=writes)
    def CP(eng, out, in_, reads, writes):
        if eng == "act":
            P.op("act", lambda e: e.activation(out=out, in_=in_, func=AF.Copy), reads=reads, writes=writes)
        else:
            P.op(eng, lambda e: e.tensor_copy(out=out, in_=in_), reads=reads, writes=writes)
    def RECIP(out, in_, reads, writes):
        P.op("dve", lambda e: e.reciprocal(out=out, in_=in_), reads=reads, writes=writes)
    def SCAN(out, d0, d1, init, reads, writes):
        P.op("dve", lambda e: e.tensor_tensor_scan(out=out, data0=d0, data1=d1, initial=init, op0=OP.mult, op1=OP.add), reads=reads, writes=writes)
    def MEMSET(eng, ap, val, writes):
        P.op(eng, lambda e: e.memset(ap, val), writes=writes)
    def DMA(eng, out, in_, reads, writes, sem):
        P.op(eng, lambda e: e.dma_start(out=out, in_=in_), reads=reads, writes=writes, dma_sem=sem)
    allb = [P.psum("pb%d" % i, [128, 512], F32) for i in range(7)]
    acc = [allb[5], allb[6]]
    psb = P.psum("psb", [128, 1024], BF16)
    rot = [0]
    rot_n = [5]

    def ps_next():
        b = allb[rot[0] % rot_n[0]]
        rot[0] += 1
        return b
    ident = P.sbuf("ident", [128, 128], F32)
    identb = P.sbuf("identb", [128, 128], BF16)
    ones_bf = P.sbuf("ones_bf", [128, 128], BF16)
    ones_f = P.sbuf("ones_f", [128, 64], F32)
    tri4 = P.sbuf("tri4", [128, 512], BF16)
    tri4s = P.sbuf("tri4s", [128, 64], BF16)
    rmask = P.sbuf("rmask", [128, 512], BF16)
    vecs = P.sbuf("vecs", [128, NV], F32)
    drv = P.sbuf("drv", [128, 48], F32)
    wgate = P.sbuf("wgate", [16, 512], BF16)
    wab = P.sbuf("wab", [128, 2 * 8 * 128], BF16)
    diag = P.sbuf("diag", [128, 3 * 4 * 128], BF16)
    DMA("sp", ident[:, :], ident_d[:, :], [], [ident], "C0")
    DMA("pool", tri4[:, :], tri4_d[:, :], [], [tri4], "C1")
    DMA("pool", tri4s[:, :], tri4s_d[:, :], [], [tri4s], "C1")
    DMA("pool", rmask[:, :], rmask_d[:, :], [], [rmask], "C1")
    DMA("sp", vecs[:, :], vecs_d[:, :], [], [vecs], "C0")
    DMA("pool", identb[:, :], ident_d[:, :], [], [identb], "C1")
    DMA("pool", wgate[:, :], w_gate[:, :], [], [wgate], "C1")
    DMA("pool", wab[:, 0:1024].rearrange("p (n d) -> p n d", d=128), lru_wa.rearrange("(n c) d -> c n d", c=128), [], [wab], "C1")
    DMA("pool", wab[:, 1024:2048].rearrange("p (n d) -> p n d", d=128), lru_wx.rearrange("(n c) d -> c n d", c=128), [], [wab], "C1")
    MEMSET("dve", ones_bf[:, :], 1.0, [ones_bf])
    MEMSET("dve", ones_f[:, :], 1.0, [ones_f])
    def vcol(name, i=0):
        o = VOFF[name] + i
        return vecs[:, o:o + 1]
    TS_("dve", drv[:, 0:4], vecs[:, VOFF["b_gate"]:VOFF["b_gate"] + 4], -1.0, None, OP.mult, None, [vecs], [drv])
    TS_("dve", drv[:, 4:12], vecs[:, VOFF["lru_ba"]:VOFF["lru_ba"] + 8], -1.0, None, OP.mult, None, [vecs], [drv])
    TS_("dve", drv[:, 12:20], vecs[:, VOFF["lru_bx"]:VOFF["lru_bx"] + 8], -1.0, None, OP.mult, None, [vecs], [drv])
    ACT(drv[:, 36:44], vecs[:, VOFF["lam"]:VOFF["lam"] + 8], AF.Exp, [vecs], [drv], scale=-1.0)
    ACT(drv[:, 36:44], drv[:, 36:44], AF.Ln, [drv], [drv], bias=1.0)
    TS_("dve", drv[:, 20:28], drv[:, 36:44], -8.0, None, OP.mult, None, [drv], [drv])
    TS_("dve", drv[:, 28:36], drv[:, 36:44], -16.0, None, OP.mult, None, [drv], [drv])
    NSLOT = 3
    SLOT = 4096
    slots = [P.sbuf("wslot%d" % i, [128, SLOT], BF16) for i in range(NSLOT)]
    sl_i = [0]
    def wload(src2d, kp, nk, ncols):
        i = sl_i[0] % NSLOT
        sl_i[0] += 1
        sb = slots[i]
        assert nk * ncols <= SLOT
        view = sb.t[0:kp, 0:nk * ncols].rearrange("p (k n) -> p k n", n=ncols)
        srcv = src2d.rearrange("(k p) n -> p k n", p=kp)
        kstep = max(1, 1024 // kp)
        k0 = 0
        while k0 < nk:
            k1 = min(nk, k0 + kstep)
            DMA("pool", view[:, k0:k1, :], srcv[:, k0:k1, :], [], [(sb, k0)], "W%d" % i)
            k0 = k1
        return sb, view
    def make_ctx(tag, Tn, is_s):
        c = Ctx()
        c.tag, c.T, c.s = tag, Tn, is_s
        c.tw = min(128, Tn)
        c.nt = Tn // c.tw
        c.C = c.tw
        c.pb = 0 if is_s else 64
        c.hT = P.sbuf("hT" + tag, [128, 8 * Tn], F32)
        c.uT = P.sbuf("uT" + tag, [128, 8 * Tn], BF16)
        c.F2 = P.sbuf("F2" + tag, [128, max(8 * Tn, 1024)], F32)
        c.cum = P.sbuf("cum" + tag, [128, 4 * Tn], F32)
        c.b1n = max(22 * Tn, 20 * Tn + 24, 8 * Tn + c.nt * 1536)
        c.B1 = P.sbuf("B1" + tag, [128, c.b1n], BF16)
        c.B2 = P.sbuf("B2" + tag, [128, 16 * Tn], BF16)
        c.tf = [P.sbuf("tf%d%s" % (i, tag), [128, Tn], F32) for i in range(6)]
        c.tb = [P.sbuf("tb%d%s" % (i, tag), [128, Tn], BF16) for i in range(2)]
        c.rstd = P.sbuf("rstd" + tag, [128, Tn], F32)
        c.S = P.sbuf("S" + tag, [128, 4 * 256], F32)
        c.Sbf = P.sbuf("Sbf" + tag, [128, 4 * 256], BF16)
        c.Se = P.sbuf("Se" + tag, [128, 256], F32)
        c.hst = P.sbuf("hst" + tag, [128, 8], F32)
        c.carry = P.sbuf("carry" + tag, [128, 8 * 3], BF16)
        c.convo = P.sbuf("convo" + tag, [128, 8 * 3], F32)
        c.ebl = P.sbuf("ebl" + tag, [128, 4 * c.nt], F32)
        c.rope = P.sbuf("rope" + tag, [96 if not is_s else 32, 2 * Tn], F32)
        c.atb = [P.sbuf("atb%d%s" % (i, tag), [128, 4 * c.tw], BF16) for i in range(2)]
        c.tbi = 0
        return c
    cp = make_ctx("p", T, False)
    cp.nb, cp.pend = allb[5], []
    cs = make_ctx("s", TS, True)
    cs.nb, cs.pend = allb[6], []
    ckvnb = P.sbuf("ckvnb", [128, 2 * 2048], BF16)
    KT = [P.sbuf("KT%d" % i, [96, 2048], BF16) for i in range(2)]
    Vp = P.sbuf("Vp", [128, 16 * 2 * 65], BF16)
    PTs = [P.sbuf("PT%d" % i, [128, 512], BF16) for i in range(3)]
    pt_i = [0]
    def pt_next():
        b = PTs[pt_i[0] % 3]
        pt_i[0] += 1
        return b
    MEMSET("pool", Vp[:, :], 1.0, [Vp])
    def v3(buf, off, nk, Tn, p0=0, p1=128):
        return buf.t[p0:p1, off:off + nk * Tn].rearrange("p (k t) -> p k t", t=Tn)
    MEMSET("dve", cp.S[:, :], 0.0, [cp.S])
    MEMSET("dve", cp.Sbf[:, :], 0.0, [cp.Sbf])
    MEMSET("dve", cp.hst[:, :], 0.0, [cp.hst])
    MEMSET("dve", cp.carry[:, :], 0.0, [cp.carry])
    DMA("sp", cs.S[:, :].rearrange("p (h v) -> p h v", v=256), gla_s.rearrange("h k v -> k h v"), [], [cs.S], "C0")
    CP("act", cs.Sbf[:, :], cs.S[:, :], [cs.S], [cs.Sbf])
    CP("dve", cs.hst[:, :], vecs[:, VOFF["h0"]:VOFF["h0"] + 8], [vecs], [cs.hst])
    CP("dve", cs.carry[:, :], vecs[:, VOFF["convs"]:VOFF["convs"] + 24], [vecs], [cs.carry])
    DMA("sp", cs.rope[0:32, :], ropeS_d[:, :], [], [cs.rope], "C0")
    def rms(c, src, off, nk, nfeat, keyed=False):
        Tn = c.T
        ps = ps_next()
        for k in range(nk):
            sq = c.tb[c.tbi % 2]
            c.tbi += 1
            ACT(sq[:, :], src.t[:, off + k * Tn: off + (k + 1) * Tn], AF.Square, [(src, k)] if keyed else [src], [sq])
            MM(ps, ps[:, 0:Tn], ones_bf[:, :], sq[:, :], k == 0, k == nk - 1, [ones_bf, sq])
        ACT(c.rstd[:, :], ps[:, 0:Tn], AF.Ln, [ps], [c.rstd], bias=EPS, scale=1.0 / nfeat)
        ACT(c.rstd[:, :], c.rstd[:, :], AF.Exp, [c.rstd], [c.rstd], scale=-0.5)
    def load_x_dma(c, src, row0):
        tw = c.tw
        xb32 = c.B2.t[:, :].bitcast(F32)
        for t in range(c.nt):
            DMA("sp", xb32[0:tw, t * 1024:(t + 1) * 1024], src[row0 + t * tw: row0 + (t + 1) * tw, :], [], [c.B2], "IO" + c.tag)

    def load_x(c, src, row0, prefetched=False):
        Tn, tw = c.T, c.tw
        if c.s:
            buf, bt = c.F2, c.F2.t
        else:
            buf, bt = c.B2, c.B2.t[:, :].bitcast(F32)
            if not prefetched:
                load_x_dma(c, src, row0)
        for t in range(c.nt):
            if c.s:
                DMA("sp", bt[0:tw, t * 1024:(t + 1) * 1024], src[row0 + t * tw: row0 + (t + 1) * tw, :], [], [buf], "IO" + c.tag)
            for kq in range(2):
                ps = ps_next()
                for kk in range(4):
                    kc = kq * 4 + kk
                    TR(ps, ps[:, kk * tw:(kk + 1) * tw], bt[0:tw, t * 1024 + kc * 128: t * 1024 + (kc + 1) * 128],
                       ident[0:tw, 0:tw], [buf, ident])
                CP("act" if kq == 0 else "dve", v3(c.hT, 0, 8, Tn)[:, kq * 4:(kq + 1) * 4, t * tw:(t + 1) * tw],
                   ps[:, 0:4 * tw].rearrange("p (k t) -> p k t", t=tw), [ps], [c.hT])
    def store_y(c, dst, row0):
        Tn, tw = c.T, c.tw
        fence(c.F2)
        for t in range(c.nt):
            for kq in range(2):
                ps = ps_next()
                for kk in range(4):
                    kc = kq * 4 + kk
                    TR(ps, ps[0:tw, kk * 128:(kk + 1) * 128], c.hT.t[:, kc * Tn + t * tw: kc * Tn + (t + 1) * tw],
                       ident[:, :], [c.hT, ident])
                CP("act" if kq == 0 else "dve", c.F2.t[0:tw, t * 1024 + kq * 512: t * 1024 + (kq + 1) * 512], ps[0:tw, :],
                   [ps], [(c.F2, ("io", t))])
            DMA("sp", dst[row0 + t * tw: row0 + (t + 1) * tw, :], c.F2.t[0:tw, t * 1024:(t + 1) * 1024],
                [(c.F2, ("io", t))], [], "IO" + c.tag)
    def prenorm(c, gname):
        Tn = c.T
        rms(c, c.hT, 0, 8, 1024, keyed=True)
        for k in range(8):
            STT(c.uT.t[:, k * Tn:(k + 1) * Tn], c.hT.t[:, k * Tn:(k + 1) * Tn], vcol(gname, k), c.rstd[:, :],
                OP.mult, OP.mult, [(c.hT, k), c.rstd, vecs], [(c.uT, k)])
    def evac_sq(c, ps, dc):
        Tn = c.T
        CP("act", c.F2.t[:, dc * Tn:(dc + 1) * Tn], ps[:, 0:Tn], [ps], [c.F2])
        sq = c.tb[c.tbi % 2]
        c.tbi += 1
        ACT(sq[:, :], ps[:, 0:Tn], AF.Square, [ps], [sq])
        c.pend.append((sq, dc == 0, dc == 7))
        flush_sq(c, 1)

    def flush_sq(c, keep):
        Tn = c.T
        while len(c.pend) > keep:
            sq, f_, l_ = c.pend.pop(0)
            MM(c.nb, c.nb[:, 0:Tn], ones_bf[:, :], sq[:, :], f_, l_, [ones_bf, sq])

    def postnorm_add(c, gname):
        Tn = c.T
        flush_sq(c, 0)
        ACT(c.rstd[:, :], c.nb[:, 0:Tn], AF.Ln, [c.nb], [c.rstd], bias=EPS, scale=1.0 / 1024)
        ACT(c.rstd[:, :], c.rstd[:, :], AF.Exp, [c.rstd], [c.rstd], scale=-0.5)
        for k in range(8):
            tmp = c.tf[k % 2]
            TT("dve", tmp[:, :], c.F2.t[:, k * Tn:(k + 1) * Tn], c.rstd[:, :], OP.mult, [c.F2, c.rstd], [tmp])
            STT(c.hT.t[:, k * Tn:(k + 1) * Tn], tmp[:, :], vcol(gname, k), c.hT.t[:, k * Tn:(k + 1) * Tn], OP.mult, OP.add,
                [tmp, (c.hT, k), vecs], [(c.hT, k)])
    def fence(buf):
        P.op("dve", lambda e: e.memset(drv[:, 47:48], 0.0), writes=[buf, (drv, "f")])
    def ffn(ctxs, layer):
        for c in ctxs:
            prenorm(c, "ffn_pre%d" % layer)
            fence(c.B1)
        col = 0
        while col < DFF:
            nc_ = min(512, DFF - col)
            sg, wg = wload(ffn_wg[layer * D:(layer + 1) * D, col:col + nc_], 128, 8, nc_)
            su, wu = wload(ffn_wu[layer * D:(layer + 1) * D, col:col + nc_], 128, 8, nc_)
            for c in ctxs:
                Tn = c.T
                nj = nc_ // 128
                for j in range(nj):
                    pg = ps_next()
                    for k in range(8):
                        MM(pg, pg[:, 0:Tn], wg[:, k, j * 128:(j + 1) * 128], c.uT.t[:, k * Tn:(k + 1) * Tn], k == 0, k == 7, [sg, (c.uT, k)])
                    tmp = c.tf[2 + j]
                    ACT(tmp[:, :], pg[:, 0:Tn], AF.Silu, [pg], [tmp])
                for j in range(nj):
                    fj = col // 128 + j
                    pu = ps_next()
                    for k in range(8):
                        MM(pu, pu[:, 0:Tn], wu[:, k, j * 128:(j + 1) * 128], c.uT.t[:, k * Tn:(k + 1) * Tn], k == 0, k == 7, [su, (c.uT, k)])
                    tmp = c.tf[2 + j]
                    TT("dve", c.B1.t[:, fj * Tn:(fj + 1) * Tn], tmp[:, :], pu[:, 0:Tn], OP.mult, [tmp, pu], [(c.B1, ("ff", fj))])
            col += nc_
        for dc in range(8):
            sd, wd = wload(ffn_wd[layer * DFF:(layer + 1) * DFF, dc * 128:(dc + 1) * 128], 128, NFF, 128)
            for c in ctxs:
                Tn = c.T
                ps = ps_next()
                for f in range(NFF):
                    MM(ps, ps[:, 0:Tn], wd[:, f, :], c.B1.t[:, f * Tn:(f + 1) * Tn], f == 0, f == NFF - 1, [sd, (c.B1, ("ff", f))])
                evac_sq(c, ps, dc)
        for c in ctxs:
            postnorm_add(c, "ffn_post%d" % layer)
    def mixer_ab(ctxs, last):
        for c in ctxs:
            prenorm(c, "mix_pre0")
            fence(c.B1)
            fence(c.F2)
        wz = P_wz
        DMA("pool", wz.t[:, :].rearrange("p (k n) -> p k n", n=16), w_in_ab[:, 5120:5136].rearrange("(k p) n -> p k n", p=128), [], [wz], "C1")
        for c in ctxs:
            Tn = c.T
            ps = ps_next()
            for k in range(8):
                MM(ps, ps[0:16, 0:Tn], wz.t[:, k * 16:(k + 1) * 16], c.uT.t[:, k * Tn:(k + 1) * Tn], k == 0, k == 7, [wz, (c.uT, k)])
            zr = c.tb[0]
            CP("act", zr[0:16, :], ps[0:16, 0:Tn], [ps], [zr])
            for h in range(4):
                pz = ps_next()
                MM(pz, pz[:, 0:Tn], wgate[0:16, h * 128:(h + 1) * 128], zr[0:16, :], True, True, [wgate, zr])
                e1 = c.tf[0]
                ACT(e1[:, :], pz[:, 0:Tn], AF.Exp, [pz, drv], [e1], bias=drv[:, h:h + 1], scale=-1.0)
                ACT(e1[:, :], e1[:, :], AF.Ln, [e1], [e1], bias=1.0)
                msk = rmask[:, 0:Tn] if not c.s else rmask[:, 1:1 + Tn]
                SCAN(c.cum.t[:, h * Tn:(h + 1) * Tn], msk, e1[:, :], 0.0, [rmask, e1], [(c.cum, h)])
                ACT(c.ebl.t[:, h * c.nt:(h + 1) * c.nt],
                    c.cum.t[:, h * Tn:(h + 1) * Tn].rearrange("p (n c) -> p n c", c=c.C)[:, :, c.C - 1],
                    AF.Exp, [(c.cum, h)], [(c.ebl, h)], scale=-1.0 / 16.0)
        for half in range(2):
            sx, wx = wload(w_in_ab[:, half * 512:(half + 1) * 512], 128, 8, 512)
            sgw, wgl = wload(w_in_ab[:, 1024 + half * 512:1024 + (half + 1) * 512], 128, 8, 512)
            sqk, wqk = wload(w_in_ab[:, 2048 + half * 512:2560 + half * 512], 128, 8, 512)

            def qk_head(c, h):
                Tn = c.T
                ps = ps_next()
                for k in range(8):
                    MM(ps, ps[:, 0:Tn], wqk[:, k, h * 128:(h + 1) * 128], c.uT.t[:, k * Tn:(k + 1) * Tn], k == 0, k == 7, [sqk, (c.uT, k)])
                e1 = c.rstd
                if half == 0:
                    ACT(e1[:, :], c.cum.t[:, h * Tn:(h + 1) * Tn], AF.Exp, [(c.cum, h)], [e1], scale=-1.0 / 16.0)
                    STT(c.B1.t[:, h * Tn:(h + 1) * Tn], ps[:, 0:Tn], 128.0 ** -0.5, e1[:, :], OP.mult, OP.mult, [ps, e1], [(c.B1, ("q", h))])
                else:
                    ACT(e1[:, :], c.cum.t[:, h * Tn:(h + 1) * Tn], AF.Exp, [(c.cum, h)], [e1], scale=1.0 / 16.0)
                    TT("dve", c.B1.t[:, 4 * Tn + h * Tn: 4 * Tn + (h + 1) * Tn], ps[:, 0:Tn], e1[:, :], OP.mult, [ps, e1], [(c.B1, ("k", h))])
            for c in ctxs:
                Tn = c.T
                XO = 8 * Tn
                GO = 16 * Tn + 24
                for j in range(4):
                    kc = half * 4 + j
                    xo = XO + kc * (Tn + 3)
                    ps = ps_next()
                    for k in range(8):
                        MM(ps, ps[:, 0:Tn], wx[:, k, j * 128:(j + 1) * 128], c.uT.t[:, k * Tn:(k + 1) * Tn], k == 0, k == 7, [sx, (c.uT, k)])
                    CP("act", c.B1.t[:, xo + 3: xo + 3 + Tn], ps[:, 0:Tn], [ps], [(c.B1, ("xa", kc))])
                    CP("act", c.B1.t[:, xo: xo + 3], c.carry.t[:, kc * 3:(kc + 1) * 3], [(c.carry, kc)], [(c.B1, ("xa", kc))])
                    if last or c.s:
                        CP("dve", c.convo.t[:, kc * 3:(kc + 1) * 3], ps[:, Tn - 3:Tn], [ps], [(c.convo, kc)])
                def tv(i):
                    if i < 6:
                        return c.tf[i].t[:, 0:Tn], c.tf[i]
                    return c.F2.t[:, (i - 6) * Tn:(i - 5) * Tn], (c.F2, ("t", i - 6))
                gt = [tv(11), tv(12), tv(13), tv(5)]
                gps = []
                for j in range(4):
                    ps = ps_next()
                    gps.append(ps)
                    for k in range(8):
                        MM(ps, ps[:, 0:Tn], wgl[:, k, j * 128:(j + 1) * 128], c.uT.t[:, k * Tn:(k + 1) * Tn], k == 0, k == 7, [sgw, (c.uT, k)])
                    ACT(c.B1.t[:, GO + j * Tn: GO + (j + 1) * Tn], ps[:, 0:Tn], AF.Gelu_apprx_tanh, [ps], [(c.B1, ("gl", j))])
                st_ = {}
                XS = [0, 6, 5]

                def build_diag(j):
                    kc = half * 4 + j
                    dg = (kc % 3) * 512
                    for tap in range(4):
                        TS_("dve", diag[:, dg + tap * 128: dg + (tap + 1) * 128], identb[:, :], vcol("conv_w", tap * 8 + kc), None, OP.mult, None,
                            [identb, vecs], [(diag, (kc % 3, tap))])

                def prep_conv(j):
                    kc = half * 4 + j
                    xo = XO + kc * (Tn + 3)
                    dg = (kc % 3) * 512
                    pc = allb[5 + (kc % 2)]
                    for tap in range(4):
                        MM(pc, pc[:, 0:Tn], diag[:, dg + tap * 128: dg + (tap + 1) * 128], c.B1.t[:, xo + tap: xo + tap + Tn], tap == 0, tap == 3,
                           [(diag, (kc % 3, tap)), (c.B1, ("xa", kc))])
                    CP("act", c.carry.t[:, kc * 3:(kc + 1) * 3], c.B1.t[:, xo + Tn: xo + Tn + 3], [(c.B1, ("xa", kc))], [(c.carry, kc)])
                    base = 0 if kc % 2 == 0 else 6
                    tvs = [tv(XS[kc % 3])] + [tv(base + q) for q in range(1, 5)]
                    xb = c.tb[kc % 2]
                    TS_("dve", xb[:, :], pc[:, 0:Tn], vcol("conv_b", kc), None, OP.add, None, [pc, vecs], [xb])
                    st_[j] = [tvs, xb, None, None, pc]

                def gates(j):
                    kc = half * 4 + j
                    xb = st_[j][1]
                    pr = ps_next()
                    pi = ps_next()
                    MM(pr, pr[:, 0:Tn], wab[:, kc * 128:(kc + 1) * 128], xb[:, :], True, True, [wab, xb])
                    MM(pi, pi[:, 0:Tn], wab[:, 1024 + kc * 128:1024 + (kc + 1) * 128], xb[:, :], True, True, [wab, xb])
                    st_[j][2], st_[j][3] = pr, pi

                def chain_head(j):
                    kc = half * 4 + j
                    (X, Xd), (R, Rd), (I, Id), (A, Ad), (M, Md) = st_[j][0]
                    pr, pi = st_[j][2], st_[j][3]
                    ACT(R, pr[:, 0:Tn], AF.Exp, [pr, drv], [Rd], bias=drv[:, 4 + kc:5 + kc], scale=-1.0)
                    ACT(I, pi[:, 0:Tn], AF.Exp, [pi, drv], [Id], bias=drv[:, 12 + kc:13 + kc], scale=-1.0)

                def chain_tail(j):
                    kc = half * 4 + j
                    (X, Xd), (R, Rd), (I, Id), (A, Ad), (M, Md) = st_[j][0]
                    ACT(R, R, AF.Ln, [Rd], [Rd], bias=1.0)
                    ACT(R, R, AF.Exp, [Rd], [Rd], scale=-1.0)
                    ACT(A, R, AF.Exp, [Rd, drv], [Ad], scale=drv[:, 20 + kc:21 + kc])
                    TT("dve", M, A, A, OP.mult, [Ad], [Md])
                    ACT(I, I, AF.Ln, [Id], [Id], bias=1.0)
                    ACT(I, I, AF.Exp, [Id], [Id], scale=-1.0)
                    pc = st_[j][4]
                    STT(I, pc[:, 0:Tn], vcol("conv_b", kc), I, OP.add, OP.mult, [pc, vecs, Id], [Id])
                    ACT(M, M, AF.Ln, [Md], [Md], bias=1.0, scale=-1.0)
                    ACT(M, M, AF.Exp, [Md], [Md], scale=0.5)
                    TT("dve", I, I, M, OP.mult, [Id, Md], [Id])
                    SCAN(X, A, I, c.hst.t[:, kc:kc + 1], [Ad, Id, (c.hst, kc), Xd], [Xd])
                    CP("dve", c.hst.t[:, kc:kc + 1], X[:, Tn - 1:Tn], [Xd], [(c.hst, kc)])
                    TT("dve", c.B2.t[:, kc * Tn:(kc + 1) * Tn], X, c.B1.t[:, GO + j * Tn: GO + (j + 1) * Tn], OP.mult,
                       [Xd, (c.B1, ("gl", j))], [(c.B2, kc)])

                build_diag(0)
                build_diag(1)
                prep_conv(0)
                gates(0)
                for j in range(4):
                    if j + 2 < 4:
                        build_diag(j + 2)
                    if j + 1 < 4:
                        prep_conv(j + 1)
                    chain_head(j)
                    if j + 1 < 4:
                        gates(j + 1)
                    qk_head(c, j)
                    chain_tail(j)
        for c in ctxs:
            fence(c.F2)
        for c in ctxs:
            fence(c.B1)
        for c in ctxs:
            Tn, tw = c.T, c.tw
            KO = 4 * Tn
            KTO = 8 * Tn
            for t in range(c.nt):
                for h in range(4):
                    TR(psb, psb[0:tw, h * 128:(h + 1) * 128], c.B1.t[:, KO + h * Tn + t * tw: KO + h * Tn + (t + 1) * tw], identb[:, :],
                       [(c.B1, ("k", h)), identb])
                CP("act", c.B1.t[0:tw, KTO + t * 512: KTO + (t + 1) * 512], psb[0:tw, 0:512], [psb], [(c.B1, ("kt", t))])
        for vb in range(2):
            sv_, wv_ = wload(w_in_ab[:, 3072 + vb * 512:3072 + (vb + 1) * 512], 128, 8, 512)
            for c in ctxs:
                Tn, tw = c.T, c.tw
                VO = 8 * Tn + c.nt * 512
                for t in range(c.nt):
                    ps = ps_next()
                    for k in range(8):
                        MM(ps, ps[0:tw, :], c.uT.t[:, k * Tn + t * tw: k * Tn + (t + 1) * tw], wv_[:, k, :], k == 0, k == 7, [sv_, (c.uT, k)])
                    CP("act" if t % 2 == 0 else "dve", c.B1.t[0:tw, VO + t * 1024 + vb * 512: VO + t * 1024 + (vb + 1) * 512], ps[0:tw, :],
                       [ps], [(c.B1, ("v", t))])
        for c in ctxs:
            Tn, tw, C = c.T, c.tw, c.C
            KO, KTO = 4 * Tn, 8 * Tn
            VO = 8 * Tn + c.nt * 512
            trm = tri4 if not c.s else tri4s
            def a_mask(t):
                pa = ps_next()
                for h in range(4):
                    MM(pa, pa[0:C, h * C:(h + 1) * C], c.B1.t[:, KO + h * Tn + t * C: KO + h * Tn + (t + 1) * C],
                       c.B1.t[:, h * Tn + t * C: h * Tn + (t + 1) * C], True, True, [(c.B1, ("k", h)), (c.B1, ("q", h))])
                at = c.atb[t % 2]
                TT("dve", at[0:C, 0:4 * C], pa[0:C, 0:4 * C], trm[0:C, 0:4 * C], OP.mult, [pa, trm], [at])
                return at
            at = a_mask(0)
            for t in range(c.nt):
                for hp in range(2):
                    po = ps_next()
                    for hh in range(2):
                        h = hp * 2 + hh
                        for j in range(2):
                            c0 = (hh * 2 + j) * C
                            MM(po, po[:, c0:c0 + C], c.B1.t[0:C, VO + t * 1024 + h * 256 + j * 128: VO + t * 1024 + h * 256 + (j + 1) * 128],
                               at[0:C, h * C:(h + 1) * C], (hh == 0 and j == 0), False, [(c.B1, ("v", t)), at], skip=True)
                    for hh in range(2):
                        h = hp * 2 + hh
                        for j in range(2):
                            c0 = (hh * 2 + j) * C
                            MM(po, po[:, c0:c0 + C], c.Sbf.t[:, h * 256 + j * 128: h * 256 + (j + 1) * 128],
                               c.B1.t[:, h * Tn + t * C: h * Tn + (t + 1) * C], False, True, [(c.Sbf, h), (c.B1, ("q", h))], skip=True)
                    CP("act", v3(c.F2, 0, 8, Tn)[:, hp * 4:(hp + 1) * 4, t * C:(t + 1) * C],
                       po[:, 0:4 * C].rearrange("p (k t) -> p k t", t=C), [po], [(c.F2, ("o", hp))])
                pds = []
                for hp in range(2):
                    pd = ps_next()
                    pds.append(pd)
                    for hh in range(2):
                        h = hp * 2 + hh
                        MM(pd, pd[:, hh * 256:(hh + 1) * 256], c.B1.t[0:C, KTO + t * 512 + h * 128: KTO + t * 512 + (h + 1) * 128],
                           c.B1.t[0:C, VO + t * 1024 + h * 256: VO + t * 1024 + (h + 1) * 256], True, True,
                           [(c.B1, ("kt", t)), (c.B1, ("v", t))])
                if t + 1 < c.nt:
                    at = a_mask(t + 1)
                for hp in range(2):
                    pd = pds[hp]
                    for hh in range(2):
                        h = hp * 2 + hh
                        eb = c.ebl.t[:, h * c.nt + t: h * c.nt + t + 1]
                        Sh = c.S.t[:, h * 256:(h + 1) * 256]
                        TT("dve", Sh, Sh, pd[:, hh * 256:(hh + 1) * 256], OP.add, [(c.S, h), pd], [(c.S, h)])
                        ACT(c.Sbf.t[:, h * 256:(h + 1) * 256], Sh, AF.Copy, [(c.S, h), (c.ebl, h)], [(c.Sbf, h)], scale=eb)
                        TS_("dve", Sh, Sh, eb, None, OP.mult, None, [(c.S, h), (c.ebl, h)], [(c.S, h)])
            pss_ = []
            for h in range(4):
                ps = ps_next()
                pss_.append(ps)
                for j in range(2):
                    sq = c.tb[c.tbi % 2]
                    c.tbi += 1
                    ACT(sq[:, :], c.F2.t[:, (2 * h + j) * Tn:(2 * h + j + 1) * Tn], AF.Square, [c.F2], [sq])
                    MM(ps, ps[:, 0:Tn], ones_bf[:, :], sq[:, :], j == 0, j == 1, [ones_bf, sq])
            for h in range(4):
                ACT(c.tf[h][:, :], pss_[h][:, 0:Tn], AF.Ln, [pss_[h]], [c.tf[h]], bias=EPS, scale=1.0 / 256)
            for h in range(4):
                ACT(c.tf[h][:, :], c.tf[h][:, :], AF.Exp, [c.tf[h]], [c.tf[h]], scale=-0.5)
            for h in range(4):
                for j in range(2):
                    hj = 2 * h + j
                    STT(c.F2.t[:, hj * Tn:(hj + 1) * Tn], c.F2.t[:, hj * Tn:(hj + 1) * Tn], vcol("gla_norm", j), c.tf[h][:, :],
                        OP.mult, OP.mult, [c.F2, c.tf[h], vecs], [(c.F2, ("o", h // 2))])
        for gbk in range(2):
            sb_, wb_ = wload(w_in_ab[:, 4096 + gbk * 512:4096 + (gbk + 1) * 512], 128, 8, 512)
            for c in ctxs:
                Tn = c.T
                for j in range(4):
                    hj = gbk * 4 + j
                    ps = ps_next()
                    for k in range(8):
                        MM(ps, ps[:, 0:Tn], wb_[:, k, j * 128:(j + 1) * 128], c.uT.t[:, k * Tn:(k + 1) * Tn], k == 0, k == 7, [sb_, (c.uT, k)])
                    tmp = c.tf[4 + j % 2]
                    ACT(tmp[:, :], ps[:, 0:Tn], AF.Silu, [ps], [tmp])
                    TT("dve", c.B2.t[:, (8 + hj) * Tn:(9 + hj) * Tn], tmp[:, :], c.F2.t[:, hj * Tn:(hj + 1) * Tn], OP.mult,
                       [tmp, c.F2], [(c.B2, 8 + hj)])
        for blk in range(4):
            so_, wo_ = wload(w_out_ab[:, blk * 256:(blk + 1) * 256], 128, 16, 256)
            for c in ctxs:
                Tn = c.T
                for dl in range(2):
                    dc = blk * 2 + dl
                    ps = ps_next()
                    for k in range(16):
                        MM(ps, ps[:, 0:Tn], wo_[:, k, dl * 128:(dl + 1) * 128], c.B2.t[:, k * Tn:(k + 1) * Tn], k == 0, k == 15, [so_, (c.B2, k)])
                    evac_sq(c, ps, dc)
        for c in ctxs:
            postnorm_add(c, "mix_post0")
    P_wz = P.sbuf("wz", [128, 8 * 16], BF16)
    def out_states(c, idx):
        for half in range(2):
            ps = ps_next()
            for kk in range(4):
                kc = half * 4 + kk
                TR(ps, ps[0:3, kk * 128:(kk + 1) * 128], c.convo.t[:, kc * 3:(kc + 1) * 3], ident[:, :], [c.convo, ident])
            CP("act", c.F2.t[0:3, half * 512:(half + 1) * 512], ps[0:3, 0:512], [ps], [c.F2])
        DMA("sp", o_conv[idx][:, :], c.F2.t[0:3, 0:1024], [c.F2], [], "O" + c.tag)
        ps = ps_next()
        TR(ps, ps[0:8, 0:128], c.hst.t[:, 0:8], ident[:, :], [c.hst, ident])
        hh = c.Se
        CP("act", hh[0:8, 0:128], ps[0:8, 0:128], [ps], [hh])
        DMA("sp", o_h[idx][:, :], hh[0:8, 0:128], [hh], [], "O" + c.tag)
        DMA("sp", o_S[idx].rearrange("h k v -> k h v"), c.S.t[:, :].rearrange("p (h v) -> p h v", v=256), [c.S], [], "O" + c.tag)
    def mla_proj(ctxs, g):
        for c in ctxs:
            prenorm(c, "mix_pre1")
            fence(c.B1)
        s1, w1 = wload(w_in_c[:, 0:384], 128, 8, 384)
        s2, w2 = wload(w_in_c[:, 384:768], 128, 8, 384)
        for c in ctxs:
            Tn, tw, pb = c.T, c.tw, c.pb
            for k3 in range(3):
                ps = ps_next()
                for k in range(8):
                    MM(ps, ps[:, 0:Tn], w1[:, k, k3 * 128:(k3 + 1) * 128], c.uT.t[:, k * Tn:(k + 1) * Tn], k == 0, k == 7, [s1, (c.uT, k)])
                CP("act", c.F2.t[:, k3 * Tn:(k3 + 1) * Tn], ps[:, 0:Tn], [ps], [(c.F2, ("cq", k3))])
            for k2 in range(2):
                ps = ps_next()
                for k in range(8):
                    MM(ps, ps[:, 0:Tn], w2[:, k, k2 * 128:(k2 + 1) * 128], c.uT.t[:, k * Tn:(k + 1) * Tn], k == 0, k == 7, [s2, (c.uT, k)])
                CP("act", c.F2.t[:, (3 + k2) * Tn:(4 + k2) * Tn], ps[:, 0:Tn], [ps], [(c.F2, ("ckv", k2))])
            if SUB < 1:
                continue
            pA = ps_next()
            pB = ps_next()
            if not c.s:
                cA, cB, M_ = (192, 288), (288, 384), 96
            else:
                cA, cB, M_ = (256, 288), (352, 384), 32
            for k in range(8):
                MM(pA, pA[0:M_, 0:Tn], w2[:, k, cA[0]:cA[1]], c.uT.t[:, k * Tn:(k + 1) * Tn], k == 0, k == 7, [s2, (c.uT, k)])
            for k in range(8):
                MM(pB, pB[0:M_, 0:Tn], w2[:, k, cB[0]:cB[1]], c.uT.t[:, k * Tn:(k + 1) * Tn], k == 0, k == 7, [s2, (c.uT, k)])
            if not c.s:
                DMA("sp", c.rope.t[64:96, :].rearrange("p (a t) -> p a t", t=Tn),
                    ropeP_d.rearrange("p (a t) -> p a t", t=2048)[:, :, g * T:(g + 1) * T], [], [c.rope], "R" + c.tag)
            t1, t2 = c.tf[0], c.tf[1]
            kpr = c.F2.t[pb:pb + 32, 5 * Tn:6 * Tn]
            TT("dve", t1[pb:pb + 32, :], pA[pb:pb + 32, 0:Tn], c.rope.t[pb:pb + 32, 0:Tn], OP.mult, [pA, c.rope], [t1])
            TT("dve", t2[pb:pb + 32, :], pB[pb:pb + 32, 0:Tn], c.rope.t[pb:pb + 32, Tn:2 * Tn], OP.mult, [pB, c.rope], [t2])
            TT("dve", kpr, t1[pb:pb + 32, :], t2[pb:pb + 32, :], OP.add, [t1, t2], [(c.F2, "kpr")])
            if not c.s:
                for i in range(2):
                    CP("act", KT[i].t[64:96, g * T:(g + 1) * T], kpr, [(c.F2, "kpr")], [(KT[i], "pe")])
            else:
                CP("act", c.kprb.t[0:32, 0:Tn], kpr, [(c.F2, "kpr")], [c.kprb])
            if SUB < 2:
                continue
            rms(c, c.F2, 0, 3, 384)
            for k3 in range(3):
                STT(c.uT.t[:, k3 * Tn:(k3 + 1) * Tn], c.F2.t[:, k3 * Tn:(k3 + 1) * Tn], vcol("q_norm", k3), c.rstd[:, :], OP.mult, OP.mult,
                    [(c.F2, ("cq", k3)), c.rstd, vecs], [c.uT])
            rms(c, c.F2, 3 * Tn, 2, 256)
            for k2 in range(2):
                STT(c.F2.t[:, (3 + k2) * Tn:(4 + k2) * Tn], c.F2.t[:, (3 + k2) * Tn:(4 + k2) * Tn], vcol("kv_norm", k2), c.rstd[:, :],
                    OP.mult, OP.mult, [(c.F2, ("ckv", k2)), c.rstd, vecs], [(c.F2, ("ckv", k2))])
                if not c.s:
                    CP("act", ckvnb.t[:, k2 * 2048 + g * T: k2 * 2048 + (g + 1) * T], c.F2.t[:, (3 + k2) * Tn:(4 + k2) * Tn],
                       [(c.F2, ("ckv", k2))], [(ckvnb, g)])
                else:
                    CP("act", c.ckvb.t[:, k2 * Tn:(k2 + 1) * Tn], c.F2.t[:, (3 + k2) * Tn:(4 + k2) * Tn], [(c.F2, ("ckv", k2))], [c.ckvb])
            if SUB < 3:
                continue
            idx = 1 if c.s else 0
            for t in range(c.nt):
                ps = ps_next()
                for k2 in range(2):
                    TR(ps, ps[0:tw, k2 * 128:(k2 + 1) * 128], c.F2.t[:, (3 + k2) * Tn + t * tw:(3 + k2) * Tn + (t + 1) * tw], ident[:, :],
                       [(c.F2, ("ckv", k2)), ident])
                if KPEOUT:
                    MM(ps, ps[0:tw, 256:288], c.F2.t[pb:pb + 32, 5 * Tn + t * tw: 5 * Tn + (t + 1) * tw], ident[pb:pb + 32, pb:pb + 32],
                       True, True, [(c.F2, "kpr"), ident])
                st = c.ost[t % 2]
                CP("act", st[0:tw, 0:288], ps[0:tw, 0:288], [ps], [st])
                r0 = (g * T if not c.s else 0) + t * tw
                if OUTV >= 2:
                    DMA("sp", o_ckv[idx][r0:r0 + tw, :], st[0:tw, 0:256], [st], [], "O" + c.tag)
                if OUTV >= 3:
                    DMA("sp", o_kpe[idx][r0:r0 + tw, :], st[0:tw, 256:288], [st], [], "O" + c.tag)
                if c.s:
                    CP("dve", c.ckvtok.t[0:tw, 0:256], ps[0:tw, 0:256], [ps], [c.ckvtok])
        if SUB < 4:
            return
        for c in ctxs:
            fence(c.B1)
        for hf in range(4):
            sa, wcat = wload(w_uq_cat[:, hf * 768:(hf + 1) * 768], 128, 3, 768)
            sb2 = sa
            wa_ = wcat[:, :, 0:384]
            wb2 = wcat[:, :, 384:768]
            for c in ctxs:
                Tn = c.T
                for hl in range(4):
                    h = hf * 4 + hl
                    if not c.s:
                        pA = ps_next()
                        pB = ps_next()
                        for k in range(3):
                            MM(pA, pA[0:96, 0:Tn], wa_[:, k, hl * 96:(hl + 1) * 96], c.uT.t[:, k * Tn:(k + 1) * Tn], k == 0, k == 2, [sa, c.uT])
                        for k in range(3):
                            MM(pB, pB[0:96, 0:Tn], wb2[:, k, hl * 96:(hl + 1) * 96], c.uT.t[:, k * Tn:(k + 1) * Tn], k == 0, k == 2, [sb2, c.uT])
                        CP("act", c.B1.t[0:64, h * Tn:(h + 1) * Tn], pA[0:64, 0:Tn], [pA], [(c.B1, ("Q", h))])
                        t1, t2 = c.tf[0], c.tf[1]
                        TT("dve", t1[64:96, :], pA[64:96, 0:Tn], c.rope.t[64:96, 0:Tn], OP.mult, [pA, c.rope], [t1])
                        TT("dve", t2[64:96, :], pB[64:96, 0:Tn], c.rope.t[64:96, Tn:2 * Tn], OP.mult, [pB, c.rope], [t2])
                        TT("dve", c.B1.t[64:96, h * Tn:(h + 1) * Tn], t1[64:96, :], t2[64:96, :], OP.add, [t1, t2], [(c.B1, ("Q", h))])
                    else:
                        pq = ps_next()
                        for k in range(3):
                            MM(pq, pq[0:64, 0:16], wa_[:, k, hl * 96:hl * 96 + 64], c.uT.t[:, k * Tn:(k + 1) * Tn], k == 0, k == 2, [sa, c.uT])
                        for k in range(3):
                            MM(pq, pq[0:32, 16:32], wa_[:, k, hl * 96 + 64:hl * 96 + 96], c.uT.t[:, k * Tn:(k + 1) * Tn], k == 0, k == 2, [sa, c.uT])
                        for k in range(3):
                            MM(pq, pq[0:32, 32:48], wb2[:, k, hl * 96 + 64:hl * 96 + 96], c.uT.t[:, k * Tn:(k + 1) * Tn], k == 0, k == 2, [sb2, c.uT])
                        CP("act", c.qn.t[0:64, h * 16:(h + 1) * 16], pq[0:64, 0:16], [pq], [c.qn])
                        t1, t2 = c.tf[0], c.tf[1]
                        TT("dve", t1[0:32, :], pq[0:32, 16:32], c.rope.t[0:32, 0:Tn], OP.mult, [pq, c.rope], [t1])
                        TT("dve", t2[0:32, :], pq[0:32, 32:48], c.rope.t[0:32, Tn:2 * Tn], OP.mult, [pq, c.rope], [t2])
                        TT("dve", c.qpe.t[0:32, h * 16:(h + 1) * 16], t1[0:32, :], t2[0:32, :], OP.add, [t1, t2], [c.qpe])
    def mla_attn_prompt(c, g, suk, wuk, suv, wuv):
        Tn = c.T
        nkb = g + 1
        nkt = 4 * (g + 1)
        rot_n[0] = 3
        Vp4 = Vp.t[:, 0:2048].rearrange("p (k m) -> p k m", m=128)

        def kt_recompute(h):
            hh = h % 2
            for kb in range(nkb):
                ps = ps_next()
                for cc in range(2):
                    MM(ps, ps[0:64, 0:512], wuk[:, cc, h * 64:(h + 1) * 64], ckvnb.t[:, cc * 2048 + kb * 512: cc * 2048 + (kb + 1) * 512],
                       cc == 0, cc == 1, [suk, (ckvnb, kb)])
                CP("dve", KT[hh].t[0:64, kb * 512:(kb + 1) * 512], ps[0:64, 0:512], [ps], [(KT[hh], ("n", kb))])

        def v_recompute(hp):
            for k4 in range(nkt // 4):
                ps = ps_next()
                for kk in range(4):
                    kt = k4 * 4 + kk
                    for cc in range(2):
                        MM(ps, ps[:, kk * 128:(kk + 1) * 128], ckvnb.t[:, cc * 2048 + kt * 128: cc * 2048 + (kt + 1) * 128],
                           wuv[:, cc, hp * 128:(hp + 1) * 128], cc == 0, cc == 1, [suv, (ckvnb, kt // 4)])
                CP("dve", Vp4[:, k4 * 4:(k4 + 1) * 4, :], ps[:, :].rearrange("p (k m) -> p k m", m=128), [ps], [(Vp, k4)])

        kt_recompute(0)
        v_recompute(0)
        for h in range(16):
            hp, hh = h // 2, h % 2
            po = allb[3 + 2 * hh]
            pss = allb[4 + 2 * hh]

            def s_exp(kt):
                qlo = max(0, kt - 4 * g) * 128
                nq = Tn - qlo
                pS = ps_next()
                MM(pS, pS[:, 0:nq], KT[hh].t[0:96, kt * 128:(kt + 1) * 128], c.B1.t[0:96, h * Tn + qlo:(h + 1) * Tn], True, True,
                   [(KT[hh], ("n", kt // 4)), (KT[hh], "pe"), (c.B1, ("Q", h))])
                pt = pt_next()
                ACT(pt[:, 0:nq], pS[:, 0:nq], AF.Exp, [pS], [pt], scale=SM_SCALE)
                if kt >= 4 * g:
                    MEMSET("dve", pt[64:128, 0:64], 0.0, [pt])
                return pt, qlo, nq
            q_ = [s_exp(0)]
            if nkt > 1:
                q_.append(s_exp(1))
            for kt in range(nkt):
                pt, qlo, nq = q_.pop(0)
                if kt + 2 < nkt:
                    q_.append(s_exp(kt + 2))
                MM(po, po[:, qlo:Tn], Vp4[:, kt, :], pt[:, 0:nq], kt == 0, kt == nkt - 1, [(Vp, kt // 4), pt], skip=True)
                MM(pss, pss[:, qlo:Tn], ones_bf[:, :], pt[:, 0:nq], kt == 0, kt == nkt - 1, [ones_bf, pt], skip=True)
                if kt == 0 and h + 1 < 16:
                    kt_recompute(h + 1)
            r0 = hh * 64
            rl = c.tf[2 + hh]
            ACT(rl[r0:r0 + 64, :], pss[r0:r0 + 64, 0:Tn], AF.Ln, [pss], [rl])
            ACT(rl[r0:r0 + 64, :], rl[r0:r0 + 64, :], AF.Exp, [rl], [rl], scale=-1.0)
            TT("dve", c.B2.t[r0:r0 + 64, hp * Tn:(hp + 1) * Tn], po[r0:r0 + 64, 0:Tn], rl[r0:r0 + 64, :], OP.mult, [po, rl], [(c.B2, hp)])
            if hh == 1 and hp + 1 < 8:
                v_recompute(hp + 1)
        rot_n[0] = 5
    def mla_attn_sample(c, suk, wuk, suv, wuv):
        Tn = c.T
        skt, wkt = wload(w_ukT[:, :], 64, 1, 4096)
        ps = ps_next()
        for h in range(16):
            for cc in range(2):
                MM(ps, ps[:, cc * 256 + h * 16: cc * 256 + (h + 1) * 16], wkt[0:64, 0, h * 256 + cc * 128: h * 256 + (cc + 1) * 128],
                   c.qn.t[0:64, h * 16:(h + 1) * 16], True, True, [skt, c.qn])
        CP("act", c.qlat.t[:, :], ps[:, :], [ps], [c.qlat])
        olat, sums = acc[0], acc[1]
        first = True
        def s_part(lhs_c0, lhs_c1, lhs_pe, nk, rd):
            pS = ps_next()
            MM(pS, pS[0:nk, 0:256], lhs_c0, c.qlat.t[:, 0:256], True, False, rd + [c.qlat])
            MM(pS, pS[0:nk, 0:256], lhs_c1, c.qlat.t[:, 256:512], False, False, rd + [c.qlat])
            MM(pS, pS[0:nk, 0:256], lhs_pe, c.qpe.t[0:32, :], False, True, rd + [c.qpe])
            pt = pt_next()
            ACT(pt[0:nk, 0:256], pS[0:nk, 0:256], AF.Exp, [pS], [pt], scale=SM_SCALE)
            return pt

        def pv_part(pt, tok_c0, tok_c1, nk, rd, last):
            nonlocal first
            MM(olat, olat[:, 0:256], tok_c0, pt[0:nk, 0:256], first, last, rd + [pt], skip=True)
            MM(olat, olat[:, 256:512], tok_c1, pt[0:nk, 0:256], False, last, rd + [pt], skip=True)
            MM(sums, sums[:, 0:256], ones_bf[0:nk, :], pt[0:nk, 0:256], first, last, [ones_bf, pt])
            first = False

        ct = c.ct[0]

        def prep(blk):
            cb = c.cb[blk % 2]
            DMA("pool", cb.t[:, :].rearrange("p (k c) -> p k c", c=288)[:, :, 0:256],
                cckv[blk * 512:(blk + 1) * 512, :].rearrange("(k p) c -> p k c", p=128), [], [cb], "CB%d" % (blk % 2))
            DMA("pool", cb.t[:, :].rearrange("p (k c) -> p k c", c=288)[:, :, 256:288],
                ckpe[blk * 512:(blk + 1) * 512, :].rearrange("(k p) c -> p k c", p=128), [], [cb], "CB%d" % (blk % 2))
            for cc in range(2):
                for kt in range(4):
                    TR(psb, psb[:, (cc * 4 + kt) * 128:(cc * 4 + kt + 1) * 128], cb.t[:, kt * 288 + cc * 128: kt * 288 + (cc + 1) * 128],
                       identb[:, :], [cb, identb])
            CP("act", ct.t[:, 0:1024], psb[:, 0:1024], [psb], [ct])
            for kt in range(4):
                TR(psb, psb[0:32, kt * 128:(kt + 1) * 128], cb.t[:, kt * 288 + 256: kt * 288 + 288], identb[:, :], [cb, identb])
            CP("dve", ct.t[0:32, 1024:1536], psb[0:32, 0:512], [psb], [ct])

        def s_blk(blk, kt):
            return s_part(ct.t[:, kt * 128:(kt + 1) * 128], ct.t[:, 512 + kt * 128:512 + (kt + 1) * 128],
                          ct.t[0:32, 1024 + kt * 128:1024 + (kt + 1) * 128], 128, [ct])

        prep(0)
        for blk in range(8):
            cb = c.cb[blk % 2]
            q_ = [s_blk(blk, 0), s_blk(blk, 1)]
            for kt in range(4):
                pt = q_.pop(0)
                if kt + 2 < 4:
                    q_.append(s_blk(blk, kt + 2))
                if kt == 1 and blk + 1 < 8:
                    prep(blk + 1)
                pv_part(pt, cb.t[:, kt * 288: kt * 288 + 128], cb.t[:, kt * 288 + 128: kt * 288 + 256], 128, [cb], False)
        CP("dve", c.ckvtokb.t[0:16, :], c.ckvtok.t[0:16, :], [c.ckvtok], [c.ckvtokb])
        pt = s_part(c.ckvb.t[:, 0:16], c.ckvb.t[:, 16:32], c.kprb.t[0:32, 0:16], 16, [c.ckvb, c.kprb])
        pv_part(pt, c.ckvtokb.t[0:16, 0:128], c.ckvtokb.t[0:16, 128:256], 16, [c.ckvtokb], True)
        rs = c.rs
        RECIP(rs.t[:, 0:256], sums[:, 0:256], [sums], [rs])
        for cc in range(2):
            TT("dve", c.olatn.t[:, cc * 256:(cc + 1) * 256], olat[:, cc * 256:(cc + 1) * 256], rs.t[:, 0:256], OP.mult, [olat, rs], [c.olatn])
        ps = ps_next()
        for h in range(16):
            hp = h // 2
            for cc in range(2):
                MM(ps, ps[:, h * 16:(h + 1) * 16], wuv[:, cc, hp * 128:(hp + 1) * 128], c.olatn.t[:, cc * 256 + h * 16: cc * 256 + (h + 1) * 16],
                   cc == 0, cc == 1, [suv, c.olatn])
        for hh in range(2):
            CP("act", c.B2.t[hh * 64:(hh + 1) * 64, 0:128].rearrange("p (k t) -> p k t", t=16),
               ps[hh * 64:(hh + 1) * 64, 0:256].rearrange("p (k h t) -> p k h t", h=2, t=16)[:, :, hh, :], [ps], [c.B2])
    def mla_out(ctxs):
        for blk in range(4):
            so_, wo_ = wload(w_out_c[:, blk * 256:(blk + 1) * 256], 128, 8, 256)
            for c in ctxs:
                Tn = c.T
                for dl in range(2):
                    dc = blk * 2 + dl
                    ps = ps_next()
                    for hp in range(8):
                        MM(ps, ps[:, 0:Tn], wo_[:, hp, dl * 128:(dl + 1) * 128], c.B2.t[:, hp * Tn:(hp + 1) * Tn], hp == 0, hp == 7, [so_, c.B2])
                    evac_sq(c, ps, dc)
        for c in ctxs:
            postnorm_add(c, "mix_post1")
    cp.ost = [cp.tf[2], cp.tf[3]]
    _ost = P.sbuf("ost", [16, 288], F32)
    cs.ost = [_ost, _ost]
    cs.kprb = P.sbuf("kprb", [32, 16], BF16)
    cs.ckvb = P.sbuf("ckvb", [128, 32], BF16)
    cs.ckvtok = P.sbuf("ckvtok", [16, 256], F32)
    cs.ckvtokb = P.sbuf("ckvtokb", [16, 256], BF16)
    cs.qn = P.sbuf("qn", [64, 256], BF16)
    cs.qpe = P.sbuf("qpe", [32, 256], BF16)
    cs.qlat = P.sbuf("qlat", [128, 512], BF16)
    cs.olatn = P.sbuf("olatn", [128, 512], BF16)
    cs.rs = P.sbuf("rs", [128, 256], F32)
    cs.cb = [P.sbuf("cb%d" % i, [128, 4 * 288], BF16) for i in range(2)]
    _ct = P.sbuf("ct0", [128, 1536], BF16)
    cs.ct = [_ct, _ct]
    for g in range(ngroups):
        ctxs = [cp] + ([cs] if g == 0 else [])
        load_x(cp, xp, g * T, prefetched=(g > 0))
        if g == 0:
            load_x(cs, xs, 0)
        if stage >= 1:
            mixer_ab(ctxs, g == NG - 1)
            if g == 0:
                out_states(cs, 1)
            if g == NG - 1:
                out_states(cp, 0)
        if stage >= 2:
            ffn(ctxs, 0)
        if stage >= 3:
            mla_proj(ctxs, g)
        if stage >= 4:
            sukv, wukv = wload(w_ukv[:, :], 128, 2, 2048)
            suk, wuk = sukv, wukv[:, :, 0:1024]
            suv, wuv = sukv, wukv[:, :, 1024:2048]
            mla_attn_prompt(cp, g, suk, wuk, suv, wuv)
            if g == 0 and stage >= 5:
                mla_attn_sample(cs, suk, wuk, suv, wuv)
        if stage >= 6:
            mla_out(ctxs)
        if stage >= 7:
            if g + 1 < ngroups:
                load_x_dma(cp, xp, (g + 1) * T)
            ffn(ctxs, 1)
        store_y(cp, y_p, g * T)
        if g == 0:
            store_y(cs, y_s, 0)
    P.emit()
    P.close()
    return nc
_NC = None
SUB = int(os.environ.get('SUB', '99'))
KPEOUT = int(os.environ.get('KPEOUT', '1'))
OUTV = int(os.environ.get('OUTV', '3'))
def _fm(v, n):
    return np.ascontiguousarray(np.asarray(v, np.float32).reshape(n, 128).T)
def prep_inputs(inp, cores=range(8)):
    f = lambda k: np.asarray(inp[k], np.float32)
    w_in_c = f("w_in_c")[0]
    sw = np.concatenate([np.arange(16, 32), np.arange(0, 16)])
    w_in_c_ext = np.ascontiguousarray(np.concatenate(
        [w_in_c, w_in_c[:, 576:640], w_in_c[:, 640:672][:, sw]], axis=1))
    w_uq = f("w_uq")[0]
    idx = np.arange(1536).reshape(16, 96).copy()
    idx[:, 64:96] = idx[:, 64:96][:, sw]
    w_uq_sw = np.ascontiguousarray(w_uq[:, idx.reshape(-1)])
    w_uk = f("w_uk")[0]
    w_ukT = np.ascontiguousarray(w_uk.transpose(2, 1, 0).reshape(64, 16 * 256))
    tri = np.triu(np.ones((128, 128), np.float32))
    tri4 = np.ascontiguousarray(np.tile(tri, (1, 4)))
    tri4s = np.zeros((128, 64), np.float32)
    tri4s[:16] = np.tile(np.triu(np.ones((16, 16), np.float32)), (1, 4))
    rmask = np.ones((128, 512), np.float32)
    rmask[:, ::128] = 0.0
    half = 16
    inv = (10000.0 ** (-np.arange(half, dtype=np.float32) / np.float32(half))).astype(np.float32)
    def rope_tab(pos):
        ang = pos.astype(np.float32)[None, :] * inv[:, None]
        cos = np.cos(ang).astype(np.float32)
        sin = np.sin(ang).astype(np.float32)
        c32 = np.concatenate([cos, cos], 0)
        s32 = np.concatenate([-sin, sin], 0)
        return np.ascontiguousarray(np.concatenate([c32, s32], 1))
    ropeP = rope_tab(np.arange(2048))
    ropeS = rope_tab(4096 + np.arange(TS))
    shared = {
        "w_in_ab": f("w_in_ab")[0], "lru_wa": f("lru_w_a")[0].reshape(1024, 128), "lru_wx": f("lru_w_x")[0].reshape(1024, 128),
        "w_gate": f("gla_w_gate")[0], "w_out_ab": f("w_out_ab")[0], "w_in_c": w_in_c_ext, "w_uq_cat": np.concatenate([np.concatenate([w_uq[:, b * 384:(b + 1) * 384], w_uq_sw[:, b * 384:(b + 1) * 384]], axis=1) for b in range(4)], axis=1),
        "w_ukv": np.concatenate([w_uk.reshape(256, 1024), f("w_uv")[0].reshape(256, 1024)], axis=1), "w_ukT": w_ukT, "w_out_c": f("w_out_c")[0],
        "ffn_wg": f("ffn_w_gate").reshape(2 * D, DFF), "ffn_wu": f("ffn_w_up").reshape(2 * D, DFF), "ffn_wd": f("ffn_w_down").reshape(2 * DFF, D),
        "ident": np.eye(128, dtype=np.float32), "tri4": tri4, "tri4s": tri4s, "rmask": rmask, "ropeP": ropeP, "ropeS": ropeS,
    }
    shared = {k: np.ascontiguousarray(v, dtype=np.float32) for k, v in shared.items()}
    vparts = {}
    for l in range(2):
        vparts["mix_pre%d" % l] = _fm(f("norm_mix_pre")[l], 8)
        vparts["mix_post%d" % l] = _fm(f("norm_mix_post")[l], 8)
        vparts["ffn_pre%d" % l] = _fm(f("norm_ffn_pre")[l], 8)
        vparts["ffn_post%d" % l] = _fm(f("norm_ffn_post")[l], 8)
    cw = f("conv_w_a")[0]
    vparts["conv_w"] = np.concatenate([_fm(cw[j], 8) for j in range(4)], 1)
    vparts["conv_b"] = _fm(f("conv_b_a")[0], 8)
    vparts["lru_ba"] = _fm(f("lru_b_a")[0], 8)
    vparts["lru_bx"] = _fm(f("lru_b_x")[0], 8)
    vparts["lam"] = _fm(f("lru_lambda")[0], 8)
    vparts["b_gate"] = _fm(f("gla_b_gate")[0], 4)
    vparts["gla_norm"] = _fm(f("gla_norm")[0], 2)
    vparts["q_norm"] = _fm(f("mla_q_norm")[0], 3)
    vparts["kv_norm"] = _fm(f("mla_kv_norm")[0], 2)
    in_maps = []
    for c in cores:
        vp = dict(vparts)
        vp["h0"] = _fm(f("state_lru_h")[0, c], 8)
        cs_ = f("state_conv_a")[0, c]
        vp["convs"] = np.ascontiguousarray(cs_.reshape(3, 8, 128).transpose(2, 1, 0).reshape(128, 24))
        vecs = np.ascontiguousarray(np.concatenate([vp[n] for n, _ in _VNAMES], 1), dtype=np.float32)
        m = dict(shared)
        m.update({
            "xp": np.ascontiguousarray(f("x_prompt")[c]), "xs": np.ascontiguousarray(f("x_sample")[c]),
            "gla_s": np.ascontiguousarray(f("state_gla_S")[0, c]), "cckv": np.ascontiguousarray(f("cache_mla_ckv")[0, c]),
            "ckpe": np.ascontiguousarray(f("cache_mla_kpe")[0, c]), "vecs": vecs,
        })
        in_maps.append(m)
    return in_maps
def kernel(**inp):
    global _NC
    if _NC is None:
        _NC = build()
    nc = _NC
    in_maps = prep_inputs(inp)
    res = run_bass_kernel_spmd(nc, in_maps, core_ids=list(range(8)))
    R = res.results
    def st(name):
        return np.stack([np.asarray(R[c][name], np.float32) for c in range(8)], 0)
    y_prompt = st("y_p")
    y_sample = st("y_s")
    outs = [y_prompt, y_sample]
    for pfx in ("p", "s"):
        outs.append(st(pfx + "_conv")[None])
        outs.append(st(pfx + "_h").reshape(8, 1024)[None])
        outs.append(st(pfx + "_S")[None])
        outs.append(st(pfx + "_ckv")[None])
        outs.append(st(pfx + "_kpe")[None])
    return tuple(outs)
```
